# Optimizing a Trainium2 kernel written in Bass

```python
import math
import jax, jax.numpy as jnp
from jax import lax
import numpy as np

D_MODEL = 1024
BATCH = 4
SEQ = 8192
DEPTH = 4

N_MIXERS = 2
EPS = 1e-6
NEG_INF = -1e30
FORCE_SCORE = 1e9

N_HEADS = 16
HEAD_DIM = 64
N_KV_GROUPS = 4
HEADS_PER_GROUP = N_HEADS // N_KV_GROUPS
ATTN_WIDTH = N_HEADS * HEAD_DIM
KV_WIDTH = N_KV_GROUPS * HEAD_DIM
CMP_BLOCK = 32
CMP_STRIDE = 16
CMP_HIDDEN = 256
SEL_BLOCK = 64
SEL_TOPK = 16
WINDOW = 512
Q_BLOCK = 64
NSA_IN = ATTN_WIDTH + 6 * KV_WIDTH + 3 * N_HEADS + ATTN_WIDTH

SSM_WIDTH = D_MODEL
SSM_GROUP = 16
SSM_GROUPS = SSM_WIDTH // SSM_GROUP
SSM_STATE = 64
SCAN_CHUNK = 128
DT_MIN = 1e-3
DT_MAX = 1e-1

kernel_name = "nsa_s5_interleaved_hybrid"


def _rmsnorm(x, g):
    xf = x.astype(jnp.float32)
    y = xf * lax.rsqrt(jnp.mean(xf * xf, axis=-1, keepdims=True) + EPS) * g.astype(jnp.float32)
    return y.astype(x.dtype)


def _masked_softmax(s, mask):
    s = jnp.where(mask, s, NEG_INF)
    m = jnp.max(s, axis=-1, keepdims=True)
    p = jnp.where(mask, jnp.exp(s - m), 0.0)
    return p / jnp.maximum(jnp.sum(p, axis=-1, keepdims=True), 1e-30)


def _alibi_slopes():
    h = jnp.arange(1, N_HEADS + 1, dtype=jnp.float32)
    return jnp.exp2(-8.0 * h / N_HEADS).reshape(N_KV_GROUPS, HEADS_PER_GROUP)


def _compress(k, cmp_idx, pe, w1, w2):
    bsz = k.shape[0]
    n_cmp = cmp_idx.shape[0]
    blk = k[:, cmp_idx] + pe[:, None, :]
    blk = jnp.transpose(blk, (0, 1, 3, 2, 4)).reshape(bsz, n_cmp, N_KV_GROUPS, CMP_BLOCK * HEAD_DIM)
    return jax.nn.silu(blk @ w1) @ w2


def _nsa_mixer(h, w_in, cmp_k_pe, cmp_k_w1, cmp_k_w2, cmp_v_pe, cmp_v_w1, cmp_v_w2, w_out):
    f32 = jnp.float32
    bsz, seq, _ = h.shape
    G, R = N_KV_GROUPS, HEADS_PER_GROUP
    n_cmp = (seq - CMP_BLOCK) // CMP_STRIDE + 1
    n_blocks = seq // SEL_BLOCK
    n_sel = min(SEL_TOPK, n_blocks)

    splits = [ATTN_WIDTH + i * KV_WIDTH for i in range(7)] + [ATTN_WIDTH + 6 * KV_WIDTH + 3 * N_HEADS]
    q, kc, vc, ks, vs, kw, vw, gates, z = jnp.split(h @ w_in, splits, axis=-1)
    q = q.reshape(bsz, seq, G, R, HEAD_DIM) * (HEAD_DIM ** -0.5)
    kc, vc, ks, vs, kw, vw = [a.reshape(bsz, seq, G, HEAD_DIM) for a in (kc, vc, ks, vs, kw, vw)]
    gates = jax.nn.sigmoid(gates.astype(f32)).reshape(bsz, seq, 3, G, R)
    slopes = _alibi_slopes()

    cmp_start = jnp.arange(n_cmp) * CMP_STRIDE
    cmp_idx = cmp_start[:, None] + jnp.arange(CMP_BLOCK)[None, :]
    k_cmp = _compress(kc, cmp_idx, cmp_k_pe, cmp_k_w1, cmp_k_w2)
    v_cmp = _compress(vc, cmp_idx, cmp_v_pe, cmp_v_w1, cmp_v_w2)
    cmp_end = cmp_start + CMP_BLOCK - 1
    cmp_center = cmp_start.astype(f32) + 0.5 * (CMP_BLOCK - 1)

    sel_start = jnp.arange(n_blocks) * SEL_BLOCK
    overlap = ((cmp_start[:, None] <= sel_start[None, :] + SEL_BLOCK - 1)
               & (cmp_end[:, None] >= sel_start[None, :])).astype(f32)

    ks_blk = jnp.transpose(ks.reshape(bsz, n_blocks, SEL_BLOCK, G, HEAD_DIM), (0, 3, 1, 2, 4))
    vs_blk = jnp.transpose(vs.reshape(bsz, n_blocks, SEL_BLOCK, G, HEAD_DIM), (0, 3, 1, 2, 4))
    gather = jax.vmap(jax.vmap(lambda kb, ix: kb[ix]))

    kw_pad = jnp.pad(kw, ((0, 0), (WINDOW, 0), (0, 0), (0, 0)))
    vw_pad = jnp.pad(vw, ((0, 0), (WINDOW, 0), (0, 0), (0, 0)))

    def block(qi):
        q0 = qi * Q_BLOCK
        t = q0 + jnp.arange(Q_BLOCK, dtype=jnp.int32)
        tf = t.astype(f32)
        qb = lax.dynamic_slice_in_dim(q, q0, Q_BLOCK, axis=1)
        gb = lax.dynamic_slice_in_dim(gates, q0, Q_BLOCK, axis=1)

        s_c = jnp.einsum('bqgrd,bngd->bgrqn', qb, k_cmp, preferred_element_type=f32)
        s_c = s_c - slopes[None, :, :, None, None] * (tf[:, None] - cmp_center[None, :])
        p_c = _masked_softmax(s_c, (cmp_end[None, :] <= t[:, None]))
        o_c = jnp.einsum('bgrqn,bngd->bqgrd', p_c, v_cmp)

        imp = jnp.einsum('bgrqn,nj->bgqj', p_c, overlap)
        j = jnp.arange(n_blocks)[None, :]
        jt = (t // SEL_BLOCK)[:, None]
        forced = (j == 0) | (j == jt) | (j == jt - 1)
        imp = jnp.where(forced, FORCE_SCORE, jnp.where(j > jt, -FORCE_SCORE, imp))
        _, sel_idx = lax.top_k(imp, n_sel)

        k_sel = gather(ks_blk, sel_idx)
        v_sel = gather(vs_blk, sel_idx).reshape(bsz, G, Q_BLOCK, n_sel * SEL_BLOCK, HEAD_DIM)
        pos_s = sel_idx[..., None] * SEL_BLOCK + jnp.arange(SEL_BLOCK)
        d_s = t[None, None, :, None, None] - pos_s
        s_s = jnp.einsum('bqgrd,bgqnkd->bgrqnk', qb, k_sel, preferred_element_type=f32)
        s_s = s_s - slopes[None, :, :, None, None, None] * d_s[:, :, None].astype(f32)
        s_s = s_s.reshape(bsz, G, R, Q_BLOCK, n_sel * SEL_BLOCK)
        m_s = (d_s >= 0)[:, :, None].reshape(bsz, G, 1, Q_BLOCK, n_sel * SEL_BLOCK)
        p_s = _masked_softmax(s_s, m_s)
        o_s = jnp.einsum('bgrqm,bgqmd->bqgrd', p_s, v_sel)

        kwb = lax.dynamic_slice_in_dim(kw_pad, q0, Q_BLOCK + WINDOW, axis=1)
        vwb = lax.dynamic_slice_in_dim(vw_pad, q0, Q_BLOCK + WINDOW, axis=1)
        pos_w = q0 - WINDOW + jnp.arange(Q_BLOCK + WINDOW, dtype=jnp.int32)
        d_w = t[:, None] - pos_w[None, :]
        m_w = (pos_w[None, :] >= 0) & (d_w >= 0) & (d_w < WINDOW)
        s_w = jnp.einsum('bqgrd,bkgd->bgrqk', qb, kwb, preferred_element_type=f32)
        s_w = s_w - slopes[None, :, :, None, None] * d_w.astype(f32)
        p_w = _masked_softmax(s_w, m_w)
        o_w = jnp.einsum('bgrqk,bkgd->bqgrd', p_w, vwb)

        return (gb[:, :, 0, :, :, None] * o_c + gb[:, :, 1, :, :, None] * o_s
                + gb[:, :, 2, :, :, None] * o_w)

    o = lax.map(block, jnp.arange(seq // Q_BLOCK, dtype=jnp.int32))
    o = jnp.moveaxis(o, 0, 1).reshape(bsz, seq, ATTN_WIDTH)
    o = (o * jax.nn.silu(z.astype(f32))).astype(h.dtype)
    return o @ w_out


def _complex_affine_combine(left, right):
    ar_i, ai_i, br_i, bi_i = left
    ar_j, ai_j, br_j, bi_j = right
    ar = ar_j * ar_i - ai_j * ai_i
    ai = ar_j * ai_i + ai_j * ar_i
    br = ar_j * br_i - ai_j * bi_i + br_j
    bi = ar_j * bi_i + ai_j * br_i + bi_j
    return ar, ai, br, bi


def _s5_mixer(h, w_in, log_dt, lambda_re, lambda_im, b_re, b_im, c_re, c_im, d_skip, w_glu, w_out):
    f32 = jnp.float32
    bsz, seq, _ = h.shape
    G, C, P = SSM_GROUPS, SSM_GROUP, SSM_STATE
    u, z = jnp.split(h @ w_in, 2, axis=-1)
    u = u.astype(f32).reshape(bsz, seq, G, C)

    dt = jnp.exp(log_dt.astype(f32))[:, None]
    lre = jnp.minimum(lambda_re.astype(f32), -1e-4)
    lim = lambda_im.astype(f32)
    mag = jnp.exp(lre * dt)
    ab_re = mag * jnp.cos(lim * dt)
    ab_im = mag * jnp.sin(lim * dt)
    den = lre * lre + lim * lim
    nr = ab_re - 1.0
    coef_re = (nr * lre + ab_im * lim) / den
    coef_im = (ab_im * lre - nr * lim) / den
    br32, bi32 = b_re.astype(f32), b_im.astype(f32)
    bb_re = coef_re[..., None] * br32 - coef_im[..., None] * bi32
    bb_im = coef_re[..., None] * bi32 + coef_im[..., None] * br32
    cr32, ci32 = c_re.astype(f32), c_im.astype(f32)

    n_chunks = seq // SCAN_CHUNK
    u_chunks = jnp.moveaxis(u.reshape(bsz, n_chunks, SCAN_CHUNK, G, C), 1, 0)
    a_re = jnp.broadcast_to(ab_re, (bsz, SCAN_CHUNK, G, P))
    a_im = jnp.broadcast_to(ab_im, (bsz, SCAN_CHUNK, G, P))

    def chunk_step(carry, u_c):
        x_re, x_im = carry
        bu_re = jnp.einsum('gpc,bkgc->bkgp', bb_re, u_c)
        bu_im = jnp.einsum('gpc,bkgc->bkgp', bb_im, u_c)
        ar, ai, sr, si = lax.associative_scan(_complex_affine_combine, (a_re, a_im, bu_re, bu_im), axis=1)
        xr = sr + ar * x_re[:, None] - ai * x_im[:, None]
        xi = si + ar * x_im[:, None] + ai * x_re[:, None]
        y = jnp.einsum('gcp,bkgp->bkgc', cr32, xr) - jnp.einsum('gcp,bkgp->bkgc', ci32, xi)
        return (xr[:, -1], xi[:, -1]), y

    x0 = jnp.zeros((bsz, G, P), f32)
    _, y = lax.scan(chunk_step, (x0, x0), u_chunks)
    y = jnp.moveaxis(y, 0, 1).reshape(bsz, seq, G, C) + d_skip.astype(f32) * u
    y = jax.nn.gelu(y.reshape(bsz, seq, SSM_WIDTH)).astype(h.dtype)
    ga, gb = jnp.split(y @ w_glu, 2, axis=-1)
    y = ga * jax.nn.sigmoid(gb)
    y = y * jax.nn.silu(z)
    return y @ w_out


def _normal(key, shape, scale):
    return jax.random.normal(key, shape, jnp.float32) * scale


def _nsa_params(key, p):
    ks = jax.random.split(key, 9)
    return {
        p + 'norm': 1.0 + _normal(ks[0], (D_MODEL,), 0.02),
        p + 'w_in': _normal(ks[1], (D_MODEL, NSA_IN), D_MODEL ** -0.5),
        p + 'cmp_k_pe': _normal(ks[2], (CMP_BLOCK, HEAD_DIM), 0.02),
        p + 'cmp_k_w1': _normal(ks[3], (CMP_BLOCK * HEAD_DIM, CMP_HIDDEN), (CMP_BLOCK * HEAD_DIM) ** -0.5),
        p + 'cmp_k_w2': _normal(ks[4], (CMP_HIDDEN, HEAD_DIM), CMP_HIDDEN ** -0.5),
        p + 'cmp_v_pe': _normal(ks[5], (CMP_BLOCK, HEAD_DIM), 0.02),
        p + 'cmp_v_w1': _normal(ks[6], (CMP_BLOCK * HEAD_DIM, CMP_HIDDEN), (CMP_BLOCK * HEAD_DIM) ** -0.5),
        p + 'cmp_v_w2': _normal(ks[7], (CMP_HIDDEN, HEAD_DIM), CMP_HIDDEN ** -0.5),
        p + 'w_out': _normal(ks[8], (ATTN_WIDTH, D_MODEL), ATTN_WIDTH ** -0.5),
    }


def _s5_params(key, p):
    ks = jax.random.split(key, 12)
    n = jnp.arange(SSM_STATE, dtype=jnp.float32)[None, :]
    return {
        p + 'norm': 1.0 + _normal(ks[0], (D_MODEL,), 0.02),
        p + 'w_in': _normal(ks[1], (D_MODEL, 2 * SSM_WIDTH), D_MODEL ** -0.5),
        p + 'log_dt': jax.random.uniform(ks[2], (SSM_GROUPS,), jnp.float32, math.log(DT_MIN), math.log(DT_MAX)),
        p + 'lambda_re': -0.5 + _normal(ks[3], (SSM_GROUPS, SSM_STATE), 0.01),
        p + 'lambda_im': math.pi * n + _normal(ks[4], (SSM_GROUPS, SSM_STATE), 0.01),
        p + 'b_re': _normal(ks[5], (SSM_GROUPS, SSM_STATE, SSM_GROUP), (2 * SSM_GROUP) ** -0.5),
        p + 'b_im': _normal(ks[6], (SSM_GROUPS, SSM_STATE, SSM_GROUP), (2 * SSM_GROUP) ** -0.5),
        p + 'c_re': _normal(ks[7], (SSM_GROUPS, SSM_GROUP, SSM_STATE), SSM_STATE ** -0.5),
        p + 'c_im': _normal(ks[8], (SSM_GROUPS, SSM_GROUP, SSM_STATE), SSM_STATE ** -0.5),
        p + 'd': _normal(ks[9], (SSM_GROUPS, SSM_GROUP), 1.0),
        p + 'w_glu': _normal(ks[10], (SSM_WIDTH, 2 * SSM_WIDTH), SSM_WIDTH ** -0.5),
        p + 'w_out': _normal(ks[11], (SSM_WIDTH, D_MODEL), SSM_WIDTH ** -0.5),
    }


def setup_inputs(seed: int = 0) -> dict:
    key = jax.random.key(seed)
    kx, k0, k1, k2, k3, kf = jax.random.split(key, 6)
    inputs = {'x': jax.random.normal(kx, (BATCH, SEQ, D_MODEL), jnp.float32)}
    inputs.update(_nsa_params(k0, 'l0_'))
    inputs.update(_s5_params(k1, 'l1_'))
    inputs.update(_nsa_params(k2, 'l2_'))
    inputs.update(_s5_params(k3, 'l3_'))
    inputs['final_norm'] = 1.0 + _normal(kf, (D_MODEL,), 0.02)
    return inputs


def reference(x,
              l0_norm, l0_w_in, l0_cmp_k_pe, l0_cmp_k_w1, l0_cmp_k_w2, l0_cmp_v_pe, l0_cmp_v_w1, l0_cmp_v_w2, l0_w_out,
              l1_norm, l1_w_in, l1_log_dt, l1_lambda_re, l1_lambda_im, l1_b_re, l1_b_im, l1_c_re, l1_c_im, l1_d, l1_w_glu, l1_w_out,
              l2_norm, l2_w_in, l2_cmp_k_pe, l2_cmp_k_w1, l2_cmp_k_w2, l2_cmp_v_pe, l2_cmp_v_w1, l2_cmp_v_w2, l2_w_out,
              l3_norm, l3_w_in, l3_log_dt, l3_lambda_re, l3_lambda_im, l3_b_re, l3_b_im, l3_c_re, l3_c_im, l3_d, l3_w_glu, l3_w_out,
              final_norm):
    nsa_layers = [
        (l0_norm, l0_w_in, l0_cmp_k_pe, l0_cmp_k_w1, l0_cmp_k_w2, l0_cmp_v_pe, l0_cmp_v_w1, l0_cmp_v_w2, l0_w_out),
        (l2_norm, l2_w_in, l2_cmp_k_pe, l2_cmp_k_w1, l2_cmp_k_w2, l2_cmp_v_pe, l2_cmp_v_w1, l2_cmp_v_w2, l2_w_out),
    ]
    s5_layers = [
        (l1_norm, l1_w_in, l1_log_dt, l1_lambda_re, l1_lambda_im, l1_b_re, l1_b_im, l1_c_re, l1_c_im, l1_d, l1_w_glu, l1_w_out),
        (l3_norm, l3_w_in, l3_log_dt, l3_lambda_re, l3_lambda_im, l3_b_re, l3_b_im, l3_c_re, l3_c_im, l3_d, l3_w_glu, l3_w_out),
    ]
    h = x
    for i in range(DEPTH):
        if i % N_MIXERS == 0:
            params = nsa_layers[i // N_MIXERS]
            y = _nsa_mixer(_rmsnorm(h, params[0]), *params[1:])
        else:
            params = s5_layers[i // N_MIXERS]
            y = _s5_mixer(_rmsnorm(h, params[0]), *params[1:])
        h = h + y.astype(h.dtype)
    return _rmsnorm(h, final_norm)
```

```python
import math
from contextlib import ExitStack
import numpy as np
import ml_dtypes
import concourse.bass as bass
import concourse.mybir as mybir
from concourse.bass_utils import run_bass_kernel_spmd

F32 = mybir.dt.float32
BF16 = mybir.dt.bfloat16
I32 = mybir.dt.int32
AF = mybir.ActivationFunctionType
ALU = mybir.AluOpType
AX = mybir.AxisListType
NPBF = ml_dtypes.bfloat16

S = 8192
D = 1024
NH = 16
DH = 64
NG = 4
NT = S // 128
NC = S // 512
NSA_IN = 3632
EPS = 1e-6
MASKV = -30000.0

ENGS = ['pe', 'act', 'dve', 'pool', 'sp']
KRING = 8
SAME_ENGINE_SYNC = {'act': True, 'dve': True, 'pool': True, 'pe': False, 'sp': False}


class Prog:
    def __init__(self, nc):
        self.nc = nc
        self.q = {e: [] for e in ENGS}
        self.ncomp = {e: 0 for e in ENGS}
        self.ndma = {e: 0 for e in ENGS}
        self.lastw = {}
        self.readers = {}

    def add(self, eng, fn, r=(), w=(), dma=False, nosync=False):
        deps = {}
        dd = set()

        def addtok(t):
            if t is None:
                return
            if t[0] == 'c':
                if deps.get(t[1], 0) < t[2]:
                    deps[t[1]] = t[2]
            else:
                dd.add(t)

        for k in r:
            addtok(self.lastw.get(k))
        for k in w:
            addtok(self.lastw.get(k))
            rd = self.readers.get(k)
            if rd:
                for e2, n2 in rd[0].items():
                    addtok(('c', e2, n2))
                for t in rd[1]:
                    addtok(t)
        if dma:
            tok = ('d', eng, self.ndma[eng])
            self.ndma[eng] += 1
        else:
            self.ncomp[eng] += 1
            tok = ('c', eng, self.ncomp[eng])
        if nosync:
            deps.pop(eng, None)
        self.q[eng].append(('op', fn, deps, dd, tok))
        for k in w:
            self.lastw[k] = tok
            self.readers[k] = [{}, set()]
        for k in r:
            rd = self.readers.setdefault(k, [{}, set()])
            if tok[0] == 'c':
                if rd[0].get(tok[1], 0) < tok[2]:
                    rd[0][tok[1]] = tok[2]
            else:
                rd[1].add(tok)
        return tok

    def barrier(self):
        sc = dict(self.ncomp)
        sd = dict(self.ndma)
        for e in ENGS:
            self.q[e].append(('bar', sc, sd))
        self.lastw = {}
        self.readers = {}

    def run_engine(self, ename, eng, semc, semd):
        waited_c = {}
        waited_d = {}

        def wait_c(f, n):
            if n <= 0 or waited_c.get(f, 0) >= n:
                return
            eng.wait_ge(semc[f], n)
            waited_c[f] = n

        def wait_d(qn, i):
            if i < 0:
                return
            slot = i % KRING
            tgt = 16 * (i // KRING + 1)
            if waited_d.get((qn, slot), 0) >= tgt:
                return
            eng.wait_ge(semd[qn][slot], tgt)
            waited_d[(qn, slot)] = tgt

        for item in self.q[ename]:
            if item[0] == 'bar':
                _, sc, sd = item
                for f in ENGS:
                    if f == ename and not SAME_ENGINE_SYNC[ename]:
                        continue
                    wait_c(f, sc[f])
                for qn in ENGS:
                    n = sd[qn]
                    for i in range(max(0, n - KRING), n):
                        wait_d(qn, i)
                continue
            _, fn, deps, dd, tok = item
            for f, n in deps.items():
                if f == ename and not SAME_ENGINE_SYNC[ename]:
                    continue
                wait_c(f, n)
            for t in dd:
                wait_d(t[1], t[2])
            if tok[0] == 'd':
                i = tok[2]
                if i >= KRING:
                    wait_d(ename, i - KRING)
                ins = fn(eng)
                ins.then_inc(semd[ename][i % KRING], 16)
            else:
                ins = fn(eng)
                ins.then_inc(semc[ename], 1)
        n = self.ndma[ename]
        for i in range(max(0, n - KRING), n):
            wait_d(ename, i)


def run_prog(nc, prog):
    with ExitStack() as st:
        semc = {e: st.enter_context(nc.semaphore('c_' + e)) for e in ENGS}
        semd = {e: [st.enter_context(nc.semaphore('d_%s_%d' % (e, i))) for i in range(KRING)] for e in ENGS}
        block = st.enter_context(nc.Block())

        @block.tensor
        def _(eng):
            prog.run_engine('pe', eng, semc, semd)

        @block.scalar
        def _(eng):
            prog.run_engine('act', eng, semc, semd)

        @block.vector
        def _(eng):
            prog.run_engine('dve', eng, semc, semd)

        @block.gpsimd
        def _(eng):
            prog.run_engine('pool', eng, semc, semd)

        @block.sync
        def _(eng):
            prog.run_engine('sp', eng, semc, semd)


def DMA(P, out, in_, r=(), w=(), q='sp', slow=False):
    if slow:
        P.add(q, lambda e: e.dma_start(out=out, in_=in_, allow_slow_non_contiguous=True), r=r, w=w, dma=True)
    else:
        P.add(q, lambda e: e.dma_start(out=out, in_=in_), r=r, w=w, dma=True)


def MM(P, out, lhsT, rhs, start, stop, r=(), w=(), skip=False):
    P.add('pe', lambda e: e.matmul(out, lhsT=lhsT, rhs=rhs, start=start, stop=stop, skip_group_check=skip), r=r, w=w)


def TR(P, out, in_, ident, r=(), w=()):
    P.add('pe', lambda e: e.transpose(out=out, in_=in_, identity=ident), r=r, w=w)


def ACTF(P, out, in_, func, r=(), w=(), scale=None, bias=None, accum=None):
    kw = {}
    if scale is not None:
        kw['scale'] = scale
    if bias is not None:
        kw['bias'] = bias
    if accum is not None:
        kw['accum_out'] = accum
    P.add('act', lambda e: e.activation(out=out, in_=in_, func=func, **kw), r=r, w=w)


def COPY(P, eng, out, in_, r=(), w=()):
    if eng == 'act':
        P.add('act', lambda e: e.copy(out=out, in_=in_), r=r, w=w)
    else:
        P.add(eng, lambda e: e.tensor_copy(out=out, in_=in_), r=r, w=w)


def TT(P, eng, out, in0, in1, op, r=(), w=(), nosync=False):
    P.add(eng, lambda e: e.tensor_tensor(out=out, in0=in0, in1=in1, op=op), r=r, w=w, nosync=nosync)


def TS(P, eng, out, in0, s1, op0, s2=None, op1=None, r=(), w=()):
    if op1 is None:
        P.add(eng, lambda e: e.tensor_scalar(out=out, in0=in0, scalar1=s1, scalar2=None, op0=op0), r=r, w=w)
    else:
        P.add(eng, lambda e: e.tensor_scalar(out=out, in0=in0, scalar1=s1, scalar2=s2, op0=op0, op1=op1), r=r, w=w)


def STT(P, out, in0, scalar, in1, op0, op1, r=(), w=(), accum=None):
    if accum is None:
        P.add('dve', lambda e: e.scalar_tensor_tensor(out=out, in0=in0, scalar=scalar, in1=in1, op0=op0, op1=op1), r=r, w=w)
    else:
        P.add('dve', lambda e: e.scalar_tensor_tensor(out=out, in0=in0, scalar=scalar, in1=in1, op0=op0, op1=op1, accum_out=accum), r=r, w=w)


def MEMSET(P, eng, ap, val, r=(), w=()):
    P.add(eng, lambda e: e.memset(ap, val), r=r, w=w)


def _bf(x):
    return np.asarray(x, dtype=np.float32).astype(NPBF)


def _split3(c):
    c = np.asarray(c, dtype=np.float64)
    hi = _bf(c)
    r1 = c - hi.astype(np.float64)
    mid = _bf(r1)
    r2 = r1 - mid.astype(np.float64)
    lo = _bf(r2)
    return hi, mid, lo


_CONST_CACHE = {}


def host_constants():
    if _CONST_CACHE:
        return _CONST_CACHE
    C = {}
    C['ident_b'] = np.eye(128, dtype=np.float32).astype(NPBF)
    C['ident_f'] = np.eye(128, dtype=np.float32)
    hh = np.arange(1, NH + 1, dtype=np.float64)
    slopes = np.exp2(-8.0 * hh / NH).astype(np.float32).astype(np.float64)
    s_hi = _bf(slopes)
    s_lo = _bf(slopes - s_hi.astype(np.float64))
    sp = s_hi.astype(np.float64) + s_lo.astype(np.float64)
    t = np.arange(S, dtype=np.float64)
    QA = np.zeros((NH, 7, S), dtype=NPBF)
    for h in range(NH):
        QA[h, 0, :] = _bf(64.0 * s_hi[h].astype(np.float64))
        QA[h, 1, :] = _bf(64.0 * s_lo[h].astype(np.float64))
        QA[h, 2, :] = s_hi[h]
        QA[h, 3, :] = s_lo[h]
        c = np.float32(-(sp[h] * t)).astype(np.float64)
        a, b, cc = _split3(c)
        QA[h, 4, :] = a
        QA[h, 5, :] = b
        QA[h, 6, :] = cc
    C['QA'] = QA
    KA = np.zeros((7, S), dtype=NPBF)
    pos = np.arange(S)
    KA[0] = _bf(pos // 64)
    KA[1] = _bf(pos // 64)
    KA[2] = _bf(pos % 64)
    KA[3] = _bf(pos % 64)
    KA[4:7] = _bf(1.0)
    C['KA'] = KA
    KAc = np.zeros((7, 512), dtype=NPBF)
    n = np.arange(511)
    KAc[0, :511] = _bf(n // 4)
    KAc[1, :511] = _bf(n // 4)
    KAc[2, :511] = _bf(16.0 * (n % 4) + 15.5)
    KAc[3, :511] = _bf(16.0 * (n % 4) + 15.5)
    KAc[4:7, :511] = _bf(1.0)
    C['KAc'] = KAc
    OH = np.zeros((32, S), dtype=np.float32)
    OH[(np.arange(S) // 64) % 32, np.arange(S)] = 1.0
    C['OH'] = OH.astype(NPBF)
    OV = np.zeros((128, 4, 128), dtype=np.float32)
    for nn in range(511):
        cs = 16 * nn
        ce = cs + 31
        for j in range(128):
            if cs <= 64 * j + 63 and ce >= 64 * j:
                OV[nn % 128, nn // 128, j] = 1.0
    C['OV'] = OV.astype(NPBF)
    k = np.arange(128)[:, None, None]
    r = np.arange(4)[None, :, None]
    q = np.arange(512)[None, None, :]
    C['causal'] = np.where(128 * r + k > q, MASKV, 0.0).astype(np.float32).astype(NPBF)
    r8 = np.arange(8)[None, :, None]
    dwin = q - k + 512 - 128 * r8
    C['band'] = np.where((dwin >= 0) & (dwin < 512), 0.0, MASKV).astype(np.float32).astype(NPBF)
    r5 = np.array([0, 512, 1024, 1536, 2048])[None, :, None]
    C['cmpmask'] = np.where(16 * k + 31 - r5 > q, MASKV, 0.0).astype(np.float32).astype(NPBF)
    FT = np.zeros((128, 256), dtype=np.float32)
    for p in range(128):
        jr = 1 if p >= 64 else 0
        for c in range(255):
            jj = c - 127
            if jj == jr or jj == jr - 1:
                FT[p, c] = 1e9
            elif jj > jr:
                FT[p, c] = -1e9
    C['FT'] = FT
    Sel = np.zeros((128, 8, 8, 128), dtype=np.float32)
    SelT = np.zeros((128, 8, 8, 128), dtype=np.float32)
    for gl in range(8):
        for s in range(8):
            for c in range(16):
                Sel[16 * gl + c, gl, s, 16 * (7 - s) + c] = 1.0
                SelT[16 * s + c, gl, s, 16 * gl + c] = 1.0
    C['Sel'] = Sel.astype(NPBF)
    C['SelT'] = SelT.astype(NPBF)
    _CONST_CACHE.update(C)
    return C


CONST_SPECS = [
    ('ident_b', [128, 128], BF16), ('ident_f', [128, 128], F32), ('QA', [NH, 7, S], BF16),
    ('KA', [7, S], BF16), ('KAc', [7, 512], BF16), ('OH', [32, S], BF16),
    ('OV', [128, 4, 128], BF16), ('causal', [128, 4, 512], BF16), ('band', [128, 8, 512], BF16),
    ('cmpmask', [128, 5, 512], BF16), ('FT', [128, 256], F32),
    ('Sel', [128, 8, 8, 128], BF16), ('SelT', [128, 8, 8, 128], BF16),
]

NSA_PARAMS = [('norm', [D]), ('w_in', [D, NSA_IN]), ('cmp_k_pe', [32, 64]), ('cmp_k_w1', [2048, 256]),
              ('cmp_k_w2', [256, 64]), ('cmp_v_pe', [32, 64]), ('cmp_v_w1', [2048, 256]),
              ('cmp_v_w2', [256, 64]), ('w_out', [D, D])]
S5_PARAMS = [('norm', [D]), ('w_in', [D, 2048]), ('log_dt', [64]), ('lambda_re', [64, 64]),
             ('lambda_im', [64, 64]), ('b_re', [64, 64, 16]), ('b_im', [64, 64, 16]),
             ('c_re', [64, 16, 64]), ('c_im', [64, 16, 64]), ('d', [64, 16]),
             ('w_glu', [D, 2048]), ('w_out', [D, D])]


class Ctx:
    pass


DBG = {}


def TAP(P, nc, name, ap, shape, dt, r=()):
    if not DBG.get('taps'):
        return
    t = nc.dram_tensor('dbg_' + name, shape, dt, kind='ExternalOutput').ap()
    DMA(P, t, ap, r=list(r))


def load_weight_bf16(P, K, wb, wdram, ncols, stage, tag):
    for kc in range(8):
        sb = stage[kc % 2]
        key = 'wstage%d' % (kc % 2)
        DMA(P, sb[:, 0:ncols], wdram[kc * 128:(kc + 1) * 128, :], w=[key])
        eng = 'dve' if kc % 2 == 0 else 'pool'
        COPY(P, eng, wb[:, kc, :], sb[:, 0:ncols], r=[key], w=['%s_wb' % tag])


def rmsnorm_tile(P, K, xt, gb, xnb, tg, rkeys, wkey):
    STT(P, K.junkf[:], xt, 1.0, xt, ALU.mult, ALU.mult, r=rkeys, w=['junkf', 'ss' + tg], accum=K.ss[:])
    ACTF(P, K.sq[:], K.ss[:], AF.Sqrt, r=['ss' + tg], w=['sq' + tg], scale=1.0 / D, bias=EPS)
    P.add('dve', lambda e: e.reciprocal(out=K.rs[:], in_=K.sq[:]), r=['sq' + tg], w=['rs' + tg])
    STT(P, xnb, xt, K.rs[:], gb, ALU.mult, ALU.mult, r=rkeys + ['rs' + tg, 'gb'], w=[wkey])


def transpose_tile(P, K, src_b, dst, skey, dkey, evac_eng):
    for kc in range(8):
        TR(P, K.pT[:, kc, :], src_b[:, kc * 128:(kc + 1) * 128], K.ident_b[:], r=[skey, 'const'], w=['pT'])
    COPY(P, evac_eng, dst, K.pT[:], r=['pT'], w=[dkey])


def nsa_layer(P, nc, K, li, prm, hsrc, hdst, final_g):
    QOFF, KCOFF, VCOFF, KSOFF, VSOFF, KWOFF, VWOFF, GOFF, ZOFF = 0, 1024, 1280, 1536, 1792, 2048, 2304, 2560, 2608
    ident = K.ident_b
    with ExitStack() as L:
        sbL = lambda name, shape, dt: L.enter_context(nc.sbuf_tensor('%s_l%d' % (name, li), shape, dt))
        kcmpT = sbL('kcmpT', [128, NG, 512], BF16)
        vcmpa = sbL('vcmpa', [128, 4, NG, 65], BF16)

        with ExitStack() as A:
            sb = lambda name, shape, dt: A.enter_context(nc.sbuf_tensor('%sA%d' % (name, li), shape, dt))
            ps = lambda name, shape, dt: A.enter_context(nc.psum_tensor('%sA%d' % (name, li), shape, dt))
            wb = sb('wb', [128, 8, NSA_IN], BF16)
            stage = [sb('wst0', [128, NSA_IN], F32), sb('wst1', [128, NSA_IN], F32)]
            gb = sb('gb', [128, D], F32)
            xt = [sb('xt0', [128, D], F32), sb('xt1', [128, D], F32)]
            xnb = [sb('xnb0', [128, D], BF16), sb('xnb1', [128, D], BF16)]
            xT4 = [sb('xT40', [128, 8, 512], BF16), sb('xT41', [128, 8, 512], BF16)]
            fo = [sb('fo%d' % i, [128, 512], BF16) for i in range(4)]
            go = [sb('go%d' % i, [128, 48], F32) for i in range(2)]
            K.pT = ps('pT', [128, 8, 128], BF16)
            pm = [ps('pm%d' % i, [128, 512], F32) for i in range(4)]

            load_weight_bf16(P, K, wb, prm['w_in'], NSA_IN, stage, 'win')
            DMA(P, gb[:], prm['norm'].partition_broadcast(128), w=['gb'])
            nmm = [0]
            nfo = [0]

            def mm_group(out_ps, pkey, lhs_fn, rhs_fn, rk):
                for kc in range(8):
                    MM(P, out_ps, lhs_fn(kc), rhs_fn(kc), kc == 0, kc == 7, r=rk + ['win_wb'], w=[pkey])

            for c in range(NC):
                cb = c % 2
                for tt in range(4):
                    i = 4 * c + tt
                    b = i % 2
                    DMA(P, xt[b][:], hsrc[i * 128:(i + 1) * 128, :], w=['xt%d' % b])
                    rmsnorm_tile(P, K, xt[b][:], gb[:], xnb[b][:], 'A', ['xt%d' % b], 'xnb%d' % b)
                    transpose_tile(P, K, xnb[b], xT4[cb][:, :, tt * 128:(tt + 1) * 128], 'xnb%d' % b, 'xT4%d' % cb,
                                   'act' if tt % 2 == 0 else 'dve')
                xk = 'xT4%d' % cb
                fm = [(m, 'q') for m in range(8)] + [(0, 'kc'), (1, 'kc'), (0, 'vc'), (1, 'vc'), (0, 'ks'), (1, 'ks'), (0, 'kw'), (1, 'kw')]
                for (m, kind) in fm:
                    off = {'q': QOFF, 'kc': KCOFF, 'vc': VCOFF, 'ks': KSOFF, 'kw': KWOFF}[kind] + m * 128
                    dst = {'q': K.qT_d, 'kc': K.kcT_d, 'vc': K.vcT_d, 'ks': K.ksT_d, 'kw': K.kwT_d}[kind]
                    pi = nmm[0] % 4
                    nmm[0] += 1
                    mm_group(pm[pi][:], 'pm%d' % pi, lambda kc, off=off: wb[:, kc, off:off + 128],
                             lambda kc: xT4[cb][:, kc, :], [xk])
                    fi = nfo[0] % 4
                    nfo[0] += 1
                    if kind == 'q':
                        ACTF(P, fo[fi][:], pm[pi][:], AF.Copy, r=['pm%d' % pi], w=['fo%d' % fi], scale=0.125)
                    else:
                        COPY(P, 'dve', fo[fi][:], pm[pi][:], r=['pm%d' % pi], w=['fo%d' % fi])
                    DMA(P, dst[m * 128:(m + 1) * 128, c * 512:(c + 1) * 512], fo[fi][:], r=['fo%d' % fi])
                for tt in range(4):
                    i = 4 * c + tt
                    rows = slice(i * 128, (i + 1) * 128)
                    lhs = lambda kc, tt=tt: xT4[cb][:, kc, tt * 128:(tt + 1) * 128]
                    pi = nmm[0] % 4
                    nmm[0] += 1
                    for kc in range(8):
                        MM(P, pm[pi][:, 0:256], lhs(kc), wb[:, kc, VSOFF:VSOFF + 256], kc == 0, kc == 7, r=[xk, 'win_wb'], w=['pm%d' % pi])
                    for kc in range(8):
                        MM(P, pm[pi][:, 256:512], lhs(kc), wb[:, kc, VWOFF:VWOFF + 256], kc == 0, kc == 7, r=[xk, 'win_wb'], w=['pm%d' % pi])
                    fi = nfo[0] % 4
                    nfo[0] += 1
                    COPY(P, 'dve', fo[fi][:], pm[pi][:], r=['pm%d' % pi], w=['fo%d' % fi])
                    DMA(P, K.vs_d[rows, :], fo[fi][:, 0:256], r=['fo%d' % fi])
                    DMA(P, K.vw_d[rows, :], fo[fi][:, 256:512], r=['fo%d' % fi])
                    pi = nmm[0] % 4
                    nmm[0] += 1
                    for kc in range(8):
                        MM(P, pm[pi][:, 0:48], lhs(kc), wb[:, kc, GOFF:GOFF + 48], kc == 0, kc == 7, r=[xk, 'win_wb'], w=['pm%d' % pi])
                    gi = i % 2
                    ACTF(P, go[gi][:], pm[pi][:, 0:48], AF.Sigmoid, r=['pm%d' % pi], w=['go%d' % gi])
                    DMA(P, K.gates_d[rows, :], go[gi][:], r=['go%d' % gi])
                    for j in range(2):
                        pi = nmm[0] % 4
                        nmm[0] += 1
                        mm_group(pm[pi][:], 'pm%d' % pi, lhs, lambda kc, j=j: wb[:, kc, ZOFF + 512 * j:ZOFF + 512 * (j + 1)], [xk])
                        fi = nfo[0] % 4
                        nfo[0] += 1
                        ACTF(P, fo[fi][:], pm[pi][:], AF.Silu, r=['pm%d' % pi], w=['fo%d' % fi])
                        DMA(P, K.zs_d[rows, 512 * j:512 * (j + 1)], fo[fi][:], r=['fo%d' % fi])
            P.barrier()

        with ExitStack() as B:
            sb = lambda name, shape, dt: B.enter_context(nc.sbuf_tensor('%sB%d' % (name, li), shape, dt))
            ps = lambda name, shape, dt: B.enter_context(nc.psum_tensor('%sB%d' % (name, li), shape, dt))
            w1st = sb('w1st', [64, 32, 256], F32)
            w1b = [sb('w1b0', [64, 32, 256], BF16), sb('w1b1', [64, 32, 256], BF16)]
            w2st = sb('w2st', [128, 2, 64], F32)
            w2b = [sb('w2b0', [128, 2, 96], BF16), sb('w2b1', [128, 2, 64], BF16)]
            pest = sb('pest', [64, 32], F32)
            peb = [sb('peb0', [64, 32], BF16), sb('peb1', [64, 32], BF16)]
            hb = [sb('hb0', [128, 2], F32), sb('hb1', [128, 2], F32)]
            xin = [sb('xin0', [64, S], BF16), sb('xin1', [64, S], BF16)]
            hid = sb('hid', [128, 2, 512], BF16)
            ph = [ps('ph0', [128, 512], F32), ps('ph1', [128, 512], F32)]
            pb = ps('pb', [128, 512], F32)
            po = ps('po', [128, 512], F32)

            MEMSET(P, 'pool', kcmpT[:], 0.0, w=['kcmpT'])
            MEMSET(P, 'pool', vcmpa[:], 0.0, w=['vcmpa'])
            MEMSET(P, 'pool', hid[:], 0.0, w=['hid'])
            for kv in range(2):
                nm = 'cmp_k' if kv == 0 else 'cmp_v'
                DMA(P, w1st[:], prm[nm + '_w1'].rearrange("(l d) h -> d l h", d=64), w=['w1st'])
                COPY(P, 'dve', w1b[kv][:], w1st[:], r=['w1st'], w=['w1b%d' % kv])
                DMA(P, w2st[:], prm[nm + '_w2'].rearrange("(m p) d -> p m d", p=128), w=['w2st'])
                if kv == 0:
                    MEMSET(P, 'pool', w2b[0][:], 0.0, w=['w2b0'])
                    COPY(P, 'dve', w2b[0][:, :, 32:96], w2st[:], r=['w2st', 'w2b0'], w=['w2b0'])
                else:
                    COPY(P, 'dve', w2b[1][:], w2st[:], r=['w2st'], w=['w2b1'])
                DMA(P, pest[:], prm[nm + '_pe'].rearrange("l d -> d l"), w=['pest'], slow=True)
                COPY(P, 'dve', peb[kv][:], pest[:], r=['pest'], w=['peb%d' % kv])
                for m in range(2):
                    for l in range(32):
                        MM(P, pb[:, m:m + 1], w1b[kv][:, l, m * 128:(m + 1) * 128], peb[kv][:, l:l + 1], l == 0, l == 31,
                           r=['w1b%d' % kv, 'peb%d' % kv], w=['pb'])
                COPY(P, 'dve', hb[kv][:], pb[:, 0:2], r=['pb'], w=['hb%d' % kv])
            DMA(P, kcmpT[96:103, :, :], K.c_KAc.unsqueeze(1).broadcast_to([7, NG, 512]), r=['kcmpT'], w=['kcmpT'])
            it = 0
            for g in range(NG):
                for kv in range(2):
                    xb = it % 2
                    it += 1
                    src = K.kcT_d if kv == 0 else K.vcT_d
                    DMA(P, xin[xb][:], src[g * 64:(g + 1) * 64, :], w=['xin%d' % xb])
                    for m in range(2):
                        for l in range(32):
                            MM(P, ph[m][:, 0:511], w1b[kv][:, l, m * 128:(m + 1) * 128], xin[xb][:, l:l + 16 * 510 + 1:16],
                               l == 0, l == 31, r=['w1b%d' % kv, 'xin%d' % xb], w=['ph%d' % m])
                        ACTF(P, hid[:, m, 0:511], ph[m][:, 0:511], AF.Silu, r=['ph%d' % m, 'hb%d' % kv], w=['hid'], bias=hb[kv][:, m:m + 1])
                    if kv == 0:
                        for m in range(2):
                            MM(P, po[0:96, 0:511], w2b[0][:, m, :], hid[:, m, 0:511], m == 0, m == 1, r=['w2b0', 'hid'], w=['po'])
                        COPY(P, 'dve', kcmpT[0:96, g, 0:511], po[0:96, 0:511], r=['po'], w=['kcmpT'])
                    else:
                        for nt in range(4):
                            for m in range(2):
                                MM(P, po[:, nt * 64:(nt + 1) * 64], hid[:, m, nt * 128:(nt + 1) * 128], w2b[1][:, m, :], m == 0, m == 1,
                                   r=['w2b1', 'hid'], w=['po'])
                        COPY(P, 'dve', vcmpa[:, :, g, 0:64], po[:, 0:256].rearrange("p (a b) -> p a b", a=4), r=['po'], w=['vcmpa'])
            MEMSET(P, 'pool', vcmpa[:, :, :, 64:65], 1.0, r=['vcmpa'], w=['vcmpa'])
            P.barrier()

        if DBG.get('stop') == 'nsaB':
            return
        nsa_attention(P, nc, K, li, kcmpT, vcmpa)
        if DBG.get('stop') == 'nsaC':
            return

    out_proj_phase(P, nc, K, li, prm['w_out'], hsrc, hdst, final_g, token_major_src=True)


THR_DECAY = 64.0


def _thr():
    return DBG.get("thr", THR_DECAY)


def _slope(h):
    return 2.0 ** (-(h + 1) / 2.0)


def nsa_attention(P, nc, K, li, kcmpT, vcmpa):
    with ExitStack() as Cx:
        sb = lambda name, shape, dt: Cx.enter_context(nc.sbuf_tensor('%sC%d' % (name, li), shape, dt))
        ps = lambda name, shape, dt: Cx.enter_context(nc.psum_tensor('%sC%d' % (name, li), shape, dt))
        ident = K.ident_b
        OV = sb('OV', [128, 4, 128], BF16)
        causal = sb('causal', [128, 4, 512], BF16)
        band = sb('band', [128, 8, 512], BF16)
        cmpm = sb('cmpm', [128, 5, 512], BF16)
        FT = sb('FT', [128, 256], F32)
        zer = sb('zer', [128, 512], BF16)
        ksT = sb('ksT', [128, S], BF16)
        vsa = sb('vsa', [128, 64, 65], BF16)
        qa = [sb('qa%d' % i, [128, 4, 4, 512], BF16) for i in range(2)]
        kwT = [sb('kwT%d' % i, [128, 1024], BF16) for i in range(2)]
        vwa = [sb('vwa%d' % i, [128, 8, 65], BF16) for i in range(2)]
        gt = [sb('gt%d' % i, [128, 4, 48], F32) for i in range(2)]
        zt = [sb('zt%d' % i, [128, 4, 256], BF16) for i in range(2)]
        pc = [[sb('pc%d_%d' % (r, nt), [128, 512], BF16) for nt in range(4)] for r in range(4)]
        pt = [sb('pt%d' % i, [128, 512], BF16) for i in range(4)]
        oacc = sb('oacc', [128, 4, 4, 64], F32)
        tmpo = [sb('tmpo%d' % i, [128, 4, 64], F32) for i in range(2)]
        rcs = sb('rcs', [128, 4, 4], F32)
        rr = [sb('rr%d' % i, [128, 4], F32) for i in range(2)]
        ww = [sb('ww%d' % i, [128, 4], F32) for i in range(2)]
        impacc = [sb('impacc%d' % i, [128, 128], F32) for i in range(2)]
        impf = [sb('impf%d' % i, [128, 128], F32) for i in range(2)]
        work = [sb('work%d' % i, [128, 128], F32) for i in range(2)]
        m8a = [sb('m8a%d' % i, [128, 8], F32) for i in range(2)]
        m8b = [sb('m8b%d' % i, [128, 8], F32) for i in range(2)]
        selb = [sb('selb%d' % i, [128, 128], BF16) for i in range(4)]
        ozb = [sb('ozb%d' % i, [128, 4, 256], BF16) for i in range(2)]
        Sps = [ps('S%d' % i, [128, 512], F32) for i in range(3)]
        Ops = [ps('O%d' % i, [128, 512], F32) for i in range(2)]
        pimp = [ps('pimp%d' % i, [128, 512], F32) for i in range(2)]
        pst = ps('pst', [128, 1024], BF16)

        DMA(P, OV[:], K.c_OV, w=['const'])
        DMA(P, causal[:], K.c_causal, w=['const'])
        DMA(P, band[:], K.c_band, w=['const'])
        DMA(P, cmpm[:], K.c_cmpmask, w=['const'])
        DMA(P, FT[:], K.c_FT, w=['const'])
        MEMSET(P, 'pool', zer[:], 0.0, w=['const'])
        MEMSET(P, 'pool', vsa[:, :, 64:65], 1.0, w=['vsa1'])
        for i in range(2):
            MEMSET(P, 'pool', vwa[i][:, :, 64:65], 1.0, w=['vwa1_%d' % i])
            MEMSET(P, 'pool', qa[i][0:32, :, :, :], 0.0, w=['qsel%d' % i])
            MEMSET(P, 'pool', kwT[i][0:32, :], 0.0, w=['kwT0_%d' % i])
        DMA(P, ksT[0:32, :], K.c_OH, w=['ksToh'])
        DMA(P, ksT[96:103, :], K.c_KA, w=['ksTaug'])
        st = {'s': 0, 'p': 0, 'o': 0, 'x': 0}

        def combine(oi, first, gcol, r, keep_rc, cb):
            O3 = Ops[oi][:, 0:260].rearrange("p (a b) -> p a b", a=4)
            x = st['x'] % 2
            st['x'] += 1
            TS(P, 'dve', rr[x][:], O3[:, :, 64], 1e-30, ALU.max, r=['O%d' % oi], w=['rr%d' % x])
            P.add('dve', lambda e: e.reciprocal(out=rr[x][:], in_=rr[x][:]), r=['rr%d' % x], w=['rr%d' % x])
            if keep_rc:
                COPY(P, 'dve', rcs[:, r, :], rr[x][:], r=['rr%d' % x], w=['rcs%d' % r])
            TT(P, 'dve', ww[x][:], rr[x][:], gt[cb][:, :, gcol], ALU.mult, r=['rr%d' % x, 'gt%d' % cb], w=['ww%d' % x])
            wbc = ww[x][:].unsqueeze(2).broadcast_to([128, 4, 64])
            if first:
                TT(P, 'dve', oacc[:, :, r, :], O3[:, :, 0:64], wbc, ALU.mult, r=['O%d' % oi, 'ww%d' % x], w=['oacc%d' % r])
            else:
                TT(P, 'dve', tmpo[x][:], O3[:, :, 0:64], wbc, ALU.mult, r=['O%d' % oi, 'ww%d' % x], w=['tmpo%d' % x])
                TT(P, 'pool', oacc[:, :, r, :], oacc[:, :, r, :], tmpo[x][:], ALU.add, r=['tmpo%d' % x, 'oacc%d' % r], w=['oacc%d' % r])

        def run_stream(jobs):
            n = len(jobs)
            LAG = 2
            for i in range(n + LAG):
                if i < n:
                    j = jobs[i]
                    si = st['s'] % 3
                    st['s'] += 1
                    nm = len(j['mms'])
                    for idx, (l_, r_, rk) in enumerate(j['mms']):
                        MM(P, Sps[si][:], l_, r_, idx == 0, idx == nm - 1, r=rk, w=['S%d' % si])
                    if j['pbuf'] is None:
                        pi = st['p'] % 4
                        st['p'] += 1
                        j['pbuf'] = (pt[pi], 'pt%d' % pi)
                    ACTF(P, j['pbuf'][0][:], Sps[si][:], AF.Exp, r=['S%d' % si], w=[j['pbuf'][1]])
                if i >= LAG:
                    j = jobs[i - LAG]
                    oi = j['oi']
                    if j['first']:
                        MM(P, Ops[oi][:, 0:260], zer[:, 0:128], zer[:, 0:260], True, False, r=['const'], w=['O%d' % oi], skip=True)
                    pb, pk = j['pbuf']
                    for sub in range(4):
                        MM(P, Ops[oi][:, sub * 65:(sub + 1) * 65], pb[:, sub * 128:(sub + 1) * 128], j['v'], False,
                           (j['last'] and sub == 3), r=[pk] + j['vk'], w=['O%d' % oi], skip=True)
                    if j['last']:
                        j['fin'](oi)

        it = 0
        for g in range(DBG.get('c_ng', NG)):
            DMA(P, ksT[32:96, :], K.ksT_d[g * 64:(g + 1) * 64, :], w=['ksT'])
            for q4 in range(4):
                DMA(P, vsa[:, q4 * 16:(q4 + 1) * 16, 0:64],
                    K.vs_d[q4 * 2048:(q4 + 1) * 2048, g * 64:(g + 1) * 64].rearrange("(kt p) d -> p kt d", p=128), r=['vsa'], w=['vsa'])
            for c in range(DBG.get('c_nc', NC)):
                cb = it % 2
                it += 1
                cs = slice(c * 512, (c + 1) * 512)
                ntmax = c // 4
                sel_kts, win_rs, cmp_nts = [], [], []
                for r in range(4):
                    sl = _slope(4 * g + r)
                    sel_kts.append([kt for kt in range(4 * c + 4) if sl * max(0, 512 * c - 128 * kt - 127) <= _thr()])
                    win_rs.append([rw for rw in range(4 if c == 0 else 0, 8) if sl * max(0, 385 - 128 * rw) <= _thr()])
                    cmp_nts.append([nt for nt in range(ntmax + 1)
                                    if sl * max(0.0, 512 * c - (2048 * nt + 2047.5) - 48.0) <= _thr()])
                VD = 1000 if DBG.get('one_ver') else 16
                selvers = sorted(set(kt // VD for r in range(4) for kt in sel_kts[r]))
                vers = sorted(set(selvers) | {0})
                qkeys = []
                for v in vers:
                    DMA(P, qa[cb][32:96, v, :, :], K.qT_d[g * 256:(g + 1) * 256, cs].rearrange("(r d) t -> d r t", d=64), w=['qa%d_%d' % (cb, v)])
                    DMA(P, qa[cb][96:103, v, :, :], K.c_QA[g * 4:(g + 1) * 4, :, cs].rearrange("h r t -> r h t"), w=['qaug%d_%d' % (cb, v)])
                    qkeys += ['qa%d_%d' % (cb, v), 'qaug%d_%d' % (cb, v)]
                q0k = ['qa%d_0' % cb, 'qaug%d_0' % cb, 'qsel%d' % cb]
                k0 = (c - 1) * 512
                if c == 0:
                    DMA(P, kwT[cb][32:96, 512:1024], K.kwT_d[g * 64:(g + 1) * 64, 0:512], w=['kwT%d' % cb])
                    DMA(P, kwT[cb][96:103, 512:1024], K.c_KA[:, 0:512], w=['kwTa%d' % cb])
                    DMA(P, vwa[cb][:, 4:8, 0:64], K.vw_d[0:512, g * 64:(g + 1) * 64].rearrange("(kt p) d -> p kt d", p=128), w=['vwa%d' % cb])
                else:
                    DMA(P, kwT[cb][32:96, :], K.kwT_d[g * 64:(g + 1) * 64, k0:k0 + 1024], w=['kwT%d' % cb])
                    DMA(P, kwT[cb][96:103, :], K.c_KA[:, k0:k0 + 1024], w=['kwTa%d' % cb])
                    DMA(P, vwa[cb][:, :, 0:64], K.vw_d[k0:k0 + 1024, g * 64:(g + 1) * 64].rearrange("(kt p) d -> p kt d", p=128), w=['vwa%d' % cb])
                DMA(P, gt[cb][:], K.gates_d[cs, :].rearrange("(s p) c -> p s c", p=128), w=['gt%d' % cb])
                DMA(P, zt[cb][:], K.zs_d[cs, g * 256:(g + 1) * 256].rearrange("(s p) c -> p s c", p=128), w=['zt%d' % cb])

                jobs = []
                for r in range(4):
                    for nt in cmp_nts[r]:
                        mms = [(kcmpT[0:103, g, nt * 128:(nt + 1) * 128], qa[cb][0:103, 0, r, :], q0k + ['kcmpT'])]
                        if nt == ntmax:
                            mms.append((ident[:], cmpm[:, c % 4, :], ['const']))
                        elif nt == ntmax - 1 and c % 4 == 0 and not DBG.get('no_elif'):
                            mms.append((ident[:], cmpm[:, 4, :], ['const']))
                        jobs.append(dict(mms=mms, pbuf=(pc[r][nt], 'pc%d_%d' % (r, nt)), v=vcmpa[:, nt, g, :], vk=['vcmpa'],
                                         oi=(st['o'] + r) % 2, first=(nt == cmp_nts[r][0]), last=(nt == cmp_nts[r][-1]),
                                         fin=(lambda oi, r=r: combine(oi, True, 0 * 16 + g * 4 + r, r, True, cb))))
                st['o'] += 4
                run_stream(jobs)

                for sub in range(4):
                    ti = 4 * c + sub
                    x2 = sub % 2
                    pb_ = pimp[x2]
                    pk_ = 'pimp%d' % x2
                    for r in range(4):
                        for nt in cmp_nts[r]:
                            MM(P, pb_[:, r * 128:(r + 1) * 128], pc[r][nt][:, sub * 128:(sub + 1) * 128], OV[:, nt, :],
                               nt == cmp_nts[r][0], nt == cmp_nts[r][-1], r=['pc%d_%d' % (r, nt), 'const'], w=[pk_])
                    ia, if_, wk, ma, mb = impacc[x2], impf[x2], work[x2], m8a[x2], m8b[x2]
                    kx = '_%d' % x2
                    for r in range(4):
                        if r == 0:
                            TS(P, 'dve', ia[:], pb_[:, 0:128], rcs[:, r, sub:sub + 1], ALU.mult, r=[pk_, 'rcs%d' % r], w=['impacc' + kx])
                        else:
                            STT(P, ia[:], pb_[:, r * 128:(r + 1) * 128], rcs[:, r, sub:sub + 1], ia[:], ALU.mult, ALU.add,
                                r=[pk_, 'rcs%d' % r, 'impacc' + kx], w=['impacc' + kx])
                    TT(P, 'dve', if_[:], ia[:], FT[:, 127 - 2 * ti:255 - 2 * ti], ALU.add, r=['impacc' + kx, 'const'], w=['impf' + kx])
                    MEMSET(P, 'dve', if_[:, 0:1], 1e9, r=['impf' + kx], w=['impf' + kx])
                    P.add('dve', (lambda e, ma=ma, if_=if_: e.max(out=ma[:], in_=if_[:])), r=['impf' + kx], w=['m8a' + kx])
                    P.add('dve', (lambda e, ma=ma, if_=if_, wk=wk: e.match_replace(out=wk[:], in_to_replace=ma[:], in_values=if_[:], imm_value=-3.0e38)),
                          r=['impf' + kx, 'm8a' + kx], w=['work' + kx])
                    P.add('dve', (lambda e, mb=mb, wk=wk: e.max(out=mb[:], in_=wk[:])), r=['work' + kx], w=['m8b' + kx])
                    TS(P, 'dve', selb[sub][:], if_[:], mb[:, 7:8], ALU.is_lt, DBG.get('maskv', MASKV), ALU.mult, r=['impf' + kx, 'm8b' + kx], w=['selb%d' % sub])
                    pcol = slice((sub % 2) * 128, (sub % 2) * 128 + 128)
                    pkey = 'pst%d' % (sub % 2)
                    TR(P, pst[:, pcol], selb[sub][:], ident[:], r=['selb%d' % sub, 'const'], w=[pkey])
                    for v in selvers:
                        COPY(P, 'dve', qa[cb][0:32, v, :, sub * 128:(sub + 1) * 128],
                             pst[32 * v:32 * v + 32, pcol].unsqueeze(1).broadcast_to([32, 4, 128]), r=[pkey], w=['qsel%d' % cb])

                jobs = []
                for r in range(4):
                    for rw in win_rs[r]:
                        mms = [(kwT[cb][0:103, rw * 128:(rw + 1) * 128], qa[cb][0:103, 0, r, :], q0k + ['kwT%d' % cb, 'kwTa%d' % cb, 'kwT0_%d' % cb]),
                               (ident[:], band[:, rw, :], ['const'])]
                        jobs.append(dict(mms=mms, pbuf=None, v=vwa[cb][:, rw, :], vk=['vwa%d' % cb, 'vwa1_%d' % cb],
                                         oi=(st['o'] + r) % 2, first=(rw == win_rs[r][0]), last=(rw == win_rs[r][-1]),
                                         fin=(lambda oi, r=r: combine(oi, False, 2 * 16 + g * 4 + r, r, False, cb))))
                st['o'] += 4
                run_stream(jobs)

                jobs = []
                for r in range(4):
                    for kt in sel_kts[r]:
                        mms = [(ksT[0:103, kt * 128:(kt + 1) * 128], qa[cb][0:103, kt // VD, r, :],
                                qkeys + ['qsel%d' % cb, 'ksT', 'ksTaug', 'ksToh'])]
                        if kt >= 4 * c:
                            mms.append((ident[:], causal[:, kt - 4 * c, :], ['const']))
                        jobs.append(dict(mms=mms, pbuf=None, v=vsa[:, kt, :], vk=['vsa', 'vsa1'],
                                         oi=(st['o'] + r) % 2, first=(kt == sel_kts[r][0]), last=(kt == sel_kts[r][-1]),
                                         fin=(lambda oi, r=r: combine(oi, False, 1 * 16 + g * 4 + r, r, False, cb))))
                st['o'] += 4
                run_stream(jobs)

                TT(P, 'pool', ozb[cb][:], oacc[:].rearrange("p s r d -> p s (r d)"), zt[cb][:], ALU.mult,
                   r=['oacc0', 'oacc1', 'oacc2', 'oacc3', 'zt%d' % cb], w=['ozb%d' % cb])
                DMA(P, K.oz_d[cs, g * 256:(g + 1) * 256].rearrange("(s p) d -> p s d", p=128), ozb[cb][:], r=['ozb%d' % cb])
        P.barrier()


def out_proj_phase(P, nc, K, li, wout, hsrc, hdst, final_g, token_major_src):
    with ExitStack() as Dx:
        sb = lambda name, shape, dt: Dx.enter_context(nc.sbuf_tensor('%sD%d' % (name, li), shape, dt))
        ps = lambda name, shape, dt: Dx.enter_context(nc.psum_tensor('%sD%d' % (name, li), shape, dt))
        wb = sb('wb', [128, 8, D], BF16)
        stage = [sb('wst0', [128, D], F32), sb('wst1', [128, D], F32)]
        ozt = [sb('ozt%d' % i, [128, D], BF16) for i in range(2)]
        ozT = [sb('ozT%d' % i, [128, 8, 128], BF16) for i in range(2)]
        ht = [sb('ht%d' % i, [128, D], F32) for i in range(2)]
        hn = [sb('hn%d' % i, [128, D], F32) for i in range(2)]
        K.pT = ps('pT', [128, 8, 128], BF16)
        pm = [ps('pm%d' % i, [128, 512], F32) for i in range(4)]
        gfb = None
        if final_g is not None:
            gfb = sb('gfb', [128, D], F32)
            DMA(P, gfb[:], final_g.partition_broadcast(128), w=['gb'])
        load_weight_bf16(P, K, wb, wout, D, stage, 'wout')
        for i in range(NT):
            b = i % 2
            rows = slice(i * 128, (i + 1) * 128)
            DMA(P, ozt[b][:], K.oz_d[rows, :], w=['ozt%d' % b])
            DMA(P, ht[b][:], hsrc[rows, :], w=['ht%d' % b])
            transpose_tile(P, K, ozt[b], ozT[b][:], 'ozt%d' % b, 'ozT%d' % b, 'act' if i % 2 == 0 else 'dve')
            for half in range(2):
                pi = (2 * i + half) % 4
                for kc in range(8):
                    MM(P, pm[pi][:], ozT[b][:, kc, :], wb[:, kc, half * 512:(half + 1) * 512], kc == 0, kc == 7,
                       r=['ozT%d' % b, 'wout_wb'], w=['pm%d' % pi])
                TT(P, 'dve', hn[b][:, half * 512:(half + 1) * 512], pm[pi][:], ht[b][:, half * 512:(half + 1) * 512], ALU.add,
                   r=['pm%d' % pi, 'ht%d' % b], w=['hn%d_%d' % (b, half)])
            hk = ['hn%d_0' % b, 'hn%d_1' % b]
            if final_g is None:
                DMA(P, hdst[rows, :], hn[b][:], r=hk)
            else:
                final_norm_store(P, K, hn[b], hk, gfb, ht[b], ['ht%d' % b], [hdst[rows, :]], [slice(0, 128)])
        P.barrier()


def final_norm_store(P, K, hn, hk, gfb, outbuf, okeys, dsts, prs):
    STT(P, K.junkf[:], hn[:], 1.0, hn[:], ALU.mult, ALU.mult, r=hk, w=['junkf', 'ssF'], accum=K.ss[:])
    ACTF(P, K.sq[:], K.ss[:], AF.Sqrt, r=['ssF'], w=['sqF'], scale=1.0 / D, bias=EPS)
    P.add('dve', lambda e: e.reciprocal(out=K.rs[:], in_=K.sq[:]), r=['sqF'], w=['rsF'])
    STT(P, outbuf[:], hn[:], K.rs[:], gfb[:], ALU.mult, ALU.mult, r=hk + ['rsF', 'gb'], w=okeys)
    for d_, pr in zip(dsts, prs):
        DMA(P, d_, outbuf[pr, :], r=okeys)


def s5_layer(P, nc, K, li, prm, hsrc, hdst, final_g):
    hv_src = hsrc.rearrange("(g n s) d -> g s n d", n=64, s=8)
    hv_dst = hdst.rearrange("(g n s) d -> g s n d", n=64, s=8)
    TWO_PI = 2.0 * math.pi

    with ExitStack() as A:
        sb = lambda name, shape, dt: A.enter_context(nc.sbuf_tensor('%sA%d' % (name, li), shape, dt))
        ps = lambda name, shape, dt: A.enter_context(nc.psum_tensor('%sA%d' % (name, li), shape, dt))
        wb = sb('wb', [128, 8, 2048], BF16)
        stage = [sb('wst0', [128, 2048], F32), sb('wst1', [128, 2048], F32)]
        gb = sb('gb', [128, D], F32)
        xt = [sb('xt0', [128, D], F32), sb('xt1', [128, D], F32)]
        xnb = [sb('xnb0', [128, D], BF16), sb('xnb1', [128, D], BF16)]
        xT4 = [sb('xT40', [128, 8, 512], BF16), sb('xT41', [128, 8, 512], BF16)]
        fo = [sb('fo%d' % i, [128, 512], BF16) for i in range(4)]
        K.pT = ps('pT', [128, 8, 128], BF16)
        pm = [ps('pm%d' % i, [128, 512], F32) for i in range(4)]
        load_weight_bf16(P, K, wb, prm['w_in'], 2048, stage, 'win')
        DMA(P, gb[:], prm['norm'].partition_broadcast(128), w=['gb'])
        cnt = 0
        for seg in range(NC):
            cb = seg % 2
            for tt in range(4):
                b = (4 * seg + tt) % 2
                DMA(P, xt[b][0:64, :], hv_src[seg, 2 * tt], w=['xt%da' % b])
                DMA(P, xt[b][64:128, :], hv_src[seg, 2 * tt + 1], w=['xt%db' % b])
                rmsnorm_tile(P, K, xt[b][:], gb[:], xnb[b][:], 'A', ['xt%da' % b, 'xt%db' % b], 'xnb%d' % b)
                transpose_tile(P, K, xnb[b], xT4[cb][:, :, tt * 128:(tt + 1) * 128], 'xnb%d' % b, 'xT4%d' % cb,
                               'act' if tt % 2 == 0 else 'dve')
            for m in range(16):
                pi = cnt % 4
                fi = cnt % 4
                cnt += 1
                for kc in range(8):
                    MM(P, pm[pi][:], wb[:, kc, m * 128:(m + 1) * 128], xT4[cb][:, kc, :], kc == 0, kc == 7,
                       r=['xT4%d' % cb, 'win_wb'], w=['pm%d' % pi])
                if m < 8:
                    COPY(P, 'dve', fo[fi][:], pm[pi][:], r=['pm%d' % pi], w=['fo%d' % fi])
                    DMA(P, K.uT_d[m * 128:(m + 1) * 128, seg * 512:(seg + 1) * 512], fo[fi][:], r=['fo%d' % fi])
                else:
                    ACTF(P, fo[fi][:], pm[pi][:], AF.Silu, r=['pm%d' % pi], w=['fo%d' % fi])
                    DMA(P, K.zsT_d[(m - 8) * 128:(m - 7) * 128, seg * 512:(seg + 1) * 512], fo[fi][:], r=['fo%d' % fi])
        P.barrier()
    if DBG.get('stop') == 's5p1':
        return

    with ExitStack() as B:
        sb = lambda name, shape, dt: B.enter_context(nc.sbuf_tensor('%sB%d' % (name, li), shape, dt))
        ps = lambda name, shape, dt: B.enter_context(nc.psum_tensor('%sB%d' % (name, li), shape, dt))
        identf = K.ident_f
        Tb = sb('Tb', [128, 64, 128], BF16)
        Pm = sb('Pm', [128, 64, 2, 64], BF16)
        QmR = sb('QmR', [64, 64, 8, 16], BF16)
        QmI = sb('QmI', [64, 64, 8, 16], BF16)
        ar8 = sb('ar8', [64, 64], F32)
        ai8 = sb('ai8', [64, 64], F32)
        dcol = sb('dcol', [128, 8], F32)
        pA = ps('pA', [128, 512], F32)
        pB = ps('pB', [128, 512], F32)
        pC = ps('pC', [128, 512], F32)
        pD = ps('pD', [128, 512], F32)
        pK = ps('pK', [128, 1024], F32)
        DMA(P, dcol[:], prm['d'].rearrange("g c -> (g c)").rearrange("(k p) -> p k", p=128), w=['dcol'], slow=True)

        with ExitStack() as T:
            tb = lambda name, shape, dt: T.enter_context(nc.sbuf_tensor('%sT%d' % (name, li), shape, dt))
            t64 = lambda name: tb(name, [64, 64], F32)
            lamre_g, lamim_g, lre, lim, dtb = t64('lamre_g'), t64('lamim_g'), t64('lre'), t64('lim'), t64('dtb')
            xr_, yi_, mag, sn, cs_, abr, abi = t64('xr_'), t64('yi_'), t64('mag'), t64('sn'), t64('cs_'), t64('abr'), t64('abi')
            den, nr, cfr, cfi, tA, tB, tC = t64('den'), t64('nr'), t64('cfr'), t64('cfi'), t64('tA'), t64('tB'), t64('tC')
            tI = tb('tI', [64, 64], I32)
            bre = tb('bre', [64, 64, 16], F32)
            bim = tb('bim', [64, 64, 16], F32)
            bbr = tb('bbr', [64, 64, 16], F32)
            bbi = tb('bbi', [64, 64, 16], F32)
            t3 = tb('t3', [64, 64, 16], F32)
            t4 = tb('t4', [64, 64, 16], F32)
            Lr = tb('Lr', [64, 9, 64], F32)
            Li = tb('Li', [64, 9, 64], F32)
            Wr = tb('Wr', [64, 64, 8, 16], F32)
            Wi = tb('Wi', [64, 64, 8, 16], F32)
            Cst = tb('Cst', [128, 8, 64], F32)
            CrT = tb('CrT', [64, 64, 16], F32)
            CiT = tb('CiT', [64, 64, 16], F32)
            nCrT = tb('nCrT', [64, 64, 16], F32)
            nCiT = tb('nCiT', [64, 64, 16], F32)
            Ktsb = tb('Ktsb', [128, 64, 16], F32)

            def V(eng, out, a, b, op, r, w):
                TT(P, eng, out, a, b, op, r=r, w=w)

            def VN(eng, out, a, b, op, r, w):
                TT(P, eng, out, a, b, op, r=r, w=w, nosync=True)

            DMA(P, lamre_g[:], prm['lambda_re'], w=['lamre_g'])
            DMA(P, lamim_g[:], prm['lambda_im'], w=['lamim_g'])
            TR(P, pA[0:64, 0:64], lamre_g[:], identf[0:64, 0:64], r=['lamre_g', 'identf'], w=['pA'])
            TR(P, pA[0:64, 64:128], lamim_g[:], identf[0:64, 0:64], r=['lamim_g', 'identf'], w=['pA'])
            COPY(P, 'dve', lre[:], pA[0:64, 0:64], r=['pA'], w=['lre'])
            COPY(P, 'dve', lim[:], pA[0:64, 64:128], r=['pA'], w=['lim'])
            DMA(P, dtb[:], prm['log_dt'].partition_broadcast(64), w=['dtb'])
            ACTF(P, dtb[:], dtb[:], AF.Exp, r=['dtb'], w=['dtb'])
            TS(P, 'dve', lre[:], lre[:], -1e-4, ALU.min, r=['lre'], w=['lre'])
            V('dve', xr_[:], lre[:], dtb[:], ALU.mult, ['lre', 'dtb'], ['xr_'])
            V('dve', yi_[:], lim[:], dtb[:], ALU.mult, ['lim', 'dtb'], ['yi_'])
            ACTF(P, mag[:], xr_[:], AF.Exp, r=['xr_'], w=['mag'])

            def sin_shift(out, okey, shift):
                TS(P, 'dve', tA[:], yi_[:], 1.0 / TWO_PI, ALU.mult, shift / TWO_PI, ALU.add, r=['yi_'], w=['tA'])
                COPY(P, 'dve', tI[:], tA[:], r=['tA'], w=['tI'])
                COPY(P, 'dve', tB[:], tI[:], r=['tI'], w=['tB'])
                STT(P, tC[:], tB[:], -TWO_PI, yi_[:], ALU.mult, ALU.add, r=['tB', 'yi_'], w=['tC'])
                TS(P, 'dve', tC[:], tC[:], shift, ALU.add, 3.141592, ALU.min, r=['tC'], w=['tC'])
                TS(P, 'dve', tC[:], tC[:], -3.141592, ALU.max, r=['tC'], w=['tC'])
                ACTF(P, out, tC[:], AF.Sin, r=['tC'], w=[okey])

            sin_shift(sn[:], 'sn', 0.0)
            sin_shift(cs_[:], 'cs_', math.pi / 2.0)
            V('dve', abr[:], mag[:], cs_[:], ALU.mult, ['mag', 'cs_'], ['abr'])
            V('dve', abi[:], mag[:], sn[:], ALU.mult, ['mag', 'sn'], ['abi'])
            V('dve', den[:], lre[:], lre[:], ALU.mult, ['lre'], ['den'])
            V('dve', tA[:], lim[:], lim[:], ALU.mult, ['lim'], ['tA'])
            V('dve', den[:], den[:], tA[:], ALU.add, ['den', 'tA'], ['den'])
            P.add('dve', lambda e: e.reciprocal(out=den[:], in_=den[:]), r=['den'], w=['den'])
            TS(P, 'dve', nr[:], abr[:], -1.0, ALU.add, r=['abr'], w=['nr'])
            V('dve', tA[:], nr[:], lre[:], ALU.mult, ['nr', 'lre'], ['tA'])
            V('dve', tB[:], abi[:], lim[:], ALU.mult, ['abi', 'lim'], ['tB'])
            V('dve', tA[:], tA[:], tB[:], ALU.add, ['tA', 'tB'], ['tA'])
            V('dve', cfr[:], tA[:], den[:], ALU.mult, ['tA', 'den'], ['cfr'])
            V('dve', tA[:], abi[:], lre[:], ALU.mult, ['abi', 'lre'], ['tA'])
            V('dve', tB[:], nr[:], lim[:], ALU.mult, ['nr', 'lim'], ['tB'])
            V('dve', tA[:], tA[:], tB[:], ALU.subtract, ['tA', 'tB'], ['tA'])
            V('dve', cfi[:], tA[:], den[:], ALU.mult, ['tA', 'den'], ['cfi'])
            DMA(P, bre[:], prm['b_re'].rearrange("g p c -> p g c"), w=['bre'])
            DMA(P, bim[:], prm['b_im'].rearrange("g p c -> p g c"), w=['bim'])
            bc = lambda ap2: ap2.unsqueeze(2).broadcast_to([64, 64, 16])
            V('dve', bbr[:], bre[:], bc(cfr[:]), ALU.mult, ['bre', 'cfr'], ['bbr'])
            V('dve', t3[:], bim[:], bc(cfi[:]), ALU.mult, ['bim', 'cfi'], ['t3'])
            V('dve', bbr[:], bbr[:], t3[:], ALU.subtract, ['bbr', 't3'], ['bbr'])
            V('dve', bbi[:], bim[:], bc(cfr[:]), ALU.mult, ['bim', 'cfr'], ['bbi'])
            V('dve', t3[:], bre[:], bc(cfi[:]), ALU.mult, ['bre', 'cfi'], ['t3'])
            V('dve', bbi[:], bbi[:], t3[:], ALU.add, ['bbi', 't3'], ['bbi'])
            MEMSET(P, 'dve', Lr[:, 0, :], 1.0, w=['L'])
            MEMSET(P, 'dve', Li[:, 0, :], 0.0, r=['L'], w=['L'])
            for tau in range(1, 9):
                V('dve', tA[:], Lr[:, tau - 1, :], abr[:], ALU.mult, ['L', 'abr'], ['tA'])
                V('dve', tB[:], Li[:, tau - 1, :], abi[:], ALU.mult, ['L', 'abi'], ['tB'])
                V('dve', tC[:], Lr[:, tau - 1, :], abi[:], ALU.mult, ['L', 'abi'], ['tC'])
                V('dve', den[:], Li[:, tau - 1, :], abr[:], ALU.mult, ['L', 'abr'], ['den'])
                V('dve', Lr[:, tau, :], tA[:], tB[:], ALU.subtract, ['tA', 'tB', 'L'], ['L'])
                V('dve', Li[:, tau, :], tC[:], den[:], ALU.add, ['tC', 'den', 'L'], ['L'])
            for tau in range(8):
                lrb = bc(Lr[:, tau, :])
                lib = bc(Li[:, tau, :])
                V('dve', Wr[:, :, tau, :], bbr[:], lrb, ALU.mult, ['bbr', 'L'], ['Wr'])
                V('dve', t3[:], bbi[:], lib, ALU.mult, ['bbi', 'L'], ['t3'])
                V('dve', Wr[:, :, tau, :], Wr[:, :, tau, :], t3[:], ALU.subtract, ['Wr', 't3'], ['Wr'])
                V('dve', Wi[:, :, tau, :], bbi[:], lrb, ALU.mult, ['bbi', 'L'], ['Wi'])
                V('dve', t4[:], bbr[:], lib, ALU.mult, ['bbr', 'L'], ['t4'])
                V('dve', Wi[:, :, tau, :], Wi[:, :, tau, :], t4[:], ALU.add, ['Wi', 't4'], ['Wi'])
            for nm, dstT in (('c_re', CrT), ('c_im', CiT)):
                DMA(P, Cst[:], prm[nm].rearrange("g c p -> (g c) p").rearrange("(k r) p -> r k p", r=128), r=[], w=['Cst'])
                for k in range(8):
                    TR(P, pK[0:64, k * 128:(k + 1) * 128], Cst[:, k, :], identf[:], r=['Cst', 'identf'], w=['pK'])
                COPY(P, 'dve', dstT[:].rearrange("p g c -> p (g c)"), pK[0:64, :], r=['pK'], w=[nm])
            TS(P, 'dve', nCrT[:], CrT[:], -1.0, ALU.mult, r=['c_re'], w=['nCrT'])
            TS(P, 'dve', nCiT[:], CiT[:], -1.0, ALU.mult, r=['c_im'], w=['nCiT'])
            for g in range(64):
                MM(P, pK[:, g * 16:(g + 1) * 16], Wr[:, g, :, :].rearrange("p t c -> p (t c)"), CrT[:, g, :], True, False,
                   r=['Wr', 'c_re'], w=['pK'])
                MM(P, pK[:, g * 16:(g + 1) * 16], Wi[:, g, :, :].rearrange("p t c -> p (t c)"), nCiT[:, g, :], False, True,
                   r=['Wi', 'nCiT'], w=['pK'])
            COPY(P, 'dve', Ktsb[:].rearrange("p g c -> p (g c)"), pK[:, :], r=['pK'], w=['Ktsb'])
            DMA(P, K.Kd_d.rearrange("l c g o -> (l c) g o"), Ktsb[:], r=['Ktsb'], w=['Kd'])
            for gb4 in range(16):
                pp = pA if gb4 % 2 == 0 else pB
                pk_ = 'pA' if gb4 % 2 == 0 else 'pB'
                for gi in range(4):
                    g = gb4 * 4 + gi
                    for ri, W_ in enumerate((Wr, Wi)):
                        TR(P, pp[:, (gi * 2 + ri) * 64:(gi * 2 + ri + 1) * 64], W_[:, g, :, :].rearrange("p t c -> p (t c)"),
                           identf[0:64, 0:64], r=['Wr', 'Wi', 'identf'], w=[pk_])
                COPY(P, 'dve', Pm[:, gb4 * 4:(gb4 + 1) * 4, :, :].rearrange("p g r q -> p (g r q)"), pp[:, :], r=[pk_], w=['Pm'])
            for t in range(8):
                lrb = bc(Lr[:, t + 1, :])
                lib = bc(Li[:, t + 1, :])
                V('dve', t3[:], CrT[:], lrb, ALU.mult, ['c_re', 'L'], ['t3'])
                V('dve', t4[:], nCiT[:], lib, ALU.mult, ['nCiT', 'L'], ['t4'])
                V('dve', QmR[:, :, t, :], t3[:], t4[:], ALU.add, ['t3', 't4'], ['QmR'])
                V('dve', t3[:], nCrT[:], lib, ALU.mult, ['nCrT', 'L'], ['t3'])
                V('dve', t4[:], nCiT[:], lrb, ALU.mult, ['nCiT', 'L'], ['t4'])
                V('dve', QmI[:, :, t, :], t3[:], t4[:], ALU.add, ['t3', 't4'], ['QmI'])
            COPY(P, 'dve', ar8[:], Lr[:, 8, :], r=['L'], w=['ar8'])
            COPY(P, 'dve', ai8[:], Li[:, 8, :], r=['L'], w=['ai8'])
            TAP(P, nc, 'abr', abr[:], [64, 64], F32, ['abr'])
            TAP(P, nc, 'abi', abi[:], [64, 64], F32, ['abi'])
            TAP(P, nc, 'bbr', bbr[:], [64, 64, 16], F32, ['bbr'])
            TAP(P, nc, 'bbi', bbi[:], [64, 64, 16], F32, ['bbi'])
            TAP(P, nc, 'Lr', Lr[:], [64, 9, 64], F32, ['L'])
            TAP(P, nc, 'CrT', CrT[:], [64, 64, 16], F32, ['c_re'])
            TAP(P, nc, 'Ktsb', Ktsb[:], [128, 64, 16], F32, ['Ktsb'])
            TAP(P, nc, 'Pm', Pm[:], [128, 64, 2, 64], BF16, ['Pm'])
            TAP(P, nc, 'QmR', QmR[:], [64, 64, 8, 16], BF16, ['QmR'])
            TAP(P, nc, 'QmI', QmI[:], [64, 64, 8, 16], BF16, ['QmI'])
            P.barrier()
        if DBG.get('stop') == 's5setup':
            return

        with ExitStack() as T2:
            Tsb = T2.enter_context(nc.sbuf_tensor('TsbT%d' % li, [128, 64, 8, 16], F32))
            MEMSET(P, 'pool', Tsb[:], 0.0, w=['Tsb'])
            tkeys = []
            for tp in range(8):
                for lag in range(tp + 1):
                    key = 'Tsb_%d_%d' % (tp, lag)
                    tkeys.append(key)
                    DMA(P, Tsb[16 * tp:16 * tp + 16, :, 7 - tp + lag, :], K.Kd_d[lag].rearrange("c g o -> c g o"),
                        r=['Kd', 'Tsb'], w=[key])
            COPY(P, 'dve', Tb[:].rearrange("p g m -> p (g m)"), Tsb[:].rearrange("p g t c -> p (g t c)"), r=tkeys + ['Tsb'], w=['Tb'])
            TAP(P, nc, 'Tb', Tb[:], [128, 64, 128], BF16, ['Tb'])
            P.barrier()
        if DBG.get('stop') == 's5t2':
            return

        with ExitStack() as Sg:
            sg = lambda name, shape, dt: Sg.enter_context(nc.sbuf_tensor('%sS%d' % (name, li), shape, dt))
            Sel = sg('Sel', [128, 8, 8, 128], BF16)
            SelT = sg('SelT', [128, 8, 8, 128], BF16)
            DMA(P, Sel[:], K.c_Sel, w=['const'])
            DMA(P, SelT[:], K.c_SelT, w=['const'])
            useg = [sg('useg%d' % i, [128, 8, 512], BF16) for i in range(2)]
            Uall = sg('Uall', [128, 64, 64], BF16)
            Bcr = sg('Bcr', [64, 64, 64], F32)
            Bci = sg('Bci', [64, 64, 64], F32)
            Xbr = sg('Xbr', [64, 64, 64], BF16)
            Xbi = sg('Xbi', [64, 64, 64], BF16)
            xr = [sg('xr%d' % i, [64, 64], F32) for i in range(2)]
            xi = [sg('xi%d' % i, [64, 64], F32) for i in range(2)]
            s1, s2, s3, s4 = sg('s1', [64, 64], F32), sg('s2', [64, 64], F32), sg('s3', [64, 64], F32), sg('s4', [64, 64], F32)
            Ysb = [sg('Ysb%d' % i, [128, 8, 64], BF16) for i in range(2)]
            yv = [sg('yv%d' % i, [128, 512], F32) for i in range(2)]
            yg = [sg('yg%d' % i, [128, 512], BF16) for i in range(2)]
            MEMSET(P, 'dve', xr[0][:], 0.0, w=['xr0'])
            MEMSET(P, 'dve', xi[0][:], 0.0, w=['xi0'])
            step = 0
            for seg in range(NC):
                ub = seg % 2
                uk = 'useg%d' % ub
                DMA(P, useg[ub][:], K.uT_d[:, seg * 512:(seg + 1) * 512].rearrange("(k p) t -> p k t", p=128), w=[uk])
                for k in range(8):
                    for gl in range(8):
                        for s in range(8):
                            MM(P, pA[:, gl * 64:(gl + 1) * 64], Sel[:, gl, s, :], useg[ub][:, k, s * 64:(s + 1) * 64], s == 0, s == 7,
                               r=['const', uk], w=['pA'])
                    COPY(P, 'act', Uall[:, k * 8:(k + 1) * 8, :].rearrange("p g n -> p (g n)"), pA[:, :], r=['pA'], w=['Uall%d' % k])
                    for gl in range(8):
                        g = 8 * k + gl
                        MM(P, pB[0:64, gl * 64:(gl + 1) * 64], Pm[:, g, 0, :], Uall[:, g, :], True, True, r=['Uall%d' % k], w=['pB'])
                        MM(P, pC[0:64, gl * 64:(gl + 1) * 64], Pm[:, g, 1, :], Uall[:, g, :], True, True, r=['Uall%d' % k], w=['pC'])
                    COPY(P, 'dve', Bcr[:, k * 8:(k + 1) * 8, :].rearrange("p g n -> p (g n)"), pB[0:64, :], r=['pB'], w=['Bcr%d' % k])
                    COPY(P, 'act', Bci[:, k * 8:(k + 1) * 8, :].rearrange("p g n -> p (g n)"), pC[0:64, :], r=['pC'], w=['Bci%d' % k])
                bk = ['Bcr%d' % k for k in range(8)] + ['Bci%d' % k for k in range(8)]
                for n in range(64):
                    pr = xr[0][:] if n == 0 else Bcr[:, :, n - 1]
                    pi_ = xi[0][:] if n == 0 else Bci[:, :, n - 1]
                    rk = ['xr0', 'xi0'] + bk
                    VN('dve', s1[:], ar8[:], pr, ALU.mult, rk, ['s1'])
                    VN('dve', s2[:], ai8[:], pi_, ALU.mult, rk, ['s2'])
                    VN('dve', s3[:], ar8[:], pi_, ALU.mult, rk, ['s3'])
                    VN('dve', s4[:], ai8[:], pr, ALU.mult, rk, ['s4'])
                    VN('dve', s1[:], s1[:], s2[:], ALU.subtract, ['s1', 's2'], ['s1'])
                    VN('dve', s3[:], s3[:], s4[:], ALU.add, ['s3', 's4'], ['s3'])
                    VN('dve', Bcr[:, :, n], s1[:], Bcr[:, :, n], ALU.add, ['s1'] + bk, bk)
                    VN('dve', Bci[:, :, n], s3[:], Bci[:, :, n], ALU.add, ['s3'] + bk, bk)
                COPY(P, 'act', Xbr[:, :, 0], xr[0][:], r=['xr0'], w=['Xbr'])
                COPY(P, 'act', Xbi[:, :, 0], xi[0][:], r=['xi0'], w=['Xbi'])
                COPY(P, 'act', Xbr[:, :, 1:64], Bcr[:, :, 0:63], r=bk, w=['Xbr'])
                COPY(P, 'act', Xbi[:, :, 1:64], Bci[:, :, 0:63], r=bk, w=['Xbi'])
                COPY(P, 'dve', xr[0][:], Bcr[:, :, 63], r=bk + ['xr0'], w=['xr0'])
                COPY(P, 'dve', xi[0][:], Bci[:, :, 63], r=bk + ['xi0'], w=['xi0'])
                for k in range(8):
                    yb = k % 2
                    for gl in range(8):
                        g = 8 * k + gl
                        o_ = pD[:, gl * 64:(gl + 1) * 64]
                        MM(P, o_, Tb[:, g, :], Uall[:, g, :], True, False, r=['Uall%d' % k], w=['pD'])
                        MM(P, o_, QmR[:, g, :, :].rearrange("p t c -> p (t c)"), Xbr[:, g, :], False, False, r=['Xbr'], w=['pD'])
                        MM(P, o_, QmI[:, g, :, :].rearrange("p t c -> p (t c)"), Xbi[:, g, :], False, True, r=['Xbi'], w=['pD'])
                    COPY(P, 'act', Ysb[yb][:].rearrange("p g n -> p (g n)"), pD[:, :], r=['pD'], w=['Ysb%d' % yb])
                    pY = pB if k % 2 == 0 else pC
                    pyk = 'pB' if k % 2 == 0 else 'pC'
                    for t in range(8):
                        for gl in range(8):
                            MM(P, pY[:, t * 64:(t + 1) * 64], SelT[:, gl, t, :], Ysb[yb][:, gl, :], gl == 0, gl == 7,
                               r=['const', 'Ysb%d' % yb], w=[pyk])
                    STT(P, yv[yb][:], useg[ub][:, k, :], dcol[:, k:k + 1], pY[:, :], ALU.mult, ALU.add, r=[uk, pyk, 'dcol'], w=['yv%d' % yb])
                    ACTF(P, yg[yb][:], yv[yb][:], AF.Gelu_apprx_tanh, r=['yv%d' % yb], w=['yg%d' % yb])
                    DMA(P, K.ygT_d[k * 128:(k + 1) * 128, seg * 512:(seg + 1) * 512], yg[yb][:], r=['yg%d' % yb])
            P.barrier()
    if DBG.get('stop') == 's5scan':
        return

    with ExitStack() as Cx:
        sb = lambda name, shape, dt: Cx.enter_context(nc.sbuf_tensor('%sC%d' % (name, li), shape, dt))
        ps = lambda name, shape, dt: Cx.enter_context(nc.psum_tensor('%sC%d' % (name, li), shape, dt))
        wg = sb('wg', [128, 8, 2048], BF16)
        wo = sb('wo', [128, 8, D], BF16)
        stage = [sb('wst0', [128, 2048], F32), sb('wst1', [128, 2048], F32)]
        ygs = [sb('ygs%d' % i, [128, 8, 512], BF16) for i in range(2)]
        zss = [sb('zss%d' % i, [128, 8, 512], BF16) for i in range(2)]
        sig = [sb('sig%d' % i, [128, 512], F32) for i in range(2)]
        tga = [sb('tga%d' % i, [128, 512], F32) for i in range(2)]
        ozT = [sb('ozT%d' % i, [128, 8, 512], BF16) for i in range(2)]
        ht = [sb('ht%d' % i, [128, D], F32) for i in range(2)]
        hn = [sb('hn%d' % i, [128, D], F32) for i in range(2)]
        pa = [ps('pa%d' % i, [128, 512], F32) for i in range(2)]
        pb = [ps('pb%d' % i, [128, 512], F32) for i in range(2)]
        pm = [ps('pm%d' % i, [128, 512], F32) for i in range(4)]
        gfb = None
        if final_g is not None:
            gfb = sb('gfb', [128, D], F32)
            DMA(P, gfb[:], final_g.partition_broadcast(128), w=['gb'])
        load_weight_bf16(P, K, wg, prm['w_glu'], 2048, stage, 'wglu')
        load_weight_bf16(P, K, wo, prm['w_out'], D, stage, 'wout')
        for seg in range(NC):
            sbi = seg % 2
            cols = slice(seg * 512, (seg + 1) * 512)
            DMA(P, ygs[sbi][:], K.ygT_d[:, cols].rearrange("(k p) t -> p k t", p=128), w=['ygs%d' % sbi])
            DMA(P, zss[sbi][:], K.zsT_d[:, cols].rearrange("(k p) t -> p k t", p=128), w=['zss%d' % sbi])
            for m in range(8):
                x = m % 2
                for kc in range(8):
                    MM(P, pa[x][:], wg[:, kc, m * 128:(m + 1) * 128], ygs[sbi][:, kc, :], kc == 0, kc == 7,
                       r=['ygs%d' % sbi, 'wglu_wb'], w=['pa%d' % x])
                for kc in range(8):
                    MM(P, pb[x][:], wg[:, kc, 1024 + m * 128:1024 + (m + 1) * 128], ygs[sbi][:, kc, :], kc == 0, kc == 7,
                       r=['ygs%d' % sbi, 'wglu_wb'], w=['pb%d' % x])
                ACTF(P, sig[x][:], pb[x][:], AF.Sigmoid, r=['pb%d' % x], w=['sig%d' % x])
                TT(P, 'dve', tga[x][:], pa[x][:], sig[x][:], ALU.mult, r=['pa%d' % x, 'sig%d' % x], w=['tga%d' % x])
                TT(P, 'pool', ozT[sbi][:, m, :], tga[x][:], zss[sbi][:, m, :], ALU.mult, r=['tga%d' % x, 'zss%d' % sbi], w=['ozT%d_%d' % (sbi, m)])
            ozk = ['ozT%d_%d' % (sbi, m) for m in range(8)]
            for tt in range(4):
                b = (4 * seg + tt) % 2
                DMA(P, ht[b][0:64, :], hv_src[seg, 2 * tt], w=['ht%da' % b])
                DMA(P, ht[b][64:128, :], hv_src[seg, 2 * tt + 1], w=['ht%db' % b])
                for half in range(2):
                    pi = (2 * tt + half) % 4
                    for kc in range(8):
                        MM(P, pm[pi][:], ozT[sbi][:, kc, tt * 128:(tt + 1) * 128], wo[:, kc, half * 512:(half + 1) * 512], kc == 0, kc == 7,
                           r=ozk + ['wout_wb'], w=['pm%d' % pi])
                    TT(P, 'dve', hn[b][:, half * 512:(half + 1) * 512], pm[pi][:], ht[b][:, half * 512:(half + 1) * 512], ALU.add,
                       r=['pm%d' % pi, 'ht%da' % b, 'ht%db' % b], w=['hn%d_%d' % (b, half)])
                hk = ['hn%d_0' % b, 'hn%d_1' % b]
                dsts = [hv_dst[seg, 2 * tt], hv_dst[seg, 2 * tt + 1]]
                prs = [slice(0, 64), slice(64, 128)]
                if final_g is None:
                    for d_, pr in zip(dsts, prs):
                        DMA(P, d_, hn[b][pr, :], r=hk)
                else:
                    final_norm_store(P, K, hn[b], hk, gfb, ht[b], ['ht%da' % b, 'ht%db' % b], dsts, prs)
        P.barrier()


def build(layers=(0, 1, 2, 3), with_final=True):
    nc = bass.Bass("TRN2", target_bir_lowering=False)
    K = Ctx()
    x = nc.dram_tensor("x", [S, D], F32, kind="ExternalInput").ap()
    y = nc.dram_tensor("y", [S, D], F32, kind="ExternalOutput").ap()
    prm = {}
    for li in layers:
        prm[li] = {}
        for nm, shp in (NSA_PARAMS if li % 2 == 0 else S5_PARAMS):
            prm[li][nm] = nc.dram_tensor("l%d_%s" % (li, nm), shp, F32, kind="ExternalInput").ap()
    fng = nc.dram_tensor("final_norm", [D], F32, kind="ExternalInput").ap()
    for nm, shp, dt in CONST_SPECS:
        setattr(K, 'c_' + nm, nc.dram_tensor("c_" + nm, shp, dt, kind="ExternalInput").ap())
    scr = lambda nm, shp, dt: nc.dram_tensor("scr_" + nm, shp, dt, kind=("ExternalOutput" if nm in DBG.get('dump', ()) else "Internal")).ap()
    K.qT_d = scr('qT', [1024, S], BF16)
    K.kcT_d = scr('kcT', [256, S], BF16)
    K.vcT_d = scr('vcT', [256, S], BF16)
    K.ksT_d = scr('ksT', [256, S], BF16)
    K.kwT_d = scr('kwT', [256, S], BF16)
    K.vs_d = scr('vs', [S, 256], BF16)
    K.vw_d = scr('vw', [S, 256], BF16)
    K.gates_d = scr('gates', [S, 48], F32)
    K.zs_d = scr('zs', [S, 1024], BF16)
    K.oz_d = scr('oz', [S, 1024], BF16)
    K.uT_d = scr('uT', [1024, S], BF16)
    K.zsT_d = scr('zsT', [1024, S], BF16)
    K.ygT_d = scr('ygT', [1024, S], BF16)
    K.Kd_d = scr('Kd', [8, 16, 64, 16], F32)
    P = Prog(nc)
    with ExitStack() as G:
        gs = lambda name, shape, dt: G.enter_context(nc.sbuf_tensor(name, shape, dt))
        K.ident_b = gs('ident_b', [128, 128], BF16)
        K.ident_f = gs('ident_f', [128, 128], F32)
        K.junkf = gs('junkf', [128, D], F32)
        K.ss = gs('ss', [128, 1], F32)
        K.sq = gs('sq', [128, 1], F32)
        K.rs = gs('rs', [128, 1], F32)
        DMA(P, K.ident_b[:], K.c_ident_b, w=['const'])
        DMA(P, K.ident_f[:], K.c_ident_f, w=['identf'])
        P.barrier()
        hsrc = x
        for li in layers:
            fg = fng if (with_final and li == layers[-1]) else None
            if li % 2 == 0:
                nsa_layer(P, nc, K, li, prm[li], hsrc, y, fg)
            else:
                s5_layer(P, nc, K, li, prm[li], hsrc, y, fg)
            hsrc = y
        run_prog(nc, P)
    return nc


ALL_INPUT_NAMES = (
    'x',
    'l0_norm',
    'l0_w_in',
    'l0_cmp_k_pe',
    'l0_cmp_k_w1',
    'l0_cmp_k_w2',
    'l0_cmp_v_pe',
    'l0_cmp_v_w1',
    'l0_cmp_v_w2',
    'l0_w_out',
    'l1_norm',
    'l1_w_in',
    'l1_log_dt',
    'l1_lambda_re',
    'l1_lambda_im',
    'l1_b_re',
    'l1_b_im',
    'l1_c_re',
    'l1_c_im',
    'l1_d',
    'l1_w_glu',
    'l1_w_out',
    'l2_norm',
    'l2_w_in',
    'l2_cmp_k_pe',
    'l2_cmp_k_w1',
    'l2_cmp_k_w2',
    'l2_cmp_v_pe',
    'l2_cmp_v_w1',
    'l2_cmp_v_w2',
    'l2_w_out',
    'l3_norm',
    'l3_w_in',
    'l3_log_dt',
    'l3_lambda_re',
    'l3_lambda_im',
    'l3_b_re',
    'l3_b_im',
    'l3_c_re',
    'l3_c_im',
    'l3_d',
    'l3_w_glu',
    'l3_w_out',
    'final_norm',
)


_NC_CACHE = {}


def make_in_map(inputs, b, layers=(0, 1, 2, 3)):
    C = host_constants()
    m = {'x': np.ascontiguousarray(inputs['x'][b], dtype=np.float32)}
    for li in layers:
        for nm, shp in (NSA_PARAMS if li % 2 == 0 else S5_PARAMS):
            key = 'l%d_%s' % (li, nm)
            m[key] = np.ascontiguousarray(inputs[key], dtype=np.float32)
    m['final_norm'] = np.ascontiguousarray(inputs['final_norm'], dtype=np.float32)
    for nm, shp, dt in CONST_SPECS:
        m['c_' + nm] = C[nm]
    return m


def kernel(**inputs):
    inputs = {k: np.asarray(inputs[k]) for k in ALL_INPUT_NAMES}
    if 'full' not in _NC_CACHE:
        _NC_CACHE['full'] = build()
    nc = _NC_CACHE['full']
    in_maps = [make_in_map(inputs, c % 4) for c in range(8)]
    res = run_bass_kernel_spmd(nc, in_maps, core_ids=list(range(8)))
    out = np.stack([np.asarray(res.results[b]['y'], dtype=np.float32) for b in range(4)], axis=0)
    return out
```

```python
import math
from contextlib import ExitStack
import numpy as np
import ml_dtypes
import concourse.bass as bass
import concourse.mybir as mybir
from concourse.bass_utils import run_bass_kernel_spmd

F32 = mybir.dt.float32
BF16 = mybir.dt.bfloat16
I32 = mybir.dt.int32
AF = mybir.ActivationFunctionType
ALU = mybir.AluOpType
AX = mybir.AxisListType
NPBF = ml_dtypes.bfloat16

S = 8192
D = 1024
NH = 16
DH = 64
NG = 4
NT = S // 128
NC = S // 512
NSA_IN = 3632
EPS = 1e-6
MASKV = -30000.0

ENGS = ['pe', 'act', 'dve', 'pool', 'sp']
KRING = 8
SAME_ENGINE_SYNC = {'act': True, 'dve': True, 'pool': True, 'pe': False, 'sp': False}


class Prog:
    def __init__(self, nc):
        self.nc = nc
        self.q = {e: [] for e in ENGS}
        self.ncomp = {e: 0 for e in ENGS}
        self.ndma = {e: 0 for e in ENGS}
        self.lastw = {}
        self.readers = {}

    def add(self, eng, fn, r=(), w=(), dma=False, nosync=False):
        deps = {}
        dd = set()

        def addtok(t):
            if t is None:
                return
            if t[0] == 'c':
                if deps.get(t[1], 0) < t[2]:
                    deps[t[1]] = t[2]
            else:
                dd.add(t)

        for k in r:
            addtok(self.lastw.get(k))
        for k in w:
            addtok(self.lastw.get(k))
            rd = self.readers.get(k)
            if rd:
                for e2, n2 in rd[0].items():
                    addtok(('c', e2, n2))
                for t in rd[1]:
                    addtok(t)
        if dma:
            tok = ('d', eng, self.ndma[eng])
            self.ndma[eng] += 1
        else:
            self.ncomp[eng] += 1
            tok = ('c', eng, self.ncomp[eng])
        if nosync:
            deps.pop(eng, None)
        self.q[eng].append(('op', fn, deps, dd, tok))
        for k in w:
            self.lastw[k] = tok
            self.readers[k] = [{}, set()]
        for k in r:
            rd = self.readers.setdefault(k, [{}, set()])
            if tok[0] == 'c':
                if rd[0].get(tok[1], 0) < tok[2]:
                    rd[0][tok[1]] = tok[2]
            else:
                rd[1].add(tok)
        return tok

    def barrier(self):
        sc = dict(self.ncomp)
        sd = dict(self.ndma)
        for e in ENGS:
            self.q[e].append(('bar', sc, sd))
        self.lastw = {}
        self.readers = {}

    def run_engine(self, ename, eng, semc, semd):
        waited_c = {}
        waited_d = {}

        def wait_c(f, n):
            if n <= 0 or waited_c.get(f, 0) >= n:
                return
            eng.wait_ge(semc[f], n)
            waited_c[f] = n

        def wait_d(qn, i):
            if i < 0:
                return
            slot = i % KRING
            tgt = 16 * (i // KRING + 1)
            if waited_d.get((qn, slot), 0) >= tgt:
                return
            eng.wait_ge(semd[qn][slot], tgt)
            waited_d[(qn, slot)] = tgt

        for item in self.q[ename]:
            if item[0] == 'bar':
                _, sc, sd = item
                for f in ENGS:
                    if f == ename and not SAME_ENGINE_SYNC[ename]:
                        continue
                    wait_c(f, sc[f])
                for qn in ENGS:
                    n = sd[qn]
                    for i in range(max(0, n - KRING), n):
                        wait_d(qn, i)
                continue
            _, fn, deps, dd, tok = item
            for f, n in deps.items():
                if f == ename and not SAME_ENGINE_SYNC[ename]:
                    continue
                wait_c(f, n)
            for t in dd:
                wait_d(t[1], t[2])
            if tok[0] == 'd':
                i = tok[2]
                if i >= KRING:
                    wait_d(ename, i - KRING)
                ins = fn(eng)
                ins.then_inc(semd[ename][i % KRING], 16)
            else:
                ins = fn(eng)
                ins.then_inc(semc[ename], 1)
        n = self.ndma[ename]
        for i in range(max(0, n - KRING), n):
            wait_d(ename, i)


def run_prog(nc, prog):
    with ExitStack() as st:
        semc = {e: st.enter_context(nc.semaphore('c_' + e)) for e in ENGS}
        semd = {e: [st.enter_context(nc.semaphore('d_%s_%d' % (e, i))) for i in range(KRING)] for e in ENGS}
        block = st.enter_context(nc.Block())

        @block.tensor
        def _(eng):
            prog.run_engine('pe', eng, semc, semd)

        @block.scalar
        def _(eng):
            prog.run_engine('act', eng, semc, semd)

        @block.vector
        def _(eng):
            prog.run_engine('dve', eng, semc, semd)

        @block.gpsimd
        def _(eng):
            prog.run_engine('pool', eng, semc, semd)

        @block.sync
        def _(eng):
            prog.run_engine('sp', eng, semc, semd)


def DMA(P, out, in_, r=(), w=(), q='sp', slow=False):
    if slow:
        P.add(q, lambda e: e.dma_start(out=out, in_=in_, allow_slow_non_contiguous=True), r=r, w=w, dma=True)
    else:
        P.add(q, lambda e: e.dma_start(out=out, in_=in_), r=r, w=w, dma=True)


def MM(P, out, lhsT, rhs, start, stop, r=(), w=(), skip=False):
    P.add('pe', lambda e: e.matmul(out, lhsT=lhsT, rhs=rhs, start=start, stop=stop, skip_group_check=skip), r=r, w=w)


def TR(P, out, in_, ident, r=(), w=()):
    P.add('pe', lambda e: e.transpose(out=out, in_=in_, identity=ident), r=r, w=w)


def ACTF(P, out, in_, func, r=(), w=(), scale=None, bias=None, accum=None):
    kw = {}
    if scale is not None:
        kw['scale'] = scale
    if bias is not None:
        kw['bias'] = bias
    if accum is not None:
        kw['accum_out'] = accum
    P.add('act', lambda e: e.activation(out=out, in_=in_, func=func, **kw), r=r, w=w)


def COPY(P, eng, out, in_, r=(), w=()):
    if eng == 'act':
        P.add('act', lambda e: e.copy(out=out, in_=in_), r=r, w=w)
    else:
        P.add(eng, lambda e: e.tensor_copy(out=out, in_=in_), r=r, w=w)


def TT(P, eng, out, in0, in1, op, r=(), w=(), nosync=False):
    P.add(eng, lambda e: e.tensor_tensor(out=out, in0=in0, in1=in1, op=op), r=r, w=w, nosync=nosync)


def TS(P, eng, out, in0, s1, op0, s2=None, op1=None, r=(), w=()):
    if op1 is None:
        P.add(eng, lambda e: e.tensor_scalar(out=out, in0=in0, scalar1=s1, scalar2=None, op0=op0), r=r, w=w)
    else:
        P.add(eng, lambda e: e.tensor_scalar(out=out, in0=in0, scalar1=s1, scalar2=s2, op0=op0, op1=op1), r=r, w=w)


def STT(P, out, in0, scalar, in1, op0, op1, r=(), w=(), accum=None):
    if accum is None:
        P.add('dve', lambda e: e.scalar_tensor_tensor(out=out, in0=in0, scalar=scalar, in1=in1, op0=op0, op1=op1), r=r, w=w)
    else:
        P.add('dve', lambda e: e.scalar_tensor_tensor(out=out, in0=in0, scalar=scalar, in1=in1, op0=op0, op1=op1, accum_out=accum), r=r, w=w)


def MEMSET(P, eng, ap, val, r=(), w=()):
    P.add(eng, lambda e: e.memset(ap, val), r=r, w=w)


def _bf(x):
    return np.asarray(x, dtype=np.float32).astype(NPBF)


def _split3(c):
    c = np.asarray(c, dtype=np.float64)
    hi = _bf(c)
    r1 = c - hi.astype(np.float64)
    mid = _bf(r1)
    r2 = r1 - mid.astype(np.float64)
    lo = _bf(r2)
    return hi, mid, lo


_CONST_CACHE = {}


def host_constants():
    if _CONST_CACHE:
        return _CONST_CACHE
    C = {}
    C['ident_b'] = np.eye(128, dtype=np.float32).astype(NPBF)
    C['ident_f'] = np.eye(128, dtype=np.float32)
    hh = np.arange(1, NH + 1, dtype=np.float64)
    slopes = np.exp2(-8.0 * hh / NH).astype(np.float32).astype(np.float64)
    s_hi = _bf(slopes)
    s_lo = _bf(slopes - s_hi.astype(np.float64))
    sp = s_hi.astype(np.float64) + s_lo.astype(np.float64)
    t = np.arange(S, dtype=np.float64)
    QA = np.zeros((NH, 7, S), dtype=NPBF)
    for h in range(NH):
        QA[h, 0, :] = _bf(64.0 * s_hi[h].astype(np.float64))
        QA[h, 1, :] = _bf(64.0 * s_lo[h].astype(np.float64))
        QA[h, 2, :] = s_hi[h]
        QA[h, 3, :] = s_lo[h]
        c = np.float32(-(sp[h] * t)).astype(np.float64)
        a, b, cc = _split3(c)
        QA[h, 4, :] = a
        QA[h, 5, :] = b
        QA[h, 6, :] = cc
    C['QA'] = QA
    KA = np.zeros((7, S), dtype=NPBF)
    pos = np.arange(S)
    KA[0] = _bf(pos // 64)
    KA[1] = _bf(pos // 64)
    KA[2] = _bf(pos % 64)
    KA[3] = _bf(pos % 64)
    KA[4:7] = _bf(1.0)
    C['KA'] = KA
    KAc = np.zeros((7, 512), dtype=NPBF)
    n = np.arange(511)
    KAc[0, :511] = _bf(n // 4)
    KAc[1, :511] = _bf(n // 4)
    KAc[2, :511] = _bf(16.0 * (n % 4) + 15.5)
    KAc[3, :511] = _bf(16.0 * (n % 4) + 15.5)
    KAc[4:7, :511] = _bf(1.0)
    C['KAc'] = KAc
    OH = np.zeros((32, S), dtype=np.float32)
    OH[(np.arange(S) // 64) % 32, np.arange(S)] = 1.0
    C['OH'] = OH.astype(NPBF)
    OV = np.zeros((128, 4, 128), dtype=np.float32)
    for nn in range(511):
        cs = 16 * nn
        ce = cs + 31
        for j in range(128):
            if cs <= 64 * j + 63 and ce >= 64 * j:
                OV[nn % 128, nn // 128, j] = 1.0
    C['OV'] = OV.astype(NPBF)
    k = np.arange(128)[:, None, None]
    r = np.arange(4)[None, :, None]
    q = np.arange(512)[None, None, :]
    C['causal'] = np.where(128 * r + k > q, MASKV, 0.0).astype(np.float32).astype(NPBF)
    r8 = np.arange(8)[None, :, None]
    dwin = q - k + 512 - 128 * r8
    C['band'] = np.where((dwin >= 0) & (dwin < 512), 0.0, MASKV).astype(np.float32).astype(NPBF)
    r5 = np.array([0, 512, 1024, 1536, 2048])[None, :, None]
    C['cmpmask'] = np.where(16 * k + 31 - r5 > q, MASKV, 0.0).astype(np.float32).astype(NPBF)
    FT = np.zeros((128, 256), dtype=np.float32)
    for p in range(128):
        jr = 1 if p >= 64 else 0
        for c in range(255):
            jj = c - 127
            if jj == jr or jj == jr - 1:
                FT[p, c] = 1e9
            elif jj > jr:
                FT[p, c] = -1e9
    C['FT'] = FT
    Sel = np.zeros((128, 8, 8, 128), dtype=np.float32)
    SelT = np.zeros((128, 8, 8, 128), dtype=np.float32)
    for gl in range(8):
        for s in range(8):
            for c in range(16):
                Sel[16 * gl + c, gl, s, 16 * (7 - s) + c] = 1.0
                SelT[16 * s + c, gl, s, 16 * gl + c] = 1.0
    C['Sel'] = Sel.astype(NPBF)
    C['SelT'] = SelT.astype(NPBF)
    _CONST_CACHE.update(C)
    return C


CONST_SPECS = [
    ('ident_b', [128, 128], BF16), ('ident_f', [128, 128], F32), ('QA', [NH, 7, S], BF16),
    ('KA', [7, S], BF16), ('KAc', [7, 512], BF16), ('OH', [32, S], BF16),
    ('OV', [128, 4, 128], BF16), ('causal', [128, 4, 512], BF16), ('band', [128, 8, 512], BF16),
    ('cmpmask', [128, 5, 512], BF16), ('FT', [128, 256], F32),
    ('Sel', [128, 8, 8, 128], BF16), ('SelT', [128, 8, 8, 128], BF16),
]

NSA_PARAMS = [('norm', [D]), ('w_in', [D, NSA_IN]), ('cmp_k_pe', [32, 64]), ('cmp_k_w1', [2048, 256]),
              ('cmp_k_w2', [256, 64]), ('cmp_v_pe', [32, 64]), ('cmp_v_w1', [2048, 256]),
              ('cmp_v_w2', [256, 64]), ('w_out', [D, D])]
S5_PARAMS = [('norm', [D]), ('w_in', [D, 2048]), ('log_dt', [64]), ('lambda_re', [64, 64]),
             ('lambda_im', [64, 64]), ('b_re', [64, 64, 16]), ('b_im', [64, 64, 16]),
             ('c_re', [64, 16, 64]), ('c_im', [64, 16, 64]), ('d', [64, 16]),
             ('w_glu', [D, 2048]), ('w_out', [D, D])]


class Ctx:
    pass


DBG = {}


def TAP(P, nc, name, ap, shape, dt, r=()):
    if not DBG.get('taps'):
        return
    t = nc.dram_tensor('dbg_' + name, shape, dt, kind='ExternalOutput').ap()
    DMA(P, t, ap, r=list(r))


def load_weight_bf16(P, K, wb, wdram, ncols, stage, tag):
    for kc in range(8):
        sb = stage[kc % 2]
        key = 'wstage%d' % (kc % 2)
        DMA(P, sb[:, 0:ncols], wdram[kc * 128:(kc + 1) * 128, :], w=[key])
        eng = 'dve' if kc % 2 == 0 else 'pool'
        COPY(P, eng, wb[:, kc, :], sb[:, 0:ncols], r=[key], w=['%s_wb' % tag])


def rmsnorm_tile(P, K, xt, gb, xnb, tg, rkeys, wkey):
    STT(P, K.junkf[:], xt, 1.0, xt, ALU.mult, ALU.mult, r=rkeys, w=['junkf', 'ss' + tg], accum=K.ss[:])
    ACTF(P, K.sq[:], K.ss[:], AF.Sqrt, r=['ss' + tg], w=['sq' + tg], scale=1.0 / D, bias=EPS)
    P.add('dve', lambda e: e.reciprocal(out=K.rs[:], in_=K.sq[:]), r=['sq' + tg], w=['rs' + tg])
    STT(P, xnb, xt, K.rs[:], gb, ALU.mult, ALU.mult, r=rkeys + ['rs' + tg, 'gb'], w=[wkey])


def transpose_tile(P, K, src_b, dst, skey, dkey, evac_eng):
    for kc in range(8):
        TR(P, K.pT[:, kc, :], src_b[:, kc * 128:(kc + 1) * 128], K.ident_b[:], r=[skey, 'const'], w=['pT'])
    COPY(P, evac_eng, dst, K.pT[:], r=['pT'], w=[dkey])


def nsa_layer(P, nc, K, li, prm, hsrc, hdst, final_g):
    QOFF, KCOFF, VCOFF, KSOFF, VSOFF, KWOFF, VWOFF, GOFF, ZOFF = 0, 1024, 1280, 1536, 1792, 2048, 2304, 2560, 2608
    ident = K.ident_b
    with ExitStack() as L:
        sbL = lambda name, shape, dt: L.enter_context(nc.sbuf_tensor('%s_l%d' % (name, li), shape, dt))
        kcmpT = sbL('kcmpT', [128, NG, 512], BF16)
        vcmpa = sbL('vcmpa', [128, 4, NG, 65], BF16)

        with ExitStack() as A:
            sb = lambda name, shape, dt: A.enter_context(nc.sbuf_tensor('%sA%d' % (name, li), shape, dt))
            ps = lambda name, shape, dt: A.enter_context(nc.psum_tensor('%sA%d' % (name, li), shape, dt))
            wb = sb('wb', [128, 8, NSA_IN], BF16)
            stage = [sb('wst0', [128, NSA_IN], F32), sb('wst1', [128, NSA_IN], F32)]
            gb = sb('gb', [128, D], F32)
            xt = [sb('xt0', [128, D], F32), sb('xt1', [128, D], F32)]
            xnb = [sb('xnb0', [128, D], BF16), sb('xnb1', [128, D], BF16)]
            xT4 = [sb('xT40', [128, 8, 512], BF16), sb('xT41', [128, 8, 512], BF16)]
            fo = [sb('fo%d' % i, [128, 512], BF16) for i in range(4)]
            go = [sb('go%d' % i, [128, 48], F32) for i in range(2)]
            K.pT = ps('pT', [128, 8, 128], BF16)
            pm = [ps('pm%d' % i, [128, 512], F32) for i in range(4)]

            load_weight_bf16(P, K, wb, prm['w_in'], NSA_IN, stage, 'win')
            DMA(P, gb[:], prm['norm'].partition_broadcast(128), w=['gb'])
            nmm = [0]
            nfo = [0]

            def mm_group(out_ps, pkey, lhs_fn, rhs_fn, rk):
                for kc in range(8):
                    MM(P, out_ps, lhs_fn(kc), rhs_fn(kc), kc == 0, kc == 7, r=rk + ['win_wb'], w=[pkey])

            for c in range(NC):
                cb = c % 2
                for tt in range(4):
                    i = 4 * c + tt
                    b = i % 2
                    DMA(P, xt[b][:], hsrc[i * 128:(i + 1) * 128, :], w=['xt%d' % b])
                    rmsnorm_tile(P, K, xt[b][:], gb[:], xnb[b][:], 'A', ['xt%d' % b], 'xnb%d' % b)
                    transpose_tile(P, K, xnb[b], xT4[cb][:, :, tt * 128:(tt + 1) * 128], 'xnb%d' % b, 'xT4%d' % cb,
                                   'act' if tt % 2 == 0 else 'dve')
                xk = 'xT4%d' % cb
                fm = [(m, 'q') for m in range(8)] + [(0, 'kc'), (1, 'kc'), (0, 'vc'), (1, 'vc'), (0, 'ks'), (1, 'ks'), (0, 'kw'), (1, 'kw')]
                for (m, kind) in fm:
                    off = {'q': QOFF, 'kc': KCOFF, 'vc': VCOFF, 'ks': KSOFF, 'kw': KWOFF}[kind] + m * 128
                    dst = {'q': K.qT_d, 'kc': K.kcT_d, 'vc': K.vcT_d, 'ks': K.ksT_d, 'kw': K.kwT_d}[kind]
                    pi = nmm[0] % 4
                    nmm[0] += 1
                    mm_group(pm[pi][:], 'pm%d' % pi, lambda kc, off=off: wb[:, kc, off:off + 128],
                             lambda kc: xT4[cb][:, kc, :], [xk])
                    fi = nfo[0] % 4
                    nfo[0] += 1
                    if kind == 'q':
                        ACTF(P, fo[fi][:], pm[pi][:], AF.Copy, r=['pm%d' % pi], w=['fo%d' % fi], scale=0.125)
                    else:
                        COPY(P, 'dve', fo[fi][:], pm[pi][:], r=['pm%d' % pi], w=['fo%d' % fi])
                    DMA(P, dst[m * 128:(m + 1) * 128, c * 512:(c + 1) * 512], fo[fi][:], r=['fo%d' % fi])
                for tt in range(4):
                    i = 4 * c + tt
                    rows = slice(i * 128, (i + 1) * 128)
                    lhs = lambda kc, tt=tt: xT4[cb][:, kc, tt * 128:(tt + 1) * 128]
                    pi = nmm[0] % 4
                    nmm[0] += 1
                    for kc in range(8):
                        MM(P, pm[pi][:, 0:256], lhs(kc), wb[:, kc, VSOFF:VSOFF + 256], kc == 0, kc == 7, r=[xk, 'win_wb'], w=['pm%d' % pi])
                    for kc in range(8):
                        MM(P, pm[pi][:, 256:512], lhs(kc), wb[:, kc, VWOFF:VWOFF + 256], kc == 0, kc == 7, r=[xk, 'win_wb'], w=['pm%d' % pi])
                    fi = nfo[0] % 4
                    nfo[0] += 1
                    COPY(P, 'dve', fo[fi][:], pm[pi][:], r=['pm%d' % pi], w=['fo%d' % fi])
                    DMA(P, K.vs_d[rows, :], fo[fi][:, 0:256], r=['fo%d' % fi])
                    DMA(P, K.vw_d[rows, :], fo[fi][:, 256:512], r=['fo%d' % fi])
                    pi = nmm[0] % 4
                    nmm[0] += 1
                    for kc in range(8):
                        MM(P, pm[pi][:, 0:48], lhs(kc), wb[:, kc, GOFF:GOFF + 48], kc == 0, kc == 7, r=[xk, 'win_wb'], w=['pm%d' % pi])
                    gi = i % 2
                    ACTF(P, go[gi][:], pm[pi][:, 0:48], AF.Sigmoid, r=['pm%d' % pi], w=['go%d' % gi])
                    DMA(P, K.gates_d[rows, :], go[gi][:], r=['go%d' % gi])
                    for j in range(2):
                        pi = nmm[0] % 4
                        nmm[0] += 1
                        mm_group(pm[pi][:], 'pm%d' % pi, lhs, lambda kc, j=j: wb[:, kc, ZOFF + 512 * j:ZOFF + 512 * (j + 1)], [xk])
                        fi = nfo[0] % 4
                        nfo[0] += 1
                        ACTF(P, fo[fi][:], pm[pi][:], AF.Silu, r=['pm%d' % pi], w=['fo%d' % fi])
                        DMA(P, K.zs_d[rows, 512 * j:512 * (j + 1)], fo[fi][:], r=['fo%d' % fi])
            P.barrier()

        with ExitStack() as B:
            sb = lambda name, shape, dt: B.enter_context(nc.sbuf_tensor('%sB%d' % (name, li), shape, dt))
            ps = lambda name, shape, dt: B.enter_context(nc.psum_tensor('%sB%d' % (name, li), shape, dt))
            w1st = sb('w1st', [64, 32, 256], F32)
            w1b = [sb('w1b0', [64, 32, 256], BF16), sb('w1b1', [64, 32, 256], BF16)]
            w2st = sb('w2st', [128, 2, 64], F32)
            w2b = [sb('w2b0', [128, 2, 96], BF16), sb('w2b1', [128, 2, 64], BF16)]
            pest = sb('pest', [64, 32], F32)
            peb = [sb('peb0', [64, 32], BF16), sb('peb1', [64, 32], BF16)]
            hb = [sb('hb0', [128, 2], F32), sb('hb1', [128, 2], F32)]
            xin = [sb('xin0', [64, S], BF16), sb('xin1', [64, S], BF16)]
            hid = sb('hid', [128, 2, 512], BF16)
            ph = [ps('ph0', [128, 512], F32), ps('ph1', [128, 512], F32)]
            pb = ps('pb', [128, 512], F32)
            po = ps('po', [128, 512], F32)

            MEMSET(P, 'pool', kcmpT[:], 0.0, w=['kcmpT'])
            MEMSET(P, 'pool', vcmpa[:], 0.0, w=['vcmpa'])
            MEMSET(P, 'pool', hid[:], 0.0, w=['hid'])
            for kv in range(2):
                nm = 'cmp_k' if kv == 0 else 'cmp_v'
                DMA(P, w1st[:], prm[nm + '_w1'].rearrange("(l d) h -> d l h", d=64), w=['w1st'])
                COPY(P, 'dve', w1b[kv][:], w1st[:], r=['w1st'], w=['w1b%d' % kv])
                DMA(P, w2st[:], prm[nm + '_w2'].rearrange("(m p) d -> p m d", p=128), w=['w2st'])
                if kv == 0:
                    MEMSET(P, 'pool', w2b[0][:], 0.0, w=['w2b0'])
                    COPY(P, 'dve', w2b[0][:, :, 32:96], w2st[:], r=['w2st', 'w2b0'], w=['w2b0'])
                else:
                    COPY(P, 'dve', w2b[1][:], w2st[:], r=['w2st'], w=['w2b1'])
                DMA(P, pest[:], prm[nm + '_pe'].rearrange("l d -> d l"), w=['pest'], slow=True)
                COPY(P, 'dve', peb[kv][:], pest[:], r=['pest'], w=['peb%d' % kv])
                for m in range(2):
                    for l in range(32):
                        MM(P, pb[:, m:m + 1], w1b[kv][:, l, m * 128:(m + 1) * 128], peb[kv][:, l:l + 1], l == 0, l == 31,
                           r=['w1b%d' % kv, 'peb%d' % kv], w=['pb'])
                COPY(P, 'dve', hb[kv][:], pb[:, 0:2], r=['pb'], w=['hb%d' % kv])
            DMA(P, kcmpT[96:103, :, :], K.c_KAc.unsqueeze(1).broadcast_to([7, NG, 512]), r=['kcmpT'], w=['kcmpT'])
            it = 0
            for g in range(NG):
                for kv in range(2):
                    xb = it % 2
                    it += 1
                    src = K.kcT_d if kv == 0 else K.vcT_d
                    DMA(P, xin[xb][:], src[g * 64:(g + 1) * 64, :], w=['xin%d' % xb])
                    for m in range(2):
                        for l in range(32):
                            MM(P, ph[m][:, 0:511], w1b[kv][:, l, m * 128:(m + 1) * 128], xin[xb][:, l:l + 16 * 510 + 1:16],
                               l == 0, l == 31, r=['w1b%d' % kv, 'xin%d' % xb], w=['ph%d' % m])
                        ACTF(P, hid[:, m, 0:511], ph[m][:, 0:511], AF.Silu, r=['ph%d' % m, 'hb%d' % kv], w=['hid'], bias=hb[kv][:, m:m + 1])
                    if kv == 0:
                        for m in range(2):
                            MM(P, po[0:96, 0:511], w2b[0][:, m, :], hid[:, m, 0:511], m == 0, m == 1, r=['w2b0', 'hid'], w=['po'])
                        COPY(P, 'dve', kcmpT[0:96, g, 0:511], po[0:96, 0:511], r=['po'], w=['kcmpT'])
                    else:
                        for nt in range(4):
                            for m in range(2):
                                MM(P, po[:, nt * 64:(nt + 1) * 64], hid[:, m, nt * 128:(nt + 1) * 128], w2b[1][:, m, :], m == 0, m == 1,
                                   r=['w2b1', 'hid'], w=['po'])
                        COPY(P, 'dve', vcmpa[:, :, g, 0:64], po[:, 0:256].rearrange("p (a b) -> p a b", a=4), r=['po'], w=['vcmpa'])
            MEMSET(P, 'pool', vcmpa[:, :, :, 64:65], 1.0, r=['vcmpa'], w=['vcmpa'])
            P.barrier()

        if DBG.get('stop') == 'nsaB':
            return
        nsa_attention(P, nc, K, li, kcmpT, vcmpa)
        if DBG.get('stop') == 'nsaC':
            return

    out_proj_phase(P, nc, K, li, prm['w_out'], hsrc, hdst, final_g, token_major_src=True)


THR_DECAY = 48.0


def _thr():
    return DBG.get("thr", THR_DECAY)


def _slope(h):
    return 2.0 ** (-(h + 1) / 2.0)


def nsa_attention(P, nc, K, li, kcmpT, vcmpa):
    with ExitStack() as Cx:
        sb = lambda name, shape, dt: Cx.enter_context(nc.sbuf_tensor('%sC%d' % (name, li), shape, dt))
        ps = lambda name, shape, dt: Cx.enter_context(nc.psum_tensor('%sC%d' % (name, li), shape, dt))
        ident = K.ident_b
        OV = sb('OV', [128, 4, 128], BF16)
        causal = sb('causal', [128, 4, 512], BF16)
        band = sb('band', [128, 8, 512], BF16)
        cmpm = sb('cmpm', [128, 5, 512], BF16)
        FT = sb('FT', [128, 256], F32)
        zer = sb('zer', [128, 512], BF16)
        ksT = sb('ksT', [128, S], BF16)
        vsa = sb('vsa', [128, 64, 65], BF16)
        qa = [sb('qa%d' % i, [128, 4, 4, 512], BF16) for i in range(2)]
        kwT = [sb('kwT%d' % i, [128, 1024], BF16) for i in range(2)]
        vwa = [sb('vwa%d' % i, [128, 8, 65], BF16) for i in range(2)]
        gt = [sb('gt%d' % i, [128, 4, 48], F32) for i in range(2)]
        zt = [sb('zt%d' % i, [128, 4, 256], BF16) for i in range(2)]
        pc = [[sb('pc%d_%d' % (r, nt), [128, 512], BF16) for nt in range(4)] for r in range(4)]
        pt = [sb('pt%d' % i, [128, 512], BF16) for i in range(4)]
        oacc = sb('oacc', [128, 4, 4, 64], F32)
        tmpo = [sb('tmpo%d' % i, [128, 4, 64], F32) for i in range(2)]
        rcs = sb('rcs', [128, 4, 4], F32)
        rr = [sb('rr%d' % i, [128, 4], F32) for i in range(2)]
        ww = [sb('ww%d' % i, [128, 4], F32) for i in range(2)]
        impacc = [sb('impacc%d' % i, [128, 128], F32) for i in range(2)]
        impf = [sb('impf%d' % i, [128, 128], F32) for i in range(2)]
        work = [sb('work%d' % i, [128, 128], F32) for i in range(2)]
        m8a = [sb('m8a%d' % i, [128, 8], F32) for i in range(2)]
        m8b = [sb('m8b%d' % i, [128, 8], F32) for i in range(2)]
        selb = [sb('selb%d' % i, [128, 128], BF16) for i in range(4)]
        ozb = [sb('ozb%d' % i, [128, 4, 256], BF16) for i in range(2)]
        Sps = [ps('S%d' % i, [128, 512], F32) for i in range(3)]
        Ops = [ps('O%d' % i, [128, 512], F32) for i in range(2)]
        pimp = [ps('pimp%d' % i, [128, 512], F32) for i in range(2)]
        pst = ps('pst', [128, 1024], BF16)

        DMA(P, OV[:], K.c_OV, w=['const'])
        DMA(P, causal[:], K.c_causal, w=['const'])
        DMA(P, band[:], K.c_band, w=['const'])
        DMA(P, cmpm[:], K.c_cmpmask, w=['const'])
        DMA(P, FT[:], K.c_FT, w=['const'])
        MEMSET(P, 'pool', zer[:], 0.0, w=['const'])
        MEMSET(P, 'pool', vsa[:, :, 64:65], 1.0, w=['vsa1'])
        for i in range(2):
            MEMSET(P, 'pool', vwa[i][:, :, 64:65], 1.0, w=['vwa1_%d' % i])
            MEMSET(P, 'pool', qa[i][0:32, :, :, :], 0.0, w=['qsel%d' % i])
            MEMSET(P, 'pool', kwT[i][0:32, :], 0.0, w=['kwT0_%d' % i])
        DMA(P, ksT[0:32, :], K.c_OH, w=['ksToh'])
        DMA(P, ksT[96:103, :], K.c_KA, w=['ksTaug'])
        st = {'s': 0, 'p': 0, 'o': 0, 'x': 0}

        def combine(oi, first, gcol, r, keep_rc, cb):
            O3 = Ops[oi][:, 0:260].rearrange("p (a b) -> p a b", a=4)
            x = st['x'] % 2
            st['x'] += 1
            TS(P, 'dve', rr[x][:], O3[:, :, 64], 1e-30, ALU.max, r=['O%d' % oi], w=['rr%d' % x])
            P.add('dve', lambda e: e.reciprocal(out=rr[x][:], in_=rr[x][:]), r=['rr%d' % x], w=['rr%d' % x])
            if keep_rc:
                COPY(P, 'dve', rcs[:, r, :], rr[x][:], r=['rr%d' % x], w=['rcs%d' % r])
            TT(P, 'dve', ww[x][:], rr[x][:], gt[cb][:, :, gcol], ALU.mult, r=['rr%d' % x, 'gt%d' % cb], w=['ww%d' % x])
            wbc = ww[x][:].unsqueeze(2).broadcast_to([128, 4, 64])
            if first:
                TT(P, 'dve', oacc[:, :, r, :], O3[:, :, 0:64], wbc, ALU.mult, r=['O%d' % oi, 'ww%d' % x], w=['oacc%d' % r])
            else:
                TT(P, 'dve', tmpo[x][:], O3[:, :, 0:64], wbc, ALU.mult, r=['O%d' % oi, 'ww%d' % x], w=['tmpo%d' % x])
                TT(P, 'pool', oacc[:, :, r, :], oacc[:, :, r, :], tmpo[x][:], ALU.add, r=['tmpo%d' % x, 'oacc%d' % r], w=['oacc%d' % r])

        def run_stream(jobs):
            n = len(jobs)
            LAG = 2
            for i in range(n + LAG):
                if i < n:
                    j = jobs[i]
                    si = st['s'] % 3
                    st['s'] += 1
                    nm = len(j['mms'])
                    for idx, (l_, r_, rk) in enumerate(j['mms']):
                        MM(P, Sps[si][:], l_, r_, idx == 0, idx == nm - 1, r=rk, w=['S%d' % si])
                    if j['pbuf'] is None:
                        pi = st['p'] % 4
                        st['p'] += 1
                        j['pbuf'] = (pt[pi], 'pt%d' % pi)
                    ACTF(P, j['pbuf'][0][:], Sps[si][:], AF.Exp, r=['S%d' % si], w=[j['pbuf'][1]])
                if i >= LAG:
                    j = jobs[i - LAG]
                    oi = j['oi']
                    if j['first']:
                        MM(P, Ops[oi][:, 0:260], zer[:, 0:128], zer[:, 0:260], True, False, r=['const'], w=['O%d' % oi], skip=True)
                    pb, pk = j['pbuf']
                    for sub in range(4):
                        MM(P, Ops[oi][:, sub * 65:(sub + 1) * 65], pb[:, sub * 128:(sub + 1) * 128], j['v'], False,
                           (j['last'] and sub == 3), r=[pk] + j['vk'], w=['O%d' % oi], skip=True)
                    if j['last']:
                        j['fin'](oi)

        it = 0
        for g in range(DBG.get('c_ng', NG)):
            DMA(P, ksT[32:96, :], K.ksT_d[g * 64:(g + 1) * 64, :], w=['ksT'])
            for q4 in range(4):
                DMA(P, vsa[:, q4 * 16:(q4 + 1) * 16, 0:64],
                    K.vs_d[q4 * 2048:(q4 + 1) * 2048, g * 64:(g + 1) * 64].rearrange("(kt p) d -> p kt d", p=128), r=['vsa'], w=['vsa'])
            for c in range(DBG.get('c_nc', NC)):
                cb = it % 2
                it += 1
                cs = slice(c * 512, (c + 1) * 512)
                ntmax = c // 4
                sel_kts, win_rs, cmp_nts = [], [], []
                for r in range(4):
                    sl = _slope(4 * g + r)
                    sel_kts.append([kt for kt in range(4 * c + 4) if sl * max(0, 512 * c - 128 * kt - 127) <= _thr()])
                    win_rs.append([rw for rw in range(4 if c == 0 else 0, 8) if sl * max(0, 385 - 128 * rw) <= _thr()])
                    cmp_nts.append([nt for nt in range(ntmax + 1)
                                    if sl * max(0.0, 512 * c - (2048 * nt + 2047.5) - 48.0) <= _thr()])
                VD = 1000 if DBG.get('one_ver') else 16
                selvers = sorted(set(kt // VD for r in range(4) for kt in sel_kts[r]))
                vers = sorted(set(selvers) | {0})
                qkeys = []
                for v in vers:
                    DMA(P, qa[cb][32:96, v, :, :], K.qT_d[g * 256:(g + 1) * 256, cs].rearrange("(r d) t -> d r t", d=64), w=['qa%d_%d' % (cb, v)])
                    DMA(P, qa[cb][96:103, v, :, :], K.c_QA[g * 4:(g + 1) * 4, :, cs].rearrange("h r t -> r h t"), w=['qaug%d_%d' % (cb, v)])
                    qkeys += ['qa%d_%d' % (cb, v), 'qaug%d_%d' % (cb, v)]
                q0k = ['qa%d_0' % cb, 'qaug%d_0' % cb, 'qsel%d' % cb]
                k0 = (c - 1) * 512
                if c == 0:
                    DMA(P, kwT[cb][32:96, 512:1024], K.kwT_d[g * 64:(g + 1) * 64, 0:512], w=['kwT%d' % cb])
                    DMA(P, kwT[cb][96:103, 512:1024], K.c_KA[:, 0:512], w=['kwTa%d' % cb])
                    DMA(P, vwa[cb][:, 4:8, 0:64], K.vw_d[0:512, g * 64:(g + 1) * 64].rearrange("(kt p) d -> p kt d", p=128), w=['vwa%d' % cb])
                else:
                    DMA(P, kwT[cb][32:96, :], K.kwT_d[g * 64:(g + 1) * 64, k0:k0 + 1024], w=['kwT%d' % cb])
                    DMA(P, kwT[cb][96:103, :], K.c_KA[:, k0:k0 + 1024], w=['kwTa%d' % cb])
                    DMA(P, vwa[cb][:, :, 0:64], K.vw_d[k0:k0 + 1024, g * 64:(g + 1) * 64].rearrange("(kt p) d -> p kt d", p=128), w=['vwa%d' % cb])
                DMA(P, gt[cb][:], K.gates_d[cs, :].rearrange("(s p) c -> p s c", p=128), w=['gt%d' % cb])
                DMA(P, zt[cb][:], K.zs_d[cs, g * 256:(g + 1) * 256].rearrange("(s p) c -> p s c", p=128), w=['zt%d' % cb])

                jobs = []
                for r in range(4):
                    for nt in cmp_nts[r]:
                        mms = [(kcmpT[0:103, g, nt * 128:(nt + 1) * 128], qa[cb][0:103, 0, r, :], q0k + ['kcmpT'])]
                        if nt == ntmax:
                            mms.append((ident[:], cmpm[:, c % 4, :], ['const']))
                        elif nt == ntmax - 1 and c % 4 == 0 and not DBG.get('no_elif'):
                            mms.append((ident[:], cmpm[:, 4, :], ['const']))
                        jobs.append(dict(mms=mms, pbuf=(pc[r][nt], 'pc%d_%d' % (r, nt)), v=vcmpa[:, nt, g, :], vk=['vcmpa'],
                                         oi=(st['o'] + r) % 2, first=(nt == cmp_nts[r][0]), last=(nt == cmp_nts[r][-1]),
                                         fin=(lambda oi, r=r: combine(oi, True, 0 * 16 + g * 4 + r, r, True, cb))))
                st['o'] += 4
                run_stream(jobs)

                for sub in range(4):
                    ti = 4 * c + sub
                    x2 = sub % 2
                    pb_ = pimp[x2]
                    pk_ = 'pimp%d' % x2
                    for r in range(4):
                        for nt in cmp_nts[r]:
                            MM(P, pb_[:, r * 128:(r + 1) * 128], pc[r][nt][:, sub * 128:(sub + 1) * 128], OV[:, nt, :],
                               nt == cmp_nts[r][0], nt == cmp_nts[r][-1], r=['pc%d_%d' % (r, nt), 'const'], w=[pk_])
                    ia, if_, wk, ma, mb = impacc[x2], impf[x2], work[x2], m8a[x2], m8b[x2]
                    kx = '_%d' % x2
                    for r in range(4):
                        if r == 0:
                            TS(P, 'dve', ia[:], pb_[:, 0:128], rcs[:, r, sub:sub + 1], ALU.mult, r=[pk_, 'rcs%d' % r], w=['impacc' + kx])
                        else:
                            STT(P, ia[:], pb_[:, r * 128:(r + 1) * 128], rcs[:, r, sub:sub + 1], ia[:], ALU.mult, ALU.add,
                                r=[pk_, 'rcs%d' % r, 'impacc' + kx], w=['impacc' + kx])
                    TT(P, 'dve', if_[:], ia[:], FT[:, 127 - 2 * ti:255 - 2 * ti], ALU.add, r=['impacc' + kx, 'const'], w=['impf' + kx])
                    MEMSET(P, 'dve', if_[:, 0:1], 1e9, r=['impf' + kx], w=['impf' + kx])
                    P.add('dve', (lambda e, ma=ma, if_=if_: e.max(out=ma[:], in_=if_[:])), r=['impf' + kx], w=['m8a' + kx])
                    P.add('dve', (lambda e, ma=ma, if_=if_, wk=wk: e.match_replace(out=wk[:], in_to_replace=ma[:], in_values=if_[:], imm_value=-3.0e38)),
                          r=['impf' + kx, 'm8a' + kx], w=['work' + kx])
                    P.add('dve', (lambda e, mb=mb, wk=wk: e.max(out=mb[:], in_=wk[:])), r=['work' + kx], w=['m8b' + kx])
                    TS(P, 'dve', selb[sub][:], if_[:], mb[:, 7:8], ALU.is_lt, DBG.get('maskv', MASKV), ALU.mult, r=['impf' + kx, 'm8b' + kx], w=['selb%d' % sub])
                    pcol = slice((sub % 2) * 128, (sub % 2) * 128 + 128)
                    pkey = 'pst%d' % (sub % 2)
                    TR(P, pst[:, pcol], selb[sub][:], ident[:], r=['selb%d' % sub, 'const'], w=[pkey])
                    for v in selvers:
                        COPY(P, 'dve', qa[cb][0:32, v, :, sub * 128:(sub + 1) * 128],
                             pst[32 * v:32 * v + 32, pcol].unsqueeze(1).broadcast_to([32, 4, 128]), r=[pkey], w=['qsel%d' % cb])

                jobs = []
                for r in range(4):
                    for rw in win_rs[r]:
                        mms = [(kwT[cb][0:103, rw * 128:(rw + 1) * 128], qa[cb][0:103, 0, r, :], q0k + ['kwT%d' % cb, 'kwTa%d' % cb, 'kwT0_%d' % cb]),
                               (ident[:], band[:, rw, :], ['const'])]
                        jobs.append(dict(mms=mms, pbuf=None, v=vwa[cb][:, rw, :], vk=['vwa%d' % cb, 'vwa1_%d' % cb],
                                         oi=(st['o'] + r) % 2, first=(rw == win_rs[r][0]), last=(rw == win_rs[r][-1]),
                                         fin=(lambda oi, r=r: combine(oi, False, 2 * 16 + g * 4 + r, r, False, cb))))
                st['o'] += 4
                run_stream(jobs)

                jobs = []
                for r in range(4):
                    for kt in sel_kts[r]:
                        mms = [(ksT[0:103, kt * 128:(kt + 1) * 128], qa[cb][0:103, kt // VD, r, :],
                                qkeys + ['qsel%d' % cb, 'ksT', 'ksTaug', 'ksToh'])]
                        if kt >= 4 * c:
                            mms.append((ident[:], causal[:, kt - 4 * c, :], ['const']))
                        jobs.append(dict(mms=mms, pbuf=None, v=vsa[:, kt, :], vk=['vsa', 'vsa1'],
                                         oi=(st['o'] + r) % 2, first=(kt == sel_kts[r][0]), last=(kt == sel_kts[r][-1]),
                                         fin=(lambda oi, r=r: combine(oi, False, 1 * 16 + g * 4 + r, r, False, cb))))
                st['o'] += 4
                run_stream(jobs)

                TT(P, 'pool', ozb[cb][:], oacc[:].rearrange("p s r d -> p s (r d)"), zt[cb][:], ALU.mult,
                   r=['oacc0', 'oacc1', 'oacc2', 'oacc3', 'zt%d' % cb], w=['ozb%d' % cb])
                DMA(P, K.oz_d[cs, g * 256:(g + 1) * 256].rearrange("(s p) d -> p s d", p=128), ozb[cb][:], r=['ozb%d' % cb])
        P.barrier()


def out_proj_phase(P, nc, K, li, wout, hsrc, hdst, final_g, token_major_src):
    with ExitStack() as Dx:
        sb = lambda name, shape, dt: Dx.enter_context(nc.sbuf_tensor('%sD%d' % (name, li), shape, dt))
        ps = lambda name, shape, dt: Dx.enter_context(nc.psum_tensor('%sD%d' % (name, li), shape, dt))
        wb = sb('wb', [128, 8, D], BF16)
        stage = [sb('wst0', [128, D], F32), sb('wst1', [128, D], F32)]
        ozt = [sb('ozt%d' % i, [128, D], BF16) for i in range(2)]
        ozT = [sb('ozT%d' % i, [128, 8, 128], BF16) for i in range(2)]
        ht = [sb('ht%d' % i, [128, D], F32) for i in range(2)]
        hn = [sb('hn%d' % i, [128, D], F32) for i in range(2)]
        K.pT = ps('pT', [128, 8, 128], BF16)
        pm = [ps('pm%d' % i, [128, 512], F32) for i in range(4)]
        gfb = None
        if final_g is not None:
            gfb = sb('gfb', [128, D], F32)
            DMA(P, gfb[:], final_g.partition_broadcast(128), w=['gb'])
        load_weight_bf16(P, K, wb, wout, D, stage, 'wout')
        for i in range(NT):
            b = i % 2
            rows = slice(i * 128, (i + 1) * 128)
            DMA(P, ozt[b][:], K.oz_d[rows, :], w=['ozt%d' % b])
            DMA(P, ht[b][:], hsrc[rows, :], w=['ht%d' % b])
            transpose_tile(P, K, ozt[b], ozT[b][:], 'ozt%d' % b, 'ozT%d' % b, 'act' if i % 2 == 0 else 'dve')
            for half in range(2):
                pi = (2 * i + half) % 4
                for kc in range(8):
                    MM(P, pm[pi][:], ozT[b][:, kc, :], wb[:, kc, half * 512:(half + 1) * 512], kc == 0, kc == 7,
                       r=['ozT%d' % b, 'wout_wb'], w=['pm%d' % pi])
                TT(P, 'dve', hn[b][:, half * 512:(half + 1) * 512], pm[pi][:], ht[b][:, half * 512:(half + 1) * 512], ALU.add,
                   r=['pm%d' % pi, 'ht%d' % b], w=['hn%d_%d' % (b, half)])
            hk = ['hn%d_0' % b, 'hn%d_1' % b]
            if final_g is None:
                DMA(P, hdst[rows, :], hn[b][:], r=hk)
            else:
                final_norm_store(P, K, hn[b], hk, gfb, ht[b], ['ht%d' % b], [hdst[rows, :]], [slice(0, 128)])
        P.barrier()


def final_norm_store(P, K, hn, hk, gfb, outbuf, okeys, dsts, prs):
    STT(P, K.junkf[:], hn[:], 1.0, hn[:], ALU.mult, ALU.mult, r=hk, w=['junkf', 'ssF'], accum=K.ss[:])
    ACTF(P, K.sq[:], K.ss[:], AF.Sqrt, r=['ssF'], w=['sqF'], scale=1.0 / D, bias=EPS)
    P.add('dve', lambda e: e.reciprocal(out=K.rs[:], in_=K.sq[:]), r=['sqF'], w=['rsF'])
    STT(P, outbuf[:], hn[:], K.rs[:], gfb[:], ALU.mult, ALU.mult, r=hk + ['rsF', 'gb'], w=okeys)
    for d_, pr in zip(dsts, prs):
        DMA(P, d_, outbuf[pr, :], r=okeys)


def s5_layer(P, nc, K, li, prm, hsrc, hdst, final_g):
    hv_src = hsrc.rearrange("(g n s) d -> g s n d", n=64, s=8)
    hv_dst = hdst.rearrange("(g n s) d -> g s n d", n=64, s=8)
    TWO_PI = 2.0 * math.pi

    with ExitStack() as A:
        sb = lambda name, shape, dt: A.enter_context(nc.sbuf_tensor('%sA%d' % (name, li), shape, dt))
        ps = lambda name, shape, dt: A.enter_context(nc.psum_tensor('%sA%d' % (name, li), shape, dt))
        wb = sb('wb', [128, 8, 2048], BF16)
        stage = [sb('wst0', [128, 2048], F32), sb('wst1', [128, 2048], F32)]
        gb = sb('gb', [128, D], F32)
        xt = [sb('xt0', [128, D], F32), sb('xt1', [128, D], F32)]
        xnb = [sb('xnb0', [128, D], BF16), sb('xnb1', [128, D], BF16)]
        xT4 = [sb('xT40', [128, 8, 512], BF16), sb('xT41', [128, 8, 512], BF16)]
        fo = [sb('fo%d' % i, [128, 512], BF16) for i in range(4)]
        K.pT = ps('pT', [128, 8, 128], BF16)
        pm = [ps('pm%d' % i, [128, 512], F32) for i in range(4)]
        load_weight_bf16(P, K, wb, prm['w_in'], 2048, stage, 'win')
        DMA(P, gb[:], prm['norm'].partition_broadcast(128), w=['gb'])
        cnt = 0
        for seg in range(NC):
            cb = seg % 2
            for tt in range(4):
                b = (4 * seg + tt) % 2
                DMA(P, xt[b][0:64, :], hv_src[seg, 2 * tt], w=['xt%da' % b])
                DMA(P, xt[b][64:128, :], hv_src[seg, 2 * tt + 1], w=['xt%db' % b])
                rmsnorm_tile(P, K, xt[b][:], gb[:], xnb[b][:], 'A', ['xt%da' % b, 'xt%db' % b], 'xnb%d' % b)
                transpose_tile(P, K, xnb[b], xT4[cb][:, :, tt * 128:(tt + 1) * 128], 'xnb%d' % b, 'xT4%d' % cb,
                               'act' if tt % 2 == 0 else 'dve')
            for m in range(16):
                pi = cnt % 4
                fi = cnt % 4
                cnt += 1
                for kc in range(8):
                    MM(P, pm[pi][:], wb[:, kc, m * 128:(m + 1) * 128], xT4[cb][:, kc, :], kc == 0, kc == 7,
                       r=['xT4%d' % cb, 'win_wb'], w=['pm%d' % pi])
                if m < 8:
                    COPY(P, 'dve', fo[fi][:], pm[pi][:], r=['pm%d' % pi], w=['fo%d' % fi])
                    DMA(P, K.uT_d[m * 128:(m + 1) * 128, seg * 512:(seg + 1) * 512], fo[fi][:], r=['fo%d' % fi])
                else:
                    ACTF(P, fo[fi][:], pm[pi][:], AF.Silu, r=['pm%d' % pi], w=['fo%d' % fi])
                    DMA(P, K.zsT_d[(m - 8) * 128:(m - 7) * 128, seg * 512:(seg + 1) * 512], fo[fi][:], r=['fo%d' % fi])
        P.barrier()
    if DBG.get('stop') == 's5p1':
        return

    with ExitStack() as B:
        sb = lambda name, shape, dt: B.enter_context(nc.sbuf_tensor('%sB%d' % (name, li), shape, dt))
        ps = lambda name, shape, dt: B.enter_context(nc.psum_tensor('%sB%d' % (name, li), shape, dt))
        identf = K.ident_f
        Tb = sb('Tb', [128, 64, 128], BF16)
        Pm = sb('Pm', [128, 64, 2, 64], BF16)
        QmR = sb('QmR', [64, 64, 8, 16], BF16)
        QmI = sb('QmI', [64, 64, 8, 16], BF16)
        ar8 = sb('ar8', [64, 64], F32)
        ai8 = sb('ai8', [64, 64], F32)
        dcol = sb('dcol', [128, 8], F32)
        pA = ps('pA', [128, 512], F32)
        pB = ps('pB', [128, 512], F32)
        pC = ps('pC', [128, 512], F32)
        pD = ps('pD', [128, 512], F32)
        pK = ps('pK', [128, 1024], F32)
        DMA(P, dcol[:], prm['d'].rearrange("g c -> (g c)").rearrange("(k p) -> p k", p=128), w=['dcol'], slow=True)

        with ExitStack() as T:
            tb = lambda name, shape, dt: T.enter_context(nc.sbuf_tensor('%sT%d' % (name, li), shape, dt))
            t64 = lambda name: tb(name, [64, 64], F32)
            lamre_g, lamim_g, lre, lim, dtb = t64('lamre_g'), t64('lamim_g'), t64('lre'), t64('lim'), t64('dtb')
            xr_, yi_, mag, sn, cs_, abr, abi = t64('xr_'), t64('yi_'), t64('mag'), t64('sn'), t64('cs_'), t64('abr'), t64('abi')
            den, nr, cfr, cfi, tA, tB, tC = t64('den'), t64('nr'), t64('cfr'), t64('cfi'), t64('tA'), t64('tB'), t64('tC')
            tI = tb('tI', [64, 64], I32)
            bre = tb('bre', [64, 64, 16], F32)
            bim = tb('bim', [64, 64, 16], F32)
            bbr = tb('bbr', [64, 64, 16], F32)
            bbi = tb('bbi', [64, 64, 16], F32)
            t3 = tb('t3', [64, 64, 16], F32)
            t4 = tb('t4', [64, 64, 16], F32)
            Lr = tb('Lr', [64, 9, 64], F32)
            Li = tb('Li', [64, 9, 64], F32)
            Wr = tb('Wr', [64, 64, 8, 16], F32)
            Wi = tb('Wi', [64, 64, 8, 16], F32)
            Cst = tb('Cst', [128, 8, 64], F32)
            CrT = tb('CrT', [64, 64, 16], F32)
            CiT = tb('CiT', [64, 64, 16], F32)
            nCrT = tb('nCrT', [64, 64, 16], F32)
            nCiT = tb('nCiT', [64, 64, 16], F32)
            Ktsb = tb('Ktsb', [128, 64, 16], F32)

            def V(eng, out, a, b, op, r, w):
                TT(P, eng, out, a, b, op, r=r, w=w)

            def VN(eng, out, a, b, op, r, w):
                TT(P, eng, out, a, b, op, r=r, w=w, nosync=True)

            DMA(P, lamre_g[:], prm['lambda_re'], w=['lamre_g'])
            DMA(P, lamim_g[:], prm['lambda_im'], w=['lamim_g'])
            TR(P, pA[0:64, 0:64], lamre_g[:], identf[0:64, 0:64], r=['lamre_g', 'identf'], w=['pA'])
            TR(P, pA[0:64, 64:128], lamim_g[:], identf[0:64, 0:64], r=['lamim_g', 'identf'], w=['pA'])
            COPY(P, 'dve', lre[:], pA[0:64, 0:64], r=['pA'], w=['lre'])
            COPY(P, 'dve', lim[:], pA[0:64, 64:128], r=['pA'], w=['lim'])
            DMA(P, dtb[:], prm['log_dt'].partition_broadcast(64), w=['dtb'])
            ACTF(P, dtb[:], dtb[:], AF.Exp, r=['dtb'], w=['dtb'])
            TS(P, 'dve', lre[:], lre[:], -1e-4, ALU.min, r=['lre'], w=['lre'])
            V('dve', xr_[:], lre[:], dtb[:], ALU.mult, ['lre', 'dtb'], ['xr_'])
            V('dve', yi_[:], lim[:], dtb[:], ALU.mult, ['lim', 'dtb'], ['yi_'])
            ACTF(P, mag[:], xr_[:], AF.Exp, r=['xr_'], w=['mag'])

            def sin_shift(out, okey, shift):
                TS(P, 'dve', tA[:], yi_[:], 1.0 / TWO_PI, ALU.mult, shift / TWO_PI, ALU.add, r=['yi_'], w=['tA'])
                COPY(P, 'dve', tI[:], tA[:], r=['tA'], w=['tI'])
                COPY(P, 'dve', tB[:], tI[:], r=['tI'], w=['tB'])
                STT(P, tC[:], tB[:], -TWO_PI, yi_[:], ALU.mult, ALU.add, r=['tB', 'yi_'], w=['tC'])
                TS(P, 'dve', tC[:], tC[:], shift, ALU.add, 3.141592, ALU.min, r=['tC'], w=['tC'])
                TS(P, 'dve', tC[:], tC[:], -3.141592, ALU.max, r=['tC'], w=['tC'])
                ACTF(P, out, tC[:], AF.Sin, r=['tC'], w=[okey])

            sin_shift(sn[:], 'sn', 0.0)
            sin_shift(cs_[:], 'cs_', math.pi / 2.0)
            V('dve', abr[:], mag[:], cs_[:], ALU.mult, ['mag', 'cs_'], ['abr'])
            V('dve', abi[:], mag[:], sn[:], ALU.mult, ['mag', 'sn'], ['abi'])
            V('dve', den[:], lre[:], lre[:], ALU.mult, ['lre'], ['den'])
            V('dve', tA[:], lim[:], lim[:], ALU.mult, ['lim'], ['tA'])
            V('dve', den[:], den[:], tA[:], ALU.add, ['den', 'tA'], ['den'])
            P.add('dve', lambda e: e.reciprocal(out=den[:], in_=den[:]), r=['den'], w=['den'])
            TS(P, 'dve', nr[:], abr[:], -1.0, ALU.add, r=['abr'], w=['nr'])
            V('dve', tA[:], nr[:], lre[:], ALU.mult, ['nr', 'lre'], ['tA'])
            V('dve', tB[:], abi[:], lim[:], ALU.mult, ['abi', 'lim'], ['tB'])
            V('dve', tA[:], tA[:], tB[:], ALU.add, ['tA', 'tB'], ['tA'])
            V('dve', cfr[:], tA[:], den[:], ALU.mult, ['tA', 'den'], ['cfr'])
            V('dve', tA[:], abi[:], lre[:], ALU.mult, ['abi', 'lre'], ['tA'])
            V('dve', tB[:], nr[:], lim[:], ALU.mult, ['nr', 'lim'], ['tB'])
            V('dve', tA[:], tA[:], tB[:], ALU.subtract, ['tA', 'tB'], ['tA'])
            V('dve', cfi[:], tA[:], den[:], ALU.mult, ['tA', 'den'], ['cfi'])
            DMA(P, bre[:], prm['b_re'].rearrange("g p c -> p g c"), w=['bre'])
            DMA(P, bim[:], prm['b_im'].rearrange("g p c -> p g c"), w=['bim'])
            bc = lambda ap2: ap2.unsqueeze(2).broadcast_to([64, 64, 16])
            V('dve', bbr[:], bre[:], bc(cfr[:]), ALU.mult, ['bre', 'cfr'], ['bbr'])
            V('dve', t3[:], bim[:], bc(cfi[:]), ALU.mult, ['bim', 'cfi'], ['t3'])
            V('dve', bbr[:], bbr[:], t3[:], ALU.subtract, ['bbr', 't3'], ['bbr'])
            V('dve', bbi[:], bim[:], bc(cfr[:]), ALU.mult, ['bim', 'cfr'], ['bbi'])
            V('dve', t3[:], bre[:], bc(cfi[:]), ALU.mult, ['bre', 'cfi'], ['t3'])
            V('dve', bbi[:], bbi[:], t3[:], ALU.add, ['bbi', 't3'], ['bbi'])
            MEMSET(P, 'dve', Lr[:, 0, :], 1.0, w=['L'])
            MEMSET(P, 'dve', Li[:, 0, :], 0.0, r=['L'], w=['L'])
            for tau in range(1, 9):
                V('dve', tA[:], Lr[:, tau - 1, :], abr[:], ALU.mult, ['L', 'abr'], ['tA'])
                V('dve', tB[:], Li[:, tau - 1, :], abi[:], ALU.mult, ['L', 'abi'], ['tB'])
                V('dve', tC[:], Lr[:, tau - 1, :], abi[:], ALU.mult, ['L', 'abi'], ['tC'])
                V('dve', den[:], Li[:, tau - 1, :], abr[:], ALU.mult, ['L', 'abr'], ['den'])
                V('dve', Lr[:, tau, :], tA[:], tB[:], ALU.subtract, ['tA', 'tB', 'L'], ['L'])
                V('dve', Li[:, tau, :], tC[:], den[:], ALU.add, ['tC', 'den', 'L'], ['L'])
            for tau in range(8):
                lrb = bc(Lr[:, tau, :])
                lib = bc(Li[:, tau, :])
                V('dve', Wr[:, :, tau, :], bbr[:], lrb, ALU.mult, ['bbr', 'L'], ['Wr'])
                V('dve', t3[:], bbi[:], lib, ALU.mult, ['bbi', 'L'], ['t3'])
                V('dve', Wr[:, :, tau, :], Wr[:, :, tau, :], t3[:], ALU.subtract, ['Wr', 't3'], ['Wr'])
                V('dve', Wi[:, :, tau, :], bbi[:], lrb, ALU.mult, ['bbi', 'L'], ['Wi'])
                V('dve', t4[:], bbr[:], lib, ALU.mult, ['bbr', 'L'], ['t4'])
                V('dve', Wi[:, :, tau, :], Wi[:, :, tau, :], t4[:], ALU.add, ['Wi', 't4'], ['Wi'])
            for nm, dstT in (('c_re', CrT), ('c_im', CiT)):
                DMA(P, Cst[:], prm[nm].rearrange("g c p -> (g c) p").rearrange("(k r) p -> r k p", r=128), r=[], w=['Cst'])
                for k in range(8):
                    TR(P, pK[0:64, k * 128:(k + 1) * 128], Cst[:, k, :], identf[:], r=['Cst', 'identf'], w=['pK'])
                COPY(P, 'dve', dstT[:].rearrange("p g c -> p (g c)"), pK[0:64, :], r=['pK'], w=[nm])
            TS(P, 'dve', nCrT[:], CrT[:], -1.0, ALU.mult, r=['c_re'], w=['nCrT'])
            TS(P, 'dve', nCiT[:], CiT[:], -1.0, ALU.mult, r=['c_im'], w=['nCiT'])
            for g in range(64):
                MM(P, pK[:, g * 16:(g + 1) * 16], Wr[:, g, :, :].rearrange("p t c -> p (t c)"), CrT[:, g, :], True, False,
                   r=['Wr', 'c_re'], w=['pK'])
                MM(P, pK[:, g * 16:(g + 1) * 16], Wi[:, g, :, :].rearrange("p t c -> p (t c)"), nCiT[:, g, :], False, True,
                   r=['Wi', 'nCiT'], w=['pK'])
            COPY(P, 'dve', Ktsb[:].rearrange("p g c -> p (g c)"), pK[:, :], r=['pK'], w=['Ktsb'])
            DMA(P, K.Kd_d.rearrange("l c g o -> (l c) g o"), Ktsb[:], r=['Ktsb'], w=['Kd'])
            for gb4 in range(16):
                pp = pA if gb4 % 2 == 0 else pB
                pk_ = 'pA' if gb4 % 2 == 0 else 'pB'
                for gi in range(4):
                    g = gb4 * 4 + gi
                    for ri, W_ in enumerate((Wr, Wi)):
                        TR(P, pp[:, (gi * 2 + ri) * 64:(gi * 2 + ri + 1) * 64], W_[:, g, :, :].rearrange("p t c -> p (t c)"),
                           identf[0:64, 0:64], r=['Wr', 'Wi', 'identf'], w=[pk_])
                COPY(P, 'dve', Pm[:, gb4 * 4:(gb4 + 1) * 4, :, :].rearrange("p g r q -> p (g r q)"), pp[:, :], r=[pk_], w=['Pm'])
            for t in range(8):
                lrb = bc(Lr[:, t + 1, :])
                lib = bc(Li[:, t + 1, :])
                V('dve', t3[:], CrT[:], lrb, ALU.mult, ['c_re', 'L'], ['t3'])
                V('dve', t4[:], nCiT[:], lib, ALU.mult, ['nCiT', 'L'], ['t4'])
                V('dve', QmR[:, :, t, :], t3[:], t4[:], ALU.add, ['t3', 't4'], ['QmR'])
                V('dve', t3[:], nCrT[:], lib, ALU.mult, ['nCrT', 'L'], ['t3'])
                V('dve', t4[:], nCiT[:], lrb, ALU.mult, ['nCiT', 'L'], ['t4'])
                V('dve', QmI[:, :, t, :], t3[:], t4[:], ALU.add, ['t3', 't4'], ['QmI'])
            COPY(P, 'dve', ar8[:], Lr[:, 8, :], r=['L'], w=['ar8'])
            COPY(P, 'dve', ai8[:], Li[:, 8, :], r=['L'], w=['ai8'])
            TAP(P, nc, 'abr', abr[:], [64, 64], F32, ['abr'])
            TAP(P, nc, 'abi', abi[:], [64, 64], F32, ['abi'])
            TAP(P, nc, 'bbr', bbr[:], [64, 64, 16], F32, ['bbr'])
            TAP(P, nc, 'bbi', bbi[:], [64, 64, 16], F32, ['bbi'])
            TAP(P, nc, 'Lr', Lr[:], [64, 9, 64], F32, ['L'])
            TAP(P, nc, 'CrT', CrT[:], [64, 64, 16], F32, ['c_re'])
            TAP(P, nc, 'Ktsb', Ktsb[:], [128, 64, 16], F32, ['Ktsb'])
            TAP(P, nc, 'Pm', Pm[:], [128, 64, 2, 64], BF16, ['Pm'])
            TAP(P, nc, 'QmR', QmR[:], [64, 64, 8, 16], BF16, ['QmR'])
            TAP(P, nc, 'QmI', QmI[:], [64, 64, 8, 16], BF16, ['QmI'])
            P.barrier()
        if DBG.get('stop') == 's5setup':
            return

        with ExitStack() as T2:
            Tsb = T2.enter_context(nc.sbuf_tensor('TsbT%d' % li, [128, 64, 8, 16], F32))
            MEMSET(P, 'pool', Tsb[:], 0.0, w=['Tsb'])
            tkeys = []
            for tp in range(8):
                for lag in range(tp + 1):
                    key = 'Tsb_%d_%d' % (tp, lag)
                    tkeys.append(key)
                    DMA(P, Tsb[16 * tp:16 * tp + 16, :, 7 - tp + lag, :], K.Kd_d[lag].rearrange("c g o -> c g o"),
                        r=['Kd', 'Tsb'], w=[key])
            COPY(P, 'dve', Tb[:].rearrange("p g m -> p (g m)"), Tsb[:].rearrange("p g t c -> p (g t c)"), r=tkeys + ['Tsb'], w=['Tb'])
            TAP(P, nc, 'Tb', Tb[:], [128, 64, 128], BF16, ['Tb'])
            P.barrier()
        if DBG.get('stop') == 's5t2':
            return

        with ExitStack() as Sg:
            sg = lambda name, shape, dt: Sg.enter_context(nc.sbuf_tensor('%sS%d' % (name, li), shape, dt))
            Sel = sg('Sel', [128, 8, 8, 128], BF16)
            SelT = sg('SelT', [128, 8, 8, 128], BF16)
            DMA(P, Sel[:], K.c_Sel, w=['const'])
            DMA(P, SelT[:], K.c_SelT, w=['const'])
            useg = [sg('useg%d' % i, [128, 8, 512], BF16) for i in range(2)]
            Uall = sg('Uall', [128, 64, 64], BF16)
            Bcr = sg('Bcr', [64, 64, 64], F32)
            Bci = sg('Bci', [64, 64, 64], F32)
            Xbr = sg('Xbr', [64, 64, 64], BF16)
            Xbi = sg('Xbi', [64, 64, 64], BF16)
            xr = [sg('xr%d' % i, [64, 64], F32) for i in range(2)]
            xi = [sg('xi%d' % i, [64, 64], F32) for i in range(2)]
            s1, s2, s3, s4 = sg('s1', [64, 64], F32), sg('s2', [64, 64], F32), sg('s3', [64, 64], F32), sg('s4', [64, 64], F32)
            Ysb = [sg('Ysb%d' % i, [128, 8, 64], BF16) for i in range(2)]
            yv = [sg('yv%d' % i, [128, 512], F32) for i in range(2)]
            yg = [sg('yg%d' % i, [128, 512], BF16) for i in range(2)]
            MEMSET(P, 'dve', xr[0][:], 0.0, w=['xr0'])
            MEMSET(P, 'dve', xi[0][:], 0.0, w=['xi0'])
            step = 0
            for seg in range(NC):
                ub = seg % 2
                uk = 'useg%d' % ub
                DMA(P, useg[ub][:], K.uT_d[:, seg * 512:(seg + 1) * 512].rearrange("(k p) t -> p k t", p=128), w=[uk])
                for k in range(8):
                    for gl in range(8):
                        for s in range(8):
                            MM(P, pA[:, gl * 64:(gl + 1) * 64], Sel[:, gl, s, :], useg[ub][:, k, s * 64:(s + 1) * 64], s == 0, s == 7,
                               r=['const', uk], w=['pA'])
                    COPY(P, 'act', Uall[:, k * 8:(k + 1) * 8, :].rearrange("p g n -> p (g n)"), pA[:, :], r=['pA'], w=['Uall%d' % k])
                    for gl in range(8):
                        g = 8 * k + gl
                        MM(P, pB[0:64, gl * 64:(gl + 1) * 64], Pm[:, g, 0, :], Uall[:, g, :], True, True, r=['Uall%d' % k], w=['pB'])
                        MM(P, pC[0:64, gl * 64:(gl + 1) * 64], Pm[:, g, 1, :], Uall[:, g, :], True, True, r=['Uall%d' % k], w=['pC'])
                    COPY(P, 'dve', Bcr[:, k * 8:(k + 1) * 8, :].rearrange("p g n -> p (g n)"), pB[0:64, :], r=['pB'], w=['Bcr%d' % k])
                    COPY(P, 'act', Bci[:, k * 8:(k + 1) * 8, :].rearrange("p g n -> p (g n)"), pC[0:64, :], r=['pC'], w=['Bci%d' % k])
                bk = ['Bcr%d' % k for k in range(8)] + ['Bci%d' % k for k in range(8)]
                for n in range(64):
                    cur = step % 2
                    nxt = 1 - cur
                    step += 1
                    kr, ki = 'xr%d' % cur, 'xi%d' % cur
                    COPY(P, 'act', Xbr[:, :, n], xr[cur][:], r=[kr], w=['Xbr'])
                    COPY(P, 'act', Xbi[:, :, n], xi[cur][:], r=[ki], w=['Xbi'])
                    VN('dve', s1[:], ar8[:], xr[cur][:], ALU.mult, [kr], ['s1'])
                    VN('dve', s2[:], ai8[:], xi[cur][:], ALU.mult, [ki], ['s2'])
                    VN('dve', s3[:], ar8[:], xi[cur][:], ALU.mult, [ki], ['s3'])
                    VN('dve', s4[:], ai8[:], xr[cur][:], ALU.mult, [kr], ['s4'])
                    VN('dve', s1[:], s1[:], s2[:], ALU.subtract, ['s1', 's2'], ['s1'])
                    VN('dve', s3[:], s3[:], s4[:], ALU.add, ['s3', 's4'], ['s3'])
                    VN('dve', xr[nxt][:], s1[:], Bcr[:, :, n], ALU.add, ['s1'] + bk, ['xr%d' % nxt])
                    VN('dve', xi[nxt][:], s3[:], Bci[:, :, n], ALU.add, ['s3'] + bk, ['xi%d' % nxt])
                for k in range(8):
                    yb = k % 2
                    for gl in range(8):
                        g = 8 * k + gl
                        o_ = pD[:, gl * 64:(gl + 1) * 64]
                        MM(P, o_, Tb[:, g, :], Uall[:, g, :], True, False, r=['Uall%d' % k], w=['pD'])
                        MM(P, o_, QmR[:, g, :, :].rearrange("p t c -> p (t c)"), Xbr[:, g, :], False, False, r=['Xbr'], w=['pD'])
                        MM(P, o_, QmI[:, g, :, :].rearrange("p t c -> p (t c)"), Xbi[:, g, :], False, True, r=['Xbi'], w=['pD'])
                    COPY(P, 'act', Ysb[yb][:].rearrange("p g n -> p (g n)"), pD[:, :], r=['pD'], w=['Ysb%d' % yb])
                    pY = pB if k % 2 == 0 else pC
                    pyk = 'pB' if k % 2 == 0 else 'pC'
                    for t in range(8):
                        for gl in range(8):
                            MM(P, pY[:, t * 64:(t + 1) * 64], SelT[:, gl, t, :], Ysb[yb][:, gl, :], gl == 0, gl == 7,
                               r=['const', 'Ysb%d' % yb], w=[pyk])
                    STT(P, yv[yb][:], useg[ub][:, k, :], dcol[:, k:k + 1], pY[:, :], ALU.mult, ALU.add, r=[uk, pyk, 'dcol'], w=['yv%d' % yb])
                    ACTF(P, yg[yb][:], yv[yb][:], AF.Gelu_apprx_tanh, r=['yv%d' % yb], w=['yg%d' % yb])
                    DMA(P, K.ygT_d[k * 128:(k + 1) * 128, seg * 512:(seg + 1) * 512], yg[yb][:], r=['yg%d' % yb])
            P.barrier()
    if DBG.get('stop') == 's5scan':
        return

    with ExitStack() as Cx:
        sb = lambda name, shape, dt: Cx.enter_context(nc.sbuf_tensor('%sC%d' % (name, li), shape, dt))
        ps = lambda name, shape, dt: Cx.enter_context(nc.psum_tensor('%sC%d' % (name, li), shape, dt))
        wg = sb('wg', [128, 8, 2048], BF16)
        wo = sb('wo', [128, 8, D], BF16)
        stage = [sb('wst0', [128, 2048], F32), sb('wst1', [128, 2048], F32)]
        ygs = [sb('ygs%d' % i, [128, 8, 512], BF16) for i in range(2)]
        zss = [sb('zss%d' % i, [128, 8, 512], BF16) for i in range(2)]
        sig = [sb('sig%d' % i, [128, 512], F32) for i in range(2)]
        tga = [sb('tga%d' % i, [128, 512], F32) for i in range(2)]
        ozT = [sb('ozT%d' % i, [128, 8, 512], BF16) for i in range(2)]
        ht = [sb('ht%d' % i, [128, D], F32) for i in range(2)]
        hn = [sb('hn%d' % i, [128, D], F32) for i in range(2)]
        pa = [ps('pa%d' % i, [128, 512], F32) for i in range(2)]
        pb = [ps('pb%d' % i, [128, 512], F32) for i in range(2)]
        pm = [ps('pm%d' % i, [128, 512], F32) for i in range(4)]
        gfb = None
        if final_g is not None:
            gfb = sb('gfb', [128, D], F32)
            DMA(P, gfb[:], final_g.partition_broadcast(128), w=['gb'])
        load_weight_bf16(P, K, wg, prm['w_glu'], 2048, stage, 'wglu')
        load_weight_bf16(P, K, wo, prm['w_out'], D, stage, 'wout')
        for seg in range(NC):
            sbi = seg % 2
            cols = slice(seg * 512, (seg + 1) * 512)
            DMA(P, ygs[sbi][:], K.ygT_d[:, cols].rearrange("(k p) t -> p k t", p=128), w=['ygs%d' % sbi])
            DMA(P, zss[sbi][:], K.zsT_d[:, cols].rearrange("(k p) t -> p k t", p=128), w=['zss%d' % sbi])
            for m in range(8):
                x = m % 2
                for kc in range(8):
                    MM(P, pa[x][:], wg[:, kc, m * 128:(m + 1) * 128], ygs[sbi][:, kc, :], kc == 0, kc == 7,
                       r=['ygs%d' % sbi, 'wglu_wb'], w=['pa%d' % x])
                for kc in range(8):
                    MM(P, pb[x][:], wg[:, kc, 1024 + m * 128:1024 + (m + 1) * 128], ygs[sbi][:, kc, :], kc == 0, kc == 7,
                       r=['ygs%d' % sbi, 'wglu_wb'], w=['pb%d' % x])
                ACTF(P, sig[x][:], pb[x][:], AF.Sigmoid, r=['pb%d' % x], w=['sig%d' % x])
                TT(P, 'dve', tga[x][:], pa[x][:], sig[x][:], ALU.mult, r=['pa%d' % x, 'sig%d' % x], w=['tga%d' % x])
                TT(P, 'pool', ozT[sbi][:, m, :], tga[x][:], zss[sbi][:, m, :], ALU.mult, r=['tga%d' % x, 'zss%d' % sbi], w=['ozT%d_%d' % (sbi, m)])
            ozk = ['ozT%d_%d' % (sbi, m) for m in range(8)]
            for tt in range(4):
                b = (4 * seg + tt) % 2
                DMA(P, ht[b][0:64, :], hv_src[seg, 2 * tt], w=['ht%da' % b])
                DMA(P, ht[b][64:128, :], hv_src[seg, 2 * tt + 1], w=['ht%db' % b])
                for half in range(2):
                    pi = (2 * tt + half) % 4
                    for kc in range(8):
                        MM(P, pm[pi][:], ozT[sbi][:, kc, tt * 128:(tt + 1) * 128], wo[:, kc, half * 512:(half + 1) * 512], kc == 0, kc == 7,
                           r=ozk + ['wout_wb'], w=['pm%d' % pi])
                    TT(P, 'dve', hn[b][:, half * 512:(half + 1) * 512], pm[pi][:], ht[b][:, half * 512:(half + 1) * 512], ALU.add,
                       r=['pm%d' % pi, 'ht%da' % b, 'ht%db' % b], w=['hn%d_%d' % (b, half)])
                hk = ['hn%d_0' % b, 'hn%d_1' % b]
                dsts = [hv_dst[seg, 2 * tt], hv_dst[seg, 2 * tt + 1]]
                prs = [slice(0, 64), slice(64, 128)]
                if final_g is None:
                    for d_, pr in zip(dsts, prs):
                        DMA(P, d_, hn[b][pr, :], r=hk)
                else:
                    final_norm_store(P, K, hn[b], hk, gfb, ht[b], ['ht%da' % b, 'ht%db' % b], dsts, prs)
        P.barrier()


def build(layers=(0, 1, 2, 3), with_final=True):
    nc = bass.Bass("TRN2", target_bir_lowering=False)
    K = Ctx()
    x = nc.dram_tensor("x", [S, D], F32, kind="ExternalInput").ap()
    y = nc.dram_tensor("y", [S, D], F32, kind="ExternalOutput").ap()
    prm = {}
    for li in layers:
        prm[li] = {}
        for nm, shp in (NSA_PARAMS if li % 2 == 0 else S5_PARAMS):
            prm[li][nm] = nc.dram_tensor("l%d_%s" % (li, nm), shp, F32, kind="ExternalInput").ap()
    fng = nc.dram_tensor("final_norm", [D], F32, kind="ExternalInput").ap()
    for nm, shp, dt in CONST_SPECS:
        setattr(K, 'c_' + nm, nc.dram_tensor("c_" + nm, shp, dt, kind="ExternalInput").ap())
    scr = lambda nm, shp, dt: nc.dram_tensor("scr_" + nm, shp, dt, kind=("ExternalOutput" if nm in DBG.get('dump', ()) else "Internal")).ap()
    K.qT_d = scr('qT', [1024, S], BF16)
    K.kcT_d = scr('kcT', [256, S], BF16)
    K.vcT_d = scr('vcT', [256, S], BF16)
    K.ksT_d = scr('ksT', [256, S], BF16)
    K.kwT_d = scr('kwT', [256, S], BF16)
    K.vs_d = scr('vs', [S, 256], BF16)
    K.vw_d = scr('vw', [S, 256], BF16)
    K.gates_d = scr('gates', [S, 48], F32)
    K.zs_d = scr('zs', [S, 1024], BF16)
    K.oz_d = scr('oz', [S, 1024], BF16)
    K.uT_d = scr('uT', [1024, S], BF16)
    K.zsT_d = scr('zsT', [1024, S], BF16)
    K.ygT_d = scr('ygT', [1024, S], BF16)
    K.Kd_d = scr('Kd', [8, 16, 64, 16], F32)
    P = Prog(nc)
    with ExitStack() as G:
        gs = lambda name, shape, dt: G.enter_context(nc.sbuf_tensor(name, shape, dt))
        K.ident_b = gs('ident_b', [128, 128], BF16)
        K.ident_f = gs('ident_f', [128, 128], F32)
        K.junkf = gs('junkf', [128, D], F32)
        K.ss = gs('ss', [128, 1], F32)
        K.sq = gs('sq', [128, 1], F32)
        K.rs = gs('rs', [128, 1], F32)
        DMA(P, K.ident_b[:], K.c_ident_b, w=['const'])
        DMA(P, K.ident_f[:], K.c_ident_f, w=['identf'])
        P.barrier()
        hsrc = x
        for li in layers:
            fg = fng if (with_final and li == layers[-1]) else None
            if li % 2 == 0:
                nsa_layer(P, nc, K, li, prm[li], hsrc, y, fg)
            else:
                s5_layer(P, nc, K, li, prm[li], hsrc, y, fg)
            hsrc = y
        run_prog(nc, P)
    return nc


ALL_INPUT_NAMES = (
    'x',
    'l0_norm',
    'l0_w_in',
    'l0_cmp_k_pe',
    'l0_cmp_k_w1',
    'l0_cmp_k_w2',
    'l0_cmp_v_pe',
    'l0_cmp_v_w1',
    'l0_cmp_v_w2',
    'l0_w_out',
    'l1_norm',
    'l1_w_in',
    'l1_log_dt',
    'l1_lambda_re',
    'l1_lambda_im',
    'l1_b_re',
    'l1_b_im',
    'l1_c_re',
    'l1_c_im',
    'l1_d',
    'l1_w_glu',
    'l1_w_out',
    'l2_norm',
    'l2_w_in',
    'l2_cmp_k_pe',
    'l2_cmp_k_w1',
    'l2_cmp_k_w2',
    'l2_cmp_v_pe',
    'l2_cmp_v_w1',
    'l2_cmp_v_w2',
    'l2_w_out',
    'l3_norm',
    'l3_w_in',
    'l3_log_dt',
    'l3_lambda_re',
    'l3_lambda_im',
    'l3_b_re',
    'l3_b_im',
    'l3_c_re',
    'l3_c_im',
    'l3_d',
    'l3_w_glu',
    'l3_w_out',
    'final_norm',
)


_NC_CACHE = {}


def make_in_map(inputs, b, layers=(0, 1, 2, 3)):
    C = host_constants()
    m = {'x': np.ascontiguousarray(inputs['x'][b], dtype=np.float32)}
    for li in layers:
        for nm, shp in (NSA_PARAMS if li % 2 == 0 else S5_PARAMS):
            key = 'l%d_%s' % (li, nm)
            m[key] = np.ascontiguousarray(inputs[key], dtype=np.float32)
    m['final_norm'] = np.ascontiguousarray(inputs['final_norm'], dtype=np.float32)
    for nm, shp, dt in CONST_SPECS:
        m['c_' + nm] = C[nm]
    return m


def kernel(**inputs):
    inputs = {k: np.asarray(inputs[k]) for k in ALL_INPUT_NAMES}
    if 'full' not in _NC_CACHE:
        _NC_CACHE['full'] = build()
    nc = _NC_CACHE['full']
    in_maps = [make_in_map(inputs, c % 4) for c in range(8)]
    res = run_bass_kernel_spmd(nc, in_maps, core_ids=list(range(8)))
    out = np.stack([np.asarray(res.results[b]['y'], dtype=np.float32) for b in range(4)], axis=0)
    return out
```

```python
import math
from contextlib import ExitStack
import numpy as np
import ml_dtypes
import concourse.bass as bass
import concourse.mybir as mybir
from concourse.bass_utils import run_bass_kernel_spmd

F32 = mybir.dt.float32
BF16 = mybir.dt.bfloat16
I32 = mybir.dt.int32
AF = mybir.ActivationFunctionType
ALU = mybir.AluOpType
AX = mybir.AxisListType
NPBF = ml_dtypes.bfloat16

S = 8192
D = 1024
NH = 16
DH = 64
NG = 4
NT = S // 128
NC = S // 512
NSA_IN = 3632
EPS = 1e-6
MASKV = -30000.0

ENGS = ['pe', 'act', 'dve', 'pool', 'sp']
KRING = 8
SAME_ENGINE_SYNC = {'act': True, 'dve': True, 'pool': True, 'pe': False, 'sp': False}


class Prog:
    def __init__(self, nc):
        self.nc = nc
        self.q = {e: [] for e in ENGS}
        self.ncomp = {e: 0 for e in ENGS}
        self.ndma = {e: 0 for e in ENGS}
        self.lastw = {}
        self.readers = {}

    def add(self, eng, fn, r=(), w=(), dma=False, nosync=False):
        deps = {}
        dd = set()

        def addtok(t):
            if t is None:
                return
            if t[0] == 'c':
                if deps.get(t[1], 0) < t[2]:
                    deps[t[1]] = t[2]
            else:
                dd.add(t)

        for k in r:
            addtok(self.lastw.get(k))
        for k in w:
            addtok(self.lastw.get(k))
            rd = self.readers.get(k)
            if rd:
                for e2, n2 in rd[0].items():
                    addtok(('c', e2, n2))
                for t in rd[1]:
                    addtok(t)
        if dma:
            tok = ('d', eng, self.ndma[eng])
            self.ndma[eng] += 1
        else:
            self.ncomp[eng] += 1
            tok = ('c', eng, self.ncomp[eng])
        if nosync:
            deps.pop(eng, None)
        self.q[eng].append(('op', fn, deps, dd, tok))
        for k in w:
            self.lastw[k] = tok
            self.readers[k] = [{}, set()]
        for k in r:
            rd = self.readers.setdefault(k, [{}, set()])
            if tok[0] == 'c':
                if rd[0].get(tok[1], 0) < tok[2]:
                    rd[0][tok[1]] = tok[2]
            else:
                rd[1].add(tok)
        return tok

    def barrier(self):
        sc = dict(self.ncomp)
        sd = dict(self.ndma)
        for e in ENGS:
            self.q[e].append(('bar', sc, sd))
        self.lastw = {}
        self.readers = {}

    def run_engine(self, ename, eng, semc, semd):
        waited_c = {}
        waited_d = {}

        def wait_c(f, n):
            if n <= 0 or waited_c.get(f, 0) >= n:
                return
            eng.wait_ge(semc[f], n)
            waited_c[f] = n

        def wait_d(qn, i):
            if i < 0:
                return
            slot = i % KRING
            tgt = 16 * (i // KRING + 1)
            if waited_d.get((qn, slot), 0) >= tgt:
                return
            eng.wait_ge(semd[qn][slot], tgt)
            waited_d[(qn, slot)] = tgt

        for item in self.q[ename]:
            if item[0] == 'bar':
                _, sc, sd = item
                for f in ENGS:
                    if f == ename and not SAME_ENGINE_SYNC[ename]:
                        continue
                    wait_c(f, sc[f])
                for qn in ENGS:
                    n = sd[qn]
                    for i in range(max(0, n - KRING), n):
                        wait_d(qn, i)
                continue
            _, fn, deps, dd, tok = item
            for f, n in deps.items():
                if f == ename and not SAME_ENGINE_SYNC[ename]:
                    continue
                wait_c(f, n)
            for t in dd:
                wait_d(t[1], t[2])
            if tok[0] == 'd':
                i = tok[2]
                if i >= KRING:
                    wait_d(ename, i - KRING)
                ins = fn(eng)
                ins.then_inc(semd[ename][i % KRING], 16)
            else:
                ins = fn(eng)
                ins.then_inc(semc[ename], 1)
        n = self.ndma[ename]
        for i in range(max(0, n - KRING), n):
            wait_d(ename, i)


def run_prog(nc, prog):
    with ExitStack() as st:
        semc = {e: st.enter_context(nc.semaphore('c_' + e)) for e in ENGS}
        semd = {e: [st.enter_context(nc.semaphore('d_%s_%d' % (e, i))) for i in range(KRING)] for e in ENGS}
        block = st.enter_context(nc.Block())

        @block.tensor
        def _(eng):
            prog.run_engine('pe', eng, semc, semd)

        @block.scalar
        def _(eng):
            prog.run_engine('act', eng, semc, semd)

        @block.vector
        def _(eng):
            prog.run_engine('dve', eng, semc, semd)

        @block.gpsimd
        def _(eng):
            prog.run_engine('pool', eng, semc, semd)

        @block.sync
        def _(eng):
            prog.run_engine('sp', eng, semc, semd)


def DMA(P, out, in_, r=(), w=(), q='sp', slow=False):
    if slow:
        P.add(q, lambda e: e.dma_start(out=out, in_=in_, allow_slow_non_contiguous=True), r=r, w=w, dma=True)
    else:
        P.add(q, lambda e: e.dma_start(out=out, in_=in_), r=r, w=w, dma=True)


def MM(P, out, lhsT, rhs, start, stop, r=(), w=(), skip=False):
    P.add('pe', lambda e: e.matmul(out, lhsT=lhsT, rhs=rhs, start=start, stop=stop, skip_group_check=skip), r=r, w=w)


def TR(P, out, in_, ident, r=(), w=()):
    P.add('pe', lambda e: e.transpose(out=out, in_=in_, identity=ident), r=r, w=w)


def ACTF(P, out, in_, func, r=(), w=(), scale=None, bias=None, accum=None):
    kw = {}
    if scale is not None:
        kw['scale'] = scale
    if bias is not None:
        kw['bias'] = bias
    if accum is not None:
        kw['accum_out'] = accum
    P.add('act', lambda e: e.activation(out=out, in_=in_, func=func, **kw), r=r, w=w)


def COPY(P, eng, out, in_, r=(), w=()):
    if eng == 'act':
        P.add('act', lambda e: e.copy(out=out, in_=in_), r=r, w=w)
    else:
        P.add(eng, lambda e: e.tensor_copy(out=out, in_=in_), r=r, w=w)


def TT(P, eng, out, in0, in1, op, r=(), w=(), nosync=False):
    P.add(eng, lambda e: e.tensor_tensor(out=out, in0=in0, in1=in1, op=op), r=r, w=w, nosync=nosync)


def TS(P, eng, out, in0, s1, op0, s2=None, op1=None, r=(), w=()):
    if op1 is None:
        P.add(eng, lambda e: e.tensor_scalar(out=out, in0=in0, scalar1=s1, scalar2=None, op0=op0), r=r, w=w)
    else:
        P.add(eng, lambda e: e.tensor_scalar(out=out, in0=in0, scalar1=s1, scalar2=s2, op0=op0, op1=op1), r=r, w=w)


def STT(P, out, in0, scalar, in1, op0, op1, r=(), w=(), accum=None):
    if accum is None:
        P.add('dve', lambda e: e.scalar_tensor_tensor(out=out, in0=in0, scalar=scalar, in1=in1, op0=op0, op1=op1), r=r, w=w)
    else:
        P.add('dve', lambda e: e.scalar_tensor_tensor(out=out, in0=in0, scalar=scalar, in1=in1, op0=op0, op1=op1, accum_out=accum), r=r, w=w)


def MEMSET(P, eng, ap, val, r=(), w=()):
    P.add(eng, lambda e: e.memset(ap, val), r=r, w=w)


def _bf(x):
    return np.asarray(x, dtype=np.float32).astype(NPBF)


def _split3(c):
    c = np.asarray(c, dtype=np.float64)
    hi = _bf(c)
    r1 = c - hi.astype(np.float64)
    mid = _bf(r1)
    r2 = r1 - mid.astype(np.float64)
    lo = _bf(r2)
    return hi, mid, lo


_CONST_CACHE = {}


def host_constants():
    if _CONST_CACHE:
        return _CONST_CACHE
    C = {}
    C['ident_b'] = np.eye(128, dtype=np.float32).astype(NPBF)
    C['ident_f'] = np.eye(128, dtype=np.float32)
    hh = np.arange(1, NH + 1, dtype=np.float64)
    slopes = np.exp2(-8.0 * hh / NH).astype(np.float32).astype(np.float64)
    s_hi = _bf(slopes)
    s_lo = _bf(slopes - s_hi.astype(np.float64))
    sp = s_hi.astype(np.float64) + s_lo.astype(np.float64)
    t = np.arange(S, dtype=np.float64)
    QA = np.zeros((NH, 7, S), dtype=NPBF)
    for h in range(NH):
        QA[h, 0, :] = _bf(64.0 * s_hi[h].astype(np.float64))
        QA[h, 1, :] = _bf(64.0 * s_lo[h].astype(np.float64))
        QA[h, 2, :] = s_hi[h]
        QA[h, 3, :] = s_lo[h]
        c = np.float32(-(sp[h] * t)).astype(np.float64)
        a, b, cc = _split3(c)
        QA[h, 4, :] = a
        QA[h, 5, :] = b
        QA[h, 6, :] = cc
    C['QA'] = QA
    KA = np.zeros((7, S), dtype=NPBF)
    pos = np.arange(S)
    KA[0] = _bf(pos // 64)
    KA[1] = _bf(pos // 64)
    KA[2] = _bf(pos % 64)
    KA[3] = _bf(pos % 64)
    KA[4:7] = _bf(1.0)
    C['KA'] = KA
    KAc = np.zeros((7, 512), dtype=NPBF)
    n = np.arange(511)
    KAc[0, :511] = _bf(n // 4)
    KAc[1, :511] = _bf(n // 4)
    KAc[2, :511] = _bf(16.0 * (n % 4) + 15.5)
    KAc[3, :511] = _bf(16.0 * (n % 4) + 15.5)
    KAc[4:7, :511] = _bf(1.0)
    C['KAc'] = KAc
    OH = np.zeros((32, S), dtype=np.float32)
    OH[(np.arange(S) // 64) % 32, np.arange(S)] = 1.0
    C['OH'] = OH.astype(NPBF)
    OV = np.zeros((128, 4, 128), dtype=np.float32)
    for nn in range(511):
        cs = 16 * nn
        ce = cs + 31
        for j in range(128):
            if cs <= 64 * j + 63 and ce >= 64 * j:
                OV[nn % 128, nn // 128, j] = 1.0
    C['OV'] = OV.astype(NPBF)
    k = np.arange(128)[:, None, None]
    r = np.arange(4)[None, :, None]
    q = np.arange(512)[None, None, :]
    C['causal'] = np.where(128 * r + k > q, MASKV, 0.0).astype(np.float32).astype(NPBF)
    r8 = np.arange(8)[None, :, None]
    dwin = q - k + 512 - 128 * r8
    C['band'] = np.where((dwin >= 0) & (dwin < 512), 0.0, MASKV).astype(np.float32).astype(NPBF)
    r5 = np.array([0, 512, 1024, 1536, 2048])[None, :, None]
    C['cmpmask'] = np.where(16 * k + 31 - r5 > q, MASKV, 0.0).astype(np.float32).astype(NPBF)
    FT = np.zeros((128, 256), dtype=np.float32)
    for p in range(128):
        jr = 1 if p >= 64 else 0
        for c in range(255):
            jj = c - 127
            if jj == jr or jj == jr - 1:
                FT[p, c] = 1e9
            elif jj > jr:
                FT[p, c] = -1e9
    C['FT'] = FT
    Sel = np.zeros((128, 8, 8, 128), dtype=np.float32)
    SelT = np.zeros((128, 8, 8, 128), dtype=np.float32)
    for gl in range(8):
        for s in range(8):
            for c in range(16):
                Sel[16 * gl + c, gl, s, 16 * (7 - s) + c] = 1.0
                SelT[16 * s + c, gl, s, 16 * gl + c] = 1.0
    C['Sel'] = Sel.astype(NPBF)
    C['SelT'] = SelT.astype(NPBF)
    _CONST_CACHE.update(C)
    return C


CONST_SPECS = [
    ('ident_b', [128, 128], BF16), ('ident_f', [128, 128], F32), ('QA', [NH, 7, S], BF16),
    ('KA', [7, S], BF16), ('KAc', [7, 512], BF16), ('OH', [32, S], BF16),
    ('OV', [128, 4, 128], BF16), ('causal', [128, 4, 512], BF16), ('band', [128, 8, 512], BF16),
    ('cmpmask', [128, 5, 512], BF16), ('FT', [128, 256], F32),
    ('Sel', [128, 8, 8, 128], BF16), ('SelT', [128, 8, 8, 128], BF16),
]

NSA_PARAMS = [('norm', [D]), ('w_in', [D, NSA_IN]), ('cmp_k_pe', [32, 64]), ('cmp_k_w1', [2048, 256]),
              ('cmp_k_w2', [256, 64]), ('cmp_v_pe', [32, 64]), ('cmp_v_w1', [2048, 256]),
              ('cmp_v_w2', [256, 64]), ('w_out', [D, D])]
S5_PARAMS = [('norm', [D]), ('w_in', [D, 2048]), ('log_dt', [64]), ('lambda_re', [64, 64]),
             ('lambda_im', [64, 64]), ('b_re', [64, 64, 16]), ('b_im', [64, 64, 16]),
             ('c_re', [64, 16, 64]), ('c_im', [64, 16, 64]), ('d', [64, 16]),
             ('w_glu', [D, 2048]), ('w_out', [D, D])]


class Ctx:
    pass


DBG = {}


def TAP(P, nc, name, ap, shape, dt, r=()):
    if not DBG.get('taps'):
        return
    t = nc.dram_tensor('dbg_' + name, shape, dt, kind='ExternalOutput').ap()
    DMA(P, t, ap, r=list(r))


def load_weight_bf16(P, K, wb, wdram, ncols, stage, tag):
    for kc in range(8):
        sb = stage[kc % 2]
        key = 'wstage%d' % (kc % 2)
        DMA(P, sb[:, 0:ncols], wdram[kc * 128:(kc + 1) * 128, :], w=[key])
        eng = 'dve' if kc % 2 == 0 else 'pool'
        COPY(P, eng, wb[:, kc, :], sb[:, 0:ncols], r=[key], w=['%s_wb' % tag])


def rmsnorm_tile(P, K, xt, gb, xnb, tg, rkeys, wkey):
    STT(P, K.junkf[:], xt, 1.0, xt, ALU.mult, ALU.mult, r=rkeys, w=['junkf', 'ss' + tg], accum=K.ss[:])
    ACTF(P, K.sq[:], K.ss[:], AF.Sqrt, r=['ss' + tg], w=['sq' + tg], scale=1.0 / D, bias=EPS)
    P.add('dve', lambda e: e.reciprocal(out=K.rs[:], in_=K.sq[:]), r=['sq' + tg], w=['rs' + tg])
    STT(P, xnb, xt, K.rs[:], gb, ALU.mult, ALU.mult, r=rkeys + ['rs' + tg, 'gb'], w=[wkey])


def transpose_tile(P, K, src_b, dst, skey, dkey, evac_eng):
    for kc in range(8):
        TR(P, K.pT[:, kc, :], src_b[:, kc * 128:(kc + 1) * 128], K.ident_b[:], r=[skey, 'const'], w=['pT'])
    COPY(P, evac_eng, dst, K.pT[:], r=['pT'], w=[dkey])


def nsa_layer(P, nc, K, li, prm, hsrc, hdst, final_g):
    QOFF, KCOFF, VCOFF, KSOFF, VSOFF, KWOFF, VWOFF, GOFF, ZOFF = 0, 1024, 1280, 1536, 1792, 2048, 2304, 2560, 2608
    ident = K.ident_b
    with ExitStack() as L:
        sbL = lambda name, shape, dt: L.enter_context(nc.sbuf_tensor('%s_l%d' % (name, li), shape, dt))
        kcmpT = sbL('kcmpT', [128, NG, 512], BF16)
        vcmpa = sbL('vcmpa', [128, 4, NG, 65], BF16)

        with ExitStack() as A:
            sb = lambda name, shape, dt: A.enter_context(nc.sbuf_tensor('%sA%d' % (name, li), shape, dt))
            ps = lambda name, shape, dt: A.enter_context(nc.psum_tensor('%sA%d' % (name, li), shape, dt))
            wb = sb('wb', [128, 8, NSA_IN], BF16)
            stage = [sb('wst0', [128, NSA_IN], F32), sb('wst1', [128, NSA_IN], F32)]
            gb = sb('gb', [128, D], F32)
            xt = [sb('xt0', [128, D], F32), sb('xt1', [128, D], F32)]
            xnb = [sb('xnb0', [128, D], BF16), sb('xnb1', [128, D], BF16)]
            xT4 = [sb('xT40', [128, 8, 512], BF16), sb('xT41', [128, 8, 512], BF16)]
            fo = [sb('fo%d' % i, [128, 512], BF16) for i in range(4)]
            go = [sb('go%d' % i, [128, 48], F32) for i in range(2)]
            K.pT = ps('pT', [128, 8, 128], BF16)
            pm = [ps('pm%d' % i, [128, 512], F32) for i in range(4)]

            load_weight_bf16(P, K, wb, prm['w_in'], NSA_IN, stage, 'win')
            DMA(P, gb[:], prm['norm'].partition_broadcast(128), w=['gb'])
            nmm = [0]
            nfo = [0]

            def mm_group(out_ps, pkey, lhs_fn, rhs_fn, rk):
                for kc in range(8):
                    MM(P, out_ps, lhs_fn(kc), rhs_fn(kc), kc == 0, kc == 7, r=rk + ['win_wb'], w=[pkey])

            for c in range(NC):
                cb = c % 2
                for tt in range(4):
                    i = 4 * c + tt
                    b = i % 2
                    DMA(P, xt[b][:], hsrc[i * 128:(i + 1) * 128, :], w=['xt%d' % b])
                    rmsnorm_tile(P, K, xt[b][:], gb[:], xnb[b][:], 'A', ['xt%d' % b], 'xnb%d' % b)
                    transpose_tile(P, K, xnb[b], xT4[cb][:, :, tt * 128:(tt + 1) * 128], 'xnb%d' % b, 'xT4%d' % cb,
                                   'act' if tt % 2 == 0 else 'dve')
                xk = 'xT4%d' % cb
                fm = [(m, 'q') for m in range(8)] + [(0, 'kc'), (1, 'kc'), (0, 'vc'), (1, 'vc'), (0, 'ks'), (1, 'ks'), (0, 'kw'), (1, 'kw')]
                for (m, kind) in fm:
                    off = {'q': QOFF, 'kc': KCOFF, 'vc': VCOFF, 'ks': KSOFF, 'kw': KWOFF}[kind] + m * 128
                    dst = {'q': K.qT_d, 'kc': K.kcT_d, 'vc': K.vcT_d, 'ks': K.ksT_d, 'kw': K.kwT_d}[kind]
                    pi = nmm[0] % 4
                    nmm[0] += 1
                    mm_group(pm[pi][:], 'pm%d' % pi, lambda kc, off=off: wb[:, kc, off:off + 128],
                             lambda kc: xT4[cb][:, kc, :], [xk])
                    fi = nfo[0] % 4
                    nfo[0] += 1
                    if kind == 'q':
                        ACTF(P, fo[fi][:], pm[pi][:], AF.Copy, r=['pm%d' % pi], w=['fo%d' % fi], scale=0.125)
                    else:
                        COPY(P, 'dve', fo[fi][:], pm[pi][:], r=['pm%d' % pi], w=['fo%d' % fi])
                    DMA(P, dst[m * 128:(m + 1) * 128, c * 512:(c + 1) * 512], fo[fi][:], r=['fo%d' % fi])
                for tt in range(4):
                    i = 4 * c + tt
                    rows = slice(i * 128, (i + 1) * 128)
                    lhs = lambda kc, tt=tt: xT4[cb][:, kc, tt * 128:(tt + 1) * 128]
                    pi = nmm[0] % 4
                    nmm[0] += 1
                    for kc in range(8):
                        MM(P, pm[pi][:, 0:256], lhs(kc), wb[:, kc, VSOFF:VSOFF + 256], kc == 0, kc == 7, r=[xk, 'win_wb'], w=['pm%d' % pi])
                    for kc in range(8):
                        MM(P, pm[pi][:, 256:512], lhs(kc), wb[:, kc, VWOFF:VWOFF + 256], kc == 0, kc == 7, r=[xk, 'win_wb'], w=['pm%d' % pi])
                    fi = nfo[0] % 4
                    nfo[0] += 1
                    COPY(P, 'dve', fo[fi][:], pm[pi][:], r=['pm%d' % pi], w=['fo%d' % fi])
                    DMA(P, K.vs_d[rows, :], fo[fi][:, 0:256], r=['fo%d' % fi])
                    DMA(P, K.vw_d[rows, :], fo[fi][:, 256:512], r=['fo%d' % fi])
                    pi = nmm[0] % 4
                    nmm[0] += 1
                    for kc in range(8):
                        MM(P, pm[pi][:, 0:48], lhs(kc), wb[:, kc, GOFF:GOFF + 48], kc == 0, kc == 7, r=[xk, 'win_wb'], w=['pm%d' % pi])
                    gi = i % 2
                    ACTF(P, go[gi][:], pm[pi][:, 0:48], AF.Sigmoid, r=['pm%d' % pi], w=['go%d' % gi])
                    DMA(P, K.gates_d[rows, :], go[gi][:], r=['go%d' % gi])
                    for j in range(2):
                        pi = nmm[0] % 4
                        nmm[0] += 1
                        mm_group(pm[pi][:], 'pm%d' % pi, lhs, lambda kc, j=j: wb[:, kc, ZOFF + 512 * j:ZOFF + 512 * (j + 1)], [xk])
                        fi = nfo[0] % 4
                        nfo[0] += 1
                        ACTF(P, fo[fi][:], pm[pi][:], AF.Silu, r=['pm%d' % pi], w=['fo%d' % fi])
                        DMA(P, K.zs_d[rows, 512 * j:512 * (j + 1)], fo[fi][:], r=['fo%d' % fi])
            P.barrier()

        with ExitStack() as B:
            sb = lambda name, shape, dt: B.enter_context(nc.sbuf_tensor('%sB%d' % (name, li), shape, dt))
            ps = lambda name, shape, dt: B.enter_context(nc.psum_tensor('%sB%d' % (name, li), shape, dt))
            w1st = sb('w1st', [64, 32, 256], F32)
            w1b = [sb('w1b0', [64, 32, 256], BF16), sb('w1b1', [64, 32, 256], BF16)]
            w2st = sb('w2st', [128, 2, 64], F32)
            w2b = [sb('w2b0', [128, 2, 96], BF16), sb('w2b1', [128, 2, 64], BF16)]
            pest = sb('pest', [64, 32], F32)
            peb = [sb('peb0', [64, 32], BF16), sb('peb1', [64, 32], BF16)]
            hb = [sb('hb0', [128, 2], F32), sb('hb1', [128, 2], F32)]
            xin = [sb('xin0', [64, S], BF16), sb('xin1', [64, S], BF16)]
            hid = sb('hid', [128, 2, 512], BF16)
            ph = [ps('ph0', [128, 512], F32), ps('ph1', [128, 512], F32)]
            pb = ps('pb', [128, 512], F32)
            po = ps('po', [128, 512], F32)

            MEMSET(P, 'pool', kcmpT[:], 0.0, w=['kcmpT'])
            MEMSET(P, 'pool', vcmpa[:], 0.0, w=['vcmpa'])
            MEMSET(P, 'pool', hid[:], 0.0, w=['hid'])
            for kv in range(2):
                nm = 'cmp_k' if kv == 0 else 'cmp_v'
                DMA(P, w1st[:], prm[nm + '_w1'].rearrange("(l d) h -> d l h", d=64), w=['w1st'])
                COPY(P, 'dve', w1b[kv][:], w1st[:], r=['w1st'], w=['w1b%d' % kv])
                DMA(P, w2st[:], prm[nm + '_w2'].rearrange("(m p) d -> p m d", p=128), w=['w2st'])
                if kv == 0:
                    MEMSET(P, 'pool', w2b[0][:], 0.0, w=['w2b0'])
                    COPY(P, 'dve', w2b[0][:, :, 32:96], w2st[:], r=['w2st', 'w2b0'], w=['w2b0'])
                else:
                    COPY(P, 'dve', w2b[1][:], w2st[:], r=['w2st'], w=['w2b1'])
                DMA(P, pest[:], prm[nm + '_pe'].rearrange("l d -> d l"), w=['pest'], slow=True)
                COPY(P, 'dve', peb[kv][:], pest[:], r=['pest'], w=['peb%d' % kv])
                for m in range(2):
                    for l in range(32):
                        MM(P, pb[:, m:m + 1], w1b[kv][:, l, m * 128:(m + 1) * 128], peb[kv][:, l:l + 1], l == 0, l == 31,
                           r=['w1b%d' % kv, 'peb%d' % kv], w=['pb'])
                COPY(P, 'dve', hb[kv][:], pb[:, 0:2], r=['pb'], w=['hb%d' % kv])
            DMA(P, kcmpT[96:103, :, :], K.c_KAc.unsqueeze(1).broadcast_to([7, NG, 512]), r=['kcmpT'], w=['kcmpT'])
            it = 0
            for g in range(NG):
                for kv in range(2):
                    xb = it % 2
                    it += 1
                    src = K.kcT_d if kv == 0 else K.vcT_d
                    DMA(P, xin[xb][:], src[g * 64:(g + 1) * 64, :], w=['xin%d' % xb])
                    for m in range(2):
                        for l in range(32):
                            MM(P, ph[m][:, 0:511], w1b[kv][:, l, m * 128:(m + 1) * 128], xin[xb][:, l:l + 16 * 510 + 1:16],
                               l == 0, l == 31, r=['w1b%d' % kv, 'xin%d' % xb], w=['ph%d' % m])
                        ACTF(P, hid[:, m, 0:511], ph[m][:, 0:511], AF.Silu, r=['ph%d' % m, 'hb%d' % kv], w=['hid'], bias=hb[kv][:, m:m + 1])
                    if kv == 0:
                        for m in range(2):
                            MM(P, po[0:96, 0:511], w2b[0][:, m, :], hid[:, m, 0:511], m == 0, m == 1, r=['w2b0', 'hid'], w=['po'])
                        COPY(P, 'dve', kcmpT[0:96, g, 0:511], po[0:96, 0:511], r=['po'], w=['kcmpT'])
                    else:
                        for nt in range(4):
                            for m in range(2):
                                MM(P, po[:, nt * 64:(nt + 1) * 64], hid[:, m, nt * 128:(nt + 1) * 128], w2b[1][:, m, :], m == 0, m == 1,
                                   r=['w2b1', 'hid'], w=['po'])
                        COPY(P, 'dve', vcmpa[:, :, g, 0:64], po[:, 0:256].rearrange("p (a b) -> p a b", a=4), r=['po'], w=['vcmpa'])
            MEMSET(P, 'pool', vcmpa[:, :, :, 64:65], 1.0, r=['vcmpa'], w=['vcmpa'])
            P.barrier()

        if DBG.get('stop') == 'nsaB':
            return
        nsa_attention(P, nc, K, li, kcmpT, vcmpa)
        if DBG.get('stop') == 'nsaC':
            return

    out_proj_phase(P, nc, K, li, prm['w_out'], hsrc, hdst, final_g, token_major_src=True)


THR_DECAY = 48.0


def _thr():
    return DBG.get("thr", THR_DECAY)


def _slope(h):
    return 2.0 ** (-(h + 1) / 2.0)


def nsa_attention(P, nc, K, li, kcmpT, vcmpa):
    with ExitStack() as Cx:
        sb = lambda name, shape, dt: Cx.enter_context(nc.sbuf_tensor('%sC%d' % (name, li), shape, dt))
        ps = lambda name, shape, dt: Cx.enter_context(nc.psum_tensor('%sC%d' % (name, li), shape, dt))
        ident = K.ident_b
        OV = sb('OV', [128, 4, 128], BF16)
        causal = sb('causal', [128, 4, 512], BF16)
        band = sb('band', [128, 8, 512], BF16)
        cmpm = sb('cmpm', [128, 5, 512], BF16)
        FT = sb('FT', [128, 256], F32)
        zer = sb('zer', [128, 512], BF16)
        ksT = sb('ksT', [128, S], BF16)
        vsa = sb('vsa', [128, 64, 65], BF16)
        qa = [sb('qa%d' % i, [128, 4, 4, 512], BF16) for i in range(2)]
        kwT = [sb('kwT%d' % i, [128, 1024], BF16) for i in range(2)]
        vwa = [sb('vwa%d' % i, [128, 8, 65], BF16) for i in range(2)]
        gt = [sb('gt%d' % i, [128, 4, 48], F32) for i in range(2)]
        zt = [sb('zt%d' % i, [128, 4, 256], BF16) for i in range(2)]
        pc = [[sb('pc%d_%d' % (r, nt), [128, 512], BF16) for nt in range(4)] for r in range(4)]
        pt = [sb('pt%d' % i, [128, 512], BF16) for i in range(4)]
        oacc = sb('oacc', [128, 4, 4, 64], F32)
        tmpo = [sb('tmpo%d' % i, [128, 4, 64], F32) for i in range(2)]
        rcs = sb('rcs', [128, 4, 4], F32)
        rr = [sb('rr%d' % i, [128, 4], F32) for i in range(2)]
        ww = [sb('ww%d' % i, [128, 4], F32) for i in range(2)]
        impacc = [sb('impacc%d' % i, [128, 128], F32) for i in range(2)]
        impf = [sb('impf%d' % i, [128, 128], F32) for i in range(2)]
        work = [sb('work%d' % i, [128, 128], F32) for i in range(2)]
        m8a = [sb('m8a%d' % i, [128, 8], F32) for i in range(2)]
        m8b = [sb('m8b%d' % i, [128, 8], F32) for i in range(2)]
        selb = [sb('selb%d' % i, [128, 128], BF16) for i in range(4)]
        ozb = [sb('ozb%d' % i, [128, 4, 256], BF16) for i in range(2)]
        Sps = [ps('S%d' % i, [128, 512], F32) for i in range(3)]
        Ops = [ps('O%d' % i, [128, 512], F32) for i in range(2)]
        pimp = [ps('pimp%d' % i, [128, 512], F32) for i in range(2)]
        pst = ps('pst', [128, 1024], BF16)

        DMA(P, OV[:], K.c_OV, w=['const'])
        DMA(P, causal[:], K.c_causal, w=['const'])
        DMA(P, band[:], K.c_band, w=['const'])
        DMA(P, cmpm[:], K.c_cmpmask, w=['const'])
        DMA(P, FT[:], K.c_FT, w=['const'])
        MEMSET(P, 'pool', zer[:], 0.0, w=['const'])
        MEMSET(P, 'pool', vsa[:, :, 64:65], 1.0, w=['vsa1'])
        for i in range(2):
            MEMSET(P, 'pool', vwa[i][:, :, 64:65], 1.0, w=['vwa1_%d' % i])
            MEMSET(P, 'pool', qa[i][0:32, :, :, :], 0.0, w=['qsel%d' % i])
            MEMSET(P, 'pool', kwT[i][0:32, :], 0.0, w=['kwT0_%d' % i])
        DMA(P, ksT[0:32, :], K.c_OH, w=['ksToh'])
        DMA(P, ksT[96:103, :], K.c_KA, w=['ksTaug'])
        st = {'s': 0, 'p': 0, 'o': 0, 'x': 0}

        def combine(oi, first, gcol, r, keep_rc, cb):
            O3 = Ops[oi][:, 0:260].rearrange("p (a b) -> p a b", a=4)
            x = st['x'] % 2
            st['x'] += 1
            TS(P, 'dve', rr[x][:], O3[:, :, 64], 1e-30, ALU.max, r=['O%d' % oi], w=['rr%d' % x])
            P.add('dve', lambda e: e.reciprocal(out=rr[x][:], in_=rr[x][:]), r=['rr%d' % x], w=['rr%d' % x])
            if keep_rc:
                COPY(P, 'dve', rcs[:, r, :], rr[x][:], r=['rr%d' % x], w=['rcs%d' % r])
            TT(P, 'dve', ww[x][:], rr[x][:], gt[cb][:, :, gcol], ALU.mult, r=['rr%d' % x, 'gt%d' % cb], w=['ww%d' % x])
            wbc = ww[x][:].unsqueeze(2).broadcast_to([128, 4, 64])
            if first:
                TT(P, 'dve', oacc[:, :, r, :], O3[:, :, 0:64], wbc, ALU.mult, r=['O%d' % oi, 'ww%d' % x], w=['oacc%d' % r])
            else:
                TT(P, 'dve', tmpo[x][:], O3[:, :, 0:64], wbc, ALU.mult, r=['O%d' % oi, 'ww%d' % x], w=['tmpo%d' % x])
                TT(P, 'pool', oacc[:, :, r, :], oacc[:, :, r, :], tmpo[x][:], ALU.add, r=['tmpo%d' % x, 'oacc%d' % r], w=['oacc%d' % r])

        def run_stream(jobs):
            n = len(jobs)
            LAG = 2
            for i in range(n + LAG):
                if i < n:
                    j = jobs[i]
                    si = st['s'] % 3
                    st['s'] += 1
                    nm = len(j['mms'])
                    for idx, (l_, r_, rk) in enumerate(j['mms']):
                        MM(P, Sps[si][:], l_, r_, idx == 0, idx == nm - 1, r=rk, w=['S%d' % si])
                    if j['pbuf'] is None:
                        pi = st['p'] % 4
                        st['p'] += 1
                        j['pbuf'] = (pt[pi], 'pt%d' % pi)
                    ACTF(P, j['pbuf'][0][:], Sps[si][:], AF.Exp, r=['S%d' % si], w=[j['pbuf'][1]])
                if i >= LAG:
                    j = jobs[i - LAG]
                    oi = j['oi']
                    if j['first']:
                        MM(P, Ops[oi][:, 0:260], zer[:, 0:128], zer[:, 0:260], True, False, r=['const'], w=['O%d' % oi], skip=True)
                    pb, pk = j['pbuf']
                    for sub in range(4):
                        MM(P, Ops[oi][:, sub * 65:(sub + 1) * 65], pb[:, sub * 128:(sub + 1) * 128], j['v'], False,
                           (j['last'] and sub == 3), r=[pk] + j['vk'], w=['O%d' % oi], skip=True)
                    if j['last']:
                        j['fin'](oi)

        it = 0
        for g in range(DBG.get('c_ng', NG)):
            DMA(P, ksT[32:96, :], K.ksT_d[g * 64:(g + 1) * 64, :], w=['ksT'])
            for q4 in range(4):
                DMA(P, vsa[:, q4 * 16:(q4 + 1) * 16, 0:64],
                    K.vs_d[q4 * 2048:(q4 + 1) * 2048, g * 64:(g + 1) * 64].rearrange("(kt p) d -> p kt d", p=128), r=['vsa'], w=['vsa'])
            for c in range(DBG.get('c_nc', NC)):
                cb = it % 2
                it += 1
                cs = slice(c * 512, (c + 1) * 512)
                ntmax = c // 4
                sel_kts, win_rs, cmp_nts = [], [], []
                for r in range(4):
                    sl = _slope(4 * g + r)
                    sel_kts.append([kt for kt in range(4 * c + 4) if sl * max(0, 512 * c - 128 * kt - 127) <= _thr()])
                    win_rs.append([rw for rw in range(4 if c == 0 else 0, 8) if sl * max(0, 385 - 128 * rw) <= _thr()])
                    cmp_nts.append([nt for nt in range(ntmax + 1)
                                    if sl * max(0.0, 512 * c - (2048 * nt + 2047.5) - 48.0) <= _thr()])
                VD = 1000 if DBG.get('one_ver') else 16
                selvers = sorted(set(kt // VD for r in range(4) for kt in sel_kts[r]))
                vers = sorted(set(selvers) | {0})
                qkeys = []
                for v in vers:
                    DMA(P, qa[cb][32:96, v, :, :], K.qT_d[g * 256:(g + 1) * 256, cs].rearrange("(r d) t -> d r t", d=64), w=['qa%d_%d' % (cb, v)])
                    DMA(P, qa[cb][96:103, v, :, :], K.c_QA[g * 4:(g + 1) * 4, :, cs].rearrange("h r t -> r h t"), w=['qaug%d_%d' % (cb, v)])
                    qkeys += ['qa%d_%d' % (cb, v), 'qaug%d_%d' % (cb, v)]
                q0k = ['qa%d_0' % cb, 'qaug%d_0' % cb, 'qsel%d' % cb]
                k0 = (c - 1) * 512
                if c == 0:
                    DMA(P, kwT[cb][32:96, 512:1024], K.kwT_d[g * 64:(g + 1) * 64, 0:512], w=['kwT%d' % cb])
                    DMA(P, kwT[cb][96:103, 512:1024], K.c_KA[:, 0:512], w=['kwTa%d' % cb])
                    DMA(P, vwa[cb][:, 4:8, 0:64], K.vw_d[0:512, g * 64:(g + 1) * 64].rearrange("(kt p) d -> p kt d", p=128), w=['vwa%d' % cb])
                else:
                    DMA(P, kwT[cb][32:96, :], K.kwT_d[g * 64:(g + 1) * 64, k0:k0 + 1024], w=['kwT%d' % cb])
                    DMA(P, kwT[cb][96:103, :], K.c_KA[:, k0:k0 + 1024], w=['kwTa%d' % cb])
                    DMA(P, vwa[cb][:, :, 0:64], K.vw_d[k0:k0 + 1024, g * 64:(g + 1) * 64].rearrange("(kt p) d -> p kt d", p=128), w=['vwa%d' % cb])
                DMA(P, gt[cb][:], K.gates_d[cs, :].rearrange("(s p) c -> p s c", p=128), w=['gt%d' % cb])
                DMA(P, zt[cb][:], K.zs_d[cs, g * 256:(g + 1) * 256].rearrange("(s p) c -> p s c", p=128), w=['zt%d' % cb])

                jobs = []
                for r in range(4):
                    for nt in cmp_nts[r]:
                        mms = [(kcmpT[0:103, g, nt * 128:(nt + 1) * 128], qa[cb][0:103, 0, r, :], q0k + ['kcmpT'])]
                        if nt == ntmax:
                            mms.append((ident[:], cmpm[:, c % 4, :], ['const']))
                        elif nt == ntmax - 1 and c % 4 == 0 and not DBG.get('no_elif'):
                            mms.append((ident[:], cmpm[:, 4, :], ['const']))
                        jobs.append(dict(mms=mms, pbuf=(pc[r][nt], 'pc%d_%d' % (r, nt)), v=vcmpa[:, nt, g, :], vk=['vcmpa'],
                                         oi=(st['o'] + r) % 2, first=(nt == cmp_nts[r][0]), last=(nt == cmp_nts[r][-1]),
                                         fin=(lambda oi, r=r: combine(oi, True, 0 * 16 + g * 4 + r, r, True, cb))))
                st['o'] += 4
                run_stream(jobs)

                for sub in range(4):
                    ti = 4 * c + sub
                    x2 = sub % 2
                    pb_ = pimp[x2]
                    pk_ = 'pimp%d' % x2
                    for r in range(4):
                        for nt in cmp_nts[r]:
                            MM(P, pb_[:, r * 128:(r + 1) * 128], pc[r][nt][:, sub * 128:(sub + 1) * 128], OV[:, nt, :],
                               nt == cmp_nts[r][0], nt == cmp_nts[r][-1], r=['pc%d_%d' % (r, nt), 'const'], w=[pk_])
                    ia, if_, wk, ma, mb = impacc[x2], impf[x2], work[x2], m8a[x2], m8b[x2]
                    kx = '_%d' % x2
                    for r in range(4):
                        if r == 0:
                            TS(P, 'dve', ia[:], pb_[:, 0:128], rcs[:, r, sub:sub + 1], ALU.mult, r=[pk_, 'rcs%d' % r], w=['impacc' + kx])
                        else:
                            STT(P, ia[:], pb_[:, r * 128:(r + 1) * 128], rcs[:, r, sub:sub + 1], ia[:], ALU.mult, ALU.add,
                                r=[pk_, 'rcs%d' % r, 'impacc' + kx], w=['impacc' + kx])
                    TT(P, 'dve', if_[:], ia[:], FT[:, 127 - 2 * ti:255 - 2 * ti], ALU.add, r=['impacc' + kx, 'const'], w=['impf' + kx])
                    MEMSET(P, 'dve', if_[:, 0:1], 1e9, r=['impf' + kx], w=['impf' + kx])
                    P.add('dve', (lambda e, ma=ma, if_=if_: e.max(out=ma[:], in_=if_[:])), r=['impf' + kx], w=['m8a' + kx])
                    P.add('dve', (lambda e, ma=ma, if_=if_, wk=wk: e.match_replace(out=wk[:], in_to_replace=ma[:], in_values=if_[:], imm_value=-3.0e38)),
                          r=['impf' + kx, 'm8a' + kx], w=['work' + kx])
                    P.add('dve', (lambda e, mb=mb, wk=wk: e.max(out=mb[:], in_=wk[:])), r=['work' + kx], w=['m8b' + kx])
                    TS(P, 'dve', selb[sub][:], if_[:], mb[:, 7:8], ALU.is_lt, DBG.get('maskv', MASKV), ALU.mult, r=['impf' + kx, 'm8b' + kx], w=['selb%d' % sub])
                    pcol = slice((sub % 2) * 128, (sub % 2) * 128 + 128)
                    pkey = 'pst%d' % (sub % 2)
                    TR(P, pst[:, pcol], selb[sub][:], ident[:], r=['selb%d' % sub, 'const'], w=[pkey])
                    for v in selvers:
                        COPY(P, 'dve', qa[cb][0:32, v, :, sub * 128:(sub + 1) * 128],
                             pst[32 * v:32 * v + 32, pcol].unsqueeze(1).broadcast_to([32, 4, 128]), r=[pkey], w=['qsel%d' % cb])

                jobs = []
                for r in range(4):
                    for rw in win_rs[r]:
                        mms = [(kwT[cb][0:103, rw * 128:(rw + 1) * 128], qa[cb][0:103, 0, r, :], q0k + ['kwT%d' % cb, 'kwTa%d' % cb, 'kwT0_%d' % cb]),
                               (ident[:], band[:, rw, :], ['const'])]
                        jobs.append(dict(mms=mms, pbuf=None, v=vwa[cb][:, rw, :], vk=['vwa%d' % cb, 'vwa1_%d' % cb],
                                         oi=(st['o'] + r) % 2, first=(rw == win_rs[r][0]), last=(rw == win_rs[r][-1]),
                                         fin=(lambda oi, r=r: combine(oi, False, 2 * 16 + g * 4 + r, r, False, cb))))
                st['o'] += 4
                run_stream(jobs)

                jobs = []
                for r in range(4):
                    for kt in sel_kts[r]:
                        mms = [(ksT[0:103, kt * 128:(kt + 1) * 128], qa[cb][0:103, kt // VD, r, :],
                                qkeys + ['qsel%d' % cb, 'ksT', 'ksTaug', 'ksToh'])]
                        if kt >= 4 * c:
                            mms.append((ident[:], causal[:, kt - 4 * c, :], ['const']))
                        jobs.append(dict(mms=mms, pbuf=None, v=vsa[:, kt, :], vk=['vsa', 'vsa1'],
                                         oi=(st['o'] + r) % 2, first=(kt == sel_kts[r][0]), last=(kt == sel_kts[r][-1]),
                                         fin=(lambda oi, r=r: combine(oi, False, 1 * 16 + g * 4 + r, r, False, cb))))
                st['o'] += 4
                run_stream(jobs)

                TT(P, 'pool', ozb[cb][:], oacc[:].rearrange("p s r d -> p s (r d)"), zt[cb][:], ALU.mult,
                   r=['oacc0', 'oacc1', 'oacc2', 'oacc3', 'zt%d' % cb], w=['ozb%d' % cb])
                DMA(P, K.oz_d[cs, g * 256:(g + 1) * 256].rearrange("(s p) d -> p s d", p=128), ozb[cb][:], r=['ozb%d' % cb])
        P.barrier()


def out_proj_phase(P, nc, K, li, wout, hsrc, hdst, final_g, token_major_src):
    with ExitStack() as Dx:
        sb = lambda name, shape, dt: Dx.enter_context(nc.sbuf_tensor('%sD%d' % (name, li), shape, dt))
        ps = lambda name, shape, dt: Dx.enter_context(nc.psum_tensor('%sD%d' % (name, li), shape, dt))
        wb = sb('wb', [128, 8, D], BF16)
        stage = [sb('wst0', [128, D], F32), sb('wst1', [128, D], F32)]
        ozt = [sb('ozt%d' % i, [128, D], BF16) for i in range(2)]
        ozT = [sb('ozT%d' % i, [128, 8, 128], BF16) for i in range(2)]
        ht = [sb('ht%d' % i, [128, D], F32) for i in range(2)]
        hn = [sb('hn%d' % i, [128, D], F32) for i in range(2)]
        K.pT = ps('pT', [128, 8, 128], BF16)
        pm = [ps('pm%d' % i, [128, 512], F32) for i in range(4)]
        gfb = None
        if final_g is not None:
            gfb = sb('gfb', [128, D], F32)
            DMA(P, gfb[:], final_g.partition_broadcast(128), w=['gb'])
        load_weight_bf16(P, K, wb, wout, D, stage, 'wout')
        for i in range(NT):
            b = i % 2
            rows = slice(i * 128, (i + 1) * 128)
            DMA(P, ozt[b][:], K.oz_d[rows, :], w=['ozt%d' % b])
            DMA(P, ht[b][:], hsrc[rows, :], w=['ht%d' % b])
            transpose_tile(P, K, ozt[b], ozT[b][:], 'ozt%d' % b, 'ozT%d' % b, 'act' if i % 2 == 0 else 'dve')
            for half in range(2):
                pi = (2 * i + half) % 4
                for kc in range(8):
                    MM(P, pm[pi][:], ozT[b][:, kc, :], wb[:, kc, half * 512:(half + 1) * 512], kc == 0, kc == 7,
                       r=['ozT%d' % b, 'wout_wb'], w=['pm%d' % pi])
                TT(P, 'dve', hn[b][:, half * 512:(half + 1) * 512], pm[pi][:], ht[b][:, half * 512:(half + 1) * 512], ALU.add,
                   r=['pm%d' % pi, 'ht%d' % b], w=['hn%d_%d' % (b, half)])
            hk = ['hn%d_0' % b, 'hn%d_1' % b]
            if final_g is None:
                DMA(P, hdst[rows, :], hn[b][:], r=hk)
            else:
                final_norm_store(P, K, hn[b], hk, gfb, ht[b], ['ht%d' % b], [hdst[rows, :]], [slice(0, 128)])
        P.barrier()


def final_norm_store(P, K, hn, hk, gfb, outbuf, okeys, dsts, prs):
    STT(P, K.junkf[:], hn[:], 1.0, hn[:], ALU.mult, ALU.mult, r=hk, w=['junkf', 'ssF'], accum=K.ss[:])
    ACTF(P, K.sq[:], K.ss[:], AF.Sqrt, r=['ssF'], w=['sqF'], scale=1.0 / D, bias=EPS)
    P.add('dve', lambda e: e.reciprocal(out=K.rs[:], in_=K.sq[:]), r=['sqF'], w=['rsF'])
    STT(P, outbuf[:], hn[:], K.rs[:], gfb[:], ALU.mult, ALU.mult, r=hk + ['rsF', 'gb'], w=okeys)
    for d_, pr in zip(dsts, prs):
        DMA(P, d_, outbuf[pr, :], r=okeys)


def s5_layer(P, nc, K, li, prm, hsrc, hdst, final_g):
    hv_src = hsrc.rearrange("(g n s) d -> g s n d", n=64, s=8)
    hv_dst = hdst.rearrange("(g n s) d -> g s n d", n=64, s=8)
    TWO_PI = 2.0 * math.pi

    with ExitStack() as A:
        sb = lambda name, shape, dt: A.enter_context(nc.sbuf_tensor('%sA%d' % (name, li), shape, dt))
        ps = lambda name, shape, dt: A.enter_context(nc.psum_tensor('%sA%d' % (name, li), shape, dt))
        wb = sb('wb', [128, 8, 2048], BF16)
        stage = [sb('wst0', [128, 2048], F32), sb('wst1', [128, 2048], F32)]
        gb = sb('gb', [128, D], F32)
        xt = [sb('xt0', [128, D], F32), sb('xt1', [128, D], F32)]
        xnb = [sb('xnb0', [128, D], BF16), sb('xnb1', [128, D], BF16)]
        xT4 = [sb('xT40', [128, 8, 512], BF16), sb('xT41', [128, 8, 512], BF16)]
        fo = [sb('fo%d' % i, [128, 512], BF16) for i in range(4)]
        K.pT = ps('pT', [128, 8, 128], BF16)
        pm = [ps('pm%d' % i, [128, 512], F32) for i in range(4)]
        load_weight_bf16(P, K, wb, prm['w_in'], 2048, stage, 'win')
        DMA(P, gb[:], prm['norm'].partition_broadcast(128), w=['gb'])
        cnt = 0
        for seg in range(NC):
            cb = seg % 2
            for tt in range(4):
                b = (4 * seg + tt) % 2
                DMA(P, xt[b][0:64, :], hv_src[seg, 2 * tt], w=['xt%da' % b])
                DMA(P, xt[b][64:128, :], hv_src[seg, 2 * tt + 1], w=['xt%db' % b])
                rmsnorm_tile(P, K, xt[b][:], gb[:], xnb[b][:], 'A', ['xt%da' % b, 'xt%db' % b], 'xnb%d' % b)
                transpose_tile(P, K, xnb[b], xT4[cb][:, :, tt * 128:(tt + 1) * 128], 'xnb%d' % b, 'xT4%d' % cb,
                               'act' if tt % 2 == 0 else 'dve')
            for m in range(16):
                pi = cnt % 4
                fi = cnt % 4
                cnt += 1
                for kc in range(8):
                    MM(P, pm[pi][:], wb[:, kc, m * 128:(m + 1) * 128], xT4[cb][:, kc, :], kc == 0, kc == 7,
                       r=['xT4%d' % cb, 'win_wb'], w=['pm%d' % pi])
                if m < 8:
                    COPY(P, 'dve', fo[fi][:], pm[pi][:], r=['pm%d' % pi], w=['fo%d' % fi])
                    DMA(P, K.uT_d[m * 128:(m + 1) * 128, seg * 512:(seg + 1) * 512], fo[fi][:], r=['fo%d' % fi])
                else:
                    ACTF(P, fo[fi][:], pm[pi][:], AF.Silu, r=['pm%d' % pi], w=['fo%d' % fi])
                    DMA(P, K.zsT_d[(m - 8) * 128:(m - 7) * 128, seg * 512:(seg + 1) * 512], fo[fi][:], r=['fo%d' % fi])
        P.barrier()
    if DBG.get('stop') == 's5p1':
        return

    with ExitStack() as B:
        sb = lambda name, shape, dt: B.enter_context(nc.sbuf_tensor('%sB%d' % (name, li), shape, dt))
        ps = lambda name, shape, dt: B.enter_context(nc.psum_tensor('%sB%d' % (name, li), shape, dt))
        identf = K.ident_f
        Tb = sb('Tb', [128, 64, 128], BF16)
        Pm = sb('Pm', [128, 64, 2, 64], BF16)
        QmR = sb('QmR', [64, 64, 8, 16], BF16)
        QmI = sb('QmI', [64, 64, 8, 16], BF16)
        ar8 = sb('ar8', [64, 64], F32)
        ai8 = sb('ai8', [64, 64], F32)
        dcol = sb('dcol', [128, 8], F32)
        pA = ps('pA', [128, 512], F32)
        pB = ps('pB', [128, 512], F32)
        pC = ps('pC', [128, 512], F32)
        pD = ps('pD', [128, 512], F32)
        pK = ps('pK', [128, 1024], F32)
        DMA(P, dcol[:], prm['d'].rearrange("g c -> (g c)").rearrange("(k p) -> p k", p=128), w=['dcol'], slow=True)

        with ExitStack() as T:
            tb = lambda name, shape, dt: T.enter_context(nc.sbuf_tensor('%sT%d' % (name, li), shape, dt))
            t64 = lambda name: tb(name, [64, 64], F32)
            lamre_g, lamim_g, lre, lim, dtb = t64('lamre_g'), t64('lamim_g'), t64('lre'), t64('lim'), t64('dtb')
            xr_, yi_, mag, sn, cs_, abr, abi = t64('xr_'), t64('yi_'), t64('mag'), t64('sn'), t64('cs_'), t64('abr'), t64('abi')
            den, nr, cfr, cfi, tA, tB, tC = t64('den'), t64('nr'), t64('cfr'), t64('cfi'), t64('tA'), t64('tB'), t64('tC')
            tI = tb('tI', [64, 64], I32)
            bre = tb('bre', [64, 64, 16], F32)
            bim = tb('bim', [64, 64, 16], F32)
            bbr = tb('bbr', [64, 64, 16], F32)
            bbi = tb('bbi', [64, 64, 16], F32)
            t3 = tb('t3', [64, 64, 16], F32)
            t4 = tb('t4', [64, 64, 16], F32)
            Lr = tb('Lr', [64, 9, 64], F32)
            Li = tb('Li', [64, 9, 64], F32)
            Wr = tb('Wr', [64, 64, 8, 16], F32)
            Wi = tb('Wi', [64, 64, 8, 16], F32)
            Cst = tb('Cst', [128, 8, 64], F32)
            CrT = tb('CrT', [64, 64, 16], F32)
            CiT = tb('CiT', [64, 64, 16], F32)
            nCrT = tb('nCrT', [64, 64, 16], F32)
            nCiT = tb('nCiT', [64, 64, 16], F32)
            Ktsb = tb('Ktsb', [128, 64, 16], F32)

            def V(eng, out, a, b, op, r, w):
                TT(P, eng, out, a, b, op, r=r, w=w)

            def VN(eng, out, a, b, op, r, w):
                TT(P, eng, out, a, b, op, r=r, w=w, nosync=True)

            DMA(P, lamre_g[:], prm['lambda_re'], w=['lamre_g'])
            DMA(P, lamim_g[:], prm['lambda_im'], w=['lamim_g'])
            TR(P, pA[0:64, 0:64], lamre_g[:], identf[0:64, 0:64], r=['lamre_g', 'identf'], w=['pA'])
            TR(P, pA[0:64, 64:128], lamim_g[:], identf[0:64, 0:64], r=['lamim_g', 'identf'], w=['pA'])
            COPY(P, 'dve', lre[:], pA[0:64, 0:64], r=['pA'], w=['lre'])
            COPY(P, 'dve', lim[:], pA[0:64, 64:128], r=['pA'], w=['lim'])
            DMA(P, dtb[:], prm['log_dt'].partition_broadcast(64), w=['dtb'])
            ACTF(P, dtb[:], dtb[:], AF.Exp, r=['dtb'], w=['dtb'])
            TS(P, 'dve', lre[:], lre[:], -1e-4, ALU.min, r=['lre'], w=['lre'])
            V('dve', xr_[:], lre[:], dtb[:], ALU.mult, ['lre', 'dtb'], ['xr_'])
            V('dve', yi_[:], lim[:], dtb[:], ALU.mult, ['lim', 'dtb'], ['yi_'])
            ACTF(P, mag[:], xr_[:], AF.Exp, r=['xr_'], w=['mag'])

            def sin_shift(out, okey, shift):
                TS(P, 'dve', tA[:], yi_[:], 1.0 / TWO_PI, ALU.mult, shift / TWO_PI, ALU.add, r=['yi_'], w=['tA'])
                COPY(P, 'dve', tI[:], tA[:], r=['tA'], w=['tI'])
                COPY(P, 'dve', tB[:], tI[:], r=['tI'], w=['tB'])
                STT(P, tC[:], tB[:], -TWO_PI, yi_[:], ALU.mult, ALU.add, r=['tB', 'yi_'], w=['tC'])
                TS(P, 'dve', tC[:], tC[:], shift, ALU.add, 3.141592, ALU.min, r=['tC'], w=['tC'])
                TS(P, 'dve', tC[:], tC[:], -3.141592, ALU.max, r=['tC'], w=['tC'])
                ACTF(P, out, tC[:], AF.Sin, r=['tC'], w=[okey])

            sin_shift(sn[:], 'sn', 0.0)
            sin_shift(cs_[:], 'cs_', math.pi / 2.0)
            V('dve', abr[:], mag[:], cs_[:], ALU.mult, ['mag', 'cs_'], ['abr'])
            V('dve', abi[:], mag[:], sn[:], ALU.mult, ['mag', 'sn'], ['abi'])
            V('dve', den[:], lre[:], lre[:], ALU.mult, ['lre'], ['den'])
            V('dve', tA[:], lim[:], lim[:], ALU.mult, ['lim'], ['tA'])
            V('dve', den[:], den[:], tA[:], ALU.add, ['den', 'tA'], ['den'])
            P.add('dve', lambda e: e.reciprocal(out=den[:], in_=den[:]), r=['den'], w=['den'])
            TS(P, 'dve', nr[:], abr[:], -1.0, ALU.add, r=['abr'], w=['nr'])
            V('dve', tA[:], nr[:], lre[:], ALU.mult, ['nr', 'lre'], ['tA'])
            V('dve', tB[:], abi[:], lim[:], ALU.mult, ['abi', 'lim'], ['tB'])
            V('dve', tA[:], tA[:], tB[:], ALU.add, ['tA', 'tB'], ['tA'])
            V('dve', cfr[:], tA[:], den[:], ALU.mult, ['tA', 'den'], ['cfr'])
            V('dve', tA[:], abi[:], lre[:], ALU.mult, ['abi', 'lre'], ['tA'])
            V('dve', tB[:], nr[:], lim[:], ALU.mult, ['nr', 'lim'], ['tB'])
            V('dve', tA[:], tA[:], tB[:], ALU.subtract, ['tA', 'tB'], ['tA'])
            V('dve', cfi[:], tA[:], den[:], ALU.mult, ['tA', 'den'], ['cfi'])
            DMA(P, bre[:], prm['b_re'].rearrange("g p c -> p g c"), w=['bre'])
            DMA(P, bim[:], prm['b_im'].rearrange("g p c -> p g c"), w=['bim'])
            bc = lambda ap2: ap2.unsqueeze(2).broadcast_to([64, 64, 16])
            V('dve', bbr[:], bre[:], bc(cfr[:]), ALU.mult, ['bre', 'cfr'], ['bbr'])
            V('dve', t3[:], bim[:], bc(cfi[:]), ALU.mult, ['bim', 'cfi'], ['t3'])
            V('dve', bbr[:], bbr[:], t3[:], ALU.subtract, ['bbr', 't3'], ['bbr'])
            V('dve', bbi[:], bim[:], bc(cfr[:]), ALU.mult, ['bim', 'cfr'], ['bbi'])
            V('dve', t3[:], bre[:], bc(cfi[:]), ALU.mult, ['bre', 'cfi'], ['t3'])
            V('dve', bbi[:], bbi[:], t3[:], ALU.add, ['bbi', 't3'], ['bbi'])
            MEMSET(P, 'dve', Lr[:, 0, :], 1.0, w=['L'])
            MEMSET(P, 'dve', Li[:, 0, :], 0.0, r=['L'], w=['L'])
            for tau in range(1, 9):
                V('dve', tA[:], Lr[:, tau - 1, :], abr[:], ALU.mult, ['L', 'abr'], ['tA'])
                V('dve', tB[:], Li[:, tau - 1, :], abi[:], ALU.mult, ['L', 'abi'], ['tB'])
                V('dve', tC[:], Lr[:, tau - 1, :], abi[:], ALU.mult, ['L', 'abi'], ['tC'])
                V('dve', den[:], Li[:, tau - 1, :], abr[:], ALU.mult, ['L', 'abr'], ['den'])
                V('dve', Lr[:, tau, :], tA[:], tB[:], ALU.subtract, ['tA', 'tB', 'L'], ['L'])
                V('dve', Li[:, tau, :], tC[:], den[:], ALU.add, ['tC', 'den', 'L'], ['L'])
            for tau in range(8):
                lrb = bc(Lr[:, tau, :])
                lib = bc(Li[:, tau, :])
                V('dve', Wr[:, :, tau, :], bbr[:], lrb, ALU.mult, ['bbr', 'L'], ['Wr'])
                V('dve', t3[:], bbi[:], lib, ALU.mult, ['bbi', 'L'], ['t3'])
                V('dve', Wr[:, :, tau, :], Wr[:, :, tau, :], t3[:], ALU.subtract, ['Wr', 't3'], ['Wr'])
                V('dve', Wi[:, :, tau, :], bbi[:], lrb, ALU.mult, ['bbi', 'L'], ['Wi'])
                V('dve', t4[:], bbr[:], lib, ALU.mult, ['bbr', 'L'], ['t4'])
                V('dve', Wi[:, :, tau, :], Wi[:, :, tau, :], t4[:], ALU.add, ['Wi', 't4'], ['Wi'])
            for nm, dstT in (('c_re', CrT), ('c_im', CiT)):
                DMA(P, Cst[:], prm[nm].rearrange("g c p -> (g c) p").rearrange("(k r) p -> r k p", r=128), r=[], w=['Cst'])
                for k in range(8):
                    TR(P, pK[0:64, k * 128:(k + 1) * 128], Cst[:, k, :], identf[:], r=['Cst', 'identf'], w=['pK'])
                COPY(P, 'dve', dstT[:].rearrange("p g c -> p (g c)"), pK[0:64, :], r=['pK'], w=[nm])
            TS(P, 'dve', nCrT[:], CrT[:], -1.0, ALU.mult, r=['c_re'], w=['nCrT'])
            TS(P, 'dve', nCiT[:], CiT[:], -1.0, ALU.mult, r=['c_im'], w=['nCiT'])
            for g in range(64):
                MM(P, pK[:, g * 16:(g + 1) * 16], Wr[:, g, :, :].rearrange("p t c -> p (t c)"), CrT[:, g, :], True, False,
                   r=['Wr', 'c_re'], w=['pK'])
                MM(P, pK[:, g * 16:(g + 1) * 16], Wi[:, g, :, :].rearrange("p t c -> p (t c)"), nCiT[:, g, :], False, True,
                   r=['Wi', 'nCiT'], w=['pK'])
            COPY(P, 'dve', Ktsb[:].rearrange("p g c -> p (g c)"), pK[:, :], r=['pK'], w=['Ktsb'])
            DMA(P, K.Kd_d.rearrange("l c g o -> (l c) g o"), Ktsb[:], r=['Ktsb'], w=['Kd'])
            for gb4 in range(16):
                pp = pA if gb4 % 2 == 0 else pB
                pk_ = 'pA' if gb4 % 2 == 0 else 'pB'
                for gi in range(4):
                    g = gb4 * 4 + gi
                    for ri, W_ in enumerate((Wr, Wi)):
                        TR(P, pp[:, (gi * 2 + ri) * 64:(gi * 2 + ri + 1) * 64], W_[:, g, :, :].rearrange("p t c -> p (t c)"),
                           identf[0:64, 0:64], r=['Wr', 'Wi', 'identf'], w=[pk_])
                COPY(P, 'dve', Pm[:, gb4 * 4:(gb4 + 1) * 4, :, :].rearrange("p g r q -> p (g r q)"), pp[:, :], r=[pk_], w=['Pm'])
            for t in range(8):
                lrb = bc(Lr[:, t + 1, :])
                lib = bc(Li[:, t + 1, :])
                V('dve', t3[:], CrT[:], lrb, ALU.mult, ['c_re', 'L'], ['t3'])
                V('dve', t4[:], nCiT[:], lib, ALU.mult, ['nCiT', 'L'], ['t4'])
                V('dve', QmR[:, :, t, :], t3[:], t4[:], ALU.add, ['t3', 't4'], ['QmR'])
                V('dve', t3[:], nCrT[:], lib, ALU.mult, ['nCrT', 'L'], ['t3'])
                V('dve', t4[:], nCiT[:], lrb, ALU.mult, ['nCiT', 'L'], ['t4'])
                V('dve', QmI[:, :, t, :], t3[:], t4[:], ALU.add, ['t3', 't4'], ['QmI'])
            COPY(P, 'dve', ar8[:], Lr[:, 8, :], r=['L'], w=['ar8'])
            COPY(P, 'dve', ai8[:], Li[:, 8, :], r=['L'], w=['ai8'])
            TAP(P, nc, 'abr', abr[:], [64, 64], F32, ['abr'])
            TAP(P, nc, 'abi', abi[:], [64, 64], F32, ['abi'])
            TAP(P, nc, 'bbr', bbr[:], [64, 64, 16], F32, ['bbr'])
            TAP(P, nc, 'bbi', bbi[:], [64, 64, 16], F32, ['bbi'])
            TAP(P, nc, 'Lr', Lr[:], [64, 9, 64], F32, ['L'])
            TAP(P, nc, 'CrT', CrT[:], [64, 64, 16], F32, ['c_re'])
            TAP(P, nc, 'Ktsb', Ktsb[:], [128, 64, 16], F32, ['Ktsb'])
            TAP(P, nc, 'Pm', Pm[:], [128, 64, 2, 64], BF16, ['Pm'])
            TAP(P, nc, 'QmR', QmR[:], [64, 64, 8, 16], BF16, ['QmR'])
            TAP(P, nc, 'QmI', QmI[:], [64, 64, 8, 16], BF16, ['QmI'])
            P.barrier()
        if DBG.get('stop') == 's5setup':
            return

        with ExitStack() as T2:
            Tsb = T2.enter_context(nc.sbuf_tensor('TsbT%d' % li, [128, 64, 8, 16], F32))
            MEMSET(P, 'pool', Tsb[:], 0.0, w=['Tsb'])
            tkeys = []
            for tp in range(8):
                for lag in range(tp + 1):
                    key = 'Tsb_%d_%d' % (tp, lag)
                    tkeys.append(key)
                    DMA(P, Tsb[16 * tp:16 * tp + 16, :, 7 - tp + lag, :], K.Kd_d[lag].rearrange("c g o -> c g o"),
                        r=['Kd', 'Tsb'], w=[key])
            COPY(P, 'dve', Tb[:].rearrange("p g m -> p (g m)"), Tsb[:].rearrange("p g t c -> p (g t c)"), r=tkeys + ['Tsb'], w=['Tb'])
            TAP(P, nc, 'Tb', Tb[:], [128, 64, 128], BF16, ['Tb'])
            P.barrier()
        if DBG.get('stop') == 's5t2':
            return

        with ExitStack() as Sg:
            sg = lambda name, shape, dt: Sg.enter_context(nc.sbuf_tensor('%sS%d' % (name, li), shape, dt))
            Sel = sg('Sel', [128, 8, 8, 128], BF16)
            SelT = sg('SelT', [128, 8, 8, 128], BF16)
            DMA(P, Sel[:], K.c_Sel, w=['const'])
            DMA(P, SelT[:], K.c_SelT, w=['const'])
            useg = [sg('useg%d' % i, [128, 8, 512], BF16) for i in range(2)]
            Uall = sg('Uall', [128, 64, 64], BF16)
            Bcr = sg('Bcr', [64, 64, 64], F32)
            Bci = sg('Bci', [64, 64, 64], F32)
            Xb2 = sg('Xb2', [64, 2, 64, 64], BF16)
            X2 = [sg('X2_%d' % i, [64, 2, 64], F32) for i in range(2)]
            A2 = sg('A2', [64, 2, 64], F32)
            C2 = sg('C2', [64, 2, 64], F32)
            t1 = sg('t1', [64, 2, 64], F32)
            t2 = sg('t2', [64, 2, 64], F32)
            s1, s2, s3, s4 = sg('s1', [64, 64], F32), sg('s2', [64, 64], F32), sg('s3', [64, 64], F32), sg('s4', [64, 64], F32)
            Ysb = [sg('Ysb%d' % i, [128, 8, 64], BF16) for i in range(2)]
            yv = [sg('yv%d' % i, [128, 512], F32) for i in range(2)]
            yg = [sg('yg%d' % i, [128, 512], BF16) for i in range(2)]
            MEMSET(P, 'dve', X2[0][:], 0.0, w=['X2_0'])
            for hh in range(2):
                COPY(P, 'dve', A2[:, hh, :], ar8[:], w=['A2'])
                COPY(P, 'dve', C2[:, hh, :], ai8[:], w=['C2'])
            step = 0
            for seg in range(NC):
                ub = seg % 2
                uk = 'useg%d' % ub
                DMA(P, useg[ub][:], K.uT_d[:, seg * 512:(seg + 1) * 512].rearrange("(k p) t -> p k t", p=128), w=[uk])
                for k in range(8):
                    for gl in range(8):
                        for s in range(8):
                            MM(P, pA[:, gl * 64:(gl + 1) * 64], Sel[:, gl, s, :], useg[ub][:, k, s * 64:(s + 1) * 64], s == 0, s == 7,
                               r=['const', uk], w=['pA'])
                    COPY(P, 'act', Uall[:, k * 8:(k + 1) * 8, :].rearrange("p g n -> p (g n)"), pA[:, :], r=['pA'], w=['Uall%d' % k])
                    for gl in range(8):
                        g = 8 * k + gl
                        MM(P, pB[0:64, gl * 64:(gl + 1) * 64], Pm[:, g, 0, :], Uall[:, g, :], True, True, r=['Uall%d' % k], w=['pB'])
                        MM(P, pC[0:64, gl * 64:(gl + 1) * 64], Pm[:, g, 1, :], Uall[:, g, :], True, True, r=['Uall%d' % k], w=['pC'])
                    COPY(P, 'dve', Bcr[:, k * 8:(k + 1) * 8, :].rearrange("p g n -> p (g n)"), pB[0:64, :], r=['pB'], w=['Bcr%d' % k])
                    COPY(P, 'act', Bci[:, k * 8:(k + 1) * 8, :].rearrange("p g n -> p (g n)"), pC[0:64, :], r=['pC'], w=['Bci%d' % k])
                bk = ['Bcr%d' % k for k in range(8)] + ['Bci%d' % k for k in range(8)]
                for n in range(64):
                    cur = step % 2
                    nxt = 1 - cur
                    step += 1
                    kx, kxn = 'X2_%d' % cur, 'X2_%d' % nxt
                    COPY(P, 'act', Xb2[:, :, :, n], X2[cur][:], r=[kx], w=['Xb2'])
                    VN('dve', t1[:], A2[:], X2[cur][:], ALU.mult, [kx, 'A2'], ['t1'])
                    VN('dve', t2[:], C2[:], X2[cur][:], ALU.mult, [kx, 'C2'], ['t2'])
                    VN('dve', s1[:], t1[:, 0, :], t2[:, 1, :], ALU.subtract, ['t1', 't2'], ['s1'])
                    VN('dve', X2[nxt][:, 0, :], s1[:], Bcr[:, :, n], ALU.add, ['s1'] + bk, [kxn])
                    VN('dve', s3[:], t1[:, 1, :], t2[:, 0, :], ALU.add, ['t1', 't2'], ['s3'])
                    VN('dve', X2[nxt][:, 1, :], s3[:], Bci[:, :, n], ALU.add, ['s3'] + bk, [kxn])
                for k in range(8):
                    yb = k % 2
                    for gl in range(8):
                        g = 8 * k + gl
                        o_ = pD[:, gl * 64:(gl + 1) * 64]
                        MM(P, o_, Tb[:, g, :], Uall[:, g, :], True, False, r=['Uall%d' % k], w=['pD'])
                        MM(P, o_, QmR[:, g, :, :].rearrange("p t c -> p (t c)"), Xb2[:, 0, g, :], False, False, r=['Xb2'], w=['pD'])
                        MM(P, o_, QmI[:, g, :, :].rearrange("p t c -> p (t c)"), Xb2[:, 1, g, :], False, True, r=['Xb2'], w=['pD'])
                    COPY(P, 'act', Ysb[yb][:].rearrange("p g n -> p (g n)"), pD[:, :], r=['pD'], w=['Ysb%d' % yb])
                    pY = pB if k % 2 == 0 else pC
                    pyk = 'pB' if k % 2 == 0 else 'pC'
                    for t in range(8):
                        for gl in range(8):
                            MM(P, pY[:, t * 64:(t + 1) * 64], SelT[:, gl, t, :], Ysb[yb][:, gl, :], gl == 0, gl == 7,
                               r=['const', 'Ysb%d' % yb], w=[pyk])
                    STT(P, yv[yb][:], useg[ub][:, k, :], dcol[:, k:k + 1], pY[:, :], ALU.mult, ALU.add, r=[uk, pyk, 'dcol'], w=['yv%d' % yb])
                    ACTF(P, yg[yb][:], yv[yb][:], AF.Gelu_apprx_tanh, r=['yv%d' % yb], w=['yg%d' % yb])
                    DMA(P, K.ygT_d[k * 128:(k + 1) * 128, seg * 512:(seg + 1) * 512], yg[yb][:], r=['yg%d' % yb])
            P.barrier()
    if DBG.get('stop') == 's5scan':
        return

    with ExitStack() as Cx:
        sb = lambda name, shape, dt: Cx.enter_context(nc.sbuf_tensor('%sC%d' % (name, li), shape, dt))
        ps = lambda name, shape, dt: Cx.enter_context(nc.psum_tensor('%sC%d' % (name, li), shape, dt))
        wg = sb('wg', [128, 8, 2048], BF16)
        wo = sb('wo', [128, 8, D], BF16)
        stage = [sb('wst0', [128, 2048], F32), sb('wst1', [128, 2048], F32)]
        ygs = [sb('ygs%d' % i, [128, 8, 512], BF16) for i in range(2)]
        zss = [sb('zss%d' % i, [128, 8, 512], BF16) for i in range(2)]
        sig = [sb('sig%d' % i, [128, 512], F32) for i in range(2)]
        tga = [sb('tga%d' % i, [128, 512], F32) for i in range(2)]
        ozT = [sb('ozT%d' % i, [128, 8, 512], BF16) for i in range(2)]
        ht = [sb('ht%d' % i, [128, D], F32) for i in range(2)]
        hn = [sb('hn%d' % i, [128, D], F32) for i in range(2)]
        pa = [ps('pa%d' % i, [128, 512], F32) for i in range(2)]
        pb = [ps('pb%d' % i, [128, 512], F32) for i in range(2)]
        pm = [ps('pm%d' % i, [128, 512], F32) for i in range(4)]
        gfb = None
        if final_g is not None:
            gfb = sb('gfb', [128, D], F32)
            DMA(P, gfb[:], final_g.partition_broadcast(128), w=['gb'])
        load_weight_bf16(P, K, wg, prm['w_glu'], 2048, stage, 'wglu')
        load_weight_bf16(P, K, wo, prm['w_out'], D, stage, 'wout')
        for seg in range(NC):
            sbi = seg % 2
            cols = slice(seg * 512, (seg + 1) * 512)
            DMA(P, ygs[sbi][:], K.ygT_d[:, cols].rearrange("(k p) t -> p k t", p=128), w=['ygs%d' % sbi])
            DMA(P, zss[sbi][:], K.zsT_d[:, cols].rearrange("(k p) t -> p k t", p=128), w=['zss%d' % sbi])
            for m in range(8):
                x = m % 2
                for kc in range(8):
                    MM(P, pa[x][:], wg[:, kc, m * 128:(m + 1) * 128], ygs[sbi][:, kc, :], kc == 0, kc == 7,
                       r=['ygs%d' % sbi, 'wglu_wb'], w=['pa%d' % x])
                for kc in range(8):
                    MM(P, pb[x][:], wg[:, kc, 1024 + m * 128:1024 + (m + 1) * 128], ygs[sbi][:, kc, :], kc == 0, kc == 7,
                       r=['ygs%d' % sbi, 'wglu_wb'], w=['pb%d' % x])
                ACTF(P, sig[x][:], pb[x][:], AF.Sigmoid, r=['pb%d' % x], w=['sig%d' % x])
                TT(P, 'dve', tga[x][:], pa[x][:], sig[x][:], ALU.mult, r=['pa%d' % x, 'sig%d' % x], w=['tga%d' % x])
                TT(P, 'pool', ozT[sbi][:, m, :], tga[x][:], zss[sbi][:, m, :], ALU.mult, r=['tga%d' % x, 'zss%d' % sbi], w=['ozT%d_%d' % (sbi, m)])
            ozk = ['ozT%d_%d' % (sbi, m) for m in range(8)]
            for tt in range(4):
                b = (4 * seg + tt) % 2
                DMA(P, ht[b][0:64, :], hv_src[seg, 2 * tt], w=['ht%da' % b])
                DMA(P, ht[b][64:128, :], hv_src[seg, 2 * tt + 1], w=['ht%db' % b])
                for half in range(2):
                    pi = (2 * tt + half) % 4
                    for kc in range(8):
                        MM(P, pm[pi][:], ozT[sbi][:, kc, tt * 128:(tt + 1) * 128], wo[:, kc, half * 512:(half + 1) * 512], kc == 0, kc == 7,
                           r=ozk + ['wout_wb'], w=['pm%d' % pi])
                    TT(P, 'dve', hn[b][:, half * 512:(half + 1) * 512], pm[pi][:], ht[b][:, half * 512:(half + 1) * 512], ALU.add,
                       r=['pm%d' % pi, 'ht%da' % b, 'ht%db' % b], w=['hn%d_%d' % (b, half)])
                hk = ['hn%d_0' % b, 'hn%d_1' % b]
                dsts = [hv_dst[seg, 2 * tt], hv_dst[seg, 2 * tt + 1]]
                prs = [slice(0, 64), slice(64, 128)]
                if final_g is None:
                    for d_, pr in zip(dsts, prs):
                        DMA(P, d_, hn[b][pr, :], r=hk)
                else:
                    final_norm_store(P, K, hn[b], hk, gfb, ht[b], ['ht%da' % b, 'ht%db' % b], dsts, prs)
        P.barrier()


def build(layers=(0, 1, 2, 3), with_final=True):
    nc = bass.Bass("TRN2", target_bir_lowering=False)
    K = Ctx()
    x = nc.dram_tensor("x", [S, D], F32, kind="ExternalInput").ap()
    y = nc.dram_tensor("y", [S, D], F32, kind="ExternalOutput").ap()
    prm = {}
    for li in layers:
        prm[li] = {}
        for nm, shp in (NSA_PARAMS if li % 2 == 0 else S5_PARAMS):
            prm[li][nm] = nc.dram_tensor("l%d_%s" % (li, nm), shp, F32, kind="ExternalInput").ap()
    fng = nc.dram_tensor("final_norm", [D], F32, kind="ExternalInput").ap()
    for nm, shp, dt in CONST_SPECS:
        setattr(K, 'c_' + nm, nc.dram_tensor("c_" + nm, shp, dt, kind="ExternalInput").ap())
    scr = lambda nm, shp, dt: nc.dram_tensor("scr_" + nm, shp, dt, kind=("ExternalOutput" if nm in DBG.get('dump', ()) else "Internal")).ap()
    K.qT_d = scr('qT', [1024, S], BF16)
    K.kcT_d = scr('kcT', [256, S], BF16)
    K.vcT_d = scr('vcT', [256, S], BF16)
    K.ksT_d = scr('ksT', [256, S], BF16)
    K.kwT_d = scr('kwT', [256, S], BF16)
    K.vs_d = scr('vs', [S, 256], BF16)
    K.vw_d = scr('vw', [S, 256], BF16)
    K.gates_d = scr('gates', [S, 48], F32)
    K.zs_d = scr('zs', [S, 1024], BF16)
    K.oz_d = scr('oz', [S, 1024], BF16)
    K.uT_d = scr('uT', [1024, S], BF16)
    K.zsT_d = scr('zsT', [1024, S], BF16)
    K.ygT_d = scr('ygT', [1024, S], BF16)
    K.Kd_d = scr('Kd', [8, 16, 64, 16], F32)
    P = Prog(nc)
    with ExitStack() as G:
        gs = lambda name, shape, dt: G.enter_context(nc.sbuf_tensor(name, shape, dt))
        K.ident_b = gs('ident_b', [128, 128], BF16)
        K.ident_f = gs('ident_f', [128, 128], F32)
        K.junkf = gs('junkf', [128, D], F32)
        K.ss = gs('ss', [128, 1], F32)
        K.sq = gs('sq', [128, 1], F32)
        K.rs = gs('rs', [128, 1], F32)
        DMA(P, K.ident_b[:], K.c_ident_b, w=['const'])
        DMA(P, K.ident_f[:], K.c_ident_f, w=['identf'])
        P.barrier()
        hsrc = x
        for li in layers:
            fg = fng if (with_final and li == layers[-1]) else None
            if li % 2 == 0:
                nsa_layer(P, nc, K, li, prm[li], hsrc, y, fg)
            else:
                s5_layer(P, nc, K, li, prm[li], hsrc, y, fg)
            hsrc = y
        run_prog(nc, P)
    return nc


ALL_INPUT_NAMES = (
    'x',
    'l0_norm',
    'l0_w_in',
    'l0_cmp_k_pe',
    'l0_cmp_k_w1',
    'l0_cmp_k_w2',
    'l0_cmp_v_pe',
    'l0_cmp_v_w1',
    'l0_cmp_v_w2',
    'l0_w_out',
    'l1_norm',
    'l1_w_in',
    'l1_log_dt',
    'l1_lambda_re',
    'l1_lambda_im',
    'l1_b_re',
    'l1_b_im',
    'l1_c_re',
    'l1_c_im',
    'l1_d',
    'l1_w_glu',
    'l1_w_out',
    'l2_norm',
    'l2_w_in',
    'l2_cmp_k_pe',
    'l2_cmp_k_w1',
    'l2_cmp_k_w2',
    'l2_cmp_v_pe',
    'l2_cmp_v_w1',
    'l2_cmp_v_w2',
    'l2_w_out',
    'l3_norm',
    'l3_w_in',
    'l3_log_dt',
    'l3_lambda_re',
    'l3_lambda_im',
    'l3_b_re',
    'l3_b_im',
    'l3_c_re',
    'l3_c_im',
    'l3_d',
    'l3_w_glu',
    'l3_w_out',
    'final_norm',
)


_NC_CACHE = {}


def make_in_map(inputs, b, layers=(0, 1, 2, 3)):
    C = host_constants()
    m = {'x': np.ascontiguousarray(inputs['x'][b], dtype=np.float32)}
    for li in layers:
        for nm, shp in (NSA_PARAMS if li % 2 == 0 else S5_PARAMS):
            key = 'l%d_%s' % (li, nm)
            m[key] = np.ascontiguousarray(inputs[key], dtype=np.float32)
    m['final_norm'] = np.ascontiguousarray(inputs['final_norm'], dtype=np.float32)
    for nm, shp, dt in CONST_SPECS:
        m['c_' + nm] = C[nm]
    return m


def kernel(**inputs):
    inputs = {k: np.asarray(inputs[k]) for k in ALL_INPUT_NAMES}
    if 'full' not in _NC_CACHE:
        _NC_CACHE['full'] = build()
    nc = _NC_CACHE['full']
    in_maps = [make_in_map(inputs, c % 4) for c in range(8)]
    res = run_bass_kernel_spmd(nc, in_maps, core_ids=list(range(8)))
    out = np.stack([np.asarray(res.results[b]['y'], dtype=np.float32) for b in range(4)], axis=0)
    return out
```

```python
import math
from contextlib import ExitStack
import numpy as np
import ml_dtypes
import concourse.bass as bass
import concourse.mybir as mybir
from concourse.bass_utils import run_bass_kernel_spmd

F32 = mybir.dt.float32
BF16 = mybir.dt.bfloat16
I32 = mybir.dt.int32
AF = mybir.ActivationFunctionType
ALU = mybir.AluOpType
AX = mybir.AxisListType
NPBF = ml_dtypes.bfloat16

S = 8192
D = 1024
NH = 16
DH = 64
NG = 4
NT = S // 128
NC = S // 512
NSA_IN = 3632
EPS = 1e-6
MASKV = -30000.0

ENGS = ['pe', 'act', 'dve', 'pool', 'sp']
KRING = 8
SAME_ENGINE_SYNC = {'act': True, 'dve': True, 'pool': True, 'pe': False, 'sp': False}


class Prog:
    def __init__(self, nc):
        self.nc = nc
        self.q = {e: [] for e in ENGS}
        self.ncomp = {e: 0 for e in ENGS}
        self.ndma = {e: 0 for e in ENGS}
        self.lastw = {}
        self.readers = {}

    def add(self, eng, fn, r=(), w=(), dma=False, nosync=False):
        deps = {}
        dd = set()

        def addtok(t):
            if t is None:
                return
            if t[0] == 'c':
                if deps.get(t[1], 0) < t[2]:
                    deps[t[1]] = t[2]
            else:
                dd.add(t)

        for k in r:
            addtok(self.lastw.get(k))
        for k in w:
            addtok(self.lastw.get(k))
            rd = self.readers.get(k)
            if rd:
                for e2, n2 in rd[0].items():
                    addtok(('c', e2, n2))
                for t in rd[1]:
                    addtok(t)
        if dma:
            tok = ('d', eng, self.ndma[eng])
            self.ndma[eng] += 1
        else:
            self.ncomp[eng] += 1
            tok = ('c', eng, self.ncomp[eng])
        if nosync:
            deps.pop(eng, None)
        self.q[eng].append(('op', fn, deps, dd, tok))
        for k in w:
            self.lastw[k] = tok
            self.readers[k] = [{}, set()]
        for k in r:
            rd = self.readers.setdefault(k, [{}, set()])
            if tok[0] == 'c':
                if rd[0].get(tok[1], 0) < tok[2]:
                    rd[0][tok[1]] = tok[2]
            else:
                rd[1].add(tok)
        return tok

    def barrier(self):
        sc = dict(self.ncomp)
        sd = dict(self.ndma)
        for e in ENGS:
            self.q[e].append(('bar', sc, sd))
        self.lastw = {}
        self.readers = {}

    def run_engine(self, ename, eng, semc, semd):
        waited_c = {}
        waited_d = {}

        def wait_c(f, n):
            if n <= 0 or waited_c.get(f, 0) >= n:
                return
            eng.wait_ge(semc[f], n)
            waited_c[f] = n

        def wait_d(qn, i):
            if i < 0:
                return
            slot = i % KRING
            tgt = 16 * (i // KRING + 1)
            if waited_d.get((qn, slot), 0) >= tgt:
                return
            eng.wait_ge(semd[qn][slot], tgt)
            waited_d[(qn, slot)] = tgt

        for item in self.q[ename]:
            if item[0] == 'bar':
                _, sc, sd = item
                for f in ENGS:
                    if f == ename and not SAME_ENGINE_SYNC[ename]:
                        continue
                    wait_c(f, sc[f])
                for qn in ENGS:
                    n = sd[qn]
                    for i in range(max(0, n - KRING), n):
                        wait_d(qn, i)
                continue
            _, fn, deps, dd, tok = item
            for f, n in deps.items():
                if f == ename and not SAME_ENGINE_SYNC[ename]:
                    continue
                wait_c(f, n)
            for t in dd:
                wait_d(t[1], t[2])
            if tok[0] == 'd':
                i = tok[2]
                if i >= KRING:
                    wait_d(ename, i - KRING)
                ins = fn(eng)
                ins.then_inc(semd[ename][i % KRING], 16)
            else:
                ins = fn(eng)
                ins.then_inc(semc[ename], 1)
        n = self.ndma[ename]
        for i in range(max(0, n - KRING), n):
            wait_d(ename, i)


def run_prog(nc, prog):
    with ExitStack() as st:
        semc = {e: st.enter_context(nc.semaphore('c_' + e)) for e in ENGS}
        semd = {e: [st.enter_context(nc.semaphore('d_%s_%d' % (e, i))) for i in range(KRING)] for e in ENGS}
        block = st.enter_context(nc.Block())

        @block.tensor
        def _(eng):
            prog.run_engine('pe', eng, semc, semd)

        @block.scalar
        def _(eng):
            prog.run_engine('act', eng, semc, semd)

        @block.vector
        def _(eng):
            prog.run_engine('dve', eng, semc, semd)

        @block.gpsimd
        def _(eng):
            prog.run_engine('pool', eng, semc, semd)

        @block.sync
        def _(eng):
            prog.run_engine('sp', eng, semc, semd)


def DMA(P, out, in_, r=(), w=(), q='sp', slow=False):
    if slow:
        P.add(q, lambda e: e.dma_start(out=out, in_=in_, allow_slow_non_contiguous=True), r=r, w=w, dma=True)
    else:
        P.add(q, lambda e: e.dma_start(out=out, in_=in_), r=r, w=w, dma=True)


def MM(P, out, lhsT, rhs, start, stop, r=(), w=(), skip=False):
    P.add('pe', lambda e: e.matmul(out, lhsT=lhsT, rhs=rhs, start=start, stop=stop, skip_group_check=skip), r=r, w=w)


def TR(P, out, in_, ident, r=(), w=()):
    P.add('pe', lambda e: e.transpose(out=out, in_=in_, identity=ident), r=r, w=w)


def ACTF(P, out, in_, func, r=(), w=(), scale=None, bias=None, accum=None):
    kw = {}
    if scale is not None:
        kw['scale'] = scale
    if bias is not None:
        kw['bias'] = bias
    if accum is not None:
        kw['accum_out'] = accum
    P.add('act', lambda e: e.activation(out=out, in_=in_, func=func, **kw), r=r, w=w)


def COPY(P, eng, out, in_, r=(), w=()):
    if eng == 'act':
        P.add('act', lambda e: e.copy(out=out, in_=in_), r=r, w=w)
    else:
        P.add(eng, lambda e: e.tensor_copy(out=out, in_=in_), r=r, w=w)


def TT(P, eng, out, in0, in1, op, r=(), w=(), nosync=False):
    P.add(eng, lambda e: e.tensor_tensor(out=out, in0=in0, in1=in1, op=op), r=r, w=w, nosync=nosync)


def TS(P, eng, out, in0, s1, op0, s2=None, op1=None, r=(), w=()):
    if op1 is None:
        P.add(eng, lambda e: e.tensor_scalar(out=out, in0=in0, scalar1=s1, scalar2=None, op0=op0), r=r, w=w)
    else:
        P.add(eng, lambda e: e.tensor_scalar(out=out, in0=in0, scalar1=s1, scalar2=s2, op0=op0, op1=op1), r=r, w=w)


def STT(P, out, in0, scalar, in1, op0, op1, r=(), w=(), accum=None):
    if accum is None:
        P.add('dve', lambda e: e.scalar_tensor_tensor(out=out, in0=in0, scalar=scalar, in1=in1, op0=op0, op1=op1), r=r, w=w)
    else:
        P.add('dve', lambda e: e.scalar_tensor_tensor(out=out, in0=in0, scalar=scalar, in1=in1, op0=op0, op1=op1, accum_out=accum), r=r, w=w)


def MEMSET(P, eng, ap, val, r=(), w=()):
    P.add(eng, lambda e: e.memset(ap, val), r=r, w=w)


def _bf(x):
    return np.asarray(x, dtype=np.float32).astype(NPBF)


def _split3(c):
    c = np.asarray(c, dtype=np.float64)
    hi = _bf(c)
    r1 = c - hi.astype(np.float64)
    mid = _bf(r1)
    r2 = r1 - mid.astype(np.float64)
    lo = _bf(r2)
    return hi, mid, lo


_CONST_CACHE = {}


def host_constants():
    if _CONST_CACHE:
        return _CONST_CACHE
    C = {}
    C['ident_b'] = np.eye(128, dtype=np.float32).astype(NPBF)
    C['ident_f'] = np.eye(128, dtype=np.float32)
    hh = np.arange(1, NH + 1, dtype=np.float64)
    slopes = np.exp2(-8.0 * hh / NH).astype(np.float32).astype(np.float64)
    s_hi = _bf(slopes)
    s_lo = _bf(slopes - s_hi.astype(np.float64))
    sp = s_hi.astype(np.float64) + s_lo.astype(np.float64)
    t = np.arange(S, dtype=np.float64)
    QA = np.zeros((NH, 7, S), dtype=NPBF)
    for h in range(NH):
        QA[h, 0, :] = _bf(64.0 * s_hi[h].astype(np.float64))
        QA[h, 1, :] = _bf(64.0 * s_lo[h].astype(np.float64))
        QA[h, 2, :] = s_hi[h]
        QA[h, 3, :] = s_lo[h]
        c = np.float32(-(sp[h] * t)).astype(np.float64)
        a, b, cc = _split3(c)
        QA[h, 4, :] = a
        QA[h, 5, :] = b
        QA[h, 6, :] = cc
    C['QA'] = QA
    KA = np.zeros((7, S), dtype=NPBF)
    pos = np.arange(S)
    KA[0] = _bf(pos // 64)
    KA[1] = _bf(pos // 64)
    KA[2] = _bf(pos % 64)
    KA[3] = _bf(pos % 64)
    KA[4:7] = _bf(1.0)
    C['KA'] = KA
    KAc = np.zeros((7, 512), dtype=NPBF)
    n = np.arange(511)
    KAc[0, :511] = _bf(n // 4)
    KAc[1, :511] = _bf(n // 4)
    KAc[2, :511] = _bf(16.0 * (n % 4) + 15.5)
    KAc[3, :511] = _bf(16.0 * (n % 4) + 15.5)
    KAc[4:7, :511] = _bf(1.0)
    C['KAc'] = KAc
    OH = np.zeros((32, S), dtype=np.float32)
    OH[(np.arange(S) // 64) % 32, np.arange(S)] = 1.0
    C['OH'] = OH.astype(NPBF)
    OV = np.zeros((128, 4, 128), dtype=np.float32)
    for nn in range(511):
        cs = 16 * nn
        ce = cs + 31
        for j in range(128):
            if cs <= 64 * j + 63 and ce >= 64 * j:
                OV[nn % 128, nn // 128, j] = 1.0
    C['OV'] = OV.astype(NPBF)
    k = np.arange(128)[:, None, None]
    r = np.arange(4)[None, :, None]
    q = np.arange(512)[None, None, :]
    C['causal'] = np.where(128 * r + k > q, MASKV, 0.0).astype(np.float32).astype(NPBF)
    r8 = np.arange(8)[None, :, None]
    dwin = q - k + 512 - 128 * r8
    C['band'] = np.where((dwin >= 0) & (dwin < 512), 0.0, MASKV).astype(np.float32).astype(NPBF)
    r5 = np.array([0, 512, 1024, 1536, 2048])[None, :, None]
    C['cmpmask'] = np.where(16 * k + 31 - r5 > q, MASKV, 0.0).astype(np.float32).astype(NPBF)
    FT = np.zeros((128, 256), dtype=np.float32)
    for p in range(128):
        jr = 1 if p >= 64 else 0
        for c in range(255):
            jj = c - 127
            if jj == jr or jj == jr - 1:
                FT[p, c] = 1e9
            elif jj > jr:
                FT[p, c] = -1e9
    C['FT'] = FT
    Sel = np.zeros((128, 8, 8, 128), dtype=np.float32)
    SelT = np.zeros((128, 8, 8, 128), dtype=np.float32)
    for gl in range(8):
        for s in range(8):
            for c in range(16):
                Sel[16 * gl + c, gl, s, 16 * (7 - s) + c] = 1.0
                SelT[16 * s + c, gl, s, 16 * gl + c] = 1.0
    C['Sel'] = Sel.astype(NPBF)
    C['SelT'] = SelT.astype(NPBF)
    _CONST_CACHE.update(C)
    return C


CONST_SPECS = [
    ('ident_b', [128, 128], BF16), ('ident_f', [128, 128], F32), ('QA', [NH, 7, S], BF16),
    ('KA', [7, S], BF16), ('KAc', [7, 512], BF16), ('OH', [32, S], BF16),
    ('OV', [128, 4, 128], BF16), ('causal', [128, 4, 512], BF16), ('band', [128, 8, 512], BF16),
    ('cmpmask', [128, 5, 512], BF16), ('FT', [128, 256], F32),
    ('Sel', [128, 8, 8, 128], BF16), ('SelT', [128, 8, 8, 128], BF16),
]

NSA_PARAMS = [('norm', [D]), ('w_in', [D, NSA_IN]), ('cmp_k_pe', [32, 64]), ('cmp_k_w1', [2048, 256]),
              ('cmp_k_w2', [256, 64]), ('cmp_v_pe', [32, 64]), ('cmp_v_w1', [2048, 256]),
              ('cmp_v_w2', [256, 64]), ('w_out', [D, D])]
S5_PARAMS = [('norm', [D]), ('w_in', [D, 2048]), ('log_dt', [64]), ('lambda_re', [64, 64]),
             ('lambda_im', [64, 64]), ('b_re', [64, 64, 16]), ('b_im', [64, 64, 16]),
             ('c_re', [64, 16, 64]), ('c_im', [64, 16, 64]), ('d', [64, 16]),
             ('w_glu', [D, 2048]), ('w_out', [D, D])]


class Ctx:
    pass


DBG = {}


def TAP(P, nc, name, ap, shape, dt, r=()):
    if not DBG.get('taps'):
        return
    t = nc.dram_tensor('dbg_' + name, shape, dt, kind='ExternalOutput').ap()
    DMA(P, t, ap, r=list(r))


def load_weight_bf16(P, K, wb, wdram, ncols, stage, tag):
    for kc in range(8):
        sb = stage[kc % 2]
        key = 'wstage%d' % (kc % 2)
        DMA(P, sb[:, 0:ncols], wdram[kc * 128:(kc + 1) * 128, :], w=[key])
        eng = 'dve' if kc % 2 == 0 else 'pool'
        COPY(P, eng, wb[:, kc, :], sb[:, 0:ncols], r=[key], w=['%s_wb' % tag])


def rmsnorm_tile(P, K, xt, gb, xnb, tg, rkeys, wkey):
    STT(P, K.junkf[:], xt, 1.0, xt, ALU.mult, ALU.mult, r=rkeys, w=['junkf', 'ss' + tg], accum=K.ss[:])
    ACTF(P, K.sq[:], K.ss[:], AF.Sqrt, r=['ss' + tg], w=['sq' + tg], scale=1.0 / D, bias=EPS)
    P.add('dve', lambda e: e.reciprocal(out=K.rs[:], in_=K.sq[:]), r=['sq' + tg], w=['rs' + tg])
    STT(P, xnb, xt, K.rs[:], gb, ALU.mult, ALU.mult, r=rkeys + ['rs' + tg, 'gb'], w=[wkey])


def transpose_tile(P, K, src_b, dst, skey, dkey, evac_eng):
    for kc in range(8):
        TR(P, K.pT[:, kc, :], src_b[:, kc * 128:(kc + 1) * 128], K.ident_b[:], r=[skey, 'const'], w=['pT'])
    COPY(P, evac_eng, dst, K.pT[:], r=['pT'], w=[dkey])


def nsa_layer(P, nc, K, li, prm, hsrc, hdst, final_g):
    QOFF, KCOFF, VCOFF, KSOFF, VSOFF, KWOFF, VWOFF, GOFF, ZOFF = 0, 1024, 1280, 1536, 1792, 2048, 2304, 2560, 2608
    ident = K.ident_b
    with ExitStack() as L:
        sbL = lambda name, shape, dt: L.enter_context(nc.sbuf_tensor('%s_l%d' % (name, li), shape, dt))
        kcmpT = sbL('kcmpT', [128, NG, 512], BF16)
        vcmpa = sbL('vcmpa', [128, 4, NG, 65], BF16)

        with ExitStack() as A:
            sb = lambda name, shape, dt: A.enter_context(nc.sbuf_tensor('%sA%d' % (name, li), shape, dt))
            ps = lambda name, shape, dt: A.enter_context(nc.psum_tensor('%sA%d' % (name, li), shape, dt))
            wb = sb('wb', [128, 8, NSA_IN], BF16)
            stage = [sb('wst0', [128, NSA_IN], F32), sb('wst1', [128, NSA_IN], F32)]
            gb = sb('gb', [128, D], F32)
            xt = [sb('xt0', [128, D], F32), sb('xt1', [128, D], F32)]
            xnb = [sb('xnb0', [128, D], BF16), sb('xnb1', [128, D], BF16)]
            xT4 = [sb('xT40', [128, 8, 512], BF16), sb('xT41', [128, 8, 512], BF16)]
            fo = [sb('fo%d' % i, [128, 512], BF16) for i in range(4)]
            go = [sb('go%d' % i, [128, 48], F32) for i in range(2)]
            K.pT = ps('pT', [128, 8, 128], BF16)
            pm = [ps('pm%d' % i, [128, 512], F32) for i in range(4)]

            load_weight_bf16(P, K, wb, prm['w_in'], NSA_IN, stage, 'win')
            DMA(P, gb[:], prm['norm'].partition_broadcast(128), w=['gb'])
            nmm = [0]
            nfo = [0]

            def mm_group(out_ps, pkey, lhs_fn, rhs_fn, rk):
                for kc in range(8):
                    MM(P, out_ps, lhs_fn(kc), rhs_fn(kc), kc == 0, kc == 7, r=rk + ['win_wb'], w=[pkey])

            for c in range(NC):
                cb = c % 2
                for tt in range(4):
                    i = 4 * c + tt
                    b = i % 2
                    DMA(P, xt[b][:], hsrc[i * 128:(i + 1) * 128, :], w=['xt%d' % b])
                    rmsnorm_tile(P, K, xt[b][:], gb[:], xnb[b][:], 'A', ['xt%d' % b], 'xnb%d' % b)
                    transpose_tile(P, K, xnb[b], xT4[cb][:, :, tt * 128:(tt + 1) * 128], 'xnb%d' % b, 'xT4%d' % cb,
                                   'act' if tt % 2 == 0 else 'dve')
                xk = 'xT4%d' % cb
                fm = [(m, 'q') for m in range(8)] + [(0, 'kc'), (1, 'kc'), (0, 'vc'), (1, 'vc'), (0, 'ks'), (1, 'ks'), (0, 'kw'), (1, 'kw')]
                for (m, kind) in fm:
                    off = {'q': QOFF, 'kc': KCOFF, 'vc': VCOFF, 'ks': KSOFF, 'kw': KWOFF}[kind] + m * 128
                    dst = {'q': K.qT_d, 'kc': K.kcT_d, 'vc': K.vcT_d, 'ks': K.ksT_d, 'kw': K.kwT_d}[kind]
                    pi = nmm[0] % 4
                    nmm[0] += 1
                    mm_group(pm[pi][:], 'pm%d' % pi, lambda kc, off=off: wb[:, kc, off:off + 128],
                             lambda kc: xT4[cb][:, kc, :], [xk])
                    fi = nfo[0] % 4
                    nfo[0] += 1
                    if kind == 'q':
                        ACTF(P, fo[fi][:], pm[pi][:], AF.Copy, r=['pm%d' % pi], w=['fo%d' % fi], scale=0.125)
                    else:
                        COPY(P, 'dve', fo[fi][:], pm[pi][:], r=['pm%d' % pi], w=['fo%d' % fi])
                    DMA(P, dst[m * 128:(m + 1) * 128, c * 512:(c + 1) * 512], fo[fi][:], r=['fo%d' % fi], q='pool')
                for tt in range(4):
                    i = 4 * c + tt
                    rows = slice(i * 128, (i + 1) * 128)
                    lhs = lambda kc, tt=tt: xT4[cb][:, kc, tt * 128:(tt + 1) * 128]
                    pi = nmm[0] % 4
                    nmm[0] += 1
                    for kc in range(8):
                        MM(P, pm[pi][:, 0:256], lhs(kc), wb[:, kc, VSOFF:VSOFF + 256], kc == 0, kc == 7, r=[xk, 'win_wb'], w=['pm%d' % pi])
                    for kc in range(8):
                        MM(P, pm[pi][:, 256:512], lhs(kc), wb[:, kc, VWOFF:VWOFF + 256], kc == 0, kc == 7, r=[xk, 'win_wb'], w=['pm%d' % pi])
                    fi = nfo[0] % 4
                    nfo[0] += 1
                    COPY(P, 'dve', fo[fi][:], pm[pi][:], r=['pm%d' % pi], w=['fo%d' % fi])
                    DMA(P, K.vs_d[rows, :], fo[fi][:, 0:256], r=['fo%d' % fi], q='pool')
                    DMA(P, K.vw_d[rows, :], fo[fi][:, 256:512], r=['fo%d' % fi], q='pool')
                    pi = nmm[0] % 4
                    nmm[0] += 1
                    for kc in range(8):
                        MM(P, pm[pi][:, 0:48], lhs(kc), wb[:, kc, GOFF:GOFF + 48], kc == 0, kc == 7, r=[xk, 'win_wb'], w=['pm%d' % pi])
                    gi = i % 2
                    ACTF(P, go[gi][:], pm[pi][:, 0:48], AF.Sigmoid, r=['pm%d' % pi], w=['go%d' % gi])
                    DMA(P, K.gates_d[rows, :], go[gi][:], r=['go%d' % gi], q='pool')
                    for j in range(2):
                        pi = nmm[0] % 4
                        nmm[0] += 1
                        mm_group(pm[pi][:], 'pm%d' % pi, lhs, lambda kc, j=j: wb[:, kc, ZOFF + 512 * j:ZOFF + 512 * (j + 1)], [xk])
                        fi = nfo[0] % 4
                        nfo[0] += 1
                        ACTF(P, fo[fi][:], pm[pi][:], AF.Silu, r=['pm%d' % pi], w=['fo%d' % fi])
                        DMA(P, K.zs_d[rows, 512 * j:512 * (j + 1)], fo[fi][:], r=['fo%d' % fi], q='pool')
            P.barrier()

        with ExitStack() as B:
            sb = lambda name, shape, dt: B.enter_context(nc.sbuf_tensor('%sB%d' % (name, li), shape, dt))
            ps = lambda name, shape, dt: B.enter_context(nc.psum_tensor('%sB%d' % (name, li), shape, dt))
            w1st = sb('w1st', [64, 32, 256], F32)
            w1b = [sb('w1b0', [64, 32, 256], BF16), sb('w1b1', [64, 32, 256], BF16)]
            w2st = sb('w2st', [128, 2, 64], F32)
            w2b = [sb('w2b0', [128, 2, 96], BF16), sb('w2b1', [128, 2, 64], BF16)]
            pest = sb('pest', [64, 32], F32)
            peb = [sb('peb0', [64, 32], BF16), sb('peb1', [64, 32], BF16)]
            hb = [sb('hb0', [128, 2], F32), sb('hb1', [128, 2], F32)]
            xin = [sb('xin0', [64, S], BF16), sb('xin1', [64, S], BF16)]
            hid = sb('hid', [128, 2, 512], BF16)
            ph = [ps('ph0', [128, 512], F32), ps('ph1', [128, 512], F32)]
            pb = ps('pb', [128, 512], F32)
            po = ps('po', [128, 512], F32)

            MEMSET(P, 'pool', kcmpT[:], 0.0, w=['kcmpT'])
            MEMSET(P, 'pool', vcmpa[:], 0.0, w=['vcmpa'])
            MEMSET(P, 'pool', hid[:], 0.0, w=['hid'])
            for kv in range(2):
                nm = 'cmp_k' if kv == 0 else 'cmp_v'
                DMA(P, w1st[:], prm[nm + '_w1'].rearrange("(l d) h -> d l h", d=64), w=['w1st'])
                COPY(P, 'dve', w1b[kv][:], w1st[:], r=['w1st'], w=['w1b%d' % kv])
                DMA(P, w2st[:], prm[nm + '_w2'].rearrange("(m p) d -> p m d", p=128), w=['w2st'])
                if kv == 0:
                    MEMSET(P, 'pool', w2b[0][:], 0.0, w=['w2b0'])
                    COPY(P, 'dve', w2b[0][:, :, 32:96], w2st[:], r=['w2st', 'w2b0'], w=['w2b0'])
                else:
                    COPY(P, 'dve', w2b[1][:], w2st[:], r=['w2st'], w=['w2b1'])
                DMA(P, pest[:], prm[nm + '_pe'].rearrange("l d -> d l"), w=['pest'], slow=True)
                COPY(P, 'dve', peb[kv][:], pest[:], r=['pest'], w=['peb%d' % kv])
                for m in range(2):
                    for l in range(32):
                        MM(P, pb[:, m:m + 1], w1b[kv][:, l, m * 128:(m + 1) * 128], peb[kv][:, l:l + 1], l == 0, l == 31,
                           r=['w1b%d' % kv, 'peb%d' % kv], w=['pb'])
                COPY(P, 'dve', hb[kv][:], pb[:, 0:2], r=['pb'], w=['hb%d' % kv])
            DMA(P, kcmpT[96:103, :, :], K.c_KAc.unsqueeze(1).broadcast_to([7, NG, 512]), r=['kcmpT'], w=['kcmpT'])
            it = 0
            for g in range(NG):
                for kv in range(2):
                    xb = it % 2
                    it += 1
                    src = K.kcT_d if kv == 0 else K.vcT_d
                    DMA(P, xin[xb][:], src[g * 64:(g + 1) * 64, :], w=['xin%d' % xb])
                    for m in range(2):
                        for l in range(32):
                            MM(P, ph[m][:, 0:511], w1b[kv][:, l, m * 128:(m + 1) * 128], xin[xb][:, l:l + 16 * 510 + 1:16],
                               l == 0, l == 31, r=['w1b%d' % kv, 'xin%d' % xb], w=['ph%d' % m])
                        ACTF(P, hid[:, m, 0:511], ph[m][:, 0:511], AF.Silu, r=['ph%d' % m, 'hb%d' % kv], w=['hid'], bias=hb[kv][:, m:m + 1])
                    if kv == 0:
                        for m in range(2):
                            MM(P, po[0:96, 0:511], w2b[0][:, m, :], hid[:, m, 0:511], m == 0, m == 1, r=['w2b0', 'hid'], w=['po'])
                        COPY(P, 'dve', kcmpT[0:96, g, 0:511], po[0:96, 0:511], r=['po'], w=['kcmpT'])
                    else:
                        for nt in range(4):
                            for m in range(2):
                                MM(P, po[:, nt * 64:(nt + 1) * 64], hid[:, m, nt * 128:(nt + 1) * 128], w2b[1][:, m, :], m == 0, m == 1,
                                   r=['w2b1', 'hid'], w=['po'])
                        COPY(P, 'dve', vcmpa[:, :, g, 0:64], po[:, 0:256].rearrange("p (a b) -> p a b", a=4), r=['po'], w=['vcmpa'])
            MEMSET(P, 'pool', vcmpa[:, :, :, 64:65], 1.0, r=['vcmpa'], w=['vcmpa'])
            P.barrier()

        if DBG.get('stop') == 'nsaB':
            return
        nsa_attention(P, nc, K, li, kcmpT, vcmpa)
        if DBG.get('stop') == 'nsaC':
            return

    out_proj_phase(P, nc, K, li, prm['w_out'], hsrc, hdst, final_g, token_major_src=True)


THR_DECAY = 48.0


def _thr():
    return DBG.get("thr", THR_DECAY)


def _slope(h):
    return 2.0 ** (-(h + 1) / 2.0)


def nsa_attention(P, nc, K, li, kcmpT, vcmpa):
    with ExitStack() as Cx:
        sb = lambda name, shape, dt: Cx.enter_context(nc.sbuf_tensor('%sC%d' % (name, li), shape, dt))
        ps = lambda name, shape, dt: Cx.enter_context(nc.psum_tensor('%sC%d' % (name, li), shape, dt))
        ident = K.ident_b
        OV = sb('OV', [128, 4, 128], BF16)
        causal = sb('causal', [128, 4, 512], BF16)
        band = sb('band', [128, 8, 512], BF16)
        cmpm = sb('cmpm', [128, 5, 512], BF16)
        FT = sb('FT', [128, 256], F32)
        zer = sb('zer', [128, 512], BF16)
        ksT = sb('ksT', [128, S], BF16)
        vsa = sb('vsa', [128, 64, 65], BF16)
        qa = [sb('qa%d' % i, [128, 4, 4, 512], BF16) for i in range(2)]
        kwT = [sb('kwT%d' % i, [128, 1024], BF16) for i in range(2)]
        vwa = [sb('vwa%d' % i, [128, 8, 65], BF16) for i in range(2)]
        gt = [sb('gt%d' % i, [128, 4, 48], F32) for i in range(2)]
        zt = [sb('zt%d' % i, [128, 4, 256], BF16) for i in range(2)]
        pc = [[sb('pc%d_%d' % (r, nt), [128, 512], BF16) for nt in range(4)] for r in range(4)]
        pt = [sb('pt%d' % i, [128, 512], BF16) for i in range(4)]
        oacc = sb('oacc', [128, 4, 4, 64], F32)
        tmpo = [sb('tmpo%d' % i, [128, 4, 64], F32) for i in range(2)]
        rcs = sb('rcs', [128, 4, 4], F32)
        rr = [sb('rr%d' % i, [128, 4], F32) for i in range(2)]
        ww = [sb('ww%d' % i, [128, 4], F32) for i in range(2)]
        impacc = [sb('impacc%d' % i, [128, 128], F32) for i in range(2)]
        impf = [sb('impf%d' % i, [128, 128], F32) for i in range(2)]
        work = [sb('work%d' % i, [128, 128], F32) for i in range(2)]
        m8a = [sb('m8a%d' % i, [128, 8], F32) for i in range(2)]
        m8b = [sb('m8b%d' % i, [128, 8], F32) for i in range(2)]
        selb = [sb('selb%d' % i, [128, 128], BF16) for i in range(4)]
        ozb = [sb('ozb%d' % i, [128, 4, 256], BF16) for i in range(2)]
        Sps = [ps('S%d' % i, [128, 512], F32) for i in range(3)]
        Ops = [ps('O%d' % i, [128, 512], F32) for i in range(2)]
        pimp = [ps('pimp%d' % i, [128, 512], F32) for i in range(2)]
        pst = ps('pst', [128, 1024], BF16)

        DMA(P, OV[:], K.c_OV, w=['const'])
        DMA(P, causal[:], K.c_causal, w=['const'])
        DMA(P, band[:], K.c_band, w=['const'])
        DMA(P, cmpm[:], K.c_cmpmask, w=['const'])
        DMA(P, FT[:], K.c_FT, w=['const'])
        MEMSET(P, 'pool', zer[:], 0.0, w=['const'])
        MEMSET(P, 'pool', vsa[:, :, 64:65], 1.0, w=['vsa1'])
        for i in range(2):
            MEMSET(P, 'pool', vwa[i][:, :, 64:65], 1.0, w=['vwa1_%d' % i])
            MEMSET(P, 'pool', qa[i][0:32, :, :, :], 0.0, w=['qsel%d' % i])
            MEMSET(P, 'pool', kwT[i][0:32, :], 0.0, w=['kwT0_%d' % i])
        DMA(P, ksT[0:32, :], K.c_OH, w=['ksToh'])
        DMA(P, ksT[96:103, :], K.c_KA, w=['ksTaug'])
        st = {'s': 0, 'p': 0, 'o': 0, 'x': 0}

        def combine(oi, first, gcol, r, keep_rc, cb):
            O3 = Ops[oi][:, 0:260].rearrange("p (a b) -> p a b", a=4)
            x = st['x'] % 2
            st['x'] += 1
            TS(P, 'dve', rr[x][:], O3[:, :, 64], 1e-30, ALU.max, r=['O%d' % oi], w=['rr%d' % x])
            P.add('dve', lambda e: e.reciprocal(out=rr[x][:], in_=rr[x][:]), r=['rr%d' % x], w=['rr%d' % x])
            if keep_rc:
                COPY(P, 'dve', rcs[:, r, :], rr[x][:], r=['rr%d' % x], w=['rcs%d' % r])
            TT(P, 'dve', ww[x][:], rr[x][:], gt[cb][:, :, gcol], ALU.mult, r=['rr%d' % x, 'gt%d' % cb], w=['ww%d' % x])
            wbc = ww[x][:].unsqueeze(2).broadcast_to([128, 4, 64])
            if first:
                TT(P, 'dve', oacc[:, :, r, :], O3[:, :, 0:64], wbc, ALU.mult, r=['O%d' % oi, 'ww%d' % x], w=['oacc%d' % r])
            else:
                TT(P, 'dve', tmpo[x][:], O3[:, :, 0:64], wbc, ALU.mult, r=['O%d' % oi, 'ww%d' % x], w=['tmpo%d' % x])
                TT(P, 'pool', oacc[:, :, r, :], oacc[:, :, r, :], tmpo[x][:], ALU.add, r=['tmpo%d' % x, 'oacc%d' % r], w=['oacc%d' % r])

        def run_stream(jobs):
            n = len(jobs)
            LAG = 2
            for i in range(n + LAG):
                if i < n:
                    j = jobs[i]
                    si = st['s'] % 3
                    st['s'] += 1
                    nm = len(j['mms'])
                    for idx, (l_, r_, rk) in enumerate(j['mms']):
                        MM(P, Sps[si][:], l_, r_, idx == 0, idx == nm - 1, r=rk, w=['S%d' % si])
                    if j['pbuf'] is None:
                        pi = st['p'] % 4
                        st['p'] += 1
                        j['pbuf'] = (pt[pi], 'pt%d' % pi)
                    ACTF(P, j['pbuf'][0][:], Sps[si][:], AF.Exp, r=['S%d' % si], w=[j['pbuf'][1]])
                if i >= LAG:
                    j = jobs[i - LAG]
                    oi = j['oi']
                    if j['first']:
                        MM(P, Ops[oi][:, 0:260], zer[:, 0:128], zer[:, 0:260], True, False, r=['const'], w=['O%d' % oi], skip=True)
                    pb, pk = j['pbuf']
                    for sub in range(4):
                        MM(P, Ops[oi][:, sub * 65:(sub + 1) * 65], pb[:, sub * 128:(sub + 1) * 128], j['v'], False,
                           (j['last'] and sub == 3), r=[pk] + j['vk'], w=['O%d' % oi], skip=True)
                    if j['last']:
                        j['fin'](oi)

        it = 0
        for g in range(DBG.get('c_ng', NG)):
            DMA(P, ksT[32:96, :], K.ksT_d[g * 64:(g + 1) * 64, :], w=['ksT'])
            for q4 in range(4):
                DMA(P, vsa[:, q4 * 16:(q4 + 1) * 16, 0:64],
                    K.vs_d[q4 * 2048:(q4 + 1) * 2048, g * 64:(g + 1) * 64].rearrange("(kt p) d -> p kt d", p=128), r=['vsa'], w=['vsa'])
            for c in range(DBG.get('c_nc', NC)):
                cb = it % 2
                it += 1
                cs = slice(c * 512, (c + 1) * 512)
                ntmax = c // 4
                sel_kts, win_rs, cmp_nts = [], [], []
                for r in range(4):
                    sl = _slope(4 * g + r)
                    sel_kts.append([kt for kt in range(4 * c + 4) if sl * max(0, 512 * c - 128 * kt - 127) <= _thr()])
                    win_rs.append([rw for rw in range(4 if c == 0 else 0, 8) if sl * max(0, 385 - 128 * rw) <= _thr()])
                    cmp_nts.append([nt for nt in range(ntmax + 1)
                                    if sl * max(0.0, 512 * c - (2048 * nt + 2047.5) - 48.0) <= _thr()])
                VD = 1000 if DBG.get('one_ver') else 16
                selvers = sorted(set(kt // VD for r in range(4) for kt in sel_kts[r]))
                vers = sorted(set(selvers) | {0})
                qkeys = []
                for v in vers:
                    DMA(P, qa[cb][32:96, v, :, :], K.qT_d[g * 256:(g + 1) * 256, cs].rearrange("(r d) t -> d r t", d=64), w=['qa%d_%d' % (cb, v)])
                    DMA(P, qa[cb][96:103, v, :, :], K.c_QA[g * 4:(g + 1) * 4, :, cs].rearrange("h r t -> r h t"), w=['qaug%d_%d' % (cb, v)])
                    qkeys += ['qa%d_%d' % (cb, v), 'qaug%d_%d' % (cb, v)]
                q0k = ['qa%d_0' % cb, 'qaug%d_0' % cb, 'qsel%d' % cb]
                k0 = (c - 1) * 512
                if c == 0:
                    DMA(P, kwT[cb][32:96, 512:1024], K.kwT_d[g * 64:(g + 1) * 64, 0:512], w=['kwT%d' % cb])
                    DMA(P, kwT[cb][96:103, 512:1024], K.c_KA[:, 0:512], w=['kwTa%d' % cb])
                    DMA(P, vwa[cb][:, 4:8, 0:64], K.vw_d[0:512, g * 64:(g + 1) * 64].rearrange("(kt p) d -> p kt d", p=128), w=['vwa%d' % cb])
                else:
                    DMA(P, kwT[cb][32:96, :], K.kwT_d[g * 64:(g + 1) * 64, k0:k0 + 1024], w=['kwT%d' % cb])
                    DMA(P, kwT[cb][96:103, :], K.c_KA[:, k0:k0 + 1024], w=['kwTa%d' % cb])
                    DMA(P, vwa[cb][:, :, 0:64], K.vw_d[k0:k0 + 1024, g * 64:(g + 1) * 64].rearrange("(kt p) d -> p kt d", p=128), w=['vwa%d' % cb])
                DMA(P, gt[cb][:], K.gates_d[cs, :].rearrange("(s p) c -> p s c", p=128), w=['gt%d' % cb])
                DMA(P, zt[cb][:], K.zs_d[cs, g * 256:(g + 1) * 256].rearrange("(s p) c -> p s c", p=128), w=['zt%d' % cb])

                jobs = []
                for r in range(4):
                    for nt in cmp_nts[r]:
                        mms = [(kcmpT[0:103, g, nt * 128:(nt + 1) * 128], qa[cb][0:103, 0, r, :], q0k + ['kcmpT'])]
                        if nt == ntmax:
                            mms.append((ident[:], cmpm[:, c % 4, :], ['const']))
                        elif nt == ntmax - 1 and c % 4 == 0 and not DBG.get('no_elif'):
                            mms.append((ident[:], cmpm[:, 4, :], ['const']))
                        jobs.append(dict(mms=mms, pbuf=(pc[r][nt], 'pc%d_%d' % (r, nt)), v=vcmpa[:, nt, g, :], vk=['vcmpa'],
                                         oi=(st['o'] + r) % 2, first=(nt == cmp_nts[r][0]), last=(nt == cmp_nts[r][-1]),
                                         fin=(lambda oi, r=r: combine(oi, True, 0 * 16 + g * 4 + r, r, True, cb))))
                st['o'] += 4
                run_stream(jobs)

                for sub in range(4):
                    ti = 4 * c + sub
                    x2 = sub % 2
                    pb_ = pimp[x2]
                    pk_ = 'pimp%d' % x2
                    for r in range(4):
                        for nt in cmp_nts[r]:
                            MM(P, pb_[:, r * 128:(r + 1) * 128], pc[r][nt][:, sub * 128:(sub + 1) * 128], OV[:, nt, :],
                               nt == cmp_nts[r][0], nt == cmp_nts[r][-1], r=['pc%d_%d' % (r, nt), 'const'], w=[pk_])
                    ia, if_, wk, ma, mb = impacc[x2], impf[x2], work[x2], m8a[x2], m8b[x2]
                    kx = '_%d' % x2
                    for r in range(4):
                        if r == 0:
                            TS(P, 'dve', ia[:], pb_[:, 0:128], rcs[:, r, sub:sub + 1], ALU.mult, r=[pk_, 'rcs%d' % r], w=['impacc' + kx])
                        else:
                            STT(P, ia[:], pb_[:, r * 128:(r + 1) * 128], rcs[:, r, sub:sub + 1], ia[:], ALU.mult, ALU.add,
                                r=[pk_, 'rcs%d' % r, 'impacc' + kx], w=['impacc' + kx])
                    TT(P, 'dve', if_[:], ia[:], FT[:, 127 - 2 * ti:255 - 2 * ti], ALU.add, r=['impacc' + kx, 'const'], w=['impf' + kx])
                    MEMSET(P, 'dve', if_[:, 0:1], 1e9, r=['impf' + kx], w=['impf' + kx])
                    P.add('dve', (lambda e, ma=ma, if_=if_: e.max(out=ma[:], in_=if_[:])), r=['impf' + kx], w=['m8a' + kx])
                    P.add('dve', (lambda e, ma=ma, if_=if_, wk=wk: e.match_replace(out=wk[:], in_to_replace=ma[:], in_values=if_[:], imm_value=-3.0e38)),
                          r=['impf' + kx, 'm8a' + kx], w=['work' + kx])
                    P.add('dve', (lambda e, mb=mb, wk=wk: e.max(out=mb[:], in_=wk[:])), r=['work' + kx], w=['m8b' + kx])
                    TS(P, 'dve', selb[sub][:], if_[:], mb[:, 7:8], ALU.is_lt, DBG.get('maskv', MASKV), ALU.mult, r=['impf' + kx, 'm8b' + kx], w=['selb%d' % sub])
                    pcol = slice((sub % 2) * 128, (sub % 2) * 128 + 128)
                    pkey = 'pst%d' % (sub % 2)
                    TR(P, pst[:, pcol], selb[sub][:], ident[:], r=['selb%d' % sub, 'const'], w=[pkey])
                    for v in selvers:
                        COPY(P, 'dve', qa[cb][0:32, v, :, sub * 128:(sub + 1) * 128],
                             pst[32 * v:32 * v + 32, pcol].unsqueeze(1).broadcast_to([32, 4, 128]), r=[pkey], w=['qsel%d' % cb])

                jobs = []
                for r in range(4):
                    for rw in win_rs[r]:
                        mms = [(kwT[cb][0:103, rw * 128:(rw + 1) * 128], qa[cb][0:103, 0, r, :], q0k + ['kwT%d' % cb, 'kwTa%d' % cb, 'kwT0_%d' % cb]),
                               (ident[:], band[:, rw, :], ['const'])]
                        jobs.append(dict(mms=mms, pbuf=None, v=vwa[cb][:, rw, :], vk=['vwa%d' % cb, 'vwa1_%d' % cb],
                                         oi=(st['o'] + r) % 2, first=(rw == win_rs[r][0]), last=(rw == win_rs[r][-1]),
                                         fin=(lambda oi, r=r: combine(oi, False, 2 * 16 + g * 4 + r, r, False, cb))))
                st['o'] += 4
                run_stream(jobs)

                jobs = []
                for r in range(4):
                    for kt in sel_kts[r]:
                        mms = [(ksT[0:103, kt * 128:(kt + 1) * 128], qa[cb][0:103, kt // VD, r, :],
                                qkeys + ['qsel%d' % cb, 'ksT', 'ksTaug', 'ksToh'])]
                        if kt >= 4 * c:
                            mms.append((ident[:], causal[:, kt - 4 * c, :], ['const']))
                        jobs.append(dict(mms=mms, pbuf=None, v=vsa[:, kt, :], vk=['vsa', 'vsa1'],
                                         oi=(st['o'] + r) % 2, first=(kt == sel_kts[r][0]), last=(kt == sel_kts[r][-1]),
                                         fin=(lambda oi, r=r: combine(oi, False, 1 * 16 + g * 4 + r, r, False, cb))))
                st['o'] += 4
                run_stream(jobs)

                TT(P, 'pool', ozb[cb][:], oacc[:].rearrange("p s r d -> p s (r d)"), zt[cb][:], ALU.mult,
                   r=['oacc0', 'oacc1', 'oacc2', 'oacc3', 'zt%d' % cb], w=['ozb%d' % cb])
                DMA(P, K.oz_d[cs, g * 256:(g + 1) * 256].rearrange("(s p) d -> p s d", p=128), ozb[cb][:], r=['ozb%d' % cb])
        P.barrier()


def out_proj_phase(P, nc, K, li, wout, hsrc, hdst, final_g, token_major_src):
    with ExitStack() as Dx:
        sb = lambda name, shape, dt: Dx.enter_context(nc.sbuf_tensor('%sD%d' % (name, li), shape, dt))
        ps = lambda name, shape, dt: Dx.enter_context(nc.psum_tensor('%sD%d' % (name, li), shape, dt))
        wb = sb('wb', [128, 8, D], BF16)
        stage = [sb('wst0', [128, D], F32), sb('wst1', [128, D], F32)]
        ozt = [sb('ozt%d' % i, [128, D], BF16) for i in range(2)]
        ozT = [sb('ozT%d' % i, [128, 8, 128], BF16) for i in range(2)]
        ht = [sb('ht%d' % i, [128, D], F32) for i in range(2)]
        hn = [sb('hn%d' % i, [128, D], F32) for i in range(2)]
        K.pT = ps('pT', [128, 8, 128], BF16)
        pm = [ps('pm%d' % i, [128, 512], F32) for i in range(4)]
        gfb = None
        if final_g is not None:
            gfb = sb('gfb', [128, D], F32)
            DMA(P, gfb[:], final_g.partition_broadcast(128), w=['gb'])
        load_weight_bf16(P, K, wb, wout, D, stage, 'wout')
        for i in range(NT):
            b = i % 2
            rows = slice(i * 128, (i + 1) * 128)
            DMA(P, ozt[b][:], K.oz_d[rows, :], w=['ozt%d' % b])
            DMA(P, ht[b][:], hsrc[rows, :], w=['ht%d' % b])
            transpose_tile(P, K, ozt[b], ozT[b][:], 'ozt%d' % b, 'ozT%d' % b, 'act' if i % 2 == 0 else 'dve')
            for half in range(2):
                pi = (2 * i + half) % 4
                for kc in range(8):
                    MM(P, pm[pi][:], ozT[b][:, kc, :], wb[:, kc, half * 512:(half + 1) * 512], kc == 0, kc == 7,
                       r=['ozT%d' % b, 'wout_wb'], w=['pm%d' % pi])
                TT(P, 'dve', hn[b][:, half * 512:(half + 1) * 512], pm[pi][:], ht[b][:, half * 512:(half + 1) * 512], ALU.add,
                   r=['pm%d' % pi, 'ht%d' % b], w=['hn%d_%d' % (b, half)])
            hk = ['hn%d_0' % b, 'hn%d_1' % b]
            if final_g is None:
                DMA(P, hdst[rows, :], hn[b][:], r=hk)
            else:
                final_norm_store(P, K, hn[b], hk, gfb, ht[b], ['ht%d' % b], [hdst[rows, :]], [slice(0, 128)])
        P.barrier()


def final_norm_store(P, K, hn, hk, gfb, outbuf, okeys, dsts, prs):
    STT(P, K.junkf[:], hn[:], 1.0, hn[:], ALU.mult, ALU.mult, r=hk, w=['junkf', 'ssF'], accum=K.ss[:])
    ACTF(P, K.sq[:], K.ss[:], AF.Sqrt, r=['ssF'], w=['sqF'], scale=1.0 / D, bias=EPS)
    P.add('dve', lambda e: e.reciprocal(out=K.rs[:], in_=K.sq[:]), r=['sqF'], w=['rsF'])
    STT(P, outbuf[:], hn[:], K.rs[:], gfb[:], ALU.mult, ALU.mult, r=hk + ['rsF', 'gb'], w=okeys)
    for d_, pr in zip(dsts, prs):
        DMA(P, d_, outbuf[pr, :], r=okeys)


def s5_layer(P, nc, K, li, prm, hsrc, hdst, final_g):
    hv_src = hsrc.rearrange("(g n s) d -> g s n d", n=64, s=8)
    hv_dst = hdst.rearrange("(g n s) d -> g s n d", n=64, s=8)
    TWO_PI = 2.0 * math.pi

    with ExitStack() as A:
        sb = lambda name, shape, dt: A.enter_context(nc.sbuf_tensor('%sA%d' % (name, li), shape, dt))
        ps = lambda name, shape, dt: A.enter_context(nc.psum_tensor('%sA%d' % (name, li), shape, dt))
        wb = sb('wb', [128, 8, 2048], BF16)
        stage = [sb('wst0', [128, 2048], F32), sb('wst1', [128, 2048], F32)]
        gb = sb('gb', [128, D], F32)
        xt = [sb('xt0', [128, D], F32), sb('xt1', [128, D], F32)]
        xnb = [sb('xnb0', [128, D], BF16), sb('xnb1', [128, D], BF16)]
        xT4 = [sb('xT40', [128, 8, 512], BF16), sb('xT41', [128, 8, 512], BF16)]
        fo = [sb('fo%d' % i, [128, 512], BF16) for i in range(4)]
        K.pT = ps('pT', [128, 8, 128], BF16)
        pm = [ps('pm%d' % i, [128, 512], F32) for i in range(4)]
        load_weight_bf16(P, K, wb, prm['w_in'], 2048, stage, 'win')
        DMA(P, gb[:], prm['norm'].partition_broadcast(128), w=['gb'])
        cnt = 0
        for seg in range(NC):
            cb = seg % 2
            for tt in range(4):
                b = (4 * seg + tt) % 2
                DMA(P, xt[b][0:64, :], hv_src[seg, 2 * tt], w=['xt%da' % b])
                DMA(P, xt[b][64:128, :], hv_src[seg, 2 * tt + 1], w=['xt%db' % b])
                rmsnorm_tile(P, K, xt[b][:], gb[:], xnb[b][:], 'A', ['xt%da' % b, 'xt%db' % b], 'xnb%d' % b)
                transpose_tile(P, K, xnb[b], xT4[cb][:, :, tt * 128:(tt + 1) * 128], 'xnb%d' % b, 'xT4%d' % cb,
                               'act' if tt % 2 == 0 else 'dve')
            for m in range(16):
                pi = cnt % 4
                fi = cnt % 4
                cnt += 1
                for kc in range(8):
                    MM(P, pm[pi][:], wb[:, kc, m * 128:(m + 1) * 128], xT4[cb][:, kc, :], kc == 0, kc == 7,
                       r=['xT4%d' % cb, 'win_wb'], w=['pm%d' % pi])
                if m < 8:
                    COPY(P, 'dve', fo[fi][:], pm[pi][:], r=['pm%d' % pi], w=['fo%d' % fi])
                    DMA(P, K.uT_d[m * 128:(m + 1) * 128, seg * 512:(seg + 1) * 512], fo[fi][:], r=['fo%d' % fi], q='pool')
                else:
                    ACTF(P, fo[fi][:], pm[pi][:], AF.Silu, r=['pm%d' % pi], w=['fo%d' % fi])
                    DMA(P, K.zsT_d[(m - 8) * 128:(m - 7) * 128, seg * 512:(seg + 1) * 512], fo[fi][:], r=['fo%d' % fi], q='pool')
        P.barrier()
    if DBG.get('stop') == 's5p1':
        return

    with ExitStack() as B:
        sb = lambda name, shape, dt: B.enter_context(nc.sbuf_tensor('%sB%d' % (name, li), shape, dt))
        ps = lambda name, shape, dt: B.enter_context(nc.psum_tensor('%sB%d' % (name, li), shape, dt))
        identf = K.ident_f
        Tb = sb('Tb', [128, 64, 128], BF16)
        Pm = sb('Pm', [128, 64, 2, 64], BF16)
        QmR = sb('QmR', [64, 64, 8, 16], BF16)
        QmI = sb('QmI', [64, 64, 8, 16], BF16)
        ar8 = sb('ar8', [64, 64], F32)
        ai8 = sb('ai8', [64, 64], F32)
        dcol = sb('dcol', [128, 8], F32)
        pA = ps('pA', [128, 512], F32)
        pB = ps('pB', [128, 512], F32)
        pC = ps('pC', [128, 512], F32)
        pD = ps('pD', [128, 512], F32)
        pK = ps('pK', [128, 1024], F32)
        DMA(P, dcol[:], prm['d'].rearrange("g c -> (g c)").rearrange("(k p) -> p k", p=128), w=['dcol'], slow=True)

        with ExitStack() as T:
            tb = lambda name, shape, dt: T.enter_context(nc.sbuf_tensor('%sT%d' % (name, li), shape, dt))
            t64 = lambda name: tb(name, [64, 64], F32)
            lamre_g, lamim_g, lre, lim, dtb = t64('lamre_g'), t64('lamim_g'), t64('lre'), t64('lim'), t64('dtb')
            xr_, yi_, mag, sn, cs_, abr, abi = t64('xr_'), t64('yi_'), t64('mag'), t64('sn'), t64('cs_'), t64('abr'), t64('abi')
            den, nr, cfr, cfi, tA, tB, tC = t64('den'), t64('nr'), t64('cfr'), t64('cfi'), t64('tA'), t64('tB'), t64('tC')
            tI = tb('tI', [64, 64], I32)
            bre = tb('bre', [64, 64, 16], F32)
            bim = tb('bim', [64, 64, 16], F32)
            bbr = tb('bbr', [64, 64, 16], F32)
            bbi = tb('bbi', [64, 64, 16], F32)
            t3 = tb('t3', [64, 64, 16], F32)
            t4 = tb('t4', [64, 64, 16], F32)
            Lr = tb('Lr', [64, 9, 64], F32)
            Li = tb('Li', [64, 9, 64], F32)
            Wr = tb('Wr', [64, 64, 8, 16], F32)
            Wi = tb('Wi', [64, 64, 8, 16], F32)
            Cst = tb('Cst', [128, 8, 64], F32)
            CrT = tb('CrT', [64, 64, 16], F32)
            CiT = tb('CiT', [64, 64, 16], F32)
            nCrT = tb('nCrT', [64, 64, 16], F32)
            nCiT = tb('nCiT', [64, 64, 16], F32)
            Ktsb = tb('Ktsb', [128, 64, 16], F32)

            def V(eng, out, a, b, op, r, w):
                TT(P, eng, out, a, b, op, r=r, w=w)

            def VN(eng, out, a, b, op, r, w):
                TT(P, eng, out, a, b, op, r=r, w=w, nosync=True)

            DMA(P, lamre_g[:], prm['lambda_re'], w=['lamre_g'])
            DMA(P, lamim_g[:], prm['lambda_im'], w=['lamim_g'])
            TR(P, pA[0:64, 0:64], lamre_g[:], identf[0:64, 0:64], r=['lamre_g', 'identf'], w=['pA'])
            TR(P, pA[0:64, 64:128], lamim_g[:], identf[0:64, 0:64], r=['lamim_g', 'identf'], w=['pA'])
            COPY(P, 'dve', lre[:], pA[0:64, 0:64], r=['pA'], w=['lre'])
            COPY(P, 'dve', lim[:], pA[0:64, 64:128], r=['pA'], w=['lim'])
            DMA(P, dtb[:], prm['log_dt'].partition_broadcast(64), w=['dtb'])
            ACTF(P, dtb[:], dtb[:], AF.Exp, r=['dtb'], w=['dtb'])
            TS(P, 'dve', lre[:], lre[:], -1e-4, ALU.min, r=['lre'], w=['lre'])
            V('dve', xr_[:], lre[:], dtb[:], ALU.mult, ['lre', 'dtb'], ['xr_'])
            V('dve', yi_[:], lim[:], dtb[:], ALU.mult, ['lim', 'dtb'], ['yi_'])
            ACTF(P, mag[:], xr_[:], AF.Exp, r=['xr_'], w=['mag'])

            def sin_shift(out, okey, shift):
                TS(P, 'dve', tA[:], yi_[:], 1.0 / TWO_PI, ALU.mult, shift / TWO_PI, ALU.add, r=['yi_'], w=['tA'])
                COPY(P, 'dve', tI[:], tA[:], r=['tA'], w=['tI'])
                COPY(P, 'dve', tB[:], tI[:], r=['tI'], w=['tB'])
                STT(P, tC[:], tB[:], -TWO_PI, yi_[:], ALU.mult, ALU.add, r=['tB', 'yi_'], w=['tC'])
                TS(P, 'dve', tC[:], tC[:], shift, ALU.add, 3.141592, ALU.min, r=['tC'], w=['tC'])
                TS(P, 'dve', tC[:], tC[:], -3.141592, ALU.max, r=['tC'], w=['tC'])
                ACTF(P, out, tC[:], AF.Sin, r=['tC'], w=[okey])

            sin_shift(sn[:], 'sn', 0.0)
            sin_shift(cs_[:], 'cs_', math.pi / 2.0)
            V('dve', abr[:], mag[:], cs_[:], ALU.mult, ['mag', 'cs_'], ['abr'])
            V('dve', abi[:], mag[:], sn[:], ALU.mult, ['mag', 'sn'], ['abi'])
            V('dve', den[:], lre[:], lre[:], ALU.mult, ['lre'], ['den'])
            V('dve', tA[:], lim[:], lim[:], ALU.mult, ['lim'], ['tA'])
            V('dve', den[:], den[:], tA[:], ALU.add, ['den', 'tA'], ['den'])
            P.add('dve', lambda e: e.reciprocal(out=den[:], in_=den[:]), r=['den'], w=['den'])
            TS(P, 'dve', nr[:], abr[:], -1.0, ALU.add, r=['abr'], w=['nr'])
            V('dve', tA[:], nr[:], lre[:], ALU.mult, ['nr', 'lre'], ['tA'])
            V('dve', tB[:], abi[:], lim[:], ALU.mult, ['abi', 'lim'], ['tB'])
            V('dve', tA[:], tA[:], tB[:], ALU.add, ['tA', 'tB'], ['tA'])
            V('dve', cfr[:], tA[:], den[:], ALU.mult, ['tA', 'den'], ['cfr'])
            V('dve', tA[:], abi[:], lre[:], ALU.mult, ['abi', 'lre'], ['tA'])
            V('dve', tB[:], nr[:], lim[:], ALU.mult, ['nr', 'lim'], ['tB'])
            V('dve', tA[:], tA[:], tB[:], ALU.subtract, ['tA', 'tB'], ['tA'])
            V('dve', cfi[:], tA[:], den[:], ALU.mult, ['tA', 'den'], ['cfi'])
            DMA(P, bre[:], prm['b_re'].rearrange("g p c -> p g c"), w=['bre'])
            DMA(P, bim[:], prm['b_im'].rearrange("g p c -> p g c"), w=['bim'])
            bc = lambda ap2: ap2.unsqueeze(2).broadcast_to([64, 64, 16])
            V('dve', bbr[:], bre[:], bc(cfr[:]), ALU.mult, ['bre', 'cfr'], ['bbr'])
            V('dve', t3[:], bim[:], bc(cfi[:]), ALU.mult, ['bim', 'cfi'], ['t3'])
            V('dve', bbr[:], bbr[:], t3[:], ALU.subtract, ['bbr', 't3'], ['bbr'])
            V('dve', bbi[:], bim[:], bc(cfr[:]), ALU.mult, ['bim', 'cfr'], ['bbi'])
            V('dve', t3[:], bre[:], bc(cfi[:]), ALU.mult, ['bre', 'cfi'], ['t3'])
            V('dve', bbi[:], bbi[:], t3[:], ALU.add, ['bbi', 't3'], ['bbi'])
            MEMSET(P, 'dve', Lr[:, 0, :], 1.0, w=['L'])
            MEMSET(P, 'dve', Li[:, 0, :], 0.0, r=['L'], w=['L'])
            for tau in range(1, 9):
                V('dve', tA[:], Lr[:, tau - 1, :], abr[:], ALU.mult, ['L', 'abr'], ['tA'])
                V('dve', tB[:], Li[:, tau - 1, :], abi[:], ALU.mult, ['L', 'abi'], ['tB'])
                V('dve', tC[:], Lr[:, tau - 1, :], abi[:], ALU.mult, ['L', 'abi'], ['tC'])
                V('dve', den[:], Li[:, tau - 1, :], abr[:], ALU.mult, ['L', 'abr'], ['den'])
                V('dve', Lr[:, tau, :], tA[:], tB[:], ALU.subtract, ['tA', 'tB', 'L'], ['L'])
                V('dve', Li[:, tau, :], tC[:], den[:], ALU.add, ['tC', 'den', 'L'], ['L'])
            for tau in range(8):
                lrb = bc(Lr[:, tau, :])
                lib = bc(Li[:, tau, :])
                V('dve', Wr[:, :, tau, :], bbr[:], lrb, ALU.mult, ['bbr', 'L'], ['Wr'])
                V('dve', t3[:], bbi[:], lib, ALU.mult, ['bbi', 'L'], ['t3'])
                V('dve', Wr[:, :, tau, :], Wr[:, :, tau, :], t3[:], ALU.subtract, ['Wr', 't3'], ['Wr'])
                V('dve', Wi[:, :, tau, :], bbi[:], lrb, ALU.mult, ['bbi', 'L'], ['Wi'])
                V('dve', t4[:], bbr[:], lib, ALU.mult, ['bbr', 'L'], ['t4'])
                V('dve', Wi[:, :, tau, :], Wi[:, :, tau, :], t4[:], ALU.add, ['Wi', 't4'], ['Wi'])
            for nm, dstT in (('c_re', CrT), ('c_im', CiT)):
                DMA(P, Cst[:], prm[nm].rearrange("g c p -> (g c) p").rearrange("(k r) p -> r k p", r=128), r=[], w=['Cst'])
                for k in range(8):
                    TR(P, pK[0:64, k * 128:(k + 1) * 128], Cst[:, k, :], identf[:], r=['Cst', 'identf'], w=['pK'])
                COPY(P, 'dve', dstT[:].rearrange("p g c -> p (g c)"), pK[0:64, :], r=['pK'], w=[nm])
            TS(P, 'dve', nCrT[:], CrT[:], -1.0, ALU.mult, r=['c_re'], w=['nCrT'])
            TS(P, 'dve', nCiT[:], CiT[:], -1.0, ALU.mult, r=['c_im'], w=['nCiT'])
            for g in range(64):
                MM(P, pK[:, g * 16:(g + 1) * 16], Wr[:, g, :, :].rearrange("p t c -> p (t c)"), CrT[:, g, :], True, False,
                   r=['Wr', 'c_re'], w=['pK'])
                MM(P, pK[:, g * 16:(g + 1) * 16], Wi[:, g, :, :].rearrange("p t c -> p (t c)"), nCiT[:, g, :], False, True,
                   r=['Wi', 'nCiT'], w=['pK'])
            COPY(P, 'dve', Ktsb[:].rearrange("p g c -> p (g c)"), pK[:, :], r=['pK'], w=['Ktsb'])
            DMA(P, K.Kd_d.rearrange("l c g o -> (l c) g o"), Ktsb[:], r=['Ktsb'], w=['Kd'])
            for gb4 in range(16):
                pp = pA if gb4 % 2 == 0 else pB
                pk_ = 'pA' if gb4 % 2 == 0 else 'pB'
                for gi in range(4):
                    g = gb4 * 4 + gi
                    for ri, W_ in enumerate((Wr, Wi)):
                        TR(P, pp[:, (gi * 2 + ri) * 64:(gi * 2 + ri + 1) * 64], W_[:, g, :, :].rearrange("p t c -> p (t c)"),
                           identf[0:64, 0:64], r=['Wr', 'Wi', 'identf'], w=[pk_])
                COPY(P, 'dve', Pm[:, gb4 * 4:(gb4 + 1) * 4, :, :].rearrange("p g r q -> p (g r q)"), pp[:, :], r=[pk_], w=['Pm'])
            for t in range(8):
                lrb = bc(Lr[:, t + 1, :])
                lib = bc(Li[:, t + 1, :])
                V('dve', t3[:], CrT[:], lrb, ALU.mult, ['c_re', 'L'], ['t3'])
                V('dve', t4[:], nCiT[:], lib, ALU.mult, ['nCiT', 'L'], ['t4'])
                V('dve', QmR[:, :, t, :], t3[:], t4[:], ALU.add, ['t3', 't4'], ['QmR'])
                V('dve', t3[:], nCrT[:], lib, ALU.mult, ['nCrT', 'L'], ['t3'])
                V('dve', t4[:], nCiT[:], lrb, ALU.mult, ['nCiT', 'L'], ['t4'])
                V('dve', QmI[:, :, t, :], t3[:], t4[:], ALU.add, ['t3', 't4'], ['QmI'])
            COPY(P, 'dve', ar8[:], Lr[:, 8, :], r=['L'], w=['ar8'])
            COPY(P, 'dve', ai8[:], Li[:, 8, :], r=['L'], w=['ai8'])
            TAP(P, nc, 'abr', abr[:], [64, 64], F32, ['abr'])
            TAP(P, nc, 'abi', abi[:], [64, 64], F32, ['abi'])
            TAP(P, nc, 'bbr', bbr[:], [64, 64, 16], F32, ['bbr'])
            TAP(P, nc, 'bbi', bbi[:], [64, 64, 16], F32, ['bbi'])
            TAP(P, nc, 'Lr', Lr[:], [64, 9, 64], F32, ['L'])
            TAP(P, nc, 'CrT', CrT[:], [64, 64, 16], F32, ['c_re'])
            TAP(P, nc, 'Ktsb', Ktsb[:], [128, 64, 16], F32, ['Ktsb'])
            TAP(P, nc, 'Pm', Pm[:], [128, 64, 2, 64], BF16, ['Pm'])
            TAP(P, nc, 'QmR', QmR[:], [64, 64, 8, 16], BF16, ['QmR'])
            TAP(P, nc, 'QmI', QmI[:], [64, 64, 8, 16], BF16, ['QmI'])
            P.barrier()
        if DBG.get('stop') == 's5setup':
            return

        with ExitStack() as T2:
            Tsb = T2.enter_context(nc.sbuf_tensor('TsbT%d' % li, [128, 64, 8, 16], F32))
            MEMSET(P, 'pool', Tsb[:], 0.0, w=['Tsb'])
            tkeys = []
            for tp in range(8):
                for lag in range(tp + 1):
                    key = 'Tsb_%d_%d' % (tp, lag)
                    tkeys.append(key)
                    DMA(P, Tsb[16 * tp:16 * tp + 16, :, 7 - tp + lag, :], K.Kd_d[lag].rearrange("c g o -> c g o"),
                        r=['Kd', 'Tsb'], w=[key])
            COPY(P, 'dve', Tb[:].rearrange("p g m -> p (g m)"), Tsb[:].rearrange("p g t c -> p (g t c)"), r=tkeys + ['Tsb'], w=['Tb'])
            TAP(P, nc, 'Tb', Tb[:], [128, 64, 128], BF16, ['Tb'])
            P.barrier()
        if DBG.get('stop') == 's5t2':
            return

        with ExitStack() as Sg:
            sg = lambda name, shape, dt: Sg.enter_context(nc.sbuf_tensor('%sS%d' % (name, li), shape, dt))
            Sel = sg('Sel', [128, 8, 8, 128], BF16)
            SelT = sg('SelT', [128, 8, 8, 128], BF16)
            DMA(P, Sel[:], K.c_Sel, w=['const'])
            DMA(P, SelT[:], K.c_SelT, w=['const'])
            useg = [sg('useg%d' % i, [128, 8, 512], BF16) for i in range(2)]
            Uall = sg('Uall', [128, 64, 64], BF16)
            Bcr = sg('Bcr', [64, 64, 64], F32)
            Bci = sg('Bci', [64, 64, 64], F32)
            Xb2 = sg('Xb2', [64, 2, 64, 64], BF16)
            X2 = [sg('X2_%d' % i, [64, 2, 64], F32) for i in range(2)]
            A2 = sg('A2', [64, 2, 64], F32)
            C2 = sg('C2', [64, 2, 64], F32)
            t1 = sg('t1', [64, 2, 64], F32)
            t2 = sg('t2', [64, 2, 64], F32)
            s1, s2, s3, s4 = sg('s1', [64, 64], F32), sg('s2', [64, 64], F32), sg('s3', [64, 64], F32), sg('s4', [64, 64], F32)
            Ysb = [sg('Ysb%d' % i, [128, 8, 64], BF16) for i in range(2)]
            yv = [sg('yv%d' % i, [128, 512], F32) for i in range(2)]
            yg = [sg('yg%d' % i, [128, 512], BF16) for i in range(2)]
            MEMSET(P, 'dve', X2[0][:], 0.0, w=['X2_0'])
            for hh in range(2):
                COPY(P, 'dve', A2[:, hh, :], ar8[:], w=['A2'])
                COPY(P, 'dve', C2[:, hh, :], ai8[:], w=['C2'])
            step = 0
            for seg in range(NC):
                ub = seg % 2
                uk = 'useg%d' % ub
                DMA(P, useg[ub][:], K.uT_d[:, seg * 512:(seg + 1) * 512].rearrange("(k p) t -> p k t", p=128), w=[uk])
                for k in range(8):
                    for gl in range(8):
                        for s in range(8):
                            MM(P, pA[:, gl * 64:(gl + 1) * 64], Sel[:, gl, s, :], useg[ub][:, k, s * 64:(s + 1) * 64], s == 0, s == 7,
                               r=['const', uk], w=['pA'])
                    COPY(P, 'act', Uall[:, k * 8:(k + 1) * 8, :].rearrange("p g n -> p (g n)"), pA[:, :], r=['pA'], w=['Uall%d' % k])
                    for gl in range(8):
                        g = 8 * k + gl
                        MM(P, pB[0:64, gl * 64:(gl + 1) * 64], Pm[:, g, 0, :], Uall[:, g, :], True, True, r=['Uall%d' % k], w=['pB'])
                        MM(P, pC[0:64, gl * 64:(gl + 1) * 64], Pm[:, g, 1, :], Uall[:, g, :], True, True, r=['Uall%d' % k], w=['pC'])
                    COPY(P, 'dve', Bcr[:, k * 8:(k + 1) * 8, :].rearrange("p g n -> p (g n)"), pB[0:64, :], r=['pB'], w=['Bcr%d' % k])
                    COPY(P, 'act', Bci[:, k * 8:(k + 1) * 8, :].rearrange("p g n -> p (g n)"), pC[0:64, :], r=['pC'], w=['Bci%d' % k])
                bk = ['Bcr%d' % k for k in range(8)] + ['Bci%d' % k for k in range(8)]
                for n in range(64):
                    cur = step % 2
                    nxt = 1 - cur
                    step += 1
                    kx, kxn = 'X2_%d' % cur, 'X2_%d' % nxt
                    COPY(P, 'act', Xb2[:, :, :, n], X2[cur][:], r=[kx], w=['Xb2'])
                    VN('dve', t1[:], A2[:], X2[cur][:], ALU.mult, [kx, 'A2'], ['t1'])
                    VN('dve', t2[:], C2[:], X2[cur][:], ALU.mult, [kx, 'C2'], ['t2'])
                    VN('dve', s1[:], t1[:, 0, :], t2[:, 1, :], ALU.subtract, ['t1', 't2'], ['s1'])
                    VN('dve', X2[nxt][:, 0, :], s1[:], Bcr[:, :, n], ALU.add, ['s1'] + bk, [kxn])
                    VN('dve', s3[:], t1[:, 1, :], t2[:, 0, :], ALU.add, ['t1', 't2'], ['s3'])
                    VN('dve', X2[nxt][:, 1, :], s3[:], Bci[:, :, n], ALU.add, ['s3'] + bk, [kxn])
                for k in range(8):
                    yb = k % 2
                    for gl in range(8):
                        g = 8 * k + gl
                        o_ = pD[:, gl * 64:(gl + 1) * 64]
                        MM(P, o_, Tb[:, g, :], Uall[:, g, :], True, False, r=['Uall%d' % k], w=['pD'])
                        MM(P, o_, QmR[:, g, :, :].rearrange("p t c -> p (t c)"), Xb2[:, 0, g, :], False, False, r=['Xb2'], w=['pD'])
                        MM(P, o_, QmI[:, g, :, :].rearrange("p t c -> p (t c)"), Xb2[:, 1, g, :], False, True, r=['Xb2'], w=['pD'])
                    COPY(P, 'act', Ysb[yb][:].rearrange("p g n -> p (g n)"), pD[:, :], r=['pD'], w=['Ysb%d' % yb])
                    pY = pB if k % 2 == 0 else pC
                    pyk = 'pB' if k % 2 == 0 else 'pC'
                    for t in range(8):
                        for gl in range(8):
                            MM(P, pY[:, t * 64:(t + 1) * 64], SelT[:, gl, t, :], Ysb[yb][:, gl, :], gl == 0, gl == 7,
                               r=['const', 'Ysb%d' % yb], w=[pyk])
                    STT(P, yv[yb][:], useg[ub][:, k, :], dcol[:, k:k + 1], pY[:, :], ALU.mult, ALU.add, r=[uk, pyk, 'dcol'], w=['yv%d' % yb])
                    ACTF(P, yg[yb][:], yv[yb][:], AF.Gelu_apprx_tanh, r=['yv%d' % yb], w=['yg%d' % yb])
                    DMA(P, K.ygT_d[k * 128:(k + 1) * 128, seg * 512:(seg + 1) * 512], yg[yb][:], r=['yg%d' % yb])
            P.barrier()
    if DBG.get('stop') == 's5scan':
        return

    with ExitStack() as Cx:
        sb = lambda name, shape, dt: Cx.enter_context(nc.sbuf_tensor('%sC%d' % (name, li), shape, dt))
        ps = lambda name, shape, dt: Cx.enter_context(nc.psum_tensor('%sC%d' % (name, li), shape, dt))
        wg = sb('wg', [128, 8, 2048], BF16)
        wo = sb('wo', [128, 8, D], BF16)
        stage = [sb('wst0', [128, 2048], F32), sb('wst1', [128, 2048], F32)]
        ygs = [sb('ygs%d' % i, [128, 8, 512], BF16) for i in range(2)]
        zss = [sb('zss%d' % i, [128, 8, 512], BF16) for i in range(2)]
        sig = [sb('sig%d' % i, [128, 512], F32) for i in range(2)]
        tga = [sb('tga%d' % i, [128, 512], F32) for i in range(2)]
        ozT = [sb('ozT%d' % i, [128, 8, 512], BF16) for i in range(2)]
        ht = [sb('ht%d' % i, [128, D], F32) for i in range(2)]
        hn = [sb('hn%d' % i, [128, D], F32) for i in range(2)]
        pa = [ps('pa%d' % i, [128, 512], F32) for i in range(2)]
        pb = [ps('pb%d' % i, [128, 512], F32) for i in range(2)]
        pm = [ps('pm%d' % i, [128, 512], F32) for i in range(4)]
        gfb = None
        if final_g is not None:
            gfb = sb('gfb', [128, D], F32)
            DMA(P, gfb[:], final_g.partition_broadcast(128), w=['gb'])
        load_weight_bf16(P, K, wg, prm['w_glu'], 2048, stage, 'wglu')
        load_weight_bf16(P, K, wo, prm['w_out'], D, stage, 'wout')
        for seg in range(NC):
            sbi = seg % 2
            cols = slice(seg * 512, (seg + 1) * 512)
            DMA(P, ygs[sbi][:], K.ygT_d[:, cols].rearrange("(k p) t -> p k t", p=128), w=['ygs%d' % sbi])
            DMA(P, zss[sbi][:], K.zsT_d[:, cols].rearrange("(k p) t -> p k t", p=128), w=['zss%d' % sbi])
            for m in range(8):
                x = m % 2
                for kc in range(8):
                    MM(P, pa[x][:], wg[:, kc, m * 128:(m + 1) * 128], ygs[sbi][:, kc, :], kc == 0, kc == 7,
                       r=['ygs%d' % sbi, 'wglu_wb'], w=['pa%d' % x])
                for kc in range(8):
                    MM(P, pb[x][:], wg[:, kc, 1024 + m * 128:1024 + (m + 1) * 128], ygs[sbi][:, kc, :], kc == 0, kc == 7,
                       r=['ygs%d' % sbi, 'wglu_wb'], w=['pb%d' % x])
                ACTF(P, sig[x][:], pb[x][:], AF.Sigmoid, r=['pb%d' % x], w=['sig%d' % x])
                TT(P, 'dve', tga[x][:], pa[x][:], sig[x][:], ALU.mult, r=['pa%d' % x, 'sig%d' % x], w=['tga%d' % x])
                TT(P, 'pool', ozT[sbi][:, m, :], tga[x][:], zss[sbi][:, m, :], ALU.mult, r=['tga%d' % x, 'zss%d' % sbi], w=['ozT%d_%d' % (sbi, m)])
            ozk = ['ozT%d_%d' % (sbi, m) for m in range(8)]
            for tt in range(4):
                b = (4 * seg + tt) % 2
                DMA(P, ht[b][0:64, :], hv_src[seg, 2 * tt], w=['ht%da' % b])
                DMA(P, ht[b][64:128, :], hv_src[seg, 2 * tt + 1], w=['ht%db' % b])
                for half in range(2):
                    pi = (2 * tt + half) % 4
                    for kc in range(8):
                        MM(P, pm[pi][:], ozT[sbi][:, kc, tt * 128:(tt + 1) * 128], wo[:, kc, half * 512:(half + 1) * 512], kc == 0, kc == 7,
                           r=ozk + ['wout_wb'], w=['pm%d' % pi])
                    TT(P, 'dve', hn[b][:, half * 512:(half + 1) * 512], pm[pi][:], ht[b][:, half * 512:(half + 1) * 512], ALU.add,
                       r=['pm%d' % pi, 'ht%da' % b, 'ht%db' % b], w=['hn%d_%d' % (b, half)])
                hk = ['hn%d_0' % b, 'hn%d_1' % b]
                dsts = [hv_dst[seg, 2 * tt], hv_dst[seg, 2 * tt + 1]]
                prs = [slice(0, 64), slice(64, 128)]
                if final_g is None:
                    for d_, pr in zip(dsts, prs):
                        DMA(P, d_, hn[b][pr, :], r=hk)
                else:
                    final_norm_store(P, K, hn[b], hk, gfb, ht[b], ['ht%da' % b, 'ht%db' % b], dsts, prs)
        P.barrier()


def build(layers=(0, 1, 2, 3), with_final=True):
    nc = bass.Bass("TRN2", target_bir_lowering=False)
    K = Ctx()
    x = nc.dram_tensor("x", [S, D], F32, kind="ExternalInput").ap()
    y = nc.dram_tensor("y", [S, D], F32, kind="ExternalOutput").ap()
    prm = {}
    for li in layers:
        prm[li] = {}
        for nm, shp in (NSA_PARAMS if li % 2 == 0 else S5_PARAMS):
            prm[li][nm] = nc.dram_tensor("l%d_%s" % (li, nm), shp, F32, kind="ExternalInput").ap()
    fng = nc.dram_tensor("final_norm", [D], F32, kind="ExternalInput").ap()
    for nm, shp, dt in CONST_SPECS:
        setattr(K, 'c_' + nm, nc.dram_tensor("c_" + nm, shp, dt, kind="ExternalInput").ap())
    scr = lambda nm, shp, dt: nc.dram_tensor("scr_" + nm, shp, dt, kind=("ExternalOutput" if nm in DBG.get('dump', ()) else "Internal")).ap()
    K.qT_d = scr('qT', [1024, S], BF16)
    K.kcT_d = scr('kcT', [256, S], BF16)
    K.vcT_d = scr('vcT', [256, S], BF16)
    K.ksT_d = scr('ksT', [256, S], BF16)
    K.kwT_d = scr('kwT', [256, S], BF16)
    K.vs_d = scr('vs', [S, 256], BF16)
    K.vw_d = scr('vw', [S, 256], BF16)
    K.gates_d = scr('gates', [S, 48], F32)
    K.zs_d = scr('zs', [S, 1024], BF16)
    K.oz_d = scr('oz', [S, 1024], BF16)
    K.uT_d = scr('uT', [1024, S], BF16)
    K.zsT_d = scr('zsT', [1024, S], BF16)
    K.ygT_d = scr('ygT', [1024, S], BF16)
    K.Kd_d = scr('Kd', [8, 16, 64, 16], F32)
    P = Prog(nc)
    with ExitStack() as G:
        gs = lambda name, shape, dt: G.enter_context(nc.sbuf_tensor(name, shape, dt))
        K.ident_b = gs('ident_b', [128, 128], BF16)
        K.ident_f = gs('ident_f', [128, 128], F32)
        K.junkf = gs('junkf', [128, D], F32)
        K.ss = gs('ss', [128, 1], F32)
        K.sq = gs('sq', [128, 1], F32)
        K.rs = gs('rs', [128, 1], F32)
        DMA(P, K.ident_b[:], K.c_ident_b, w=['const'])
        DMA(P, K.ident_f[:], K.c_ident_f, w=['identf'])
        P.barrier()
        hsrc = x
        for li in layers:
            fg = fng if (with_final and li == layers[-1]) else None
            if li % 2 == 0:
                nsa_layer(P, nc, K, li, prm[li], hsrc, y, fg)
            else:
                s5_layer(P, nc, K, li, prm[li], hsrc, y, fg)
            hsrc = y
        run_prog(nc, P)
    return nc


ALL_INPUT_NAMES = (
    'x',
    'l0_norm',
    'l0_w_in',
    'l0_cmp_k_pe',
    'l0_cmp_k_w1',
    'l0_cmp_k_w2',
    'l0_cmp_v_pe',
    'l0_cmp_v_w1',
    'l0_cmp_v_w2',
    'l0_w_out',
    'l1_norm',
    'l1_w_in',
    'l1_log_dt',
    'l1_lambda_re',
    'l1_lambda_im',
    'l1_b_re',
    'l1_b_im',
    'l1_c_re',
    'l1_c_im',
    'l1_d',
    'l1_w_glu',
    'l1_w_out',
    'l2_norm',
    'l2_w_in',
    'l2_cmp_k_pe',
    'l2_cmp_k_w1',
    'l2_cmp_k_w2',
    'l2_cmp_v_pe',
    'l2_cmp_v_w1',
    'l2_cmp_v_w2',
    'l2_w_out',
    'l3_norm',
    'l3_w_in',
    'l3_log_dt',
    'l3_lambda_re',
    'l3_lambda_im',
    'l3_b_re',
    'l3_b_im',
    'l3_c_re',
    'l3_c_im',
    'l3_d',
    'l3_w_glu',
    'l3_w_out',
    'final_norm',
)


_NC_CACHE = {}


def make_in_map(inputs, b, layers=(0, 1, 2, 3)):
    C = host_constants()
    m = {'x': np.ascontiguousarray(inputs['x'][b], dtype=np.float32)}
    for li in layers:
        for nm, shp in (NSA_PARAMS if li % 2 == 0 else S5_PARAMS):
            key = 'l%d_%s' % (li, nm)
            m[key] = np.ascontiguousarray(inputs[key], dtype=np.float32)
    m['final_norm'] = np.ascontiguousarray(inputs['final_norm'], dtype=np.float32)
    for nm, shp, dt in CONST_SPECS:
        m['c_' + nm] = C[nm]
    return m


def kernel(**inputs):
    inputs = {k: np.asarray(inputs[k]) for k in ALL_INPUT_NAMES}
    if 'full' not in _NC_CACHE:
        _NC_CACHE['full'] = build()
    nc = _NC_CACHE['full']
    in_maps = [make_in_map(inputs, c % 4) for c in range(8)]
    res = run_bass_kernel_spmd(nc, in_maps, core_ids=list(range(8)))
    out = np.stack([np.asarray(res.results[b]['y'], dtype=np.float32) for b in range(4)], axis=0)
    return out
```

```python
import math
from contextlib import ExitStack
import numpy as np
import ml_dtypes
import concourse.bass as bass
import concourse.mybir as mybir
from concourse.bass_utils import run_bass_kernel_spmd

F32 = mybir.dt.float32
BF16 = mybir.dt.bfloat16
I32 = mybir.dt.int32
AF = mybir.ActivationFunctionType
ALU = mybir.AluOpType
AX = mybir.AxisListType
NPBF = ml_dtypes.bfloat16

S = 8192
D = 1024
NH = 16
DH = 64
NG = 4
NT = S // 128
NC = S // 512
NSA_IN = 3632
EPS = 1e-6
MASKV = -30000.0

ENGS = ['pe', 'act', 'dve', 'pool', 'sp']
KRING = 8
SAME_ENGINE_SYNC = {'act': True, 'dve': True, 'pool': True, 'pe': False, 'sp': False}


class Prog:
    def __init__(self, nc):
        self.nc = nc
        self.q = {e: [] for e in ENGS}
        self.ncomp = {e: 0 for e in ENGS}
        self.ndma = {e: 0 for e in ENGS}
        self.lastw = {}
        self.readers = {}

    def add(self, eng, fn, r=(), w=(), dma=False, nosync=False):
        deps = {}
        dd = set()

        def addtok(t):
            if t is None:
                return
            if t[0] == 'c':
                if deps.get(t[1], 0) < t[2]:
                    deps[t[1]] = t[2]
            else:
                dd.add(t)

        for k in r:
            addtok(self.lastw.get(k))
        for k in w:
            addtok(self.lastw.get(k))
            rd = self.readers.get(k)
            if rd:
                for e2, n2 in rd[0].items():
                    addtok(('c', e2, n2))
                for t in rd[1]:
                    addtok(t)
        if dma:
            tok = ('d', eng, self.ndma[eng])
            self.ndma[eng] += 1
        else:
            self.ncomp[eng] += 1
            tok = ('c', eng, self.ncomp[eng])
        if nosync:
            deps.pop(eng, None)
        self.q[eng].append(('op', fn, deps, dd, tok))
        for k in w:
            self.lastw[k] = tok
            self.readers[k] = [{}, set()]
        for k in r:
            rd = self.readers.setdefault(k, [{}, set()])
            if tok[0] == 'c':
                if rd[0].get(tok[1], 0) < tok[2]:
                    rd[0][tok[1]] = tok[2]
            else:
                rd[1].add(tok)
        return tok

    def barrier(self):
        sc = dict(self.ncomp)
        sd = dict(self.ndma)
        for e in ENGS:
            self.q[e].append(('bar', sc, sd))
        self.lastw = {}
        self.readers = {}

    def run_engine(self, ename, eng, semc, semd):
        waited_c = {}
        waited_d = {}

        def wait_c(f, n):
            if n <= 0 or waited_c.get(f, 0) >= n:
                return
            eng.wait_ge(semc[f], n)
            waited_c[f] = n

        def wait_d(qn, i):
            if i < 0:
                return
            slot = i % KRING
            tgt = 16 * (i // KRING + 1)
            if waited_d.get((qn, slot), 0) >= tgt:
                return
            eng.wait_ge(semd[qn][slot], tgt)
            waited_d[(qn, slot)] = tgt

        for item in self.q[ename]:
            if item[0] == 'bar':
                _, sc, sd = item
                for f in ENGS:
                    if f == ename and not SAME_ENGINE_SYNC[ename]:
                        continue
                    wait_c(f, sc[f])
                for qn in ENGS:
                    n = sd[qn]
                    for i in range(max(0, n - KRING), n):
                        wait_d(qn, i)
                continue
            _, fn, deps, dd, tok = item
            for f, n in deps.items():
                if f == ename and not SAME_ENGINE_SYNC[ename]:
                    continue
                wait_c(f, n)
            for t in dd:
                wait_d(t[1], t[2])
            if tok[0] == 'd':
                i = tok[2]
                if i >= KRING:
                    wait_d(ename, i - KRING)
                ins = fn(eng)
                ins.then_inc(semd[ename][i % KRING], 16)
            else:
                ins = fn(eng)
                ins.then_inc(semc[ename], 1)
        n = self.ndma[ename]
        for i in range(max(0, n - KRING), n):
            wait_d(ename, i)


def run_prog(nc, prog):
    with ExitStack() as st:
        semc = {e: st.enter_context(nc.semaphore('c_' + e)) for e in ENGS}
        semd = {e: [st.enter_context(nc.semaphore('d_%s_%d' % (e, i))) for i in range(KRING)] for e in ENGS}
        block = st.enter_context(nc.Block())

        @block.tensor
        def _(eng):
            prog.run_engine('pe', eng, semc, semd)

        @block.scalar
        def _(eng):
            prog.run_engine('act', eng, semc, semd)

        @block.vector
        def _(eng):
            prog.run_engine('dve', eng, semc, semd)

        @block.gpsimd
        def _(eng):
            prog.run_engine('pool', eng, semc, semd)

        @block.sync
        def _(eng):
            prog.run_engine('sp', eng, semc, semd)


def DMA(P, out, in_, r=(), w=(), q='sp', slow=False):
    if slow:
        P.add(q, lambda e: e.dma_start(out=out, in_=in_, allow_slow_non_contiguous=True), r=r, w=w, dma=True)
    else:
        P.add(q, lambda e: e.dma_start(out=out, in_=in_), r=r, w=w, dma=True)


def MM(P, out, lhsT, rhs, start, stop, r=(), w=(), skip=False):
    P.add('pe', lambda e: e.matmul(out, lhsT=lhsT, rhs=rhs, start=start, stop=stop, skip_group_check=skip), r=r, w=w)


def TR(P, out, in_, ident, r=(), w=()):
    P.add('pe', lambda e: e.transpose(out=out, in_=in_, identity=ident), r=r, w=w)


def ACTF(P, out, in_, func, r=(), w=(), scale=None, bias=None, accum=None):
    kw = {}
    if scale is not None:
        kw['scale'] = scale
    if bias is not None:
        kw['bias'] = bias
    if accum is not None:
        kw['accum_out'] = accum
    P.add('act', lambda e: e.activation(out=out, in_=in_, func=func, **kw), r=r, w=w)


def COPY(P, eng, out, in_, r=(), w=()):
    if eng == 'act':
        P.add('act', lambda e: e.copy(out=out, in_=in_), r=r, w=w)
    else:
        P.add(eng, lambda e: e.tensor_copy(out=out, in_=in_), r=r, w=w)


def TT(P, eng, out, in0, in1, op, r=(), w=(), nosync=False):
    P.add(eng, lambda e: e.tensor_tensor(out=out, in0=in0, in1=in1, op=op), r=r, w=w, nosync=nosync)


def TS(P, eng, out, in0, s1, op0, s2=None, op1=None, r=(), w=()):
    if op1 is None:
        P.add(eng, lambda e: e.tensor_scalar(out=out, in0=in0, scalar1=s1, scalar2=None, op0=op0), r=r, w=w)
    else:
        P.add(eng, lambda e: e.tensor_scalar(out=out, in0=in0, scalar1=s1, scalar2=s2, op0=op0, op1=op1), r=r, w=w)


def STT(P, out, in0, scalar, in1, op0, op1, r=(), w=(), accum=None):
    if accum is None:
        P.add('dve', lambda e: e.scalar_tensor_tensor(out=out, in0=in0, scalar=scalar, in1=in1, op0=op0, op1=op1), r=r, w=w)
    else:
        P.add('dve', lambda e: e.scalar_tensor_tensor(out=out, in0=in0, scalar=scalar, in1=in1, op0=op0, op1=op1, accum_out=accum), r=r, w=w)


def MEMSET(P, eng, ap, val, r=(), w=()):
    P.add(eng, lambda e: e.memset(ap, val), r=r, w=w)


def _bf(x):
    return np.asarray(x, dtype=np.float32).astype(NPBF)


def _split3(c):
    c = np.asarray(c, dtype=np.float64)
    hi = _bf(c)
    r1 = c - hi.astype(np.float64)
    mid = _bf(r1)
    r2 = r1 - mid.astype(np.float64)
    lo = _bf(r2)
    return hi, mid, lo


_CONST_CACHE = {}


def host_constants():
    if _CONST_CACHE:
        return _CONST_CACHE
    C = {}
    C['ident_b'] = np.eye(128, dtype=np.float32).astype(NPBF)
    C['ident_f'] = np.eye(128, dtype=np.float32)
    hh = np.arange(1, NH + 1, dtype=np.float64)
    slopes = np.exp2(-8.0 * hh / NH).astype(np.float32).astype(np.float64)
    s_hi = _bf(slopes)
    s_lo = _bf(slopes - s_hi.astype(np.float64))
    sp = s_hi.astype(np.float64) + s_lo.astype(np.float64)
    t = np.arange(S, dtype=np.float64)
    QA = np.zeros((NH, 7, S), dtype=NPBF)
    for h in range(NH):
        QA[h, 0, :] = _bf(64.0 * s_hi[h].astype(np.float64))
        QA[h, 1, :] = _bf(64.0 * s_lo[h].astype(np.float64))
        QA[h, 2, :] = s_hi[h]
        QA[h, 3, :] = s_lo[h]
        c = np.float32(-(sp[h] * t)).astype(np.float64)
        a, b, cc = _split3(c)
        QA[h, 4, :] = a
        QA[h, 5, :] = b
        QA[h, 6, :] = cc
    C['QA'] = QA
    KA = np.zeros((7, S), dtype=NPBF)
    pos = np.arange(S)
    KA[0] = _bf(pos // 64)
    KA[1] = _bf(pos // 64)
    KA[2] = _bf(pos % 64)
    KA[3] = _bf(pos % 64)
    KA[4:7] = _bf(1.0)
    C['KA'] = KA
    KAc = np.zeros((7, 512), dtype=NPBF)
    n = np.arange(511)
    KAc[0, :511] = _bf(n // 4)
    KAc[1, :511] = _bf(n // 4)
    KAc[2, :511] = _bf(16.0 * (n % 4) + 15.5)
    KAc[3, :511] = _bf(16.0 * (n % 4) + 15.5)
    KAc[4:7, :511] = _bf(1.0)
    C['KAc'] = KAc
    OH = np.zeros((32, S), dtype=np.float32)
    OH[(np.arange(S) // 64) % 32, np.arange(S)] = 1.0
    C['OH'] = OH.astype(NPBF)
    OV = np.zeros((128, 4, 128), dtype=np.float32)
    for nn in range(511):
        cs = 16 * nn
        ce = cs + 31
        for j in range(128):
            if cs <= 64 * j + 63 and ce >= 64 * j:
                OV[nn % 128, nn // 128, j] = 1.0
    C['OV'] = OV.astype(NPBF)
    k = np.arange(128)[:, None, None]
    r = np.arange(4)[None, :, None]
    q = np.arange(512)[None, None, :]
    C['causal'] = np.where(128 * r + k > q, MASKV, 0.0).astype(np.float32).astype(NPBF)
    r8 = np.arange(8)[None, :, None]
    dwin = q - k + 512 - 128 * r8
    C['band'] = np.where((dwin >= 0) & (dwin < 512), 0.0, MASKV).astype(np.float32).astype(NPBF)
    r5 = np.array([0, 512, 1024, 1536, 2048])[None, :, None]
    C['cmpmask'] = np.where(16 * k + 31 - r5 > q, MASKV, 0.0).astype(np.float32).astype(NPBF)
    FT = np.zeros((128, 256), dtype=np.float32)
    for p in range(128):
        jr = 1 if p >= 64 else 0
        for c in range(255):
            jj = c - 127
            if jj == jr or jj == jr - 1:
                FT[p, c] = 1e9
            elif jj > jr:
                FT[p, c] = -1e9
    C['FT'] = FT
    Sel = np.zeros((128, 8, 8, 128), dtype=np.float32)
    SelT = np.zeros((128, 8, 8, 128), dtype=np.float32)
    for gl in range(8):
        for s in range(8):
            for c in range(16):
                Sel[16 * gl + c, gl, s, 16 * (7 - s) + c] = 1.0
                SelT[16 * s + c, gl, s, 16 * gl + c] = 1.0
    C['Sel'] = Sel.astype(NPBF)
    C['SelT'] = SelT.astype(NPBF)
    _CONST_CACHE.update(C)
    return C


CONST_SPECS = [
    ('ident_b', [128, 128], BF16), ('ident_f', [128, 128], F32), ('QA', [NH, 7, S], BF16),
    ('KA', [7, S], BF16), ('KAc', [7, 512], BF16), ('OH', [32, S], BF16),
    ('OV', [128, 4, 128], BF16), ('causal', [128, 4, 512], BF16), ('band', [128, 8, 512], BF16),
    ('cmpmask', [128, 5, 512], BF16), ('FT', [128, 256], F32),
    ('Sel', [128, 8, 8, 128], BF16), ('SelT', [128, 8, 8, 128], BF16),
]

NSA_PARAMS = [('norm', [D]), ('w_in', [D, NSA_IN]), ('cmp_k_pe', [32, 64]), ('cmp_k_w1', [2048, 256]),
              ('cmp_k_w2', [256, 64]), ('cmp_v_pe', [32, 64]), ('cmp_v_w1', [2048, 256]),
              ('cmp_v_w2', [256, 64]), ('w_out', [D, D])]
S5_PARAMS = [('norm', [D]), ('w_in', [D, 2048]), ('log_dt', [64]), ('lambda_re', [64, 64]),
             ('lambda_im', [64, 64]), ('b_re', [64, 64, 16]), ('b_im', [64, 64, 16]),
             ('c_re', [64, 16, 64]), ('c_im', [64, 16, 64]), ('d', [64, 16]),
             ('w_glu', [D, 2048]), ('w_out', [D, D])]


class Ctx:
    pass


DBG = {}


def TAP(P, nc, name, ap, shape, dt, r=()):
    if not DBG.get('taps'):
        return
    t = nc.dram_tensor('dbg_' + name, shape, dt, kind='ExternalOutput').ap()
    DMA(P, t, ap, r=list(r))


def load_weight_bf16(P, K, wb, wdram, ncols, stage, tag):
    for kc in range(8):
        sb = stage[kc % 2]
        key = 'wstage%d' % (kc % 2)
        DMA(P, sb[:, 0:ncols], wdram[kc * 128:(kc + 1) * 128, :], w=[key])
        eng = 'dve' if kc % 2 == 0 else 'pool'
        COPY(P, eng, wb[:, kc, :], sb[:, 0:ncols], r=[key], w=['%s_wb' % tag])


def rmsnorm_tile(P, K, xt, gb, xnb, tg, rkeys, wkey):
    STT(P, K.junkf[:], xt, 1.0, xt, ALU.mult, ALU.mult, r=rkeys, w=['junkf', 'ss' + tg], accum=K.ss[:])
    ACTF(P, K.sq[:], K.ss[:], AF.Sqrt, r=['ss' + tg], w=['sq' + tg], scale=1.0 / D, bias=EPS)
    P.add('dve', lambda e: e.reciprocal(out=K.rs[:], in_=K.sq[:]), r=['sq' + tg], w=['rs' + tg])
    STT(P, xnb, xt, K.rs[:], gb, ALU.mult, ALU.mult, r=rkeys + ['rs' + tg, 'gb'], w=[wkey])


def transpose_tile(P, K, src_b, dst, skey, dkey, evac_eng):
    for kc in range(8):
        TR(P, K.pT[:, kc, :], src_b[:, kc * 128:(kc + 1) * 128], K.ident_b[:], r=[skey, 'const'], w=['pT'])
    COPY(P, evac_eng, dst, K.pT[:], r=['pT'], w=[dkey])


def nsa_layer(P, nc, K, li, prm, hsrc, hdst, final_g):
    QOFF, KCOFF, VCOFF, KSOFF, VSOFF, KWOFF, VWOFF, GOFF, ZOFF = 0, 1024, 1280, 1536, 1792, 2048, 2304, 2560, 2608
    ident = K.ident_b
    with ExitStack() as L:
        sbL = lambda name, shape, dt: L.enter_context(nc.sbuf_tensor('%s_l%d' % (name, li), shape, dt))
        kcmpT = sbL('kcmpT', [128, NG, 512], BF16)
        vcmpa = sbL('vcmpa', [128, 4, NG, 65], BF16)

        with ExitStack() as A:
            sb = lambda name, shape, dt: A.enter_context(nc.sbuf_tensor('%sA%d' % (name, li), shape, dt))
            ps = lambda name, shape, dt: A.enter_context(nc.psum_tensor('%sA%d' % (name, li), shape, dt))
            wb = sb('wb', [128, 8, NSA_IN], BF16)
            stage = [sb('wst0', [128, NSA_IN], F32), sb('wst1', [128, NSA_IN], F32)]
            gb = sb('gb', [128, D], F32)
            xt = [sb('xt0', [128, D], F32), sb('xt1', [128, D], F32)]
            xnb = [sb('xnb0', [128, D], BF16), sb('xnb1', [128, D], BF16)]
            xT4 = [sb('xT40', [128, 8, 512], BF16), sb('xT41', [128, 8, 512], BF16)]
            fo = [sb('fo%d' % i, [128, 512], BF16) for i in range(4)]
            go = [sb('go%d' % i, [128, 48], F32) for i in range(2)]
            K.pT = ps('pT', [128, 8, 128], BF16)
            pm = [ps('pm%d' % i, [128, 512], F32) for i in range(4)]

            load_weight_bf16(P, K, wb, prm['w_in'], NSA_IN, stage, 'win')
            DMA(P, gb[:], prm['norm'].partition_broadcast(128), w=['gb'])
            nmm = [0]
            nfo = [0]

            def mm_group(out_ps, pkey, lhs_fn, rhs_fn, rk):
                for kc in range(8):
                    MM(P, out_ps, lhs_fn(kc), rhs_fn(kc), kc == 0, kc == 7, r=rk + ['win_wb'], w=[pkey])

            for c in range(NC):
                cb = c % 2
                for tt in range(4):
                    i = 4 * c + tt
                    b = i % 2
                    DMA(P, xt[b][:], hsrc[i * 128:(i + 1) * 128, :], w=['xt%d' % b])
                    rmsnorm_tile(P, K, xt[b][:], gb[:], xnb[b][:], 'A', ['xt%d' % b], 'xnb%d' % b)
                    transpose_tile(P, K, xnb[b], xT4[cb][:, :, tt * 128:(tt + 1) * 128], 'xnb%d' % b, 'xT4%d' % cb,
                                   'act' if tt % 2 == 0 else 'dve')
                xk = 'xT4%d' % cb
                fm = [(m, 'q') for m in range(8)] + [(0, 'kc'), (1, 'kc'), (0, 'vc'), (1, 'vc'), (0, 'ks'), (1, 'ks'), (0, 'kw'), (1, 'kw')]
                for (m, kind) in fm:
                    off = {'q': QOFF, 'kc': KCOFF, 'vc': VCOFF, 'ks': KSOFF, 'kw': KWOFF}[kind] + m * 128
                    dst = {'q': K.qT_d, 'kc': K.kcT_d, 'vc': K.vcT_d, 'ks': K.ksT_d, 'kw': K.kwT_d}[kind]
                    pi = nmm[0] % 4
                    nmm[0] += 1
                    mm_group(pm[pi][:], 'pm%d' % pi, lambda kc, off=off: wb[:, kc, off:off + 128],
                             lambda kc: xT4[cb][:, kc, :], [xk])
                    fi = nfo[0] % 4
                    nfo[0] += 1
                    if kind == 'q':
                        ACTF(P, fo[fi][:], pm[pi][:], AF.Copy, r=['pm%d' % pi], w=['fo%d' % fi], scale=0.125)
                    else:
                        COPY(P, 'dve', fo[fi][:], pm[pi][:], r=['pm%d' % pi], w=['fo%d' % fi])
                    DMA(P, dst[m * 128:(m + 1) * 128, c * 512:(c + 1) * 512], fo[fi][:], r=['fo%d' % fi], q='pool')
                for tt in range(4):
                    i = 4 * c + tt
                    rows = slice(i * 128, (i + 1) * 128)
                    lhs = lambda kc, tt=tt: xT4[cb][:, kc, tt * 128:(tt + 1) * 128]
                    pi = nmm[0] % 4
                    nmm[0] += 1
                    for kc in range(8):
                        MM(P, pm[pi][:, 0:256], lhs(kc), wb[:, kc, VSOFF:VSOFF + 256], kc == 0, kc == 7, r=[xk, 'win_wb'], w=['pm%d' % pi])
                    for kc in range(8):
                        MM(P, pm[pi][:, 256:512], lhs(kc), wb[:, kc, VWOFF:VWOFF + 256], kc == 0, kc == 7, r=[xk, 'win_wb'], w=['pm%d' % pi])
                    fi = nfo[0] % 4
                    nfo[0] += 1
                    COPY(P, 'dve', fo[fi][:], pm[pi][:], r=['pm%d' % pi], w=['fo%d' % fi])
                    DMA(P, K.vs_d[rows, :], fo[fi][:, 0:256], r=['fo%d' % fi], q='pool')
                    DMA(P, K.vw_d[rows, :], fo[fi][:, 256:512], r=['fo%d' % fi], q='pool')
                    pi = nmm[0] % 4
                    nmm[0] += 1
                    for kc in range(8):
                        MM(P, pm[pi][:, 0:48], lhs(kc), wb[:, kc, GOFF:GOFF + 48], kc == 0, kc == 7, r=[xk, 'win_wb'], w=['pm%d' % pi])
                    gi = i % 2
                    ACTF(P, go[gi][:], pm[pi][:, 0:48], AF.Sigmoid, r=['pm%d' % pi], w=['go%d' % gi])
                    DMA(P, K.gates_d[rows, :], go[gi][:], r=['go%d' % gi], q='pool')
                    for j in range(2):
                        pi = nmm[0] % 4
                        nmm[0] += 1
                        mm_group(pm[pi][:], 'pm%d' % pi, lhs, lambda kc, j=j: wb[:, kc, ZOFF + 512 * j:ZOFF + 512 * (j + 1)], [xk])
                        fi = nfo[0] % 4
                        nfo[0] += 1
                        ACTF(P, fo[fi][:], pm[pi][:], AF.Silu, r=['pm%d' % pi], w=['fo%d' % fi])
                        DMA(P, K.zs_d[rows, 512 * j:512 * (j + 1)], fo[fi][:], r=['fo%d' % fi], q='pool')
            P.barrier()

        with ExitStack() as B:
            sb = lambda name, shape, dt: B.enter_context(nc.sbuf_tensor('%sB%d' % (name, li), shape, dt))
            ps = lambda name, shape, dt: B.enter_context(nc.psum_tensor('%sB%d' % (name, li), shape, dt))
            w1st = sb('w1st', [64, 32, 256], F32)
            w1b = [sb('w1b0', [64, 32, 256], BF16), sb('w1b1', [64, 32, 256], BF16)]
            w2st = sb('w2st', [128, 2, 64], F32)
            w2b = [sb('w2b0', [128, 2, 96], BF16), sb('w2b1', [128, 2, 64], BF16)]
            pest = sb('pest', [64, 32], F32)
            peb = [sb('peb0', [64, 32], BF16), sb('peb1', [64, 32], BF16)]
            hb = [sb('hb0', [128, 2], F32), sb('hb1', [128, 2], F32)]
            xin = [sb('xin0', [64, S], BF16), sb('xin1', [64, S], BF16)]
            hid = sb('hid', [128, 2, 512], BF16)
            ph = [ps('ph0', [128, 512], F32), ps('ph1', [128, 512], F32)]
            pb = ps('pb', [128, 512], F32)
            po = ps('po', [128, 512], F32)

            MEMSET(P, 'pool', kcmpT[:], 0.0, w=['kcmpT'])
            MEMSET(P, 'pool', vcmpa[:], 0.0, w=['vcmpa'])
            MEMSET(P, 'pool', hid[:], 0.0, w=['hid'])
            for kv in range(2):
                nm = 'cmp_k' if kv == 0 else 'cmp_v'
                DMA(P, w1st[:], prm[nm + '_w1'].rearrange("(l d) h -> d l h", d=64), w=['w1st'])
                COPY(P, 'dve', w1b[kv][:], w1st[:], r=['w1st'], w=['w1b%d' % kv])
                DMA(P, w2st[:], prm[nm + '_w2'].rearrange("(m p) d -> p m d", p=128), w=['w2st'])
                if kv == 0:
                    MEMSET(P, 'pool', w2b[0][:], 0.0, w=['w2b0'])
                    COPY(P, 'dve', w2b[0][:, :, 32:96], w2st[:], r=['w2st', 'w2b0'], w=['w2b0'])
                else:
                    COPY(P, 'dve', w2b[1][:], w2st[:], r=['w2st'], w=['w2b1'])
                DMA(P, pest[:], prm[nm + '_pe'].rearrange("l d -> d l"), w=['pest'], slow=True)
                COPY(P, 'dve', peb[kv][:], pest[:], r=['pest'], w=['peb%d' % kv])
                for m in range(2):
                    for l in range(32):
                        MM(P, pb[:, m:m + 1], w1b[kv][:, l, m * 128:(m + 1) * 128], peb[kv][:, l:l + 1], l == 0, l == 31,
                           r=['w1b%d' % kv, 'peb%d' % kv], w=['pb'])
                COPY(P, 'dve', hb[kv][:], pb[:, 0:2], r=['pb'], w=['hb%d' % kv])
            DMA(P, kcmpT[96:103, :, :], K.c_KAc.unsqueeze(1).broadcast_to([7, NG, 512]), r=['kcmpT'], w=['kcmpT'])
            it = 0
            for g in range(NG):
                for kv in range(2):
                    xb = it % 2
                    it += 1
                    src = K.kcT_d if kv == 0 else K.vcT_d
                    DMA(P, xin[xb][:], src[g * 64:(g + 1) * 64, :], w=['xin%d' % xb])
                    for m in range(2):
                        for l in range(32):
                            MM(P, ph[m][:, 0:511], w1b[kv][:, l, m * 128:(m + 1) * 128], xin[xb][:, l:l + 16 * 510 + 1:16],
                               l == 0, l == 31, r=['w1b%d' % kv, 'xin%d' % xb], w=['ph%d' % m])
                        ACTF(P, hid[:, m, 0:511], ph[m][:, 0:511], AF.Silu, r=['ph%d' % m, 'hb%d' % kv], w=['hid'], bias=hb[kv][:, m:m + 1])
                    if kv == 0:
                        for m in range(2):
                            MM(P, po[0:96, 0:511], w2b[0][:, m, :], hid[:, m, 0:511], m == 0, m == 1, r=['w2b0', 'hid'], w=['po'])
                        COPY(P, 'dve', kcmpT[0:96, g, 0:511], po[0:96, 0:511], r=['po'], w=['kcmpT'])
                    else:
                        for nt in range(4):
                            for m in range(2):
                                MM(P, po[:, nt * 64:(nt + 1) * 64], hid[:, m, nt * 128:(nt + 1) * 128], w2b[1][:, m, :], m == 0, m == 1,
                                   r=['w2b1', 'hid'], w=['po'])
                        COPY(P, 'dve', vcmpa[:, :, g, 0:64], po[:, 0:256].rearrange("p (a b) -> p a b", a=4), r=['po'], w=['vcmpa'])
            MEMSET(P, 'pool', vcmpa[:, :, :, 64:65], 1.0, r=['vcmpa'], w=['vcmpa'])
            P.barrier()

        if DBG.get('stop') == 'nsaB':
            return
        nsa_attention(P, nc, K, li, kcmpT, vcmpa)
        if DBG.get('stop') == 'nsaC':
            return

    out_proj_phase(P, nc, K, li, prm['w_out'], hsrc, hdst, final_g, token_major_src=True)


THR_DECAY = 48.0


def _thr():
    return DBG.get("thr", THR_DECAY)


def _slope(h):
    return 2.0 ** (-(h + 1) / 2.0)


def nsa_attention(P, nc, K, li, kcmpT, vcmpa):
    with ExitStack() as Cx:
        sb = lambda name, shape, dt: Cx.enter_context(nc.sbuf_tensor('%sC%d' % (name, li), shape, dt))
        ps = lambda name, shape, dt: Cx.enter_context(nc.psum_tensor('%sC%d' % (name, li), shape, dt))
        ident = K.ident_b
        OV = sb('OV', [128, 4, 128], BF16)
        causal = sb('causal', [128, 4, 512], BF16)
        band = sb('band', [128, 8, 512], BF16)
        cmpm = sb('cmpm', [128, 5, 512], BF16)
        FT = sb('FT', [128, 256], F32)
        zer = sb('zer', [128, 512], BF16)
        ksT = sb('ksT', [128, S], BF16)
        vsa = sb('vsa', [128, 64, 65], BF16)
        qa = [sb('qa%d' % i, [128, 4, 4, 512], BF16) for i in range(2)]
        kwT = [sb('kwT%d' % i, [128, 1024], BF16) for i in range(2)]
        vwa = [sb('vwa%d' % i, [128, 8, 65], BF16) for i in range(2)]
        gt = [sb('gt%d' % i, [128, 4, 48], F32) for i in range(2)]
        zt = [sb('zt%d' % i, [128, 4, 256], BF16) for i in range(2)]
        pc = [[sb('pc%d_%d' % (r, nt), [128, 512], BF16) for nt in range(4)] for r in range(4)]
        pt = [sb('pt%d' % i, [128, 512], BF16) for i in range(4)]
        oacc = sb('oacc', [128, 4, 4, 64], F32)
        tmpo = [sb('tmpo%d' % i, [128, 4, 64], F32) for i in range(2)]
        rcs = sb('rcs', [128, 4, 4], F32)
        rr = [sb('rr%d' % i, [128, 4], F32) for i in range(2)]
        ww = [sb('ww%d' % i, [128, 4], F32) for i in range(2)]
        impacc = [sb('impacc%d' % i, [128, 128], F32) for i in range(2)]
        impf = [sb('impf%d' % i, [128, 128], F32) for i in range(2)]
        work = [sb('work%d' % i, [128, 128], F32) for i in range(2)]
        m8a = [sb('m8a%d' % i, [128, 8], F32) for i in range(2)]
        m8b = [sb('m8b%d' % i, [128, 8], F32) for i in range(2)]
        selb = [sb('selb%d' % i, [128, 128], BF16) for i in range(4)]
        ozb = [sb('ozb%d' % i, [128, 4, 256], BF16) for i in range(2)]
        Sps = [ps('S%d' % i, [128, 512], F32) for i in range(3)]
        Ops = [ps('O%d' % i, [128, 512], F32) for i in range(2)]
        pimp = [ps('pimp%d' % i, [128, 512], F32) for i in range(2)]
        pst = ps('pst', [128, 1024], BF16)

        DMA(P, OV[:], K.c_OV, w=['const'])
        DMA(P, causal[:], K.c_causal, w=['const'])
        DMA(P, band[:], K.c_band, w=['const'])
        DMA(P, cmpm[:], K.c_cmpmask, w=['const'])
        DMA(P, FT[:], K.c_FT, w=['const'])
        MEMSET(P, 'pool', zer[:], 0.0, w=['const'])
        MEMSET(P, 'pool', vsa[:, :, 64:65], 1.0, w=['vsa1'])
        for i in range(2):
            MEMSET(P, 'pool', vwa[i][:, :, 64:65], 1.0, w=['vwa1_%d' % i])
            MEMSET(P, 'pool', qa[i][0:32, :, :, :], 0.0, w=['qsel%d' % i])
            MEMSET(P, 'pool', kwT[i][0:32, :], 0.0, w=['kwT0_%d' % i])
        DMA(P, ksT[0:32, :], K.c_OH, w=['ksToh'])
        DMA(P, ksT[96:103, :], K.c_KA, w=['ksTaug'])
        st = {'s': 0, 'p': 0, 'o': 0, 'x': 0}

        def combine(oi, first, gcol, r, keep_rc, cb):
            O3 = Ops[oi][:, 0:260].rearrange("p (a b) -> p a b", a=4)
            x = st['x'] % 2
            st['x'] += 1
            TS(P, 'dve', rr[x][:], O3[:, :, 64], 1e-30, ALU.max, r=['O%d' % oi], w=['rr%d' % x])
            P.add('dve', lambda e: e.reciprocal(out=rr[x][:], in_=rr[x][:]), r=['rr%d' % x], w=['rr%d' % x])
            if keep_rc:
                COPY(P, 'dve', rcs[:, r, :], rr[x][:], r=['rr%d' % x], w=['rcs%d' % r])
            TT(P, 'dve', ww[x][:], rr[x][:], gt[cb][:, :, gcol], ALU.mult, r=['rr%d' % x, 'gt%d' % cb], w=['ww%d' % x])
            wbc = ww[x][:].unsqueeze(2).broadcast_to([128, 4, 64])
            if first:
                TT(P, 'dve', oacc[:, :, r, :], O3[:, :, 0:64], wbc, ALU.mult, r=['O%d' % oi, 'ww%d' % x], w=['oacc%d' % r])
            else:
                TT(P, 'dve', tmpo[x][:], O3[:, :, 0:64], wbc, ALU.mult, r=['O%d' % oi, 'ww%d' % x], w=['tmpo%d' % x])
                TT(P, 'pool', oacc[:, :, r, :], oacc[:, :, r, :], tmpo[x][:], ALU.add, r=['tmpo%d' % x, 'oacc%d' % r], w=['oacc%d' % r])

        def run_stream(jobs):
            n = len(jobs)
            LAG = 2
            for i in range(n + LAG):
                if i < n:
                    j = jobs[i]
                    si = st['s'] % 3
                    st['s'] += 1
                    nm = len(j['mms'])
                    for idx, (l_, r_, rk) in enumerate(j['mms']):
                        MM(P, Sps[si][:], l_, r_, idx == 0, idx == nm - 1, r=rk, w=['S%d' % si])
                    if j['pbuf'] is None:
                        pi = st['p'] % 4
                        st['p'] += 1
                        j['pbuf'] = (pt[pi], 'pt%d' % pi)
                    ACTF(P, j['pbuf'][0][:], Sps[si][:], AF.Exp, r=['S%d' % si], w=[j['pbuf'][1]])
                if i >= LAG:
                    j = jobs[i - LAG]
                    oi = j['oi']
                    if j['first']:
                        MM(P, Ops[oi][:, 0:260], zer[:, 0:128], zer[:, 0:260], True, False, r=['const'], w=['O%d' % oi], skip=True)
                    pb, pk = j['pbuf']
                    for sub in range(4):
                        MM(P, Ops[oi][:, sub * 65:(sub + 1) * 65], pb[:, sub * 128:(sub + 1) * 128], j['v'], False,
                           (j['last'] and sub == 3), r=[pk] + j['vk'], w=['O%d' % oi], skip=True)
                    if j['last']:
                        j['fin'](oi)

        it = 0
        for g in range(DBG.get('c_ng', NG)):
            DMA(P, ksT[32:96, :], K.ksT_d[g * 64:(g + 1) * 64, :], w=['ksT'])
            for q4 in range(4):
                DMA(P, vsa[:, q4 * 16:(q4 + 1) * 16, 0:64],
                    K.vs_d[q4 * 2048:(q4 + 1) * 2048, g * 64:(g + 1) * 64].rearrange("(kt p) d -> p kt d", p=128), r=['vsa'], w=['vsa'])
            for c in range(DBG.get('c_nc', NC)):
                cb = it % 2
                it += 1
                cs = slice(c * 512, (c + 1) * 512)
                ntmax = c // 4
                sel_kts, win_rs, cmp_nts = [], [], []
                for r in range(4):
                    sl = _slope(4 * g + r)
                    sel_kts.append([kt for kt in range(4 * c + 4) if sl * max(0, 512 * c - 128 * kt - 127) <= _thr()])
                    win_rs.append([rw for rw in range(4 if c == 0 else 0, 8) if sl * max(0, 385 - 128 * rw) <= _thr()])
                    cmp_nts.append([nt for nt in range(ntmax + 1)
                                    if sl * max(0.0, 512 * c - (2048 * nt + 2047.5) - 48.0) <= _thr()])
                VD = 1000 if DBG.get('one_ver') else 16
                selvers = sorted(set(kt // VD for r in range(4) for kt in sel_kts[r]))
                vers = sorted(set(selvers) | {0})
                qkeys = []
                for v in vers:
                    DMA(P, qa[cb][32:96, v, :, :], K.qT_d[g * 256:(g + 1) * 256, cs].rearrange("(r d) t -> d r t", d=64), w=['qa%d_%d' % (cb, v)])
                    DMA(P, qa[cb][96:103, v, :, :], K.c_QA[g * 4:(g + 1) * 4, :, cs].rearrange("h r t -> r h t"), w=['qaug%d_%d' % (cb, v)])
                    qkeys += ['qa%d_%d' % (cb, v), 'qaug%d_%d' % (cb, v)]
                q0k = ['qa%d_0' % cb, 'qaug%d_0' % cb, 'qsel%d' % cb]
                k0 = (c - 1) * 512
                if c == 0:
                    DMA(P, kwT[cb][32:96, 512:1024], K.kwT_d[g * 64:(g + 1) * 64, 0:512], w=['kwT%d' % cb])
                    DMA(P, kwT[cb][96:103, 512:1024], K.c_KA[:, 0:512], w=['kwTa%d' % cb])
                    DMA(P, vwa[cb][:, 4:8, 0:64], K.vw_d[0:512, g * 64:(g + 1) * 64].rearrange("(kt p) d -> p kt d", p=128), w=['vwa%d' % cb])
                else:
                    DMA(P, kwT[cb][32:96, :], K.kwT_d[g * 64:(g + 1) * 64, k0:k0 + 1024], w=['kwT%d' % cb])
                    DMA(P, kwT[cb][96:103, :], K.c_KA[:, k0:k0 + 1024], w=['kwTa%d' % cb])
                    DMA(P, vwa[cb][:, :, 0:64], K.vw_d[k0:k0 + 1024, g * 64:(g + 1) * 64].rearrange("(kt p) d -> p kt d", p=128), w=['vwa%d' % cb])
                DMA(P, gt[cb][:], K.gates_d[cs, :].rearrange("(s p) c -> p s c", p=128), w=['gt%d' % cb])
                DMA(P, zt[cb][:], K.zs_d[cs, g * 256:(g + 1) * 256].rearrange("(s p) c -> p s c", p=128), w=['zt%d' % cb])

                jobs = []
                for r in range(4):
                    for nt in cmp_nts[r]:
                        mms = [(kcmpT[0:103, g, nt * 128:(nt + 1) * 128], qa[cb][0:103, 0, r, :], q0k + ['kcmpT'])]
                        if nt == ntmax:
                            mms.append((ident[:], cmpm[:, c % 4, :], ['const']))
                        elif nt == ntmax - 1 and c % 4 == 0 and not DBG.get('no_elif'):
                            mms.append((ident[:], cmpm[:, 4, :], ['const']))
                        jobs.append(dict(mms=mms, pbuf=(pc[r][nt], 'pc%d_%d' % (r, nt)), v=vcmpa[:, nt, g, :], vk=['vcmpa'],
                                         oi=(st['o'] + r) % 2, first=(nt == cmp_nts[r][0]), last=(nt == cmp_nts[r][-1]),
                                         fin=(lambda oi, r=r: combine(oi, True, 0 * 16 + g * 4 + r, r, True, cb))))
                st['o'] += 4
                run_stream(jobs)

                for sub in range(4):
                    ti = 4 * c + sub
                    x2 = sub % 2
                    pb_ = pimp[x2]
                    pk_ = 'pimp%d' % x2
                    for r in range(4):
                        for nt in cmp_nts[r]:
                            MM(P, pb_[:, r * 128:(r + 1) * 128], pc[r][nt][:, sub * 128:(sub + 1) * 128], OV[:, nt, :],
                               nt == cmp_nts[r][0], nt == cmp_nts[r][-1], r=['pc%d_%d' % (r, nt), 'const'], w=[pk_])
                    ia, if_, wk, ma, mb = impacc[x2], impf[x2], work[x2], m8a[x2], m8b[x2]
                    kx = '_%d' % x2
                    for r in range(4):
                        if r == 0:
                            TS(P, 'dve', ia[:], pb_[:, 0:128], rcs[:, r, sub:sub + 1], ALU.mult, r=[pk_, 'rcs%d' % r], w=['impacc' + kx])
                        else:
                            STT(P, ia[:], pb_[:, r * 128:(r + 1) * 128], rcs[:, r, sub:sub + 1], ia[:], ALU.mult, ALU.add,
                                r=[pk_, 'rcs%d' % r, 'impacc' + kx], w=['impacc' + kx])
                    TT(P, 'dve', if_[:], ia[:], FT[:, 127 - 2 * ti:255 - 2 * ti], ALU.add, r=['impacc' + kx, 'const'], w=['impf' + kx])
                    MEMSET(P, 'dve', if_[:, 0:1], 1e9, r=['impf' + kx], w=['impf' + kx])
                    P.add('dve', (lambda e, ma=ma, if_=if_: e.max(out=ma[:], in_=if_[:])), r=['impf' + kx], w=['m8a' + kx])
                    P.add('dve', (lambda e, ma=ma, if_=if_, wk=wk: e.match_replace(out=wk[:], in_to_replace=ma[:], in_values=if_[:], imm_value=-3.0e38)),
                          r=['impf' + kx, 'm8a' + kx], w=['work' + kx])
                    P.add('dve', (lambda e, mb=mb, wk=wk: e.max(out=mb[:], in_=wk[:])), r=['work' + kx], w=['m8b' + kx])
                    TS(P, 'dve', selb[sub][:], if_[:], mb[:, 7:8], ALU.is_lt, DBG.get('maskv', MASKV), ALU.mult, r=['impf' + kx, 'm8b' + kx], w=['selb%d' % sub])
                    pcol = slice((sub % 2) * 128, (sub % 2) * 128 + 128)
                    pkey = 'pst%d' % (sub % 2)
                    TR(P, pst[:, pcol], selb[sub][:], ident[:], r=['selb%d' % sub, 'const'], w=[pkey])
                    for v in selvers:
                        COPY(P, 'dve', qa[cb][0:32, v, :, sub * 128:(sub + 1) * 128],
                             pst[32 * v:32 * v + 32, pcol].unsqueeze(1).broadcast_to([32, 4, 128]), r=[pkey], w=['qsel%d' % cb])

                jobs = []
                for r in range(4):
                    for rw in win_rs[r]:
                        mms = [(kwT[cb][0:103, rw * 128:(rw + 1) * 128], qa[cb][0:103, 0, r, :], q0k + ['kwT%d' % cb, 'kwTa%d' % cb, 'kwT0_%d' % cb]),
                               (ident[:], band[:, rw, :], ['const'])]
                        jobs.append(dict(mms=mms, pbuf=None, v=vwa[cb][:, rw, :], vk=['vwa%d' % cb, 'vwa1_%d' % cb],
                                         oi=(st['o'] + r) % 2, first=(rw == win_rs[r][0]), last=(rw == win_rs[r][-1]),
                                         fin=(lambda oi, r=r: combine(oi, False, 2 * 16 + g * 4 + r, r, False, cb))))
                st['o'] += 4
                run_stream(jobs)

                jobs = []
                for r in range(4):
                    for kt in sel_kts[r]:
                        mms = [(ksT[0:103, kt * 128:(kt + 1) * 128], qa[cb][0:103, kt // VD, r, :],
                                qkeys + ['qsel%d' % cb, 'ksT', 'ksTaug', 'ksToh'])]
                        if kt >= 4 * c:
                            mms.append((ident[:], causal[:, kt - 4 * c, :], ['const']))
                        jobs.append(dict(mms=mms, pbuf=None, v=vsa[:, kt, :], vk=['vsa', 'vsa1'],
                                         oi=(st['o'] + r) % 2, first=(kt == sel_kts[r][0]), last=(kt == sel_kts[r][-1]),
                                         fin=(lambda oi, r=r: combine(oi, False, 1 * 16 + g * 4 + r, r, False, cb))))
                st['o'] += 4
                run_stream(jobs)

                TT(P, 'pool', ozb[cb][:], oacc[:].rearrange("p s r d -> p s (r d)"), zt[cb][:], ALU.mult,
                   r=['oacc0', 'oacc1', 'oacc2', 'oacc3', 'zt%d' % cb], w=['ozb%d' % cb])
                DMA(P, K.oz_d[cs, g * 256:(g + 1) * 256].rearrange("(s p) d -> p s d", p=128), ozb[cb][:], r=['ozb%d' % cb])
        P.barrier()


def out_proj_phase(P, nc, K, li, wout, hsrc, hdst, final_g, token_major_src):
    with ExitStack() as Dx:
        sb = lambda name, shape, dt: Dx.enter_context(nc.sbuf_tensor('%sD%d' % (name, li), shape, dt))
        ps = lambda name, shape, dt: Dx.enter_context(nc.psum_tensor('%sD%d' % (name, li), shape, dt))
        wb = sb('wb', [128, 8, D], BF16)
        stage = [sb('wst0', [128, D], F32), sb('wst1', [128, D], F32)]
        ozt = [sb('ozt%d' % i, [128, D], BF16) for i in range(2)]
        ozT = [sb('ozT%d' % i, [128, 8, 128], BF16) for i in range(2)]
        ht = [sb('ht%d' % i, [128, D], F32) for i in range(2)]
        hn = [sb('hn%d' % i, [128, D], F32) for i in range(2)]
        K.pT = ps('pT', [128, 8, 128], BF16)
        pm = [ps('pm%d' % i, [128, 512], F32) for i in range(4)]
        gfb = None
        if final_g is not None:
            gfb = sb('gfb', [128, D], F32)
            DMA(P, gfb[:], final_g.partition_broadcast(128), w=['gb'])
        load_weight_bf16(P, K, wb, wout, D, stage, 'wout')
        for i in range(NT):
            b = i % 2
            rows = slice(i * 128, (i + 1) * 128)
            DMA(P, ozt[b][:], K.oz_d[rows, :], w=['ozt%d' % b])
            DMA(P, ht[b][:], hsrc[rows, :], w=['ht%d' % b])
            transpose_tile(P, K, ozt[b], ozT[b][:], 'ozt%d' % b, 'ozT%d' % b, 'act' if i % 2 == 0 else 'dve')
            for half in range(2):
                pi = (2 * i + half) % 4
                for kc in range(8):
                    MM(P, pm[pi][:], ozT[b][:, kc, :], wb[:, kc, half * 512:(half + 1) * 512], kc == 0, kc == 7,
                       r=['ozT%d' % b, 'wout_wb'], w=['pm%d' % pi])
                TT(P, 'dve', hn[b][:, half * 512:(half + 1) * 512], pm[pi][:], ht[b][:, half * 512:(half + 1) * 512], ALU.add,
                   r=['pm%d' % pi, 'ht%d' % b], w=['hn%d_%d' % (b, half)])
            hk = ['hn%d_0' % b, 'hn%d_1' % b]
            if final_g is None:
                DMA(P, hdst[rows, :], hn[b][:], r=hk, q='pool')
            else:
                final_norm_store(P, K, hn[b], hk, gfb, ht[b], ['ht%d' % b], [hdst[rows, :]], [slice(0, 128)])
        P.barrier()


def final_norm_store(P, K, hn, hk, gfb, outbuf, okeys, dsts, prs):
    STT(P, K.junkf[:], hn[:], 1.0, hn[:], ALU.mult, ALU.mult, r=hk, w=['junkf', 'ssF'], accum=K.ss[:])
    ACTF(P, K.sq[:], K.ss[:], AF.Sqrt, r=['ssF'], w=['sqF'], scale=1.0 / D, bias=EPS)
    P.add('dve', lambda e: e.reciprocal(out=K.rs[:], in_=K.sq[:]), r=['sqF'], w=['rsF'])
    STT(P, outbuf[:], hn[:], K.rs[:], gfb[:], ALU.mult, ALU.mult, r=hk + ['rsF', 'gb'], w=okeys)
    for d_, pr in zip(dsts, prs):
        DMA(P, d_, outbuf[pr, :], r=okeys)


def s5_layer(P, nc, K, li, prm, hsrc, hdst, final_g):
    hv_src = hsrc.rearrange("(g n s) d -> g s n d", n=64, s=8)
    hv_dst = hdst.rearrange("(g n s) d -> g s n d", n=64, s=8)
    TWO_PI = 2.0 * math.pi

    with ExitStack() as A:
        sb = lambda name, shape, dt: A.enter_context(nc.sbuf_tensor('%sA%d' % (name, li), shape, dt))
        ps = lambda name, shape, dt: A.enter_context(nc.psum_tensor('%sA%d' % (name, li), shape, dt))
        wb = sb('wb', [128, 8, 2048], BF16)
        stage = [sb('wst0', [128, 2048], F32), sb('wst1', [128, 2048], F32)]
        gb = sb('gb', [128, D], F32)
        xt = [sb('xt0', [128, D], F32), sb('xt1', [128, D], F32)]
        xnb = [sb('xnb0', [128, D], BF16), sb('xnb1', [128, D], BF16)]
        xT4 = [sb('xT40', [128, 8, 512], BF16), sb('xT41', [128, 8, 512], BF16)]
        fo = [sb('fo%d' % i, [128, 512], BF16) for i in range(4)]
        K.pT = ps('pT', [128, 8, 128], BF16)
        pm = [ps('pm%d' % i, [128, 512], F32) for i in range(4)]
        load_weight_bf16(P, K, wb, prm['w_in'], 2048, stage, 'win')
        DMA(P, gb[:], prm['norm'].partition_broadcast(128), w=['gb'])
        cnt = 0
        for seg in range(NC):
            cb = seg % 2
            for tt in range(4):
                b = (4 * seg + tt) % 2
                DMA(P, xt[b][0:64, :], hv_src[seg, 2 * tt], w=['xt%da' % b])
                DMA(P, xt[b][64:128, :], hv_src[seg, 2 * tt + 1], w=['xt%db' % b])
                rmsnorm_tile(P, K, xt[b][:], gb[:], xnb[b][:], 'A', ['xt%da' % b, 'xt%db' % b], 'xnb%d' % b)
                transpose_tile(P, K, xnb[b], xT4[cb][:, :, tt * 128:(tt + 1) * 128], 'xnb%d' % b, 'xT4%d' % cb,
                               'act' if tt % 2 == 0 else 'dve')
            for m in range(16):
                pi = cnt % 4
                fi = cnt % 4
                cnt += 1
                for kc in range(8):
                    MM(P, pm[pi][:], wb[:, kc, m * 128:(m + 1) * 128], xT4[cb][:, kc, :], kc == 0, kc == 7,
                       r=['xT4%d' % cb, 'win_wb'], w=['pm%d' % pi])
                if m < 8:
                    COPY(P, 'dve', fo[fi][:], pm[pi][:], r=['pm%d' % pi], w=['fo%d' % fi])
                    DMA(P, K.uT_d[m * 128:(m + 1) * 128, seg * 512:(seg + 1) * 512], fo[fi][:], r=['fo%d' % fi], q='pool')
                else:
                    ACTF(P, fo[fi][:], pm[pi][:], AF.Silu, r=['pm%d' % pi], w=['fo%d' % fi])
                    DMA(P, K.zsT_d[(m - 8) * 128:(m - 7) * 128, seg * 512:(seg + 1) * 512], fo[fi][:], r=['fo%d' % fi], q='pool')
        P.barrier()
    if DBG.get('stop') == 's5p1':
        return

    with ExitStack() as B:
        sb = lambda name, shape, dt: B.enter_context(nc.sbuf_tensor('%sB%d' % (name, li), shape, dt))
        ps = lambda name, shape, dt: B.enter_context(nc.psum_tensor('%sB%d' % (name, li), shape, dt))
        identf = K.ident_f
        Tb = sb('Tb', [128, 64, 128], BF16)
        Pm = sb('Pm', [128, 64, 2, 64], BF16)
        QmR = sb('QmR', [64, 64, 8, 16], BF16)
        QmI = sb('QmI', [64, 64, 8, 16], BF16)
        ar8 = sb('ar8', [64, 64], F32)
        ai8 = sb('ai8', [64, 64], F32)
        dcol = sb('dcol', [128, 8], F32)
        pA = ps('pA', [128, 512], F32)
        pB = ps('pB', [128, 512], F32)
        pC = ps('pC', [128, 512], F32)
        pD = ps('pD', [128, 512], F32)
        pK = ps('pK', [128, 1024], F32)
        DMA(P, dcol[:], prm['d'].rearrange("g c -> (g c)").rearrange("(k p) -> p k", p=128), w=['dcol'], slow=True)

        with ExitStack() as T:
            tb = lambda name, shape, dt: T.enter_context(nc.sbuf_tensor('%sT%d' % (name, li), shape, dt))
            t64 = lambda name: tb(name, [64, 64], F32)
            lamre_g, lamim_g, lre, lim, dtb = t64('lamre_g'), t64('lamim_g'), t64('lre'), t64('lim'), t64('dtb')
            xr_, yi_, mag, sn, cs_, abr, abi = t64('xr_'), t64('yi_'), t64('mag'), t64('sn'), t64('cs_'), t64('abr'), t64('abi')
            den, nr, cfr, cfi, tA, tB, tC = t64('den'), t64('nr'), t64('cfr'), t64('cfi'), t64('tA'), t64('tB'), t64('tC')
            tI = tb('tI', [64, 64], I32)
            bre = tb('bre', [64, 64, 16], F32)
            bim = tb('bim', [64, 64, 16], F32)
            bbr = tb('bbr', [64, 64, 16], F32)
            bbi = tb('bbi', [64, 64, 16], F32)
            t3 = tb('t3', [64, 64, 16], F32)
            t4 = tb('t4', [64, 64, 16], F32)
            Lr = tb('Lr', [64, 9, 64], F32)
            Li = tb('Li', [64, 9, 64], F32)
            Wr = tb('Wr', [64, 64, 8, 16], F32)
            Wi = tb('Wi', [64, 64, 8, 16], F32)
            Cst = tb('Cst', [128, 8, 64], F32)
            CrT = tb('CrT', [64, 64, 16], F32)
            CiT = tb('CiT', [64, 64, 16], F32)
            nCrT = tb('nCrT', [64, 64, 16], F32)
            nCiT = tb('nCiT', [64, 64, 16], F32)
            Ktsb = tb('Ktsb', [128, 64, 16], F32)

            def V(eng, out, a, b, op, r, w):
                TT(P, eng, out, a, b, op, r=r, w=w)

            def VN(eng, out, a, b, op, r, w):
                TT(P, eng, out, a, b, op, r=r, w=w, nosync=True)

            DMA(P, lamre_g[:], prm['lambda_re'], w=['lamre_g'])
            DMA(P, lamim_g[:], prm['lambda_im'], w=['lamim_g'])
            TR(P, pA[0:64, 0:64], lamre_g[:], identf[0:64, 0:64], r=['lamre_g', 'identf'], w=['pA'])
            TR(P, pA[0:64, 64:128], lamim_g[:], identf[0:64, 0:64], r=['lamim_g', 'identf'], w=['pA'])
            COPY(P, 'dve', lre[:], pA[0:64, 0:64], r=['pA'], w=['lre'])
            COPY(P, 'dve', lim[:], pA[0:64, 64:128], r=['pA'], w=['lim'])
            DMA(P, dtb[:], prm['log_dt'].partition_broadcast(64), w=['dtb'])
            ACTF(P, dtb[:], dtb[:], AF.Exp, r=['dtb'], w=['dtb'])
            TS(P, 'dve', lre[:], lre[:], -1e-4, ALU.min, r=['lre'], w=['lre'])
            V('dve', xr_[:], lre[:], dtb[:], ALU.mult, ['lre', 'dtb'], ['xr_'])
            V('dve', yi_[:], lim[:], dtb[:], ALU.mult, ['lim', 'dtb'], ['yi_'])
            ACTF(P, mag[:], xr_[:], AF.Exp, r=['xr_'], w=['mag'])

            def sin_shift(out, okey, shift):
                TS(P, 'dve', tA[:], yi_[:], 1.0 / TWO_PI, ALU.mult, shift / TWO_PI, ALU.add, r=['yi_'], w=['tA'])
                COPY(P, 'dve', tI[:], tA[:], r=['tA'], w=['tI'])
                COPY(P, 'dve', tB[:], tI[:], r=['tI'], w=['tB'])
                STT(P, tC[:], tB[:], -TWO_PI, yi_[:], ALU.mult, ALU.add, r=['tB', 'yi_'], w=['tC'])
                TS(P, 'dve', tC[:], tC[:], shift, ALU.add, 3.141592, ALU.min, r=['tC'], w=['tC'])
                TS(P, 'dve', tC[:], tC[:], -3.141592, ALU.max, r=['tC'], w=['tC'])
                ACTF(P, out, tC[:], AF.Sin, r=['tC'], w=[okey])

            sin_shift(sn[:], 'sn', 0.0)
            sin_shift(cs_[:], 'cs_', math.pi / 2.0)
            V('dve', abr[:], mag[:], cs_[:], ALU.mult, ['mag', 'cs_'], ['abr'])
            V('dve', abi[:], mag[:], sn[:], ALU.mult, ['mag', 'sn'], ['abi'])
            V('dve', den[:], lre[:], lre[:], ALU.mult, ['lre'], ['den'])
            V('dve', tA[:], lim[:], lim[:], ALU.mult, ['lim'], ['tA'])
            V('dve', den[:], den[:], tA[:], ALU.add, ['den', 'tA'], ['den'])
            P.add('dve', lambda e: e.reciprocal(out=den[:], in_=den[:]), r=['den'], w=['den'])
            TS(P, 'dve', nr[:], abr[:], -1.0, ALU.add, r=['abr'], w=['nr'])
            V('dve', tA[:], nr[:], lre[:], ALU.mult, ['nr', 'lre'], ['tA'])
            V('dve', tB[:], abi[:], lim[:], ALU.mult, ['abi', 'lim'], ['tB'])
            V('dve', tA[:], tA[:], tB[:], ALU.add, ['tA', 'tB'], ['tA'])
            V('dve', cfr[:], tA[:], den[:], ALU.mult, ['tA', 'den'], ['cfr'])
            V('dve', tA[:], abi[:], lre[:], ALU.mult, ['abi', 'lre'], ['tA'])
            V('dve', tB[:], nr[:], lim[:], ALU.mult, ['nr', 'lim'], ['tB'])
            V('dve', tA[:], tA[:], tB[:], ALU.subtract, ['tA', 'tB'], ['tA'])
            V('dve', cfi[:], tA[:], den[:], ALU.mult, ['tA', 'den'], ['cfi'])
            DMA(P, bre[:], prm['b_re'].rearrange("g p c -> p g c"), w=['bre'])
            DMA(P, bim[:], prm['b_im'].rearrange("g p c -> p g c"), w=['bim'])
            bc = lambda ap2: ap2.unsqueeze(2).broadcast_to([64, 64, 16])
            V('dve', bbr[:], bre[:], bc(cfr[:]), ALU.mult, ['bre', 'cfr'], ['bbr'])
            V('dve', t3[:], bim[:], bc(cfi[:]), ALU.mult, ['bim', 'cfi'], ['t3'])
            V('dve', bbr[:], bbr[:], t3[:], ALU.subtract, ['bbr', 't3'], ['bbr'])
            V('dve', bbi[:], bim[:], bc(cfr[:]), ALU.mult, ['bim', 'cfr'], ['bbi'])
            V('dve', t3[:], bre[:], bc(cfi[:]), ALU.mult, ['bre', 'cfi'], ['t3'])
            V('dve', bbi[:], bbi[:], t3[:], ALU.add, ['bbi', 't3'], ['bbi'])
            MEMSET(P, 'dve', Lr[:, 0, :], 1.0, w=['L'])
            MEMSET(P, 'dve', Li[:, 0, :], 0.0, r=['L'], w=['L'])
            for tau in range(1, 9):
                V('dve', tA[:], Lr[:, tau - 1, :], abr[:], ALU.mult, ['L', 'abr'], ['tA'])
                V('dve', tB[:], Li[:, tau - 1, :], abi[:], ALU.mult, ['L', 'abi'], ['tB'])
                V('dve', tC[:], Lr[:, tau - 1, :], abi[:], ALU.mult, ['L', 'abi'], ['tC'])
                V('dve', den[:], Li[:, tau - 1, :], abr[:], ALU.mult, ['L', 'abr'], ['den'])
                V('dve', Lr[:, tau, :], tA[:], tB[:], ALU.subtract, ['tA', 'tB', 'L'], ['L'])
                V('dve', Li[:, tau, :], tC[:], den[:], ALU.add, ['tC', 'den', 'L'], ['L'])
            for tau in range(8):
                lrb = bc(Lr[:, tau, :])
                lib = bc(Li[:, tau, :])
                V('dve', Wr[:, :, tau, :], bbr[:], lrb, ALU.mult, ['bbr', 'L'], ['Wr'])
                V('dve', t3[:], bbi[:], lib, ALU.mult, ['bbi', 'L'], ['t3'])
                V('dve', Wr[:, :, tau, :], Wr[:, :, tau, :], t3[:], ALU.subtract, ['Wr', 't3'], ['Wr'])
                V('dve', Wi[:, :, tau, :], bbi[:], lrb, ALU.mult, ['bbi', 'L'], ['Wi'])
                V('dve', t4[:], bbr[:], lib, ALU.mult, ['bbr', 'L'], ['t4'])
                V('dve', Wi[:, :, tau, :], Wi[:, :, tau, :], t4[:], ALU.add, ['Wi', 't4'], ['Wi'])
            for nm, dstT in (('c_re', CrT), ('c_im', CiT)):
                DMA(P, Cst[:], prm[nm].rearrange("g c p -> (g c) p").rearrange("(k r) p -> r k p", r=128), r=[], w=['Cst'])
                for k in range(8):
                    TR(P, pK[0:64, k * 128:(k + 1) * 128], Cst[:, k, :], identf[:], r=['Cst', 'identf'], w=['pK'])
                COPY(P, 'dve', dstT[:].rearrange("p g c -> p (g c)"), pK[0:64, :], r=['pK'], w=[nm])
            TS(P, 'dve', nCrT[:], CrT[:], -1.0, ALU.mult, r=['c_re'], w=['nCrT'])
            TS(P, 'dve', nCiT[:], CiT[:], -1.0, ALU.mult, r=['c_im'], w=['nCiT'])
            for g in range(64):
                MM(P, pK[:, g * 16:(g + 1) * 16], Wr[:, g, :, :].rearrange("p t c -> p (t c)"), CrT[:, g, :], True, False,
                   r=['Wr', 'c_re'], w=['pK'])
                MM(P, pK[:, g * 16:(g + 1) * 16], Wi[:, g, :, :].rearrange("p t c -> p (t c)"), nCiT[:, g, :], False, True,
                   r=['Wi', 'nCiT'], w=['pK'])
            COPY(P, 'dve', Ktsb[:].rearrange("p g c -> p (g c)"), pK[:, :], r=['pK'], w=['Ktsb'])
            DMA(P, K.Kd_d.rearrange("l c g o -> (l c) g o"), Ktsb[:], r=['Ktsb'], w=['Kd'])
            for gb4 in range(16):
                pp = pA if gb4 % 2 == 0 else pB
                pk_ = 'pA' if gb4 % 2 == 0 else 'pB'
                for gi in range(4):
                    g = gb4 * 4 + gi
                    for ri, W_ in enumerate((Wr, Wi)):
                        TR(P, pp[:, (gi * 2 + ri) * 64:(gi * 2 + ri + 1) * 64], W_[:, g, :, :].rearrange("p t c -> p (t c)"),
                           identf[0:64, 0:64], r=['Wr', 'Wi', 'identf'], w=[pk_])
                COPY(P, 'dve', Pm[:, gb4 * 4:(gb4 + 1) * 4, :, :].rearrange("p g r q -> p (g r q)"), pp[:, :], r=[pk_], w=['Pm'])
            for t in range(8):
                lrb = bc(Lr[:, t + 1, :])
                lib = bc(Li[:, t + 1, :])
                V('dve', t3[:], CrT[:], lrb, ALU.mult, ['c_re', 'L'], ['t3'])
                V('dve', t4[:], nCiT[:], lib, ALU.mult, ['nCiT', 'L'], ['t4'])
                V('dve', QmR[:, :, t, :], t3[:], t4[:], ALU.add, ['t3', 't4'], ['QmR'])
                V('dve', t3[:], nCrT[:], lib, ALU.mult, ['nCrT', 'L'], ['t3'])
                V('dve', t4[:], nCiT[:], lrb, ALU.mult, ['nCiT', 'L'], ['t4'])
                V('dve', QmI[:, :, t, :], t3[:], t4[:], ALU.add, ['t3', 't4'], ['QmI'])
            COPY(P, 'dve', ar8[:], Lr[:, 8, :], r=['L'], w=['ar8'])
            COPY(P, 'dve', ai8[:], Li[:, 8, :], r=['L'], w=['ai8'])
            TAP(P, nc, 'abr', abr[:], [64, 64], F32, ['abr'])
            TAP(P, nc, 'abi', abi[:], [64, 64], F32, ['abi'])
            TAP(P, nc, 'bbr', bbr[:], [64, 64, 16], F32, ['bbr'])
            TAP(P, nc, 'bbi', bbi[:], [64, 64, 16], F32, ['bbi'])
            TAP(P, nc, 'Lr', Lr[:], [64, 9, 64], F32, ['L'])
            TAP(P, nc, 'CrT', CrT[:], [64, 64, 16], F32, ['c_re'])
            TAP(P, nc, 'Ktsb', Ktsb[:], [128, 64, 16], F32, ['Ktsb'])
            TAP(P, nc, 'Pm', Pm[:], [128, 64, 2, 64], BF16, ['Pm'])
            TAP(P, nc, 'QmR', QmR[:], [64, 64, 8, 16], BF16, ['QmR'])
            TAP(P, nc, 'QmI', QmI[:], [64, 64, 8, 16], BF16, ['QmI'])
            P.barrier()
        if DBG.get('stop') == 's5setup':
            return

        with ExitStack() as T2:
            Tsb = T2.enter_context(nc.sbuf_tensor('TsbT%d' % li, [128, 64, 8, 16], F32))
            MEMSET(P, 'pool', Tsb[:], 0.0, w=['Tsb'])
            tkeys = []
            for tp in range(8):
                for lag in range(tp + 1):
                    key = 'Tsb_%d_%d' % (tp, lag)
                    tkeys.append(key)
                    DMA(P, Tsb[16 * tp:16 * tp + 16, :, 7 - tp + lag, :], K.Kd_d[lag].rearrange("c g o -> c g o"),
                        r=['Kd', 'Tsb'], w=[key])
            COPY(P, 'dve', Tb[:].rearrange("p g m -> p (g m)"), Tsb[:].rearrange("p g t c -> p (g t c)"), r=tkeys + ['Tsb'], w=['Tb'])
            TAP(P, nc, 'Tb', Tb[:], [128, 64, 128], BF16, ['Tb'])
            P.barrier()
        if DBG.get('stop') == 's5t2':
            return

        with ExitStack() as Sg:
            sg = lambda name, shape, dt: Sg.enter_context(nc.sbuf_tensor('%sS%d' % (name, li), shape, dt))
            Sel = sg('Sel', [128, 8, 8, 128], BF16)
            SelT = sg('SelT', [128, 8, 8, 128], BF16)
            DMA(P, Sel[:], K.c_Sel, w=['const'])
            DMA(P, SelT[:], K.c_SelT, w=['const'])
            useg = [sg('useg%d' % i, [128, 8, 512], BF16) for i in range(2)]
            Uall = sg('Uall', [128, 64, 64], BF16)
            Bcr = sg('Bcr', [64, 64, 64], F32)
            Bci = sg('Bci', [64, 64, 64], F32)
            Xb2 = sg('Xb2', [64, 2, 64, 64], BF16)
            X2 = [sg('X2_%d' % i, [64, 2, 64], F32) for i in range(2)]
            A2 = sg('A2', [64, 2, 64], F32)
            C2 = sg('C2', [64, 2, 64], F32)
            t1 = sg('t1', [64, 2, 64], F32)
            t2 = sg('t2', [64, 2, 64], F32)
            s1, s2, s3, s4 = sg('s1', [64, 64], F32), sg('s2', [64, 64], F32), sg('s3', [64, 64], F32), sg('s4', [64, 64], F32)
            Ysb = [sg('Ysb%d' % i, [128, 8, 64], BF16) for i in range(2)]
            yv = [sg('yv%d' % i, [128, 512], F32) for i in range(2)]
            yg = [sg('yg%d' % i, [128, 512], BF16) for i in range(2)]
            MEMSET(P, 'dve', X2[0][:], 0.0, w=['X2_0'])
            for hh in range(2):
                COPY(P, 'dve', A2[:, hh, :], ar8[:], w=['A2'])
                COPY(P, 'dve', C2[:, hh, :], ai8[:], w=['C2'])
            step = 0
            for seg in range(NC):
                ub = seg % 2
                uk = 'useg%d' % ub
                DMA(P, useg[ub][:], K.uT_d[:, seg * 512:(seg + 1) * 512].rearrange("(k p) t -> p k t", p=128), w=[uk])
                for k in range(8):
                    for gl in range(8):
                        for s in range(8):
                            MM(P, pA[:, gl * 64:(gl + 1) * 64], Sel[:, gl, s, :], useg[ub][:, k, s * 64:(s + 1) * 64], s == 0, s == 7,
                               r=['const', uk], w=['pA'])
                    COPY(P, 'act', Uall[:, k * 8:(k + 1) * 8, :].rearrange("p g n -> p (g n)"), pA[:, :], r=['pA'], w=['Uall%d' % k])
                    for gl in range(8):
                        g = 8 * k + gl
                        MM(P, pB[0:64, gl * 64:(gl + 1) * 64], Pm[:, g, 0, :], Uall[:, g, :], True, True, r=['Uall%d' % k], w=['pB'])
                        MM(P, pC[0:64, gl * 64:(gl + 1) * 64], Pm[:, g, 1, :], Uall[:, g, :], True, True, r=['Uall%d' % k], w=['pC'])
                    COPY(P, 'dve', Bcr[:, k * 8:(k + 1) * 8, :].rearrange("p g n -> p (g n)"), pB[0:64, :], r=['pB'], w=['Bcr%d' % k])
                    COPY(P, 'act', Bci[:, k * 8:(k + 1) * 8, :].rearrange("p g n -> p (g n)"), pC[0:64, :], r=['pC'], w=['Bci%d' % k])
                bk = ['Bcr%d' % k for k in range(8)] + ['Bci%d' % k for k in range(8)]
                for n in range(64):
                    cur = step % 2
                    nxt = 1 - cur
                    step += 1
                    kx, kxn = 'X2_%d' % cur, 'X2_%d' % nxt
                    COPY(P, 'act', Xb2[:, :, :, n], X2[cur][:], r=[kx], w=['Xb2'])
                    VN('dve', t1[:], A2[:], X2[cur][:], ALU.mult, [kx, 'A2'], ['t1'])
                    VN('dve', t2[:], C2[:], X2[cur][:], ALU.mult, [kx, 'C2'], ['t2'])
                    VN('dve', s1[:], t1[:, 0, :], t2[:, 1, :], ALU.subtract, ['t1', 't2'], ['s1'])
                    VN('dve', X2[nxt][:, 0, :], s1[:], Bcr[:, :, n], ALU.add, ['s1'] + bk, [kxn])
                    VN('dve', s3[:], t1[:, 1, :], t2[:, 0, :], ALU.add, ['t1', 't2'], ['s3'])
                    VN('dve', X2[nxt][:, 1, :], s3[:], Bci[:, :, n], ALU.add, ['s3'] + bk, [kxn])
                for k in range(8):
                    yb = k % 2
                    for gl in range(8):
                        g = 8 * k + gl
                        o_ = pD[:, gl * 64:(gl + 1) * 64]
                        MM(P, o_, Tb[:, g, :], Uall[:, g, :], True, False, r=['Uall%d' % k], w=['pD'])
                        MM(P, o_, QmR[:, g, :, :].rearrange("p t c -> p (t c)"), Xb2[:, 0, g, :], False, False, r=['Xb2'], w=['pD'])
                        MM(P, o_, QmI[:, g, :, :].rearrange("p t c -> p (t c)"), Xb2[:, 1, g, :], False, True, r=['Xb2'], w=['pD'])
                    COPY(P, 'act', Ysb[yb][:].rearrange("p g n -> p (g n)"), pD[:, :], r=['pD'], w=['Ysb%d' % yb])
                    pY = pB if k % 2 == 0 else pC
                    pyk = 'pB' if k % 2 == 0 else 'pC'
                    for t in range(8):
                        for gl in range(8):
                            MM(P, pY[:, t * 64:(t + 1) * 64], SelT[:, gl, t, :], Ysb[yb][:, gl, :], gl == 0, gl == 7,
                               r=['const', 'Ysb%d' % yb], w=[pyk])
                    STT(P, yv[yb][:], useg[ub][:, k, :], dcol[:, k:k + 1], pY[:, :], ALU.mult, ALU.add, r=[uk, pyk, 'dcol'], w=['yv%d' % yb])
                    ACTF(P, yg[yb][:], yv[yb][:], AF.Gelu_apprx_tanh, r=['yv%d' % yb], w=['yg%d' % yb])
                    DMA(P, K.ygT_d[k * 128:(k + 1) * 128, seg * 512:(seg + 1) * 512], yg[yb][:], r=['yg%d' % yb], q='pool')
            P.barrier()
    if DBG.get('stop') == 's5scan':
        return

    with ExitStack() as Cx:
        sb = lambda name, shape, dt: Cx.enter_context(nc.sbuf_tensor('%sC%d' % (name, li), shape, dt))
        ps = lambda name, shape, dt: Cx.enter_context(nc.psum_tensor('%sC%d' % (name, li), shape, dt))
        wg = sb('wg', [128, 8, 2048], BF16)
        wo = sb('wo', [128, 8, D], BF16)
        stage = [sb('wst0', [128, 2048], F32), sb('wst1', [128, 2048], F32)]
        ygs = [sb('ygs%d' % i, [128, 8, 512], BF16) for i in range(2)]
        zss = [sb('zss%d' % i, [128, 8, 512], BF16) for i in range(2)]
        sig = [sb('sig%d' % i, [128, 512], F32) for i in range(2)]
        tga = [sb('tga%d' % i, [128, 512], F32) for i in range(2)]
        ozT = [sb('ozT%d' % i, [128, 8, 512], BF16) for i in range(2)]
        ht = [sb('ht%d' % i, [128, D], F32) for i in range(2)]
        hn = [sb('hn%d' % i, [128, D], F32) for i in range(2)]
        pa = [ps('pa%d' % i, [128, 512], F32) for i in range(2)]
        pb = [ps('pb%d' % i, [128, 512], F32) for i in range(2)]
        pm = [ps('pm%d' % i, [128, 512], F32) for i in range(4)]
        gfb = None
        if final_g is not None:
            gfb = sb('gfb', [128, D], F32)
            DMA(P, gfb[:], final_g.partition_broadcast(128), w=['gb'])
        load_weight_bf16(P, K, wg, prm['w_glu'], 2048, stage, 'wglu')
        load_weight_bf16(P, K, wo, prm['w_out'], D, stage, 'wout')
        for seg in range(NC):
            sbi = seg % 2
            cols = slice(seg * 512, (seg + 1) * 512)
            DMA(P, ygs[sbi][:], K.ygT_d[:, cols].rearrange("(k p) t -> p k t", p=128), w=['ygs%d' % sbi])
            DMA(P, zss[sbi][:], K.zsT_d[:, cols].rearrange("(k p) t -> p k t", p=128), w=['zss%d' % sbi])
            for m in range(8):
                x = m % 2
                for kc in range(8):
                    MM(P, pa[x][:], wg[:, kc, m * 128:(m + 1) * 128], ygs[sbi][:, kc, :], kc == 0, kc == 7,
                       r=['ygs%d' % sbi, 'wglu_wb'], w=['pa%d' % x])
                for kc in range(8):
                    MM(P, pb[x][:], wg[:, kc, 1024 + m * 128:1024 + (m + 1) * 128], ygs[sbi][:, kc, :], kc == 0, kc == 7,
                       r=['ygs%d' % sbi, 'wglu_wb'], w=['pb%d' % x])
                ACTF(P, sig[x][:], pb[x][:], AF.Sigmoid, r=['pb%d' % x], w=['sig%d' % x])
                TT(P, 'dve', tga[x][:], pa[x][:], sig[x][:], ALU.mult, r=['pa%d' % x, 'sig%d' % x], w=['tga%d' % x])
                TT(P, 'pool', ozT[sbi][:, m, :], tga[x][:], zss[sbi][:, m, :], ALU.mult, r=['tga%d' % x, 'zss%d' % sbi], w=['ozT%d_%d' % (sbi, m)])
            ozk = ['ozT%d_%d' % (sbi, m) for m in range(8)]
            for tt in range(4):
                b = (4 * seg + tt) % 2
                DMA(P, ht[b][0:64, :], hv_src[seg, 2 * tt], w=['ht%da' % b])
                DMA(P, ht[b][64:128, :], hv_src[seg, 2 * tt + 1], w=['ht%db' % b])
                for half in range(2):
                    pi = (2 * tt + half) % 4
                    for kc in range(8):
                        MM(P, pm[pi][:], ozT[sbi][:, kc, tt * 128:(tt + 1) * 128], wo[:, kc, half * 512:(half + 1) * 512], kc == 0, kc == 7,
                           r=ozk + ['wout_wb'], w=['pm%d' % pi])
                    TT(P, 'dve', hn[b][:, half * 512:(half + 1) * 512], pm[pi][:], ht[b][:, half * 512:(half + 1) * 512], ALU.add,
                       r=['pm%d' % pi, 'ht%da' % b, 'ht%db' % b], w=['hn%d_%d' % (b, half)])
                hk = ['hn%d_0' % b, 'hn%d_1' % b]
                dsts = [hv_dst[seg, 2 * tt], hv_dst[seg, 2 * tt + 1]]
                prs = [slice(0, 64), slice(64, 128)]
                if final_g is None:
                    for d_, pr in zip(dsts, prs):
                        DMA(P, d_, hn[b][pr, :], r=hk)
                else:
                    final_norm_store(P, K, hn[b], hk, gfb, ht[b], ['ht%da' % b, 'ht%db' % b], dsts, prs)
        P.barrier()


def build(layers=(0, 1, 2, 3), with_final=True):
    nc = bass.Bass("TRN2", target_bir_lowering=False)
    K = Ctx()
    x = nc.dram_tensor("x", [S, D], F32, kind="ExternalInput").ap()
    y = nc.dram_tensor("y", [S, D], F32, kind="ExternalOutput").ap()
    prm = {}
    for li in layers:
        prm[li] = {}
        for nm, shp in (NSA_PARAMS if li % 2 == 0 else S5_PARAMS):
            prm[li][nm] = nc.dram_tensor("l%d_%s" % (li, nm), shp, F32, kind="ExternalInput").ap()
    fng = nc.dram_tensor("final_norm", [D], F32, kind="ExternalInput").ap()
    for nm, shp, dt in CONST_SPECS:
        setattr(K, 'c_' + nm, nc.dram_tensor("c_" + nm, shp, dt, kind="ExternalInput").ap())
    scr = lambda nm, shp, dt: nc.dram_tensor("scr_" + nm, shp, dt, kind=("ExternalOutput" if nm in DBG.get('dump', ()) else "Internal")).ap()
    K.qT_d = scr('qT', [1024, S], BF16)
    K.kcT_d = scr('kcT', [256, S], BF16)
    K.vcT_d = scr('vcT', [256, S], BF16)
    K.ksT_d = scr('ksT', [256, S], BF16)
    K.kwT_d = scr('kwT', [256, S], BF16)
    K.vs_d = scr('vs', [S, 256], BF16)
    K.vw_d = scr('vw', [S, 256], BF16)
    K.gates_d = scr('gates', [S, 48], F32)
    K.zs_d = scr('zs', [S, 1024], BF16)
    K.oz_d = scr('oz', [S, 1024], BF16)
    K.uT_d = scr('uT', [1024, S], BF16)
    K.zsT_d = scr('zsT', [1024, S], BF16)
    K.ygT_d = scr('ygT', [1024, S], BF16)
    K.Kd_d = scr('Kd', [8, 16, 64, 16], F32)
    P = Prog(nc)
    with ExitStack() as G:
        gs = lambda name, shape, dt: G.enter_context(nc.sbuf_tensor(name, shape, dt))
        K.ident_b = gs('ident_b', [128, 128], BF16)
        K.ident_f = gs('ident_f', [128, 128], F32)
        K.junkf = gs('junkf', [128, D], F32)
        K.ss = gs('ss', [128, 1], F32)
        K.sq = gs('sq', [128, 1], F32)
        K.rs = gs('rs', [128, 1], F32)
        DMA(P, K.ident_b[:], K.c_ident_b, w=['const'])
        DMA(P, K.ident_f[:], K.c_ident_f, w=['identf'])
        P.barrier()
        hsrc = x
        for li in layers:
            fg = fng if (with_final and li == layers[-1]) else None
            if li % 2 == 0:
                nsa_layer(P, nc, K, li, prm[li], hsrc, y, fg)
            else:
                s5_layer(P, nc, K, li, prm[li], hsrc, y, fg)
            hsrc = y
        run_prog(nc, P)
    return nc


ALL_INPUT_NAMES = (
    'x',
    'l0_norm',
    'l0_w_in',
    'l0_cmp_k_pe',
    'l0_cmp_k_w1',
    'l0_cmp_k_w2',
    'l0_cmp_v_pe',
    'l0_cmp_v_w1',
    'l0_cmp_v_w2',
    'l0_w_out',
    'l1_norm',
    'l1_w_in',
    'l1_log_dt',
    'l1_lambda_re',
    'l1_lambda_im',
    'l1_b_re',
    'l1_b_im',
    'l1_c_re',
    'l1_c_im',
    'l1_d',
    'l1_w_glu',
    'l1_w_out',
    'l2_norm',
    'l2_w_in',
    'l2_cmp_k_pe',
    'l2_cmp_k_w1',
    'l2_cmp_k_w2',
    'l2_cmp_v_pe',
    'l2_cmp_v_w1',
    'l2_cmp_v_w2',
    'l2_w_out',
    'l3_norm',
    'l3_w_in',
    'l3_log_dt',
    'l3_lambda_re',
    'l3_lambda_im',
    'l3_b_re',
    'l3_b_im',
    'l3_c_re',
    'l3_c_im',
    'l3_d',
    'l3_w_glu',
    'l3_w_out',
    'final_norm',
)


_NC_CACHE = {}


def make_in_map(inputs, b, layers=(0, 1, 2, 3)):
    C = host_constants()
    m = {'x': np.ascontiguousarray(inputs['x'][b], dtype=np.float32)}
    for li in layers:
        for nm, shp in (NSA_PARAMS if li % 2 == 0 else S5_PARAMS):
            key = 'l%d_%s' % (li, nm)
            m[key] = np.ascontiguousarray(inputs[key], dtype=np.float32)
    m['final_norm'] = np.ascontiguousarray(inputs['final_norm'], dtype=np.float32)
    for nm, shp, dt in CONST_SPECS:
        m['c_' + nm] = C[nm]
    return m


def kernel(**inputs):
    inputs = {k: np.asarray(inputs[k]) for k in ALL_INPUT_NAMES}
    if 'full' not in _NC_CACHE:
        _NC_CACHE['full'] = build()
    nc = _NC_CACHE['full']
    in_maps = [make_in_map(inputs, c % 4) for c in range(8)]
    res = run_bass_kernel_spmd(nc, in_maps, core_ids=list(range(8)))
    out = np.stack([np.asarray(res.results[b]['y'], dtype=np.float32) for b in range(4)], axis=0)
    return out
```

```python
import math
from contextlib import ExitStack
import numpy as np
import ml_dtypes
import concourse.bass as bass
import concourse.mybir as mybir
from concourse.bass_utils import run_bass_kernel_spmd

F32 = mybir.dt.float32
BF16 = mybir.dt.bfloat16
I32 = mybir.dt.int32
AF = mybir.ActivationFunctionType
ALU = mybir.AluOpType
AX = mybir.AxisListType
NPBF = ml_dtypes.bfloat16

S = 8192
D = 1024
NH = 16
DH = 64
NG = 4
NT = S // 128
NC = S // 512
NSA_IN = 3632
EPS = 1e-6
MASKV = -30000.0

ENGS = ['pe', 'act', 'dve', 'pool', 'sp']
KRING = 8
SAME_ENGINE_SYNC = {'act': True, 'dve': True, 'pool': True, 'pe': False, 'sp': False}


class Prog:
    def __init__(self, nc):
        self.nc = nc
        self.q = {e: [] for e in ENGS}
        self.ncomp = {e: 0 for e in ENGS}
        self.ndma = {e: 0 for e in ENGS}
        self.lastw = {}
        self.readers = {}

    def add(self, eng, fn, r=(), w=(), dma=False, nosync=False):
        deps = {}
        dd = set()

        def addtok(t):
            if t is None:
                return
            if t[0] == 'c':
                if deps.get(t[1], 0) < t[2]:
                    deps[t[1]] = t[2]
            else:
                dd.add(t)

        for k in r:
            addtok(self.lastw.get(k))
        for k in w:
            addtok(self.lastw.get(k))
            rd = self.readers.get(k)
            if rd:
                for e2, n2 in rd[0].items():
                    addtok(('c', e2, n2))
                for t in rd[1]:
                    addtok(t)
        if dma:
            tok = ('d', eng, self.ndma[eng])
            self.ndma[eng] += 1
        else:
            self.ncomp[eng] += 1
            tok = ('c', eng, self.ncomp[eng])
        if nosync:
            deps.pop(eng, None)
        self.q[eng].append(('op', fn, deps, dd, tok))
        for k in w:
            self.lastw[k] = tok
            self.readers[k] = [{}, set()]
        for k in r:
            rd = self.readers.setdefault(k, [{}, set()])
            if tok[0] == 'c':
                if rd[0].get(tok[1], 0) < tok[2]:
                    rd[0][tok[1]] = tok[2]
            else:
                rd[1].add(tok)
        return tok

    def barrier(self):
        sc = dict(self.ncomp)
        sd = dict(self.ndma)
        for e in ENGS:
            self.q[e].append(('bar', sc, sd))
        self.lastw = {}
        self.readers = {}

    def run_engine(self, ename, eng, semc, semd):
        waited_c = {}
        waited_d = {}

        def wait_c(f, n):
            if n <= 0 or waited_c.get(f, 0) >= n:
                return
            eng.wait_ge(semc[f], n)
            waited_c[f] = n

        def wait_d(qn, i):
            if i < 0:
                return
            slot = i % KRING
            tgt = 16 * (i // KRING + 1)
            if waited_d.get((qn, slot), 0) >= tgt:
                return
            eng.wait_ge(semd[qn][slot], tgt)
            waited_d[(qn, slot)] = tgt

        for item in self.q[ename]:
            if item[0] == 'bar':
                _, sc, sd = item
                for f in ENGS:
                    if f == ename and not SAME_ENGINE_SYNC[ename]:
                        continue
                    wait_c(f, sc[f])
                for qn in ENGS:
                    n = sd[qn]
                    for i in range(max(0, n - KRING), n):
                        wait_d(qn, i)
                continue
            _, fn, deps, dd, tok = item
            for f, n in deps.items():
                if f == ename and not SAME_ENGINE_SYNC[ename]:
                    continue
                wait_c(f, n)
            for t in dd:
                wait_d(t[1], t[2])
            if tok[0] == 'd':
                i = tok[2]
                if i >= KRING:
                    wait_d(ename, i - KRING)
                ins = fn(eng)
                ins.then_inc(semd[ename][i % KRING], 16)
            else:
                ins = fn(eng)
                ins.then_inc(semc[ename], 1)
        n = self.ndma[ename]
        for i in range(max(0, n - KRING), n):
            wait_d(ename, i)


def run_prog(nc, prog):
    with ExitStack() as st:
        semc = {e: st.enter_context(nc.semaphore('c_' + e)) for e in ENGS}
        semd = {e: [st.enter_context(nc.semaphore('d_%s_%d' % (e, i))) for i in range(KRING)] for e in ENGS}
        block = st.enter_context(nc.Block())

        @block.tensor
        def _(eng):
            prog.run_engine('pe', eng, semc, semd)

        @block.scalar
        def _(eng):
            prog.run_engine('act', eng, semc, semd)

        @block.vector
        def _(eng):
            prog.run_engine('dve', eng, semc, semd)

        @block.gpsimd
        def _(eng):
            prog.run_engine('pool', eng, semc, semd)

        @block.sync
        def _(eng):
            prog.run_engine('sp', eng, semc, semd)


def DMA(P, out, in_, r=(), w=(), q='sp', slow=False):
    if slow:
        P.add(q, lambda e: e.dma_start(out=out, in_=in_, allow_slow_non_contiguous=True), r=r, w=w, dma=True)
    else:
        P.add(q, lambda e: e.dma_start(out=out, in_=in_), r=r, w=w, dma=True)


def MM(P, out, lhsT, rhs, start, stop, r=(), w=(), skip=False):
    P.add('pe', lambda e: e.matmul(out, lhsT=lhsT, rhs=rhs, start=start, stop=stop, skip_group_check=skip), r=r, w=w)


def TR(P, out, in_, ident, r=(), w=()):
    P.add('pe', lambda e: e.transpose(out=out, in_=in_, identity=ident), r=r, w=w)


def ACTF(P, out, in_, func, r=(), w=(), scale=None, bias=None, accum=None):
    kw = {}
    if scale is not None:
        kw['scale'] = scale
    if bias is not None:
        kw['bias'] = bias
    if accum is not None:
        kw['accum_out'] = accum
    P.add('act', lambda e: e.activation(out=out, in_=in_, func=func, **kw), r=r, w=w)


def COPY(P, eng, out, in_, r=(), w=()):
    if eng == 'act':
        P.add('act', lambda e: e.copy(out=out, in_=in_), r=r, w=w)
    else:
        P.add(eng, lambda e: e.tensor_copy(out=out, in_=in_), r=r, w=w)


def TT(P, eng, out, in0, in1, op, r=(), w=(), nosync=False):
    P.add(eng, lambda e: e.tensor_tensor(out=out, in0=in0, in1=in1, op=op), r=r, w=w, nosync=nosync)


def TS(P, eng, out, in0, s1, op0, s2=None, op1=None, r=(), w=()):
    if op1 is None:
        P.add(eng, lambda e: e.tensor_scalar(out=out, in0=in0, scalar1=s1, scalar2=None, op0=op0), r=r, w=w)
    else:
        P.add(eng, lambda e: e.tensor_scalar(out=out, in0=in0, scalar1=s1, scalar2=s2, op0=op0, op1=op1), r=r, w=w)


def STT(P, out, in0, scalar, in1, op0, op1, r=(), w=(), accum=None):
    if accum is None:
        P.add('dve', lambda e: e.scalar_tensor_tensor(out=out, in0=in0, scalar=scalar, in1=in1, op0=op0, op1=op1), r=r, w=w)
    else:
        P.add('dve', lambda e: e.scalar_tensor_tensor(out=out, in0=in0, scalar=scalar, in1=in1, op0=op0, op1=op1, accum_out=accum), r=r, w=w)


def MEMSET(P, eng, ap, val, r=(), w=()):
    P.add(eng, lambda e: e.memset(ap, val), r=r, w=w)


def _bf(x):
    return np.asarray(x, dtype=np.float32).astype(NPBF)


def _split3(c):
    c = np.asarray(c, dtype=np.float64)
    hi = _bf(c)
    r1 = c - hi.astype(np.float64)
    mid = _bf(r1)
    r2 = r1 - mid.astype(np.float64)
    lo = _bf(r2)
    return hi, mid, lo


_CONST_CACHE = {}


def host_constants():
    if _CONST_CACHE:
        return _CONST_CACHE
    C = {}
    C['ident_b'] = np.eye(128, dtype=np.float32).astype(NPBF)
    C['ident_f'] = np.eye(128, dtype=np.float32)
    hh = np.arange(1, NH + 1, dtype=np.float64)
    slopes = np.exp2(-8.0 * hh / NH).astype(np.float32).astype(np.float64)
    s_hi = _bf(slopes)
    s_lo = _bf(slopes - s_hi.astype(np.float64))
    sp = s_hi.astype(np.float64) + s_lo.astype(np.float64)
    t = np.arange(S, dtype=np.float64)
    QA = np.zeros((NH, 7, S), dtype=NPBF)
    for h in range(NH):
        QA[h, 0, :] = _bf(64.0 * s_hi[h].astype(np.float64))
        QA[h, 1, :] = _bf(64.0 * s_lo[h].astype(np.float64))
        QA[h, 2, :] = s_hi[h]
        QA[h, 3, :] = s_lo[h]
        c = np.float32(-(sp[h] * t)).astype(np.float64)
        a, b, cc = _split3(c)
        QA[h, 4, :] = a
        QA[h, 5, :] = b
        QA[h, 6, :] = cc
    C['QA'] = QA
    KA = np.zeros((7, S), dtype=NPBF)
    pos = np.arange(S)
    KA[0] = _bf(pos // 64)
    KA[1] = _bf(pos // 64)
    KA[2] = _bf(pos % 64)
    KA[3] = _bf(pos % 64)
    KA[4:7] = _bf(1.0)
    C['KA'] = KA
    KAc = np.zeros((7, 512), dtype=NPBF)
    n = np.arange(511)
    KAc[0, :511] = _bf(n // 4)
    KAc[1, :511] = _bf(n // 4)
    KAc[2, :511] = _bf(16.0 * (n % 4) + 15.5)
    KAc[3, :511] = _bf(16.0 * (n % 4) + 15.5)
    KAc[4:7, :511] = _bf(1.0)
    C['KAc'] = KAc
    OH = np.zeros((32, S), dtype=np.float32)
    OH[(np.arange(S) // 64) % 32, np.arange(S)] = 1.0
    C['OH'] = OH.astype(NPBF)
    OV = np.zeros((128, 4, 128), dtype=np.float32)
    for nn in range(511):
        cs = 16 * nn
        ce = cs + 31
        for j in range(128):
            if cs <= 64 * j + 63 and ce >= 64 * j:
                OV[nn % 128, nn // 128, j] = 1.0
    C['OV'] = OV.astype(NPBF)
    k = np.arange(128)[:, None, None]
    r = np.arange(4)[None, :, None]
    q = np.arange(512)[None, None, :]
    C['causal'] = np.where(128 * r + k > q, MASKV, 0.0).astype(np.float32).astype(NPBF)
    r8 = np.arange(8)[None, :, None]
    dwin = q - k + 512 - 128 * r8
    C['band'] = np.where((dwin >= 0) & (dwin < 512), 0.0, MASKV).astype(np.float32).astype(NPBF)
    r5 = np.array([0, 512, 1024, 1536, 2048])[None, :, None]
    C['cmpmask'] = np.where(16 * k + 31 - r5 > q, MASKV, 0.0).astype(np.float32).astype(NPBF)
    FT = np.zeros((128, 256), dtype=np.float32)
    for p in range(128):
        jr = 1 if p >= 64 else 0
        for c in range(255):
            jj = c - 127
            if jj == jr or jj == jr - 1:
                FT[p, c] = 1e9
            elif jj > jr:
                FT[p, c] = -1e9
    C['FT'] = FT
    Sel = np.zeros((128, 8, 8, 128), dtype=np.float32)
    SelT = np.zeros((128, 8, 8, 128), dtype=np.float32)
    for gl in range(8):
        for s in range(8):
            for c in range(16):
                Sel[16 * gl + c, gl, s, 16 * (7 - s) + c] = 1.0
                SelT[16 * s + c, gl, s, 16 * gl + c] = 1.0
    C['Sel'] = Sel.astype(NPBF)
    C['SelT'] = SelT.astype(NPBF)
    _CONST_CACHE.update(C)
    return C


CONST_SPECS = [
    ('ident_b', [128, 128], BF16), ('ident_f', [128, 128], F32), ('QA', [NH, 7, S], BF16),
    ('KA', [7, S], BF16), ('KAc', [7, 512], BF16), ('OH', [32, S], BF16),
    ('OV', [128, 4, 128], BF16), ('causal', [128, 4, 512], BF16), ('band', [128, 8, 512], BF16),
    ('cmpmask', [128, 5, 512], BF16), ('FT', [128, 256], F32),
    ('Sel', [128, 8, 8, 128], BF16), ('SelT', [128, 8, 8, 128], BF16),
]

NSA_PARAMS = [('norm', [D]), ('w_in', [D, NSA_IN]), ('cmp_k_pe', [32, 64]), ('cmp_k_w1', [2048, 256]),
              ('cmp_k_w2', [256, 64]), ('cmp_v_pe', [32, 64]), ('cmp_v_w1', [2048, 256]),
              ('cmp_v_w2', [256, 64]), ('w_out', [D, D])]
S5_PARAMS = [('norm', [D]), ('w_in', [D, 2048]), ('log_dt', [64]), ('lambda_re', [64, 64]),
             ('lambda_im', [64, 64]), ('b_re', [64, 64, 16]), ('b_im', [64, 64, 16]),
             ('c_re', [64, 16, 64]), ('c_im', [64, 16, 64]), ('d', [64, 16]),
             ('w_glu', [D, 2048]), ('w_out', [D, D])]


class Ctx:
    pass


DBG = {}


def TAP(P, nc, name, ap, shape, dt, r=()):
    if not DBG.get('taps'):
        return
    t = nc.dram_tensor('dbg_' + name, shape, dt, kind='ExternalOutput').ap()
    DMA(P, t, ap, r=list(r))


def load_weight_bf16(P, K, wb, wdram, ncols, stage, tag):
    for kc in range(8):
        sb = stage[kc % 2]
        key = 'wstage%d' % (kc % 2)
        DMA(P, sb[:, 0:ncols], wdram[kc * 128:(kc + 1) * 128, :], w=[key])
        eng = 'dve' if kc % 2 == 0 else 'pool'
        COPY(P, eng, wb[:, kc, :], sb[:, 0:ncols], r=[key], w=['%s_wb' % tag])


def rmsnorm_tile(P, K, xt, gb, xnb, tg, rkeys, wkey):
    STT(P, K.junkf[:], xt, 1.0, xt, ALU.mult, ALU.mult, r=rkeys, w=['junkf', 'ss' + tg], accum=K.ss[:])
    ACTF(P, K.sq[:], K.ss[:], AF.Sqrt, r=['ss' + tg], w=['sq' + tg], scale=1.0 / D, bias=EPS)
    P.add('dve', lambda e: e.reciprocal(out=K.rs[:], in_=K.sq[:]), r=['sq' + tg], w=['rs' + tg])
    STT(P, xnb, xt, K.rs[:], gb, ALU.mult, ALU.mult, r=rkeys + ['rs' + tg, 'gb'], w=[wkey])


def transpose_tile(P, K, src_b, dst, skey, dkey, evac_eng):
    for kc in range(8):
        TR(P, K.pT[:, kc, :], src_b[:, kc * 128:(kc + 1) * 128], K.ident_b[:], r=[skey, 'const'], w=['pT'])
    COPY(P, evac_eng, dst, K.pT[:], r=['pT'], w=[dkey])


def nsa_layer(P, nc, K, li, prm, hsrc, hdst, final_g):
    QOFF, KCOFF, VCOFF, KSOFF, VSOFF, KWOFF, VWOFF, GOFF, ZOFF = 0, 1024, 1280, 1536, 1792, 2048, 2304, 2560, 2608
    ident = K.ident_b
    with ExitStack() as L:
        sbL = lambda name, shape, dt: L.enter_context(nc.sbuf_tensor('%s_l%d' % (name, li), shape, dt))
        kcmpT = sbL('kcmpT', [128, NG, 512], BF16)
        vcmpa = sbL('vcmpa', [128, 4, NG, 65], BF16)

        with ExitStack() as A:
            sb = lambda name, shape, dt: A.enter_context(nc.sbuf_tensor('%sA%d' % (name, li), shape, dt))
            ps = lambda name, shape, dt: A.enter_context(nc.psum_tensor('%sA%d' % (name, li), shape, dt))
            wb = sb('wb', [128, 8, NSA_IN], BF16)
            stage = [sb('wst0', [128, NSA_IN], F32), sb('wst1', [128, NSA_IN], F32)]
            gb = sb('gb', [128, D], F32)
            xt = [sb('xt0', [128, D], F32), sb('xt1', [128, D], F32)]
            xnb = [sb('xnb0', [128, D], BF16), sb('xnb1', [128, D], BF16)]
            xT4 = [sb('xT40', [128, 8, 512], BF16), sb('xT41', [128, 8, 512], BF16)]
            fo = [sb('fo%d' % i, [128, 512], BF16) for i in range(4)]
            go = [sb('go%d' % i, [128, 48], F32) for i in range(2)]
            K.pT = ps('pT', [128, 8, 128], BF16)
            pm = [ps('pm%d' % i, [128, 512], F32) for i in range(4)]

            load_weight_bf16(P, K, wb, prm['w_in'], NSA_IN, stage, 'win')
            DMA(P, gb[:], prm['norm'].partition_broadcast(128), w=['gb'])
            nmm = [0]
            nfo = [0]

            def mm_group(out_ps, pkey, lhs_fn, rhs_fn, rk):
                for kc in range(8):
                    MM(P, out_ps, lhs_fn(kc), rhs_fn(kc), kc == 0, kc == 7, r=rk + ['win_wb'], w=[pkey])

            for c in range(NC):
                cb = c % 2
                for tt in range(4):
                    i = 4 * c + tt
                    b = i % 2
                    DMA(P, xt[b][:], hsrc[i * 128:(i + 1) * 128, :], w=['xt%d' % b])
                    rmsnorm_tile(P, K, xt[b][:], gb[:], xnb[b][:], 'A', ['xt%d' % b], 'xnb%d' % b)
                    transpose_tile(P, K, xnb[b], xT4[cb][:, :, tt * 128:(tt + 1) * 128], 'xnb%d' % b, 'xT4%d' % cb,
                                   'act' if tt % 2 == 0 else 'dve')
                xk = 'xT4%d' % cb
                fm = [(m, 'q') for m in range(8)] + [(0, 'kc'), (1, 'kc'), (0, 'vc'), (1, 'vc'), (0, 'ks'), (1, 'ks'), (0, 'kw'), (1, 'kw')]
                for (m, kind) in fm:
                    off = {'q': QOFF, 'kc': KCOFF, 'vc': VCOFF, 'ks': KSOFF, 'kw': KWOFF}[kind] + m * 128
                    dst = {'q': K.qT_d, 'kc': K.kcT_d, 'vc': K.vcT_d, 'ks': K.ksT_d, 'kw': K.kwT_d}[kind]
                    pi = nmm[0] % 4
                    nmm[0] += 1
                    mm_group(pm[pi][:], 'pm%d' % pi, lambda kc, off=off: wb[:, kc, off:off + 128],
                             lambda kc: xT4[cb][:, kc, :], [xk])
                    fi = nfo[0] % 4
                    nfo[0] += 1
                    if kind == 'q':
                        ACTF(P, fo[fi][:], pm[pi][:], AF.Copy, r=['pm%d' % pi], w=['fo%d' % fi], scale=0.125)
                    else:
                        COPY(P, 'dve', fo[fi][:], pm[pi][:], r=['pm%d' % pi], w=['fo%d' % fi])
                    DMA(P, dst[m * 128:(m + 1) * 128, c * 512:(c + 1) * 512], fo[fi][:], r=['fo%d' % fi], q='pool')
                for tt in range(4):
                    i = 4 * c + tt
                    rows = slice(i * 128, (i + 1) * 128)
                    lhs = lambda kc, tt=tt: xT4[cb][:, kc, tt * 128:(tt + 1) * 128]
                    pi = nmm[0] % 4
                    nmm[0] += 1
                    for kc in range(8):
                        MM(P, pm[pi][:, 0:256], lhs(kc), wb[:, kc, VSOFF:VSOFF + 256], kc == 0, kc == 7, r=[xk, 'win_wb'], w=['pm%d' % pi])
                    for kc in range(8):
                        MM(P, pm[pi][:, 256:512], lhs(kc), wb[:, kc, VWOFF:VWOFF + 256], kc == 0, kc == 7, r=[xk, 'win_wb'], w=['pm%d' % pi])
                    fi = nfo[0] % 4
                    nfo[0] += 1
                    COPY(P, 'dve', fo[fi][:], pm[pi][:], r=['pm%d' % pi], w=['fo%d' % fi])
                    DMA(P, K.vs_d[rows, :], fo[fi][:, 0:256], r=['fo%d' % fi], q='pool')
                    DMA(P, K.vw_d[rows, :], fo[fi][:, 256:512], r=['fo%d' % fi], q='pool')
                    pi = nmm[0] % 4
                    nmm[0] += 1
                    for kc in range(8):
                        MM(P, pm[pi][:, 0:48], lhs(kc), wb[:, kc, GOFF:GOFF + 48], kc == 0, kc == 7, r=[xk, 'win_wb'], w=['pm%d' % pi])
                    gi = i % 2
                    ACTF(P, go[gi][:], pm[pi][:, 0:48], AF.Sigmoid, r=['pm%d' % pi], w=['go%d' % gi])
                    DMA(P, K.gates_d[rows, :], go[gi][:], r=['go%d' % gi], q='pool')
                    for j in range(2):
                        pi = nmm[0] % 4
                        nmm[0] += 1
                        mm_group(pm[pi][:], 'pm%d' % pi, lhs, lambda kc, j=j: wb[:, kc, ZOFF + 512 * j:ZOFF + 512 * (j + 1)], [xk])
                        fi = nfo[0] % 4
                        nfo[0] += 1
                        ACTF(P, fo[fi][:], pm[pi][:], AF.Silu, r=['pm%d' % pi], w=['fo%d' % fi])
                        DMA(P, K.zs_d[rows, 512 * j:512 * (j + 1)], fo[fi][:], r=['fo%d' % fi], q='pool')
            P.barrier()

        with ExitStack() as B:
            sb = lambda name, shape, dt: B.enter_context(nc.sbuf_tensor('%sB%d' % (name, li), shape, dt))
            ps = lambda name, shape, dt: B.enter_context(nc.psum_tensor('%sB%d' % (name, li), shape, dt))
            w1st = sb('w1st', [64, 32, 256], F32)
            w1b = [sb('w1b0', [64, 32, 256], BF16), sb('w1b1', [64, 32, 256], BF16)]
            w2st = sb('w2st', [128, 2, 64], F32)
            w2b = [sb('w2b0', [128, 2, 96], BF16), sb('w2b1', [128, 2, 64], BF16)]
            pest = sb('pest', [64, 32], F32)
            peb = [sb('peb0', [64, 32], BF16), sb('peb1', [64, 32], BF16)]
            hb = [sb('hb0', [128, 2], F32), sb('hb1', [128, 2], F32)]
            xin = [sb('xin0', [64, S], BF16), sb('xin1', [64, S], BF16)]
            hid = sb('hid', [128, 2, 512], BF16)
            ph = [ps('ph0', [128, 512], F32), ps('ph1', [128, 512], F32)]
            pb = ps('pb', [128, 512], F32)
            po = ps('po', [128, 512], F32)

            MEMSET(P, 'pool', kcmpT[:], 0.0, w=['kcmpT'])
            MEMSET(P, 'pool', vcmpa[:], 0.0, w=['vcmpa'])
            MEMSET(P, 'pool', hid[:], 0.0, w=['hid'])
            for kv in range(2):
                nm = 'cmp_k' if kv == 0 else 'cmp_v'
                DMA(P, w1st[:], prm[nm + '_w1'].rearrange("(l d) h -> d l h", d=64), w=['w1st'])
                COPY(P, 'dve', w1b[kv][:], w1st[:], r=['w1st'], w=['w1b%d' % kv])
                DMA(P, w2st[:], prm[nm + '_w2'].rearrange("(m p) d -> p m d", p=128), w=['w2st'])
                if kv == 0:
                    MEMSET(P, 'pool', w2b[0][:], 0.0, w=['w2b0'])
                    COPY(P, 'dve', w2b[0][:, :, 32:96], w2st[:], r=['w2st', 'w2b0'], w=['w2b0'])
                else:
                    COPY(P, 'dve', w2b[1][:], w2st[:], r=['w2st'], w=['w2b1'])
                DMA(P, pest[:], prm[nm + '_pe'].rearrange("l d -> d l"), w=['pest'], slow=True)
                COPY(P, 'dve', peb[kv][:], pest[:], r=['pest'], w=['peb%d' % kv])
                for m in range(2):
                    for l in range(32):
                        MM(P, pb[:, m:m + 1], w1b[kv][:, l, m * 128:(m + 1) * 128], peb[kv][:, l:l + 1], l == 0, l == 31,
                           r=['w1b%d' % kv, 'peb%d' % kv], w=['pb'])
                COPY(P, 'dve', hb[kv][:], pb[:, 0:2], r=['pb'], w=['hb%d' % kv])
            DMA(P, kcmpT[96:103, :, :], K.c_KAc.unsqueeze(1).broadcast_to([7, NG, 512]), r=['kcmpT'], w=['kcmpT'])
            it = 0
            for g in range(NG):
                for kv in range(2):
                    xb = it % 2
                    it += 1
                    src = K.kcT_d if kv == 0 else K.vcT_d
                    DMA(P, xin[xb][:], src[g * 64:(g + 1) * 64, :], w=['xin%d' % xb])
                    for m in range(2):
                        for l in range(32):
                            MM(P, ph[m][:, 0:511], w1b[kv][:, l, m * 128:(m + 1) * 128], xin[xb][:, l:l + 16 * 510 + 1:16],
                               l == 0, l == 31, r=['w1b%d' % kv, 'xin%d' % xb], w=['ph%d' % m])
                        ACTF(P, hid[:, m, 0:511], ph[m][:, 0:511], AF.Silu, r=['ph%d' % m, 'hb%d' % kv], w=['hid'], bias=hb[kv][:, m:m + 1])
                    if kv == 0:
                        for m in range(2):
                            MM(P, po[0:96, 0:511], w2b[0][:, m, :], hid[:, m, 0:511], m == 0, m == 1, r=['w2b0', 'hid'], w=['po'])
                        COPY(P, 'dve', kcmpT[0:96, g, 0:511], po[0:96, 0:511], r=['po'], w=['kcmpT'])
                    else:
                        for nt in range(4):
                            for m in range(2):
                                MM(P, po[:, nt * 64:(nt + 1) * 64], hid[:, m, nt * 128:(nt + 1) * 128], w2b[1][:, m, :], m == 0, m == 1,
                                   r=['w2b1', 'hid'], w=['po'])
                        COPY(P, 'dve', vcmpa[:, :, g, 0:64], po[:, 0:256].rearrange("p (a b) -> p a b", a=4), r=['po'], w=['vcmpa'])
            MEMSET(P, 'pool', vcmpa[:, :, :, 64:65], 1.0, r=['vcmpa'], w=['vcmpa'])
            P.barrier()

        if DBG.get('stop') == 'nsaB':
            return
        nsa_attention(P, nc, K, li, kcmpT, vcmpa)
        if DBG.get('stop') == 'nsaC':
            return

    out_proj_phase(P, nc, K, li, prm['w_out'], hsrc, hdst, final_g, token_major_src=True)


THR_DECAY = 48.0


def _thr():
    return DBG.get("thr", THR_DECAY)


def _slope(h):
    return 2.0 ** (-(h + 1) / 2.0)


def nsa_attention(P, nc, K, li, kcmpT, vcmpa):
    with ExitStack() as Cx:
        sb = lambda name, shape, dt: Cx.enter_context(nc.sbuf_tensor('%sC%d' % (name, li), shape, dt))
        ps = lambda name, shape, dt: Cx.enter_context(nc.psum_tensor('%sC%d' % (name, li), shape, dt))
        ident = K.ident_b
        OV = sb('OV', [128, 4, 128], BF16)
        causal = sb('causal', [128, 4, 512], BF16)
        band = sb('band', [128, 8, 512], BF16)
        cmpm = sb('cmpm', [128, 5, 512], BF16)
        FT = sb('FT', [128, 256], F32)
        zer = sb('zer', [128, 512], BF16)
        ksT = sb('ksT', [128, S], BF16)
        vsa = sb('vsa', [128, 64, 65], BF16)
        qa = [sb('qa%d' % i, [128, 4, 4, 512], BF16) for i in range(2)]
        kwT = [sb('kwT%d' % i, [128, 1024], BF16) for i in range(2)]
        vwa = [sb('vwa%d' % i, [128, 8, 65], BF16) for i in range(2)]
        gt = [sb('gt%d' % i, [128, 4, 48], F32) for i in range(2)]
        zt = [sb('zt%d' % i, [128, 4, 256], BF16) for i in range(2)]
        pc = [[sb('pc%d_%d' % (r, nt), [128, 512], BF16) for nt in range(4)] for r in range(4)]
        pt = [sb('pt%d' % i, [128, 512], BF16) for i in range(4)]
        oacc = sb('oacc', [128, 4, 4, 64], F32)
        tmpo = [sb('tmpo%d' % i, [128, 4, 64], F32) for i in range(2)]
        rcs = sb('rcs', [128, 4, 4], F32)
        rr = [sb('rr%d' % i, [128, 4], F32) for i in range(2)]
        ww = [sb('ww%d' % i, [128, 4], F32) for i in range(2)]
        impacc = [sb('impacc%d' % i, [128, 128], F32) for i in range(2)]
        impf = [sb('impf%d' % i, [128, 128], F32) for i in range(2)]
        work = [sb('work%d' % i, [128, 128], F32) for i in range(2)]
        m8a = [sb('m8a%d' % i, [128, 8], F32) for i in range(2)]
        m8b = [sb('m8b%d' % i, [128, 8], F32) for i in range(2)]
        selb = [sb('selb%d' % i, [128, 128], BF16) for i in range(4)]
        ozb = [sb('ozb%d' % i, [128, 4, 256], BF16) for i in range(2)]
        Sps = [ps('S%d' % i, [128, 512], F32) for i in range(3)]
        Ops = [ps('O%d' % i, [128, 512], F32) for i in range(2)]
        pimp = [ps('pimp%d' % i, [128, 512], F32) for i in range(2)]
        pst = ps('pst', [128, 1024], BF16)

        DMA(P, OV[:], K.c_OV, w=['const'])
        DMA(P, causal[:], K.c_causal, w=['const'])
        DMA(P, band[:], K.c_band, w=['const'])
        DMA(P, cmpm[:], K.c_cmpmask, w=['const'])
        DMA(P, FT[:], K.c_FT, w=['const'])
        MEMSET(P, 'pool', zer[:], 0.0, w=['const'])
        MEMSET(P, 'pool', vsa[:, :, 64:65], 1.0, w=['vsa1'])
        for i in range(2):
            MEMSET(P, 'pool', vwa[i][:, :, 64:65], 1.0, w=['vwa1_%d' % i])
            MEMSET(P, 'pool', qa[i][0:32, :, :, :], 0.0, w=['qsel%d' % i])
            MEMSET(P, 'pool', kwT[i][0:32, :], 0.0, w=['kwT0_%d' % i])
        DMA(P, ksT[0:32, :], K.c_OH, w=['ksToh'])
        DMA(P, ksT[96:103, :], K.c_KA, w=['ksTaug'])
        st = {'s': 0, 'p': 0, 'o': 0, 'x': 0}

        def combine(oi, first, gcol, r, keep_rc, cb):
            O3 = Ops[oi][:, 0:260].rearrange("p (a b) -> p a b", a=4)
            x = st['x'] % 2
            st['x'] += 1
            TS(P, 'dve', rr[x][:], O3[:, :, 64], 1e-30, ALU.max, r=['O%d' % oi], w=['rr%d' % x])
            P.add('dve', lambda e: e.reciprocal(out=rr[x][:], in_=rr[x][:]), r=['rr%d' % x], w=['rr%d' % x])
            if keep_rc:
                COPY(P, 'dve', rcs[:, r, :], rr[x][:], r=['rr%d' % x], w=['rcs%d' % r])
            TT(P, 'dve', ww[x][:], rr[x][:], gt[cb][:, :, gcol], ALU.mult, r=['rr%d' % x, 'gt%d' % cb], w=['ww%d' % x])
            wbc = ww[x][:].unsqueeze(2).broadcast_to([128, 4, 64])
            if first:
                TT(P, 'dve', oacc[:, :, r, :], O3[:, :, 0:64], wbc, ALU.mult, r=['O%d' % oi, 'ww%d' % x], w=['oacc%d' % r])
            else:
                TT(P, 'dve', tmpo[x][:], O3[:, :, 0:64], wbc, ALU.mult, r=['O%d' % oi, 'ww%d' % x], w=['tmpo%d' % x])
                TT(P, 'pool', oacc[:, :, r, :], oacc[:, :, r, :], tmpo[x][:], ALU.add, r=['tmpo%d' % x, 'oacc%d' % r], w=['oacc%d' % r])

        def run_stream(jobs):
            n = len(jobs)
            LAG = 2
            for i in range(n + LAG):
                if i < n:
                    j = jobs[i]
                    si = st['s'] % 3
                    st['s'] += 1
                    nm = len(j['mms'])
                    for idx, (l_, r_, rk) in enumerate(j['mms']):
                        MM(P, Sps[si][:], l_, r_, idx == 0, idx == nm - 1, r=rk, w=['S%d' % si])
                    if j['pbuf'] is None:
                        pi = st['p'] % 4
                        st['p'] += 1
                        j['pbuf'] = (pt[pi], 'pt%d' % pi)
                    ACTF(P, j['pbuf'][0][:], Sps[si][:], AF.Exp, r=['S%d' % si], w=[j['pbuf'][1]])
                if i >= LAG:
                    j = jobs[i - LAG]
                    oi = j['oi']
                    if j['first']:
                        MM(P, Ops[oi][:, 0:260], zer[:, 0:128], zer[:, 0:260], True, False, r=['const'], w=['O%d' % oi], skip=True)
                    pb, pk = j['pbuf']
                    for sub in range(4):
                        MM(P, Ops[oi][:, sub * 65:(sub + 1) * 65], pb[:, sub * 128:(sub + 1) * 128], j['v'], False,
                           (j['last'] and sub == 3), r=[pk] + j['vk'], w=['O%d' % oi], skip=True)
                    if j['last']:
                        j['fin'](oi)

        it = 0
        for g in range(DBG.get('c_ng', NG)):
            DMA(P, ksT[32:96, :], K.ksT_d[g * 64:(g + 1) * 64, :], w=['ksT'])
            for q4 in range(4):
                DMA(P, vsa[:, q4 * 16:(q4 + 1) * 16, 0:64],
                    K.vs_d[q4 * 2048:(q4 + 1) * 2048, g * 64:(g + 1) * 64].rearrange("(kt p) d -> p kt d", p=128), r=['vsa'], w=['vsa'])
            for c in range(DBG.get('c_nc', NC)):
                cb = it % 2
                it += 1
                cs = slice(c * 512, (c + 1) * 512)
                ntmax = c // 4
                sel_kts, win_rs, cmp_nts = [], [], []
                for r in range(4):
                    sl = _slope(4 * g + r)
                    sel_kts.append([kt for kt in range(4 * c + 4) if sl * max(0, 512 * c - 128 * kt - 127) <= _thr()])
                    win_rs.append([rw for rw in range(4 if c == 0 else 0, 8) if sl * max(0, 385 - 128 * rw) <= _thr()])
                    cmp_nts.append([nt for nt in range(ntmax + 1)
                                    if sl * max(0.0, 512 * c - (2048 * nt + 2047.5) - 48.0) <= _thr()])
                VD = 1000 if DBG.get('one_ver') else 16
                selvers = sorted(set(kt // VD for r in range(4) for kt in sel_kts[r]))
                vers = sorted(set(selvers) | {0})
                qkeys = []
                for v in vers:
                    DMA(P, qa[cb][32:96, v, :, :], K.qT_d[g * 256:(g + 1) * 256, cs].rearrange("(r d) t -> d r t", d=64), w=['qa%d_%d' % (cb, v)])
                    DMA(P, qa[cb][96:103, v, :, :], K.c_QA[g * 4:(g + 1) * 4, :, cs].rearrange("h r t -> r h t"), w=['qaug%d_%d' % (cb, v)])
                    qkeys += ['qa%d_%d' % (cb, v), 'qaug%d_%d' % (cb, v)]
                q0k = ['qa%d_0' % cb, 'qaug%d_0' % cb, 'qsel%d' % cb]
                k0 = (c - 1) * 512
                if c == 0:
                    DMA(P, kwT[cb][32:96, 512:1024], K.kwT_d[g * 64:(g + 1) * 64, 0:512], w=['kwT%d' % cb])
                    DMA(P, kwT[cb][96:103, 512:1024], K.c_KA[:, 0:512], w=['kwTa%d' % cb])
                    DMA(P, vwa[cb][:, 4:8, 0:64], K.vw_d[0:512, g * 64:(g + 1) * 64].rearrange("(kt p) d -> p kt d", p=128), w=['vwa%d' % cb])
                else:
                    DMA(P, kwT[cb][32:96, :], K.kwT_d[g * 64:(g + 1) * 64, k0:k0 + 1024], w=['kwT%d' % cb])
                    DMA(P, kwT[cb][96:103, :], K.c_KA[:, k0:k0 + 1024], w=['kwTa%d' % cb])
                    DMA(P, vwa[cb][:, :, 0:64], K.vw_d[k0:k0 + 1024, g * 64:(g + 1) * 64].rearrange("(kt p) d -> p kt d", p=128), w=['vwa%d' % cb])
                DMA(P, gt[cb][:], K.gates_d[cs, :].rearrange("(s p) c -> p s c", p=128), w=['gt%d' % cb])
                DMA(P, zt[cb][:], K.zs_d[cs, g * 256:(g + 1) * 256].rearrange("(s p) c -> p s c", p=128), w=['zt%d' % cb])

                jobs = []
                for r in range(4):
                    for nt in cmp_nts[r]:
                        mms = [(kcmpT[0:103, g, nt * 128:(nt + 1) * 128], qa[cb][0:103, 0, r, :], q0k + ['kcmpT'])]
                        if nt == ntmax:
                            mms.append((ident[:], cmpm[:, c % 4, :], ['const']))
                        elif nt == ntmax - 1 and c % 4 == 0 and not DBG.get('no_elif'):
                            mms.append((ident[:], cmpm[:, 4, :], ['const']))
                        jobs.append(dict(mms=mms, pbuf=(pc[r][nt], 'pc%d_%d' % (r, nt)), v=vcmpa[:, nt, g, :], vk=['vcmpa'],
                                         oi=(st['o'] + r) % 2, first=(nt == cmp_nts[r][0]), last=(nt == cmp_nts[r][-1]),
                                         fin=(lambda oi, r=r: combine(oi, True, 0 * 16 + g * 4 + r, r, True, cb))))
                st['o'] += 4
                run_stream(jobs)

                for sub in range(4):
                    ti = 4 * c + sub
                    x2 = sub % 2
                    pb_ = pimp[x2]
                    pk_ = 'pimp%d' % x2
                    for r in range(4):
                        for nt in cmp_nts[r]:
                            MM(P, pb_[:, r * 128:(r + 1) * 128], pc[r][nt][:, sub * 128:(sub + 1) * 128], OV[:, nt, :],
                               nt == cmp_nts[r][0], nt == cmp_nts[r][-1], r=['pc%d_%d' % (r, nt), 'const'], w=[pk_])
                    ia, if_, wk, ma, mb = impacc[x2], impf[x2], work[x2], m8a[x2], m8b[x2]
                    kx = '_%d' % x2
                    for r in range(4):
                        if r == 0:
                            TS(P, 'dve', ia[:], pb_[:, 0:128], rcs[:, r, sub:sub + 1], ALU.mult, r=[pk_, 'rcs%d' % r], w=['impacc' + kx])
                        else:
                            STT(P, ia[:], pb_[:, r * 128:(r + 1) * 128], rcs[:, r, sub:sub + 1], ia[:], ALU.mult, ALU.add,
                                r=[pk_, 'rcs%d' % r, 'impacc' + kx], w=['impacc' + kx])
                    TT(P, 'dve', if_[:], ia[:], FT[:, 127 - 2 * ti:255 - 2 * ti], ALU.add, r=['impacc' + kx, 'const'], w=['impf' + kx])
                    MEMSET(P, 'dve', if_[:, 0:1], 1e9, r=['impf' + kx], w=['impf' + kx])
                    P.add('dve', (lambda e, ma=ma, if_=if_: e.max(out=ma[:], in_=if_[:])), r=['impf' + kx], w=['m8a' + kx])
                    P.add('dve', (lambda e, ma=ma, if_=if_, wk=wk: e.match_replace(out=wk[:], in_to_replace=ma[:], in_values=if_[:], imm_value=-3.0e38)),
                          r=['impf' + kx, 'm8a' + kx], w=['work' + kx])
                    P.add('dve', (lambda e, mb=mb, wk=wk: e.max(out=mb[:], in_=wk[:])), r=['work' + kx], w=['m8b' + kx])
                    TS(P, 'dve', selb[sub][:], if_[:], mb[:, 7:8], ALU.is_lt, DBG.get('maskv', MASKV), ALU.mult, r=['impf' + kx, 'm8b' + kx], w=['selb%d' % sub])
                    pcol = slice((sub % 2) * 128, (sub % 2) * 128 + 128)
                    pkey = 'pst%d' % (sub % 2)
                    TR(P, pst[:, pcol], selb[sub][:], ident[:], r=['selb%d' % sub, 'const'], w=[pkey])
                    for v in selvers:
                        COPY(P, 'dve', qa[cb][0:32, v, :, sub * 128:(sub + 1) * 128],
                             pst[32 * v:32 * v + 32, pcol].unsqueeze(1).broadcast_to([32, 4, 128]), r=[pkey], w=['qsel%d' % cb])

                jobs = []
                for r in range(4):
                    for rw in win_rs[r]:
                        mms = [(kwT[cb][0:103, rw * 128:(rw + 1) * 128], qa[cb][0:103, 0, r, :], q0k + ['kwT%d' % cb, 'kwTa%d' % cb, 'kwT0_%d' % cb]),
                               (ident[:], band[:, rw, :], ['const'])]
                        jobs.append(dict(mms=mms, pbuf=None, v=vwa[cb][:, rw, :], vk=['vwa%d' % cb, 'vwa1_%d' % cb],
                                         oi=(st['o'] + r) % 2, first=(rw == win_rs[r][0]), last=(rw == win_rs[r][-1]),
                                         fin=(lambda oi, r=r: combine(oi, False, 2 * 16 + g * 4 + r, r, False, cb))))
                st['o'] += 4
                run_stream(jobs)

                jobs = []
                for r in range(4):
                    for kt in sel_kts[r]:
                        mms = [(ksT[0:103, kt * 128:(kt + 1) * 128], qa[cb][0:103, kt // VD, r, :],
                                qkeys + ['qsel%d' % cb, 'ksT', 'ksTaug', 'ksToh'])]
                        if kt >= 4 * c:
                            mms.append((ident[:], causal[:, kt - 4 * c, :], ['const']))
                        jobs.append(dict(mms=mms, pbuf=None, v=vsa[:, kt, :], vk=['vsa', 'vsa1'],
                                         oi=(st['o'] + r) % 2, first=(kt == sel_kts[r][0]), last=(kt == sel_kts[r][-1]),
                                         fin=(lambda oi, r=r: combine(oi, False, 1 * 16 + g * 4 + r, r, False, cb))))
                st['o'] += 4
                run_stream(jobs)

                TT(P, 'pool', ozb[cb][:], oacc[:].rearrange("p s r d -> p s (r d)"), zt[cb][:], ALU.mult,
                   r=['oacc0', 'oacc1', 'oacc2', 'oacc3', 'zt%d' % cb], w=['ozb%d' % cb])
                DMA(P, K.oz_d[cs, g * 256:(g + 1) * 256].rearrange("(s p) d -> p s d", p=128), ozb[cb][:], r=['ozb%d' % cb], q='pool')
        P.barrier()


def out_proj_phase(P, nc, K, li, wout, hsrc, hdst, final_g, token_major_src):
    with ExitStack() as Dx:
        sb = lambda name, shape, dt: Dx.enter_context(nc.sbuf_tensor('%sD%d' % (name, li), shape, dt))
        ps = lambda name, shape, dt: Dx.enter_context(nc.psum_tensor('%sD%d' % (name, li), shape, dt))
        wb = sb('wb', [128, 8, D], BF16)
        stage = [sb('wst0', [128, D], F32), sb('wst1', [128, D], F32)]
        ozt = [sb('ozt%d' % i, [128, D], BF16) for i in range(2)]
        ozT = [sb('ozT%d' % i, [128, 8, 128], BF16) for i in range(2)]
        ht = [sb('ht%d' % i, [128, D], F32) for i in range(2)]
        hn = [sb('hn%d' % i, [128, D], F32) for i in range(2)]
        K.pT = ps('pT', [128, 8, 128], BF16)
        pm = [ps('pm%d' % i, [128, 512], F32) for i in range(4)]
        gfb = None
        if final_g is not None:
            gfb = sb('gfb', [128, D], F32)
            DMA(P, gfb[:], final_g.partition_broadcast(128), w=['gb'])
        load_weight_bf16(P, K, wb, wout, D, stage, 'wout')
        for i in range(NT):
            b = i % 2
            rows = slice(i * 128, (i + 1) * 128)
            DMA(P, ozt[b][:], K.oz_d[rows, :], w=['ozt%d' % b])
            DMA(P, ht[b][:], hsrc[rows, :], w=['ht%d' % b])
            transpose_tile(P, K, ozt[b], ozT[b][:], 'ozt%d' % b, 'ozT%d' % b, 'act' if i % 2 == 0 else 'dve')
            for half in range(2):
                pi = (2 * i + half) % 4
                for kc in range(8):
                    MM(P, pm[pi][:], ozT[b][:, kc, :], wb[:, kc, half * 512:(half + 1) * 512], kc == 0, kc == 7,
                       r=['ozT%d' % b, 'wout_wb'], w=['pm%d' % pi])
                TT(P, 'dve', hn[b][:, half * 512:(half + 1) * 512], pm[pi][:], ht[b][:, half * 512:(half + 1) * 512], ALU.add,
                   r=['pm%d' % pi, 'ht%d' % b], w=['hn%d_%d' % (b, half)])
            hk = ['hn%d_0' % b, 'hn%d_1' % b]
            if final_g is None:
                DMA(P, hdst[rows, :], hn[b][:], r=hk, q='pool')
            else:
                final_norm_store(P, K, hn[b], hk, gfb, ht[b], ['ht%d' % b], [hdst[rows, :]], [slice(0, 128)])
        P.barrier()


def final_norm_store(P, K, hn, hk, gfb, outbuf, okeys, dsts, prs):
    STT(P, K.junkf[:], hn[:], 1.0, hn[:], ALU.mult, ALU.mult, r=hk, w=['junkf', 'ssF'], accum=K.ss[:])
    ACTF(P, K.sq[:], K.ss[:], AF.Sqrt, r=['ssF'], w=['sqF'], scale=1.0 / D, bias=EPS)
    P.add('dve', lambda e: e.reciprocal(out=K.rs[:], in_=K.sq[:]), r=['sqF'], w=['rsF'])
    STT(P, outbuf[:], hn[:], K.rs[:], gfb[:], ALU.mult, ALU.mult, r=hk + ['rsF', 'gb'], w=okeys)
    for d_, pr in zip(dsts, prs):
        DMA(P, d_, outbuf[pr, :], r=okeys)


def s5_layer(P, nc, K, li, prm, hsrc, hdst, final_g):
    hv_src = hsrc.rearrange("(g n s) d -> g s n d", n=64, s=8)
    hv_dst = hdst.rearrange("(g n s) d -> g s n d", n=64, s=8)
    TWO_PI = 2.0 * math.pi

    with ExitStack() as A:
        sb = lambda name, shape, dt: A.enter_context(nc.sbuf_tensor('%sA%d' % (name, li), shape, dt))
        ps = lambda name, shape, dt: A.enter_context(nc.psum_tensor('%sA%d' % (name, li), shape, dt))
        wb = sb('wb', [128, 8, 2048], BF16)
        stage = [sb('wst0', [128, 2048], F32), sb('wst1', [128, 2048], F32)]
        gb = sb('gb', [128, D], F32)
        xt = [sb('xt0', [128, D], F32), sb('xt1', [128, D], F32)]
        xnb = [sb('xnb0', [128, D], BF16), sb('xnb1', [128, D], BF16)]
        xT4 = [sb('xT40', [128, 8, 512], BF16), sb('xT41', [128, 8, 512], BF16)]
        fo = [sb('fo%d' % i, [128, 512], BF16) for i in range(4)]
        K.pT = ps('pT', [128, 8, 128], BF16)
        pm = [ps('pm%d' % i, [128, 512], F32) for i in range(4)]
        load_weight_bf16(P, K, wb, prm['w_in'], 2048, stage, 'win')
        DMA(P, gb[:], prm['norm'].partition_broadcast(128), w=['gb'])
        cnt = 0
        for seg in range(NC):
            cb = seg % 2
            for tt in range(4):
                b = (4 * seg + tt) % 2
                DMA(P, xt[b][0:64, :], hv_src[seg, 2 * tt], w=['xt%da' % b])
                DMA(P, xt[b][64:128, :], hv_src[seg, 2 * tt + 1], w=['xt%db' % b])
                rmsnorm_tile(P, K, xt[b][:], gb[:], xnb[b][:], 'A', ['xt%da' % b, 'xt%db' % b], 'xnb%d' % b)
                transpose_tile(P, K, xnb[b], xT4[cb][:, :, tt * 128:(tt + 1) * 128], 'xnb%d' % b, 'xT4%d' % cb,
                               'act' if tt % 2 == 0 else 'dve')
            for m in range(16):
                pi = cnt % 4
                fi = cnt % 4
                cnt += 1
                for kc in range(8):
                    MM(P, pm[pi][:], wb[:, kc, m * 128:(m + 1) * 128], xT4[cb][:, kc, :], kc == 0, kc == 7,
                       r=['xT4%d' % cb, 'win_wb'], w=['pm%d' % pi])
                if m < 8:
                    COPY(P, 'dve', fo[fi][:], pm[pi][:], r=['pm%d' % pi], w=['fo%d' % fi])
                    DMA(P, K.uT_d[m * 128:(m + 1) * 128, seg * 512:(seg + 1) * 512], fo[fi][:], r=['fo%d' % fi], q='pool')
                else:
                    ACTF(P, fo[fi][:], pm[pi][:], AF.Silu, r=['pm%d' % pi], w=['fo%d' % fi])
                    DMA(P, K.zsT_d[(m - 8) * 128:(m - 7) * 128, seg * 512:(seg + 1) * 512], fo[fi][:], r=['fo%d' % fi], q='pool')
        P.barrier()
    if DBG.get('stop') == 's5p1':
        return

    with ExitStack() as B:
        sb = lambda name, shape, dt: B.enter_context(nc.sbuf_tensor('%sB%d' % (name, li), shape, dt))
        ps = lambda name, shape, dt: B.enter_context(nc.psum_tensor('%sB%d' % (name, li), shape, dt))
        identf = K.ident_f
        Tb = sb('Tb', [128, 64, 128], BF16)
        Pm = sb('Pm', [128, 64, 2, 64], BF16)
        QmR = sb('QmR', [64, 64, 8, 16], BF16)
        QmI = sb('QmI', [64, 64, 8, 16], BF16)
        ar8 = sb('ar8', [64, 64], F32)
        ai8 = sb('ai8', [64, 64], F32)
        dcol = sb('dcol', [128, 8], F32)
        pA = ps('pA', [128, 512], F32)
        pB = ps('pB', [128, 512], F32)
        pC = ps('pC', [128, 512], F32)
        pD = ps('pD', [128, 512], F32)
        pK = ps('pK', [128, 1024], F32)
        DMA(P, dcol[:], prm['d'].rearrange("g c -> (g c)").rearrange("(k p) -> p k", p=128), w=['dcol'], slow=True)

        with ExitStack() as T:
            tb = lambda name, shape, dt: T.enter_context(nc.sbuf_tensor('%sT%d' % (name, li), shape, dt))
            t64 = lambda name: tb(name, [64, 64], F32)
            lamre_g, lamim_g, lre, lim, dtb = t64('lamre_g'), t64('lamim_g'), t64('lre'), t64('lim'), t64('dtb')
            xr_, yi_, mag, sn, cs_, abr, abi = t64('xr_'), t64('yi_'), t64('mag'), t64('sn'), t64('cs_'), t64('abr'), t64('abi')
            den, nr, cfr, cfi, tA, tB, tC = t64('den'), t64('nr'), t64('cfr'), t64('cfi'), t64('tA'), t64('tB'), t64('tC')
            tI = tb('tI', [64, 64], I32)
            bre = tb('bre', [64, 64, 16], F32)
            bim = tb('bim', [64, 64, 16], F32)
            bbr = tb('bbr', [64, 64, 16], F32)
            bbi = tb('bbi', [64, 64, 16], F32)
            t3 = tb('t3', [64, 64, 16], F32)
            t4 = tb('t4', [64, 64, 16], F32)
            Lr = tb('Lr', [64, 9, 64], F32)
            Li = tb('Li', [64, 9, 64], F32)
            Wr = tb('Wr', [64, 64, 8, 16], F32)
            Wi = tb('Wi', [64, 64, 8, 16], F32)
            Cst = tb('Cst', [128, 8, 64], F32)
            CrT = tb('CrT', [64, 64, 16], F32)
            CiT = tb('CiT', [64, 64, 16], F32)
            nCrT = tb('nCrT', [64, 64, 16], F32)
            nCiT = tb('nCiT', [64, 64, 16], F32)
            Ktsb = tb('Ktsb', [128, 64, 16], F32)

            def V(eng, out, a, b, op, r, w):
                TT(P, eng, out, a, b, op, r=r, w=w)

            def VN(eng, out, a, b, op, r, w):
                TT(P, eng, out, a, b, op, r=r, w=w, nosync=True)

            DMA(P, lamre_g[:], prm['lambda_re'], w=['lamre_g'])
            DMA(P, lamim_g[:], prm['lambda_im'], w=['lamim_g'])
            TR(P, pA[0:64, 0:64], lamre_g[:], identf[0:64, 0:64], r=['lamre_g', 'identf'], w=['pA'])
            TR(P, pA[0:64, 64:128], lamim_g[:], identf[0:64, 0:64], r=['lamim_g', 'identf'], w=['pA'])
            COPY(P, 'dve', lre[:], pA[0:64, 0:64], r=['pA'], w=['lre'])
            COPY(P, 'dve', lim[:], pA[0:64, 64:128], r=['pA'], w=['lim'])
            DMA(P, dtb[:], prm['log_dt'].partition_broadcast(64), w=['dtb'])
            ACTF(P, dtb[:], dtb[:], AF.Exp, r=['dtb'], w=['dtb'])
            TS(P, 'dve', lre[:], lre[:], -1e-4, ALU.min, r=['lre'], w=['lre'])
            V('dve', xr_[:], lre[:], dtb[:], ALU.mult, ['lre', 'dtb'], ['xr_'])
            V('dve', yi_[:], lim[:], dtb[:], ALU.mult, ['lim', 'dtb'], ['yi_'])
            ACTF(P, mag[:], xr_[:], AF.Exp, r=['xr_'], w=['mag'])

            def sin_shift(out, okey, shift):
                TS(P, 'dve', tA[:], yi_[:], 1.0 / TWO_PI, ALU.mult, shift / TWO_PI, ALU.add, r=['yi_'], w=['tA'])
                COPY(P, 'dve', tI[:], tA[:], r=['tA'], w=['tI'])
                COPY(P, 'dve', tB[:], tI[:], r=['tI'], w=['tB'])
                STT(P, tC[:], tB[:], -TWO_PI, yi_[:], ALU.mult, ALU.add, r=['tB', 'yi_'], w=['tC'])
                TS(P, 'dve', tC[:], tC[:], shift, ALU.add, 3.141592, ALU.min, r=['tC'], w=['tC'])
                TS(P, 'dve', tC[:], tC[:], -3.141592, ALU.max, r=['tC'], w=['tC'])
                ACTF(P, out, tC[:], AF.Sin, r=['tC'], w=[okey])

            sin_shift(sn[:], 'sn', 0.0)
            sin_shift(cs_[:], 'cs_', math.pi / 2.0)
            V('dve', abr[:], mag[:], cs_[:], ALU.mult, ['mag', 'cs_'], ['abr'])
            V('dve', abi[:], mag[:], sn[:], ALU.mult, ['mag', 'sn'], ['abi'])
            V('dve', den[:], lre[:], lre[:], ALU.mult, ['lre'], ['den'])
            V('dve', tA[:], lim[:], lim[:], ALU.mult, ['lim'], ['tA'])
            V('dve', den[:], den[:], tA[:], ALU.add, ['den', 'tA'], ['den'])
            P.add('dve', lambda e: e.reciprocal(out=den[:], in_=den[:]), r=['den'], w=['den'])
            TS(P, 'dve', nr[:], abr[:], -1.0, ALU.add, r=['abr'], w=['nr'])
            V('dve', tA[:], nr[:], lre[:], ALU.mult, ['nr', 'lre'], ['tA'])
            V('dve', tB[:], abi[:], lim[:], ALU.mult, ['abi', 'lim'], ['tB'])
            V('dve', tA[:], tA[:], tB[:], ALU.add, ['tA', 'tB'], ['tA'])
            V('dve', cfr[:], tA[:], den[:], ALU.mult, ['tA', 'den'], ['cfr'])
            V('dve', tA[:], abi[:], lre[:], ALU.mult, ['abi', 'lre'], ['tA'])
            V('dve', tB[:], nr[:], lim[:], ALU.mult, ['nr', 'lim'], ['tB'])
            V('dve', tA[:], tA[:], tB[:], ALU.subtract, ['tA', 'tB'], ['tA'])
            V('dve', cfi[:], tA[:], den[:], ALU.mult, ['tA', 'den'], ['cfi'])
            DMA(P, bre[:], prm['b_re'].rearrange("g p c -> p g c"), w=['bre'])
            DMA(P, bim[:], prm['b_im'].rearrange("g p c -> p g c"), w=['bim'])
            bc = lambda ap2: ap2.unsqueeze(2).broadcast_to([64, 64, 16])
            V('dve', bbr[:], bre[:], bc(cfr[:]), ALU.mult, ['bre', 'cfr'], ['bbr'])
            V('dve', t3[:], bim[:], bc(cfi[:]), ALU.mult, ['bim', 'cfi'], ['t3'])
            V('dve', bbr[:], bbr[:], t3[:], ALU.subtract, ['bbr', 't3'], ['bbr'])
            V('dve', bbi[:], bim[:], bc(cfr[:]), ALU.mult, ['bim', 'cfr'], ['bbi'])
            V('dve', t3[:], bre[:], bc(cfi[:]), ALU.mult, ['bre', 'cfi'], ['t3'])
            V('dve', bbi[:], bbi[:], t3[:], ALU.add, ['bbi', 't3'], ['bbi'])
            MEMSET(P, 'dve', Lr[:, 0, :], 1.0, w=['L'])
            MEMSET(P, 'dve', Li[:, 0, :], 0.0, r=['L'], w=['L'])
            for tau in range(1, 9):
                V('dve', tA[:], Lr[:, tau - 1, :], abr[:], ALU.mult, ['L', 'abr'], ['tA'])
                V('dve', tB[:], Li[:, tau - 1, :], abi[:], ALU.mult, ['L', 'abi'], ['tB'])
                V('dve', tC[:], Lr[:, tau - 1, :], abi[:], ALU.mult, ['L', 'abi'], ['tC'])
                V('dve', den[:], Li[:, tau - 1, :], abr[:], ALU.mult, ['L', 'abr'], ['den'])
                V('dve', Lr[:, tau, :], tA[:], tB[:], ALU.subtract, ['tA', 'tB', 'L'], ['L'])
                V('dve', Li[:, tau, :], tC[:], den[:], ALU.add, ['tC', 'den', 'L'], ['L'])
            for tau in range(8):
                lrb = bc(Lr[:, tau, :])
                lib = bc(Li[:, tau, :])
                V('dve', Wr[:, :, tau, :], bbr[:], lrb, ALU.mult, ['bbr', 'L'], ['Wr'])
                V('dve', t3[:], bbi[:], lib, ALU.mult, ['bbi', 'L'], ['t3'])
                V('dve', Wr[:, :, tau, :], Wr[:, :, tau, :], t3[:], ALU.subtract, ['Wr', 't3'], ['Wr'])
                V('dve', Wi[:, :, tau, :], bbi[:], lrb, ALU.mult, ['bbi', 'L'], ['Wi'])
                V('dve', t4[:], bbr[:], lib, ALU.mult, ['bbr', 'L'], ['t4'])
                V('dve', Wi[:, :, tau, :], Wi[:, :, tau, :], t4[:], ALU.add, ['Wi', 't4'], ['Wi'])
            for nm, dstT in (('c_re', CrT), ('c_im', CiT)):
                DMA(P, Cst[:], prm[nm].rearrange("g c p -> (g c) p").rearrange("(k r) p -> r k p", r=128), r=[], w=['Cst'])
                for k in range(8):
                    TR(P, pK[0:64, k * 128:(k + 1) * 128], Cst[:, k, :], identf[:], r=['Cst', 'identf'], w=['pK'])
                COPY(P, 'dve', dstT[:].rearrange("p g c -> p (g c)"), pK[0:64, :], r=['pK'], w=[nm])
            TS(P, 'dve', nCrT[:], CrT[:], -1.0, ALU.mult, r=['c_re'], w=['nCrT'])
            TS(P, 'dve', nCiT[:], CiT[:], -1.0, ALU.mult, r=['c_im'], w=['nCiT'])
            for g in range(64):
                MM(P, pK[:, g * 16:(g + 1) * 16], Wr[:, g, :, :].rearrange("p t c -> p (t c)"), CrT[:, g, :], True, False,
                   r=['Wr', 'c_re'], w=['pK'])
                MM(P, pK[:, g * 16:(g + 1) * 16], Wi[:, g, :, :].rearrange("p t c -> p (t c)"), nCiT[:, g, :], False, True,
                   r=['Wi', 'nCiT'], w=['pK'])
            COPY(P, 'dve', Ktsb[:].rearrange("p g c -> p (g c)"), pK[:, :], r=['pK'], w=['Ktsb'])
            DMA(P, K.Kd_d.rearrange("l c g o -> (l c) g o"), Ktsb[:], r=['Ktsb'], w=['Kd'])
            for gb4 in range(16):
                pp = pA if gb4 % 2 == 0 else pB
                pk_ = 'pA' if gb4 % 2 == 0 else 'pB'
                for gi in range(4):
                    g = gb4 * 4 + gi
                    for ri, W_ in enumerate((Wr, Wi)):
                        TR(P, pp[:, (gi * 2 + ri) * 64:(gi * 2 + ri + 1) * 64], W_[:, g, :, :].rearrange("p t c -> p (t c)"),
                           identf[0:64, 0:64], r=['Wr', 'Wi', 'identf'], w=[pk_])
                COPY(P, 'dve', Pm[:, gb4 * 4:(gb4 + 1) * 4, :, :].rearrange("p g r q -> p (g r q)"), pp[:, :], r=[pk_], w=['Pm'])
            for t in range(8):
                lrb = bc(Lr[:, t + 1, :])
                lib = bc(Li[:, t + 1, :])
                V('dve', t3[:], CrT[:], lrb, ALU.mult, ['c_re', 'L'], ['t3'])
                V('dve', t4[:], nCiT[:], lib, ALU.mult, ['nCiT', 'L'], ['t4'])
                V('dve', QmR[:, :, t, :], t3[:], t4[:], ALU.add, ['t3', 't4'], ['QmR'])
                V('dve', t3[:], nCrT[:], lib, ALU.mult, ['nCrT', 'L'], ['t3'])
                V('dve', t4[:], nCiT[:], lrb, ALU.mult, ['nCiT', 'L'], ['t4'])
                V('dve', QmI[:, :, t, :], t3[:], t4[:], ALU.add, ['t3', 't4'], ['QmI'])
            COPY(P, 'dve', ar8[:], Lr[:, 8, :], r=['L'], w=['ar8'])
            COPY(P, 'dve', ai8[:], Li[:, 8, :], r=['L'], w=['ai8'])
            TAP(P, nc, 'abr', abr[:], [64, 64], F32, ['abr'])
            TAP(P, nc, 'abi', abi[:], [64, 64], F32, ['abi'])
            TAP(P, nc, 'bbr', bbr[:], [64, 64, 16], F32, ['bbr'])
            TAP(P, nc, 'bbi', bbi[:], [64, 64, 16], F32, ['bbi'])
            TAP(P, nc, 'Lr', Lr[:], [64, 9, 64], F32, ['L'])
            TAP(P, nc, 'CrT', CrT[:], [64, 64, 16], F32, ['c_re'])
            TAP(P, nc, 'Ktsb', Ktsb[:], [128, 64, 16], F32, ['Ktsb'])
            TAP(P, nc, 'Pm', Pm[:], [128, 64, 2, 64], BF16, ['Pm'])
            TAP(P, nc, 'QmR', QmR[:], [64, 64, 8, 16], BF16, ['QmR'])
            TAP(P, nc, 'QmI', QmI[:], [64, 64, 8, 16], BF16, ['QmI'])
            P.barrier()
        if DBG.get('stop') == 's5setup':
            return

        with ExitStack() as T2:
            Tsb = T2.enter_context(nc.sbuf_tensor('TsbT%d' % li, [128, 64, 8, 16], F32))
            MEMSET(P, 'pool', Tsb[:], 0.0, w=['Tsb'])
            tkeys = []
            for tp in range(8):
                for lag in range(tp + 1):
                    key = 'Tsb_%d_%d' % (tp, lag)
                    tkeys.append(key)
                    DMA(P, Tsb[16 * tp:16 * tp + 16, :, 7 - tp + lag, :], K.Kd_d[lag].rearrange("c g o -> c g o"),
                        r=['Kd', 'Tsb'], w=[key])
            COPY(P, 'dve', Tb[:].rearrange("p g m -> p (g m)"), Tsb[:].rearrange("p g t c -> p (g t c)"), r=tkeys + ['Tsb'], w=['Tb'])
            TAP(P, nc, 'Tb', Tb[:], [128, 64, 128], BF16, ['Tb'])
            P.barrier()
        if DBG.get('stop') == 's5t2':
            return

        with ExitStack() as Sg:
            sg = lambda name, shape, dt: Sg.enter_context(nc.sbuf_tensor('%sS%d' % (name, li), shape, dt))
            Sel = sg('Sel', [128, 8, 8, 128], BF16)
            SelT = sg('SelT', [128, 8, 8, 128], BF16)
            DMA(P, Sel[:], K.c_Sel, w=['const'])
            DMA(P, SelT[:], K.c_SelT, w=['const'])
            useg = [sg('useg%d' % i, [128, 8, 512], BF16) for i in range(2)]
            Uall = sg('Uall', [128, 64, 64], BF16)
            Bcr = sg('Bcr', [64, 64, 64], F32)
            Bci = sg('Bci', [64, 64, 64], F32)
            Xb2 = sg('Xb2', [64, 2, 64, 64], BF16)
            X2 = [sg('X2_%d' % i, [64, 2, 64], F32) for i in range(2)]
            A2 = sg('A2', [64, 2, 64], F32)
            C2 = sg('C2', [64, 2, 64], F32)
            t1 = sg('t1', [64, 2, 64], F32)
            t2 = sg('t2', [64, 2, 64], F32)
            s1, s2, s3, s4 = sg('s1', [64, 64], F32), sg('s2', [64, 64], F32), sg('s3', [64, 64], F32), sg('s4', [64, 64], F32)
            Ysb = [sg('Ysb%d' % i, [128, 8, 64], BF16) for i in range(2)]
            yv = [sg('yv%d' % i, [128, 512], F32) for i in range(2)]
            yg = [sg('yg%d' % i, [128, 512], BF16) for i in range(2)]
            MEMSET(P, 'dve', X2[0][:], 0.0, w=['X2_0'])
            for hh in range(2):
                COPY(P, 'dve', A2[:, hh, :], ar8[:], w=['A2'])
                COPY(P, 'dve', C2[:, hh, :], ai8[:], w=['C2'])
            step = 0
            for seg in range(NC):
                ub = seg % 2
                uk = 'useg%d' % ub
                DMA(P, useg[ub][:], K.uT_d[:, seg * 512:(seg + 1) * 512].rearrange("(k p) t -> p k t", p=128), w=[uk])
                for k in range(8):
                    for gl in range(8):
                        for s in range(8):
                            MM(P, pA[:, gl * 64:(gl + 1) * 64], Sel[:, gl, s, :], useg[ub][:, k, s * 64:(s + 1) * 64], s == 0, s == 7,
                               r=['const', uk], w=['pA'])
                    COPY(P, 'act', Uall[:, k * 8:(k + 1) * 8, :].rearrange("p g n -> p (g n)"), pA[:, :], r=['pA'], w=['Uall%d' % k])
                    for gl in range(8):
                        g = 8 * k + gl
                        MM(P, pB[0:64, gl * 64:(gl + 1) * 64], Pm[:, g, 0, :], Uall[:, g, :], True, True, r=['Uall%d' % k], w=['pB'])
                        MM(P, pC[0:64, gl * 64:(gl + 1) * 64], Pm[:, g, 1, :], Uall[:, g, :], True, True, r=['Uall%d' % k], w=['pC'])
                    COPY(P, 'dve', Bcr[:, k * 8:(k + 1) * 8, :].rearrange("p g n -> p (g n)"), pB[0:64, :], r=['pB'], w=['Bcr%d' % k])
                    COPY(P, 'act', Bci[:, k * 8:(k + 1) * 8, :].rearrange("p g n -> p (g n)"), pC[0:64, :], r=['pC'], w=['Bci%d' % k])
                bk = ['Bcr%d' % k for k in range(8)] + ['Bci%d' % k for k in range(8)]
                for n in range(64):
                    cur = step % 2
                    nxt = 1 - cur
                    step += 1
                    kx, kxn = 'X2_%d' % cur, 'X2_%d' % nxt
                    COPY(P, 'act', Xb2[:, :, :, n], X2[cur][:], r=[kx], w=['Xb2'])
                    VN('dve', t1[:], A2[:], X2[cur][:], ALU.mult, [kx, 'A2'], ['t1'])
                    VN('dve', t2[:], C2[:], X2[cur][:], ALU.mult, [kx, 'C2'], ['t2'])
                    VN('dve', s1[:], t1[:, 0, :], t2[:, 1, :], ALU.subtract, ['t1', 't2'], ['s1'])
                    VN('dve', X2[nxt][:, 0, :], s1[:], Bcr[:, :, n], ALU.add, ['s1'] + bk, [kxn])
                    VN('dve', s3[:], t1[:, 1, :], t2[:, 0, :], ALU.add, ['t1', 't2'], ['s3'])
                    VN('dve', X2[nxt][:, 1, :], s3[:], Bci[:, :, n], ALU.add, ['s3'] + bk, [kxn])
                for k in range(8):
                    yb = k % 2
                    for gl in range(8):
                        g = 8 * k + gl
                        o_ = pD[:, gl * 64:(gl + 1) * 64]
                        MM(P, o_, Tb[:, g, :], Uall[:, g, :], True, False, r=['Uall%d' % k], w=['pD'])
                        MM(P, o_, QmR[:, g, :, :].rearrange("p t c -> p (t c)"), Xb2[:, 0, g, :], False, False, r=['Xb2'], w=['pD'])
                        MM(P, o_, QmI[:, g, :, :].rearrange("p t c -> p (t c)"), Xb2[:, 1, g, :], False, True, r=['Xb2'], w=['pD'])
                    COPY(P, 'act', Ysb[yb][:].rearrange("p g n -> p (g n)"), pD[:, :], r=['pD'], w=['Ysb%d' % yb])
                    pY = pB if k % 2 == 0 else pC
                    pyk = 'pB' if k % 2 == 0 else 'pC'
                    for t in range(8):
                        for gl in range(8):
                            MM(P, pY[:, t * 64:(t + 1) * 64], SelT[:, gl, t, :], Ysb[yb][:, gl, :], gl == 0, gl == 7,
                               r=['const', 'Ysb%d' % yb], w=[pyk])
                    STT(P, yv[yb][:], useg[ub][:, k, :], dcol[:, k:k + 1], pY[:, :], ALU.mult, ALU.add, r=[uk, pyk, 'dcol'], w=['yv%d' % yb])
                    ACTF(P, yg[yb][:], yv[yb][:], AF.Gelu_apprx_tanh, r=['yv%d' % yb], w=['yg%d' % yb])
                    DMA(P, K.ygT_d[k * 128:(k + 1) * 128, seg * 512:(seg + 1) * 512], yg[yb][:], r=['yg%d' % yb], q='pool')
            P.barrier()
    if DBG.get('stop') == 's5scan':
        return

    with ExitStack() as Cx:
        sb = lambda name, shape, dt: Cx.enter_context(nc.sbuf_tensor('%sC%d' % (name, li), shape, dt))
        ps = lambda name, shape, dt: Cx.enter_context(nc.psum_tensor('%sC%d' % (name, li), shape, dt))
        wg = sb('wg', [128, 8, 2048], BF16)
        wo = sb('wo', [128, 8, D], BF16)
        stage = [sb('wst0', [128, 2048], F32), sb('wst1', [128, 2048], F32)]
        ygs = [sb('ygs%d' % i, [128, 8, 512], BF16) for i in range(2)]
        zss = [sb('zss%d' % i, [128, 8, 512], BF16) for i in range(2)]
        sig = [sb('sig%d' % i, [128, 512], F32) for i in range(2)]
        tga = [sb('tga%d' % i, [128, 512], F32) for i in range(2)]
        ozT = [sb('ozT%d' % i, [128, 8, 512], BF16) for i in range(2)]
        ht = [sb('ht%d' % i, [128, D], F32) for i in range(2)]
        hn = [sb('hn%d' % i, [128, D], F32) for i in range(2)]
        pa = [ps('pa%d' % i, [128, 512], F32) for i in range(2)]
        pb = [ps('pb%d' % i, [128, 512], F32) for i in range(2)]
        pm = [ps('pm%d' % i, [128, 512], F32) for i in range(4)]
        gfb = None
        if final_g is not None:
            gfb = sb('gfb', [128, D], F32)
            DMA(P, gfb[:], final_g.partition_broadcast(128), w=['gb'])
        load_weight_bf16(P, K, wg, prm['w_glu'], 2048, stage, 'wglu')
        load_weight_bf16(P, K, wo, prm['w_out'], D, stage, 'wout')
        for seg in range(NC):
            sbi = seg % 2
            cols = slice(seg * 512, (seg + 1) * 512)
            DMA(P, ygs[sbi][:], K.ygT_d[:, cols].rearrange("(k p) t -> p k t", p=128), w=['ygs%d' % sbi])
            DMA(P, zss[sbi][:], K.zsT_d[:, cols].rearrange("(k p) t -> p k t", p=128), w=['zss%d' % sbi])
            for m in range(8):
                x = m % 2
                for kc in range(8):
                    MM(P, pa[x][:], wg[:, kc, m * 128:(m + 1) * 128], ygs[sbi][:, kc, :], kc == 0, kc == 7,
                       r=['ygs%d' % sbi, 'wglu_wb'], w=['pa%d' % x])
                for kc in range(8):
                    MM(P, pb[x][:], wg[:, kc, 1024 + m * 128:1024 + (m + 1) * 128], ygs[sbi][:, kc, :], kc == 0, kc == 7,
                       r=['ygs%d' % sbi, 'wglu_wb'], w=['pb%d' % x])
                ACTF(P, sig[x][:], pb[x][:], AF.Sigmoid, r=['pb%d' % x], w=['sig%d' % x])
                TT(P, 'dve', tga[x][:], pa[x][:], sig[x][:], ALU.mult, r=['pa%d' % x, 'sig%d' % x], w=['tga%d' % x])
                TT(P, 'pool', ozT[sbi][:, m, :], tga[x][:], zss[sbi][:, m, :], ALU.mult, r=['tga%d' % x, 'zss%d' % sbi], w=['ozT%d_%d' % (sbi, m)])
            ozk = ['ozT%d_%d' % (sbi, m) for m in range(8)]
            for tt in range(4):
                b = (4 * seg + tt) % 2
                DMA(P, ht[b][0:64, :], hv_src[seg, 2 * tt], w=['ht%da' % b])
                DMA(P, ht[b][64:128, :], hv_src[seg, 2 * tt + 1], w=['ht%db' % b])
                for half in range(2):
                    pi = (2 * tt + half) % 4
                    for kc in range(8):
                        MM(P, pm[pi][:], ozT[sbi][:, kc, tt * 128:(tt + 1) * 128], wo[:, kc, half * 512:(half + 1) * 512], kc == 0, kc == 7,
                           r=ozk + ['wout_wb'], w=['pm%d' % pi])
                    TT(P, 'dve', hn[b][:, half * 512:(half + 1) * 512], pm[pi][:], ht[b][:, half * 512:(half + 1) * 512], ALU.add,
                       r=['pm%d' % pi, 'ht%da' % b, 'ht%db' % b], w=['hn%d_%d' % (b, half)])
                hk = ['hn%d_0' % b, 'hn%d_1' % b]
                dsts = [hv_dst[seg, 2 * tt], hv_dst[seg, 2 * tt + 1]]
                prs = [slice(0, 64), slice(64, 128)]
                if final_g is None:
                    for d_, pr in zip(dsts, prs):
                        DMA(P, d_, hn[b][pr, :], r=hk)
                else:
                    final_norm_store(P, K, hn[b], hk, gfb, ht[b], ['ht%da' % b, 'ht%db' % b], dsts, prs)
        P.barrier()


def build(layers=(0, 1, 2, 3), with_final=True):
    nc = bass.Bass("TRN2", target_bir_lowering=False)
    K = Ctx()
    x = nc.dram_tensor("x", [S, D], F32, kind="ExternalInput").ap()
    y = nc.dram_tensor("y", [S, D], F32, kind="ExternalOutput").ap()
    prm = {}
    for li in layers:
        prm[li] = {}
        for nm, shp in (NSA_PARAMS if li % 2 == 0 else S5_PARAMS):
            prm[li][nm] = nc.dram_tensor("l%d_%s" % (li, nm), shp, F32, kind="ExternalInput").ap()
    fng = nc.dram_tensor("final_norm", [D], F32, kind="ExternalInput").ap()
    for nm, shp, dt in CONST_SPECS:
        setattr(K, 'c_' + nm, nc.dram_tensor("c_" + nm, shp, dt, kind="ExternalInput").ap())
    scr = lambda nm, shp, dt: nc.dram_tensor("scr_" + nm, shp, dt, kind=("ExternalOutput" if nm in DBG.get('dump', ()) else "Internal")).ap()
    K.qT_d = scr('qT', [1024, S], BF16)
    K.kcT_d = scr('kcT', [256, S], BF16)
    K.vcT_d = scr('vcT', [256, S], BF16)
    K.ksT_d = scr('ksT', [256, S], BF16)
    K.kwT_d = scr('kwT', [256, S], BF16)
    K.vs_d = scr('vs', [S, 256], BF16)
    K.vw_d = scr('vw', [S, 256], BF16)
    K.gates_d = scr('gates', [S, 48], F32)
    K.zs_d = scr('zs', [S, 1024], BF16)
    K.oz_d = scr('oz', [S, 1024], BF16)
    K.uT_d = scr('uT', [1024, S], BF16)
    K.zsT_d = scr('zsT', [1024, S], BF16)
    K.ygT_d = scr('ygT', [1024, S], BF16)
    K.Kd_d = scr('Kd', [8, 16, 64, 16], F32)
    P = Prog(nc)
    with ExitStack() as G:
        gs = lambda name, shape, dt: G.enter_context(nc.sbuf_tensor(name, shape, dt))
        K.ident_b = gs('ident_b', [128, 128], BF16)
        K.ident_f = gs('ident_f', [128, 128], F32)
        K.junkf = gs('junkf', [128, D], F32)
        K.ss = gs('ss', [128, 1], F32)
        K.sq = gs('sq', [128, 1], F32)
        K.rs = gs('rs', [128, 1], F32)
        DMA(P, K.ident_b[:], K.c_ident_b, w=['const'])
        DMA(P, K.ident_f[:], K.c_ident_f, w=['identf'])
        P.barrier()
        hsrc = x
        for li in layers:
            fg = fng if (with_final and li == layers[-1]) else None
            if li % 2 == 0:
                nsa_layer(P, nc, K, li, prm[li], hsrc, y, fg)
            else:
                s5_layer(P, nc, K, li, prm[li], hsrc, y, fg)
            hsrc = y
        run_prog(nc, P)
    return nc


ALL_INPUT_NAMES = (
    'x',
    'l0_norm',
    'l0_w_in',
    'l0_cmp_k_pe',
    'l0_cmp_k_w1',
    'l0_cmp_k_w2',
    'l0_cmp_v_pe',
    'l0_cmp_v_w1',
    'l0_cmp_v_w2',
    'l0_w_out',
    'l1_norm',
    'l1_w_in',
    'l1_log_dt',
    'l1_lambda_re',
    'l1_lambda_im',
    'l1_b_re',
    'l1_b_im',
    'l1_c_re',
    'l1_c_im',
    'l1_d',
    'l1_w_glu',
    'l1_w_out',
    'l2_norm',
    'l2_w_in',
    'l2_cmp_k_pe',
    'l2_cmp_k_w1',
    'l2_cmp_k_w2',
    'l2_cmp_v_pe',
    'l2_cmp_v_w1',
    'l2_cmp_v_w2',
    'l2_w_out',
    'l3_norm',
    'l3_w_in',
    'l3_log_dt',
    'l3_lambda_re',
    'l3_lambda_im',
    'l3_b_re',
    'l3_b_im',
    'l3_c_re',
    'l3_c_im',
    'l3_d',
    'l3_w_glu',
    'l3_w_out',
    'final_norm',
)


_NC_CACHE = {}


def make_in_map(inputs, b, layers=(0, 1, 2, 3)):
    C = host_constants()
    m = {'x': np.ascontiguousarray(inputs['x'][b], dtype=np.float32)}
    for li in layers:
        for nm, shp in (NSA_PARAMS if li % 2 == 0 else S5_PARAMS):
            key = 'l%d_%s' % (li, nm)
            m[key] = np.ascontiguousarray(inputs[key], dtype=np.float32)
    m['final_norm'] = np.ascontiguousarray(inputs['final_norm'], dtype=np.float32)
    for nm, shp, dt in CONST_SPECS:
        m['c_' + nm] = C[nm]
    return m


def kernel(**inputs):
    inputs = {k: np.asarray(inputs[k]) for k in ALL_INPUT_NAMES}
    if 'full' not in _NC_CACHE:
        _NC_CACHE['full'] = build()
    nc = _NC_CACHE['full']
    in_maps = [make_in_map(inputs, c % 4) for c in range(8)]
    res = run_bass_kernel_spmd(nc, in_maps, core_ids=list(range(8)))
    out = np.stack([np.asarray(res.results[b]['y'], dtype=np.float32) for b in range(4)], axis=0)
    return out
```
